# Optimizing a Trainium2 kernel written in Bass

```python
import math
import jax, jax.numpy as jnp
from jax import lax
import numpy as np

D_MODEL = 1024
BATCH = 8
SEQ = 2048
DEPTH = 4

A_GROUPS = ((128, 1), (512, 4), (2048, 16))
A_HEADS_PER_GROUP = 2
A_HEADS = 6
A_HEAD_DIM = 64
A_WIDTH = A_HEADS * A_HEAD_DIM
B_HEADS = 4
B_HEAD_DIM = 64
B_WIDTH = B_HEADS * 2 * B_HEAD_DIM
C_HEADS = 4
C_KEY_DIM = 64
C_VAL_DIM = 128
C_QK_WIDTH = C_HEADS * C_KEY_DIM
C_WIDTH = C_HEADS * C_VAL_DIM
RET_CHUNK = 128
RET_THETA = 10000.0
ROPE_THETA = 500000.0
ROPE_FRACTION = 4
Q_BLOCK = 128
D_FF = 4 * D_MODEL
N_BRANCHES = 3
NORM_EPS = 1e-6
NEG_INF = -1e30
IN_SPLITS = (A_WIDTH, A_WIDTH, A_WIDTH,
             B_WIDTH, B_WIDTH, B_WIDTH,
             C_QK_WIDTH, C_QK_WIDTH, C_WIDTH, C_WIDTH,
             N_BRANCHES * D_MODEL)
N_IN = sum(IN_SPLITS)

kernel_name = "hybrid_dilated_diff_retention_encoder"


def rms_norm(t, g):
    tf = t.astype(jnp.float32)
    y = tf * lax.rsqrt(jnp.mean(tf * tf, axis=-1, keepdims=True) + NORM_EPS)
    return y.astype(t.dtype) * g


def rope_tables(seq, rot_dims, theta):
    inv = 1.0 / (theta ** (jnp.arange(0, rot_dims, 2, dtype=jnp.float32) / rot_dims))
    ang = jnp.arange(seq, dtype=jnp.float32)[:, None] * inv[None, :]
    return jnp.cos(ang), jnp.sin(ang)


def apply_rope(t, cos, sin):
    r2 = cos.shape[-1]
    shape = (t.shape[1],) + (1,) * (t.ndim - 3) + (r2,)
    c = cos.reshape(shape).astype(t.dtype)
    s = sin.reshape(shape).astype(t.dtype)
    t1, t2, tp = t[..., :r2], t[..., r2:2 * r2], t[..., 2 * r2:]
    return jnp.concatenate([t1 * c - t2 * s, t2 * c + t1 * s, tp], axis=-1)


def dilated_window_attention(q, k, v, dilation, half):
    B, S, H, d = q.shape
    L = S // dilation
    blk = half
    nb = -(-L // blk)
    Lp = nb * blk

    def to_sub(t):
        return t.reshape(B, L, dilation, H, d).transpose(0, 2, 3, 1, 4)

    qs, ks, vs = to_sub(q), to_sub(k), to_sub(v)
    qb = jnp.pad(qs, ((0, 0),) * 3 + ((0, Lp - L), (0, 0))).reshape(B, dilation, H, nb, blk, d)

    def windows(t):
        tb = jnp.pad(t, ((0, 0),) * 3 + ((blk, Lp - L + blk), (0, 0))).reshape(B, dilation, H, nb + 2, blk, d)
        return jnp.concatenate([tb[:, :, :, :-2], tb[:, :, :, 1:-1], tb[:, :, :, 2:]], axis=-2)

    kw, vw = windows(ks), windows(vs)
    qi = jnp.arange(nb)[:, None, None] * blk + jnp.arange(blk)[None, :, None]
    kj = jnp.arange(nb)[:, None, None] * blk - blk + jnp.arange(3 * blk)[None, None, :]
    valid = (jnp.abs(qi - kj) <= half) & (kj >= 0) & (kj < L)

    s = jnp.einsum('brhnqd,brhnkd->brhnqk', qb, kw).astype(jnp.float32) * (d ** -0.5)
    s = jnp.where(valid, s, NEG_INF)
    m = jnp.max(s, axis=-1, keepdims=True)
    e = jnp.exp(s - m)
    den = jnp.sum(e, axis=-1, keepdims=True)
    p = e / den
    lse = (m + jnp.log(den))[..., 0]
    o = jnp.einsum('brhnqk,brhnkd->brhnqd', p.astype(v.dtype), vw)
    o = o.reshape(B, dilation, H, Lp, d)[:, :, :, :L].transpose(0, 3, 1, 2, 4).reshape(B, S, H, d)
    lse = lse.reshape(B, dilation, H, Lp)[..., :L].transpose(0, 3, 1, 2).reshape(B, S, H)
    return o, lse


def dilated_mixer(q, k, v):
    B, S = q.shape[:2]
    outs, lses = [], []
    for g, (window, dilation) in enumerate(A_GROUPS):
        hs = slice(g * A_HEADS_PER_GROUP, (g + 1) * A_HEADS_PER_GROUP)
        o, lse = dilated_window_attention(q[:, :, hs], k[:, :, hs], v[:, :, hs], dilation, window // (2 * dilation))
        outs.append(o)
        lses.append(lse)
    alpha = jax.nn.softmax(jnp.stack(lses, axis=2), axis=2)
    o = jnp.stack(outs, axis=2) * alpha[..., None].astype(v.dtype)
    return o.reshape(B, S, A_WIDTH)


def diff_attention(q, k, v, lam):
    B, S, H, _, d = q.shape
    nb = S // Q_BLOCK
    qb = q.reshape(B, nb, Q_BLOCK, H, 2, d).transpose(1, 0, 2, 3, 4, 5)
    scale = d ** -0.5

    def block(qblk):
        s = jnp.einsum('bqhmd,bkhmd->bhmqk', qblk, k).astype(jnp.float32) * scale
        p = jax.nn.softmax(s, axis=-1)
        w = p[:, :, 0] - lam * p[:, :, 1]
        return jnp.einsum('bhqk,bkhe->bqhe', w.astype(v.dtype), v)

    out = lax.map(block, qb)
    return out.transpose(1, 0, 2, 3, 4).reshape(B, S, H, 2 * d)


def retention_direction(q, k, v, log_gamma):
    B, S, H, dk = q.shape
    dv = v.shape[-1]
    C = RET_CHUNK
    n = S // C
    pos = jnp.arange(C, dtype=jnp.float32)
    lg = log_gamma.astype(jnp.float32)
    rel = pos[:, None] - pos[None, :]
    decay_in = jnp.where(rel >= 0, jnp.exp(lg[:, None, None] * jnp.maximum(rel, 0.0)), 0.0)
    decay_q = jnp.exp(lg[:, None] * (pos + 1.0))[..., None]
    decay_k = jnp.exp(lg[:, None] * (C - 1.0 - pos))[..., None]
    decay_chunk = jnp.exp(lg * C)[:, None, None]

    def chunks(t):
        return t.reshape(B, n, C, H, t.shape[-1]).transpose(1, 0, 3, 2, 4)

    def step(state, inp):
        qc, kc, vc = inp
        scores = jnp.einsum('bhqd,bhkd->bhqk', qc, kc) * decay_in
        out = (jnp.einsum('bhqk,bhkv->bhqv', scores, vc)
               + jnp.einsum('bhqd,bhdv->bhqv', qc, state) * decay_q)
        state = decay_chunk * state + jnp.einsum('bhkd,bhkv->bhdv', kc * decay_k, vc)
        return state, out

    state0 = jnp.zeros((B, H, dk, dv), jnp.float32)
    _, ys = lax.scan(step, state0, (chunks(q), chunks(k), chunks(v)))
    return ys.transpose(1, 0, 3, 2, 4).reshape(B, S, H, dv)


def bidirectional_retention(q, k, v, decay_f, decay_b):
    qf, kf, vf = q.astype(jnp.float32), k.astype(jnp.float32), v.astype(jnp.float32)
    fwd = retention_direction(qf, kf, vf, -jnp.exp(decay_f.astype(jnp.float32)))
    flip = lambda t: jnp.flip(t, axis=1)
    bwd = flip(retention_direction(flip(qf), flip(kf), flip(vf), -jnp.exp(decay_b.astype(jnp.float32))))
    return (fwd + bwd).astype(v.dtype)


def setup_inputs(seed: int = 0) -> dict:
    key = jax.random.key(seed)
    ks = jax.random.split(key, 24)
    f32 = jnp.float32

    def nrm(k, shape, scale):
        return jax.random.normal(k, shape, f32) * scale

    def gain(k, shape):
        return 1.0 + 0.02 * jax.random.normal(k, shape, f32)

    base = jnp.log(-jnp.log1p(-(2.0 ** (-5.0 - jnp.arange(C_HEADS, dtype=f32)))))
    return {
        "x": jax.random.normal(ks[0], (BATCH, SEQ, D_MODEL), f32),
        "norm1_g": gain(ks[1], (DEPTH, D_MODEL)),
        "w_in": nrm(ks[2], (DEPTH, D_MODEL, N_IN), D_MODEL ** -0.5),
        "a_q_norm_g": gain(ks[3], (DEPTH, A_HEAD_DIM)),
        "a_k_norm_g": gain(ks[4], (DEPTH, A_HEAD_DIM)),
        "b_q_norm_g": gain(ks[5], (DEPTH, B_HEAD_DIM)),
        "b_k_norm_g": gain(ks[6], (DEPTH, B_HEAD_DIM)),
        "b_lambda_q1": nrm(ks[7], (DEPTH, B_HEAD_DIM), 0.1),
        "b_lambda_k1": nrm(ks[8], (DEPTH, B_HEAD_DIM), 0.1),
        "b_lambda_q2": nrm(ks[9], (DEPTH, B_HEAD_DIM), 0.1),
        "b_lambda_k2": nrm(ks[10], (DEPTH, B_HEAD_DIM), 0.1),
        "b_out_norm_g": gain(ks[11], (DEPTH, 2 * B_HEAD_DIM)),
        "c_decay_f": base[None, :] + nrm(ks[12], (DEPTH, C_HEADS), 0.05),
        "c_decay_b": base[None, :] + nrm(ks[13], (DEPTH, C_HEADS), 0.05),
        "c_out_norm_g": gain(ks[14], (DEPTH, C_VAL_DIM)),
        "w_br_a": nrm(ks[15], (DEPTH, A_WIDTH, D_MODEL), A_WIDTH ** -0.5),
        "w_br_b": nrm(ks[16], (DEPTH, B_WIDTH, D_MODEL), B_WIDTH ** -0.5),
        "w_br_c": nrm(ks[17], (DEPTH, C_WIDTH, D_MODEL), C_WIDTH ** -0.5),
        "w_o": nrm(ks[18], (DEPTH, D_MODEL, D_MODEL), D_MODEL ** -0.5),
        "norm2_g": gain(ks[19], (DEPTH, D_MODEL)),
        "w_mlp1": nrm(ks[20], (DEPTH, D_MODEL, D_FF), D_MODEL ** -0.5),
        "w_mlp2": nrm(ks[21], (DEPTH, D_FF, D_MODEL), D_FF ** -0.5),
    }


def reference(x, norm1_g, w_in, a_q_norm_g, a_k_norm_g, b_q_norm_g, b_k_norm_g,
              b_lambda_q1, b_lambda_k1, b_lambda_q2, b_lambda_k2, b_out_norm_g,
              c_decay_f, c_decay_b, c_out_norm_g, w_br_a, w_br_b, w_br_c, w_o,
              norm2_g, w_mlp1, w_mlp2):
    B, S, D = x.shape
    cos_p, sin_p = rope_tables(S, A_HEAD_DIM // ROPE_FRACTION, ROPE_THETA)
    cos_r, sin_r = rope_tables(S, C_KEY_DIM, RET_THETA)
    split_pts = [int(p) for p in np.cumsum(IN_SPLITS)[:-1]]

    for l in range(DEPTH):
        xn = rms_norm(x, norm1_g[l])
        proj = xn @ w_in[l]
        aq, ak, av, bq, bk, bv, cq, ck, cv, cg, gates = jnp.split(proj, split_pts, axis=-1)

        aq = apply_rope(rms_norm(aq.reshape(B, S, A_HEADS, A_HEAD_DIM), a_q_norm_g[l]), cos_p, sin_p)
        ak = apply_rope(rms_norm(ak.reshape(B, S, A_HEADS, A_HEAD_DIM), a_k_norm_g[l]), cos_p, sin_p)
        av = av.reshape(B, S, A_HEADS, A_HEAD_DIM)
        ya = dilated_mixer(aq, ak, av)

        lam_init = 0.8 - 0.6 * math.exp(-0.3 * l)
        lam = (jnp.exp(jnp.sum(b_lambda_q1[l] * b_lambda_k1[l]))
               - jnp.exp(jnp.sum(b_lambda_q2[l] * b_lambda_k2[l])) + lam_init)
        bq = apply_rope(rms_norm(bq.reshape(B, S, B_HEADS, 2, B_HEAD_DIM), b_q_norm_g[l]), cos_p, sin_p)
        bk = apply_rope(rms_norm(bk.reshape(B, S, B_HEADS, 2, B_HEAD_DIM), b_k_norm_g[l]), cos_p, sin_p)
        bv = bv.reshape(B, S, B_HEADS, 2 * B_HEAD_DIM)
        yb = diff_attention(bq, bk, bv, lam)
        yb = (rms_norm(yb, b_out_norm_g[l]) * (1.0 - lam_init)).reshape(B, S, B_WIDTH)

        cq = apply_rope(cq.reshape(B, S, C_HEADS, C_KEY_DIM), cos_r, sin_r)
        ck = apply_rope(ck.reshape(B, S, C_HEADS, C_KEY_DIM), cos_r, sin_r) * (C_KEY_DIM ** -0.5)
        cv = cv.reshape(B, S, C_HEADS, C_VAL_DIM)
        yc = bidirectional_retention(cq, ck, cv, c_decay_f[l], c_decay_b[l])
        yc = jax.nn.silu(cg) * rms_norm(yc, c_out_norm_g[l]).reshape(B, S, C_WIDTH)

        g = jax.nn.sigmoid(gates.reshape(B, S, N_BRANCHES, D))
        merged = (g[:, :, 0] * (ya @ w_br_a[l])
                  + g[:, :, 1] * (yb @ w_br_b[l])
                  + g[:, :, 2] * (yc @ w_br_c[l]))
        x = x + merged @ w_o[l]

        h = jnp.square(jax.nn.relu(rms_norm(x, norm2_g[l]) @ w_mlp1[l]))
        x = x + h @ w_mlp2[l]
    return x
```

```python
import math
from contextlib import ExitStack
import numpy as np
import ml_dtypes
import concourse.bass as bass
import concourse.mybir as mybir
from concourse.bass_utils import run_bass_kernel_spmd

F32, BF16 = mybir.dt.float32, mybir.dt.bfloat16
ALU, AF = mybir.AluOpType, mybir.ActivationFunctionType

S = 2048
D = 1024
KC = 8
NB = 4
TB = 512
DEPTH = 4
EPS = 1e-6
NPC = 38
C_AQ, C_AK, C_AV = 0, 3, 6
C_BQ, C_BK, C_BV = 9, 13, 17
C_CQ, C_CK, C_CV, C_CG = 21, 23, 25, 29
C_GATE = 33
NWSLOT = 5


class Res:
    __slots__ = ("w", "r", "x")

    def __init__(self, x=False):
        self.w = None
        self.r = {}
        self.x = x


class Prog:
    ENG = ("pe", "act", "dve", "pool", "sp")

    def __init__(self):
        self.q = {e: [] for e in self.ENG}
        self.cnt = {e: 0 for e in self.ENG}
        self.seen = {e: {} for e in self.ENG}
        self.dcnt = {}

    def _need(self, eng, dep, waits, raw):
        if dep is None:
            return
        k, n = dep
        if k == eng:
            if eng == "pe" or not raw:
                return
            if n <= self.cnt[eng] - 2:
                return
        if k in self.dcnt:
            n = max(n, self.dcnt[k])
        if self.seen[eng].get(k, 0) >= n:
            return
        waits[k] = max(waits.get(k, 0), n)

    def _deps(self, eng, rd, wr):
        waits = {}
        for r in rd:
            self._need(eng, r.w, waits, True)
        for w in wr:
            self._need(eng, w.w, waits, False)
            for k, n in w.r.items():
                self._need(eng, (k, n), waits, False)
        for k, n in waits.items():
            self.seen[eng][k] = n
        return tuple(waits.items())

    def op(self, eng, fn, rd=(), wr=()):
        wr = list(wr) + [r for r in rd if r.x]
        rd = [r for r in rd if not r.x]
        waits = self._deps(eng, rd, wr)
        self.cnt[eng] += 1
        n = self.cnt[eng]
        self.q[eng].append((waits, fn, (eng, 1)))
        for r in rd:
            r.r[eng] = max(r.r.get(eng, 0), n)
        for w in wr:
            w.w = (eng, n)
            w.r = {}

    def dma(self, qeng, semkey, fn, rd=(), wr=()):
        waits = self._deps(qeng, rd, wr)
        self.dcnt[semkey] = self.dcnt.get(semkey, 0) + 16
        n = self.dcnt[semkey]
        self.q[qeng].append((waits, fn, (semkey, 16)))
        for r in rd:
            r.r[semkey] = max(r.r.get(semkey, 0), n)
        for w in wr:
            w.w = (semkey, n)
            w.r = {}

    def barrier(self, dma_keys=()):
        for e in self.ENG:
            waits = {}
            for f in self.ENG:
                if f != e and self.cnt[f] > self.seen[e].get(f, 0):
                    waits[f] = self.cnt[f]
            for k in dma_keys:
                if self.dcnt.get(k, 0) > self.seen[e].get(k, 0):
                    waits[k] = self.dcnt[k]
            for k, n in waits.items():
                self.seen[e][k] = n
            if waits:
                self.q[e].append((tuple(waits.items()), None, None))


class _Stop(Exception):
    pass


def build(nl, dbg=False, stop=None, l0=0):
    nc = bass.Bass("TRN2", target_bir_lowering=False)

    def din(name, shape, dt=F32):
        return nc.dram_tensor(name, list(shape), dt, kind="ExternalInput").ap()

    xT_d = din("xT", [D, S])
    w_in_d = din("w_in", [nl, 57, 128, 8, 128])
    w_br_d = din("w_br", [nl, 8, 128, 11, 128])
    w_o_d = din("w_o", [nl, 8, 128, 8, 128])
    w_m1_d = din("w_m1", [nl, 32, 128, 8, 128])
    w_m2_d = din("w_m2", [nl, 32, 128, 8, 128])
    pcol_d = din("pcol", [nl, 128, NPC])
    arow_d = din("arow", [nl, 4, 128, 128])
    cmat_d = din("c_mats", [5, 128, 128])
    cmask_d = din("c_masks", [7, 128, 512])
    cf32_d = din("c_f32", [128, 7 * 128 + 2])
    rope_d = din("rope", [4, 128, S])
    out_d = nc.dram_tensor("outT", [D, S], F32, kind="ExternalOutput").ap()
    xsp_d = nc.dram_tensor("xspill", [D, S], F32, kind="Internal").ap()
    if dbg:
        ydbg_d = nc.dram_tensor("ydbg", [11 * 128, S], BF16, kind="ExternalOutput").ap()
        xdbg_d = nc.dram_tensor("xdbg", [D, S], F32, kind="ExternalOutput").ap()

    P = Prog()
    es = ExitStack()

    def sb(name, shape, dt):
        return es.enter_context(nc.sbuf_tensor(name, list(shape), dt))

    XN = sb("XN", [128, KC * S], BF16)
    XY = sb("XY", [128, KC * S], F32)
    M = sb("M", [128, KC * S], BF16)
    WR = sb("WR", [128, NWSLOT * 11 * 128], BF16)
    CMB = sb("CMB", [128, 5 * 128], BF16)
    MASKS = sb("MASKS", [128, 7 * 512], BF16)
    CF = sb("CF", [128, 7 * 128 + 2], F32)
    PCOL = sb("PCOL", [128, NPC], F32)
    SMALL = sb("SMALL", [128, 64], F32)
    CTAB = sb("CTAB", [128, 13 * 128], F32)
    CDER = sb("CDER", [128, 6 * 2048], BF16)
    S32 = sb("S32", [128, 2 * 128], F32)
    SQB = sb("SQB", [128, 2 * 512], BF16)
    UB = sb("UB", [128, 2 * 512], BF16)
    FT = sb("FT", [128, 8 * 512], F32)

    Ybf = XY[:, 0:11 * 1024].bitcast(BF16)
    ROPE = XY[:, 11 * 1024:11 * 1024 + 4096]
    PB = XY[:, 15 * 1024:16 * 1024].bitcast(BF16)
    DTOT = CDER[:, 4 * 2048:6 * 2048].bitcast(F32)

    def x_(kc, tb):
        return XY[:, kc * S + tb * TB: kc * S + (tb + 1) * TB]

    def xn_(kc, tb):
        return XN[:, kc * S + tb * TB: kc * S + (tb + 1) * TB]

    def xnfull(kc):
        return XN[:, kc * S:(kc + 1) * S]

    def y_(ch):
        return Ybf[:, ch * S:(ch + 1) * S]

    def m_(kc, tb):
        return M[:, kc * S + tb * TB: kc * S + (tb + 1) * TB]

    def uq(s):
        return M[:, s * 8192: s * 8192 + 2048]

    def uk(s):
        return M[:, s * 8192 + 2048: s * 8192 + 4096]

    def uv(s):
        return M[:, s * 8192 + 4096: s * 8192 + 8192].rearrange("p (t c) -> p t c", c=256)

    def wslot(s, nk=8):
        return WR[:, s * 1408: s * 1408 + nk * 128].rearrange("p (k n) -> p k n", n=128)

    ONES = CMB[:, 0:128]
    BONES = CMB[:, 128:256]
    RAB = CMB[:, 256:384]
    RC = CMB[:, 384:512]
    IDENT = CMB[:, 512:640]

    def mask_(i):
        return MASKS[:, i * 512:(i + 1) * 512]

    RELF, RELB = CF[:, 0:128], CF[:, 128:256]
    MKF8, MKB8 = CF[:, 256:384], CF[:, 384:512]
    POS1, POSB = CF[:, 512:640], CF[:, 640:768]
    ONESF = CF[:, 768:896]
    TOKC, TOKCR = CF[:, 896:897], CF[:, 897:898]

    def ctab(i):
        return CTAB[:, i * 128:(i + 1) * 128]

    def cder(i):
        return CDER[:, i * 2048:(i + 1) * 2048]

    KDF, KDB = cder(0).rearrange("p (t c) -> p t c", c=128), cder(1).rearrange("p (t c) -> p t c", c=128)
    QDF, QDB = cder(2), cder(3)
    STF, STB = cder(4).rearrange("p (t c) -> p t c", c=128), cder(5).rearrange("p (t c) -> p t c", c=128)

    def pb(i):
        return PB[:, i * 512:(i + 1) * 512]

    def sqb(i):
        return SQB[:, i * 512:(i + 1) * 512]

    def ub(i):
        return UB[:, i * 512:(i + 1) * 512]

    def ft(i):
        return FT[:, i * 512:(i + 1) * 512]

    PS = [es.enter_context(nc.psum_tensor("ps%d" % i, [128, 512], F32)) for i in range(8)]

    r_x = [[Res() for _ in range(NB)] for _ in range(KC)]
    r_xn = [Res() for _ in range(NB)]
    r_y = [[Res() for _ in range(NB)] for _ in range(11)]
    r_yall = [Res() for _ in range(11)]
    r_m = [[Res() for _ in range(NB)] for _ in range(KC)]
    r_uq = [Res(), Res()]
    r_uk = [Res(), Res()]
    r_uv = [Res(), Res()]
    r_w = [Res() for _ in range(NWSLOT)]
    r_ps = [Res(True) for _ in range(8)]
    r_pb = [Res() for _ in range(4)]
    r_sqb = [Res(), Res()]
    r_ub = [Res(), Res()]
    r_ft = [Res() for _ in range(8)]
    r_rope = Res()
    r_const = Res()
    r_pcol = Res()
    r_small = Res()
    r_ctab = Res()
    r_cder = [Res() for _ in range(6)]
    r_s32 = [Res(), Res()]
    r_xsp = Res()

    wstate = {"n": 0}

    def load_w(dram_ap, nk=8):
        s = wstate["n"] % NWSLOT
        wstate["n"] += 1
        dst = wslot(s, nk)
        P.dma("pool", "w%d" % s, lambda e, dst=dst, src=dram_ap: e.dma_start(out=dst, in_=src), wr=[r_w[s]])
        return s

    class WStream:
        def __init__(self):
            self.pending = []
            self.loaded = []
            self.inuse = 0

        def push(self, ap, nk=8):
            self.pending.append((ap, nk))

        def prefetch(self):
            while self.pending and len(self.loaded) + self.inuse < NWSLOT:
                ap, nk = self.pending.pop(0)
                self.loaded.append(load_w(ap, nk))

        def pop(self):
            self.prefetch()
            s = self.loaded.pop(0)
            self.inuse += 1
            return s

        def release(self, s):
            self.inuse -= 1
            self.prefetch()

    WS = WStream()

    def mm(out, lhsT, rhs, start, stop, rd, wr):
        P.op("pe", lambda e: e.matmul(out, lhsT=lhsT, rhs=rhs, start=start, stop=stop), rd=rd, wr=wr)

    def act(out, in_, func, rd, wr, scale=None, bias=None):
        kw = {}
        if scale is not None:
            kw["scale"] = scale
        if bias is not None:
            kw["bias"] = bias
        P.op("act", lambda e: e.activation(out=out, in_=in_, func=func, **kw), rd=rd, wr=wr)

    def tt(eng, out, in0, in1, op, rd, wr):
        P.op(eng, lambda e: e.tensor_tensor(out=out, in0=in0, in1=in1, op=op), rd=rd, wr=wr)

    def stt(eng, out, in0, scalar, in1, op0, op1, rd, wr):
        P.op(eng, lambda e: e.scalar_tensor_tensor(out=out, in0=in0, scalar=scalar, in1=in1, op0=op0, op1=op1),
             rd=rd, wr=wr)

    def ts(eng, out, in0, s1, s2, op0, op1, rd, wr):
        if op1 is None:
            P.op(eng, lambda e: e.tensor_scalar(out=out, in0=in0, scalar1=s1, scalar2=None, op0=op0), rd=rd, wr=wr)
        else:
            P.op(eng, lambda e: e.tensor_scalar(out=out, in0=in0, scalar1=s1, scalar2=s2, op0=op0, op1=op1),
                 rd=rd, wr=wr)

    def cp(eng, out, in_, rd, wr):
        if eng == "act":
            act(out, in_, AF.Copy, rd, wr)
        else:
            P.op(eng, lambda e: e.tensor_copy(out=out, in_=in_), rd=rd, wr=wr)

    def recip(out, in_, rd, wr):
        P.op("dve", lambda e: e.reciprocal(out=out, in_=in_), rd=rd, wr=wr)

    def rstd_from(ps_ap, r_in, ftile, r_ftile, inv_n):
        act(ftile, ps_ap, AF.Ln, rd=[r_in], wr=[r_ftile], scale=inv_n, bias=EPS)
        act(ftile, ftile, AF.Exp, rd=[r_ftile], wr=[r_ftile], scale=-0.5)

    rot = {"ps": 0, "alt": 0}

    P.dma("pool", "cst", lambda e: e.dma_start(out=CMB[:, :].rearrange("p (c n) -> p c n", n=128),
                                                  in_=cmat_d.rearrange("c p n -> p c n")), wr=[r_const])
    P.dma("pool", "cst", lambda e: e.dma_start(out=MASKS[:, :].rearrange("p (c n) -> p c n", n=512),
                                                  in_=cmask_d.rearrange("c p n -> p c n")), wr=[r_const])
    P.dma("sp", "cst", lambda e: e.dma_start(out=CF[:, :], in_=cf32_d), wr=[r_const])
    for kc in range(KC):
        P.dma("sp", "xld", lambda e, kc=kc: e.dma_start(out=XY[:, kc * S:(kc + 1) * S],
                                                       in_=xT_d[kc * 128:(kc + 1) * 128, :]),
              wr=r_x[kc])

    def norm(gcol0):
        for tb in range(NB):
            for kc in range(KC):
                i = kc % 2
                act(sqb(i), x_(kc, tb), AF.Square, rd=[r_x[kc][tb]], wr=[r_sqb[i]])
                mm(PS[2][:, :], ONES, sqb(i), kc == 0, kc == KC - 1, rd=[r_sqb[i], r_const], wr=[r_ps[2]])
            rstd_from(PS[2][:, :], r_ps[2], ft(0), r_ft[0], 1.0 / D)
            for kc in range(KC):
                stt("dve", xn_(kc, tb), x_(kc, tb), PCOL[:, gcol0 + kc: gcol0 + kc + 1], ft(0), ALU.mult, ALU.mult,
                    rd=[r_x[kc][tb], r_ft[0], r_pcol], wr=[r_xn[tb]])

    def proj_fm(slot, tb, bank):
        w = wslot(slot)
        for kc in range(KC):
            mm(PS[bank][:, :], w[:, kc, :], xn_(kc, tb), kc == 0, kc == KC - 1,
               rd=[r_w[slot], r_xn[tb]], wr=[r_ps[bank]])

    def perm_dst(buf, tb, delta):
        if delta == 1:
            return buf[:, tb * TB:(tb + 1) * TB], None
        n0 = tb * TB // delta
        return buf.rearrange("p (r n) -> p n r", r=delta)[:, n0:n0 + TB // delta, :], delta

    def srcv(ap, delta):
        if delta is None:
            return ap
        return ap.rearrange("p (n r) -> p n r", r=delta)

    def qk_post(bank, tb, gcol, rmat, normed, dst_buf, r_dst, delta):
        j = rot["alt"] % 2
        rot["alt"] += 1
        pj = PS[bank][:, :]
        cos = ROPE[:, tb * TB:(tb + 1) * TB]
        sin = ROPE[:, S + tb * TB: S + (tb + 1) * TB]
        dst, dl = perm_dst(dst_buf, tb, delta)
        if normed:
            act(sqb(j), pj, AF.Square, rd=[r_ps[bank]], wr=[r_sqb[j]])
            mm(PS[2][:, :], BONES, sqb(j), True, True, rd=[r_sqb[j], r_const], wr=[r_ps[2]])
            rstd_from(PS[2][:, :], r_ps[2], ft(j), r_ft[j], 1.0 / 64)
            stt("dve", ub(j), pj, PCOL[:, gcol:gcol + 1], ft(j), ALU.mult, ALU.mult,
                rd=[r_ps[bank], r_ft[j], r_pcol], wr=[r_ub[j]])
            mm(PS[3][:, :], rmat, ub(j), True, True, rd=[r_ub[j], r_const], wr=[r_ps[3]])
            tt("pool", ft(2 + j), ub(j), cos, ALU.mult, rd=[r_ub[j], r_rope], wr=[r_ft[2 + j]])
        else:
            cp("act", ub(j), pj, rd=[r_ps[bank]], wr=[r_ub[j]])
            mm(PS[3][:, :], rmat, ub(j), True, True, rd=[r_ub[j], r_const], wr=[r_ps[3]])
            tt("dve", ft(2 + j), pj, cos, ALU.mult, rd=[r_ps[bank], r_rope], wr=[r_ft[2 + j]])
        tt("dve", ft(4 + j), PS[3][:, :], sin, ALU.mult, rd=[r_ps[3], r_rope], wr=[r_ft[4 + j]])
        tt("pool", dst, srcv(ft(2 + j), dl), srcv(ft(4 + j), dl), ALU.add,
           rd=[r_ft[2 + j], r_ft[4 + j]], wr=[r_dst])

    def v_proj(slot, set_, coloff, delta):
        w = wslot(slot)
        V = uv(set_)
        for g4 in range(4):
            bank = rot["ps"] % 2
            rot["ps"] += 1
            for ti in range(4):
                t = g4 * 4 + ti
                for kc in range(KC):
                    if delta == 1:
                        lhs = xnfull(kc)[:, t * 128:(t + 1) * 128]
                    elif delta == 4:
                        lhs = xnfull(kc).rearrange("p (n r) -> p r n", r=4)[:, t // 4, (t % 4) * 128:(t % 4 + 1) * 128]
                    else:
                        lhs = xnfull(kc).rearrange("p (n r) -> p r n", r=16)[:, t, :]
                    mm(PS[bank][:, ti * 128:(ti + 1) * 128], lhs, w[:, kc, :], kc == 0, kc == KC - 1,
                       rd=[r_w[slot]] + r_xn, wr=[r_ps[bank]])
            cp("act", V[:, g4 * 4:(g4 + 1) * 4, coloff:coloff + 128],
               PS[bank][:, :].rearrange("p (t c) -> p t c", c=128), rd=[r_ps[bank]], wr=[r_uv[set_]])
            yield

    units = []
    for g in range(3):
        units.append(dict(kind="A", idx=g, q=C_AQ + g, k=C_AK + g, v=[C_AV + g], gq=16, gk=17, delta=(1, 4, 16)[g]))
    for h in range(4):
        units.append(dict(kind="B", idx=h, q=C_BQ + h, k=C_BK + h, v=[C_BV + h], gq=18, gk=19, delta=1))
    for p in range(2):
        units.append(dict(kind="C", idx=p, q=C_CQ + p, k=C_CK + p, v=[C_CV + 2 * p, C_CV + 2 * p + 1], gq=0, gk=0,
                          delta=1))

    def push_unit_weights(l, u):
        WS.push(w_in_d[l, u["q"]])
        WS.push(w_in_d[l, u["k"]])
        for c in u["v"]:
            WS.push(w_in_d[l, c])

    def rope_load(which):
        P.dma("sp", "rope", lambda e: e.dma_start(out=ROPE[:, 0:S], in_=rope_d[2 * which]), wr=[r_rope])
        P.dma("sp", "rope", lambda e: e.dma_start(out=ROPE[:, S:2 * S], in_=rope_d[2 * which + 1]), wr=[r_rope])

    def inproj(l, ui):
        u = units[ui]
        set_ = ui % 2
        normed = u["kind"] != "C"
        rmat = RC if u["kind"] == "C" else RAB
        if ui == 0:
            rope_load(0)
        if ui == 7:
            rope_load(1)
        for which, buf, rres, gcol in (("q", uq(set_), r_uq[set_], u["gq"]), ("k", uk(set_), r_uk[set_], u["gk"])):
            slot = WS.pop()
            for tb in range(NB):
                bank = rot["ps"] % 2
                rot["ps"] += 1
                proj_fm(slot, tb, bank)
                qk_post(bank, tb, gcol, rmat, normed, buf, rres, u["delta"])
                yield
            WS.release(slot)
        for vi, c in enumerate(u["v"]):
            slot = WS.pop()
            yield from v_proj(slot, set_, vi * 128, u["delta"])
            WS.release(slot)

    def attn_B(l, ui):
        u = units[ui]
        h = u["idx"]
        set_ = ui % 2
        qT, kT, V = uq(set_), uk(set_), uv(set_)
        rdq = [r_uq[set_], r_uk[set_]]
        ch = 3 + h
        for qb in range(NB):
            for m in range(2):
                rows = slice(64 * m, 64 * m + 64)

                def smm(kt):
                    bank = 4 + kt % 2
                    mm(PS[bank][:, :], kT[rows, kt * 128:(kt + 1) * 128], qT[rows, qb * TB:(qb + 1) * TB], True, True,
                       rd=rdq, wr=[r_ps[bank]])

                smm(0)
                for kt in range(16):
                    if kt + 1 < 16:
                        smm(kt + 1)
                    bank = 4 + kt % 2
                    pi = kt % 3
                    act(pb(pi), PS[bank][:, :], AF.Exp, rd=[r_ps[bank]], wr=[r_pb[pi]], scale=0.125)
                    mm(PS[6][:, :], V[:, kt, 0:128], pb(pi), kt == 0, kt == 15, rd=[r_pb[pi], r_uv[set_]],
                       wr=[r_ps[6]])
                    mm(PS[7][:, :], ONES, pb(pi), kt == 0, kt == 15, rd=[r_pb[pi], r_const], wr=[r_ps[7]])
                recip(ft(6), PS[7][:, :], rd=[r_ps[7]], wr=[r_ft[6]])
                if m == 0:
                    tt("dve", ft(7), PS[6][:, :], ft(6), ALU.mult, rd=[r_ps[6], r_ft[6]], wr=[r_ft[7]])
                else:
                    tt("dve", ft(6), PS[6][:, :], ft(6), ALU.mult, rd=[r_ps[6], r_ft[6]], wr=[r_ft[6]])
                    stt("dve", ft(7), ft(6), SMALL[:, 0:1], ft(7), ALU.mult, ALU.add,
                        rd=[r_ft[6], r_ft[7], r_small], wr=[r_ft[7]])
                    act(pb(3), ft(7), AF.Square, rd=[r_ft[7]], wr=[r_pb[3]])
                    mm(PS[2][:, :], ONES, pb(3), True, True, rd=[r_pb[3], r_const], wr=[r_ps[2]])
                    rstd_from(PS[2][:, :], r_ps[2], ft(6), r_ft[6], 1.0 / 128)
                    stt("dve", y_(ch)[:, qb * TB:(qb + 1) * TB], ft(7), SMALL[:, 1:2], ft(6), ALU.mult, ALU.mult,
                        rd=[r_ft[7], r_ft[6], r_small], wr=[r_y[ch][qb], r_yall[ch]])
                yield

    def attn_A(l, ui):
        u = units[ui]
        g = u["idx"]
        delta = u["delta"]
        set_ = ui % 2
        qT, kT, V = uq(set_), uk(set_), uv(set_)
        rdq = [r_uq[set_], r_uk[set_]]
        for s2 in range(2):
            rows = slice(64 * s2, 64 * s2 + 64)
            for qb in range(NB):
                if g < 2:
                    rels = list(range(-1, 5)) if g == 0 else list(range(4))
                    kts = [(4 * qb + r, r) for r in rels if 0 <= 4 * qb + r < 16]
                    n = len(kts)

                    def smm(i):
                        kt, rel = kts[i]
                        bank = 4 + i % 2
                        mm(PS[bank][:, :], kT[rows, kt * 128:(kt + 1) * 128], qT[rows, qb * TB:(qb + 1) * TB], True,
                           True, rd=rdq, wr=[r_ps[bank]])

                    smm(0)
                    for i in range(n):
                        if i + 1 < n:
                            smm(i + 1)
                        kt, rel = kts[i]
                        bank = 4 + i % 2
                        pi = i % 3
                        act(pb(pi), PS[bank][:, :], AF.Exp, rd=[r_ps[bank]], wr=[r_pb[pi]], scale=0.125)
                        tt("pool", pb(pi), pb(pi), mask_(rel + 1), ALU.mult, rd=[r_pb[pi], r_const], wr=[r_pb[pi]])
                        mm(PS[6][:, :], V[:, kt, 0:128], pb(pi), i == 0, i == n - 1, rd=[r_pb[pi], r_uv[set_]],
                           wr=[r_ps[6]])
                        mm(PS[7][:, :], ONES, pb(pi), i == 0, i == n - 1, rd=[r_pb[pi], r_const], wr=[r_ps[7]])
                else:
                    bank = 4 + qb % 2
                    for rel in range(4):
                        kt = 4 * qb + rel
                        mm(PS[bank][:, rel * 128:(rel + 1) * 128], kT[rows, kt * 128:(kt + 1) * 128],
                           qT[rows, kt * 128:(kt + 1) * 128], True, True, rd=rdq, wr=[r_ps[bank]])
                    pi = qb % 3
                    act(pb(pi), PS[bank][:, :], AF.Exp, rd=[r_ps[bank]], wr=[r_pb[pi]], scale=0.125)
                    tt("pool", pb(pi), pb(pi), mask_(6), ALU.mult, rd=[r_pb[pi], r_const], wr=[r_pb[pi]])
                    for rel in range(4):
                        kt = 4 * qb + rel
                        mm(PS[6][:, rel * 128:(rel + 1) * 128], V[:, kt, 0:128], pb(pi)[:, rel * 128:(rel + 1) * 128],
                           True, True, rd=[r_pb[pi], r_uv[set_]], wr=[r_ps[6]])
                        mm(PS[7][:, rel * 128:(rel + 1) * 128], ONES, pb(pi)[:, rel * 128:(rel + 1) * 128],
                           True, True, rd=[r_pb[pi], r_const], wr=[r_ps[7]])
                if delta == 1:
                    ydst = y_(g)[rows, qb * TB:(qb + 1) * TB]
                    ddst = DTOT[rows, qb * TB:(qb + 1) * TB]
                    nsrc, dsrc = PS[6][rows, :], PS[7][rows, :]
                elif delta == 4:
                    ydst = y_(g).rearrange("p (n r) -> p r n", r=4)[rows, qb, :]
                    ddst = DTOT[:, :].rearrange("p (n r) -> p r n", r=4)[rows, qb, :]
                    nsrc, dsrc = PS[6][rows, :], PS[7][rows, :]
                else:
                    ydst = y_(g).rearrange("p (n r) -> p r n", r=16)[rows, 4 * qb:4 * qb + 4, :]
                    ddst = DTOT[:, :].rearrange("p (n r) -> p r n", r=16)[rows, 4 * qb:4 * qb + 4, :]
                    nsrc = PS[6][rows, :].rearrange("p (a b) -> p a b", a=4)
                    dsrc = PS[7][rows, :].rearrange("p (a b) -> p a b", a=4)
                cp("act", ydst, nsrc, rd=[r_ps[6]], wr=[r_yall[g]] + r_y[g])
                if g == 0:
                    cp("dve", ddst, dsrc, rd=[r_ps[7]], wr=[r_cder[4], r_cder[5]])
                else:
                    tt("dve", ddst, ddst, dsrc, ALU.add, rd=[r_ps[7], r_cder[4], r_cder[5]], wr=[r_cder[4], r_cder[5]])
                yield
        if g == 2:
            for tb in range(NB):
                recip(ft(6), DTOT[:, tb * TB:(tb + 1) * TB], rd=[r_cder[4], r_cder[5]], wr=[r_ft[6]])
                for gg in range(3):
                    ysl = y_(gg)[:, tb * TB:(tb + 1) * TB]
                    tt("pool", ysl, ysl, ft(6), ALU.mult, rd=[r_ft[6], r_yall[gg]] + r_y[gg], wr=[r_y[gg][tb], r_yall[gg]])
                yield

    def attn_C(l, ui):
        u = units[ui]
        p = u["idx"]
        set_ = ui % 2
        qT, kT, V = uq(set_), uk(set_), uv(set_)
        DQF, DQB, DKF, DKB = ctab(4 + p), ctab(6 + p), ctab(8 + p), ctab(10 + p)
        g128f, g128b = SMALL[:, 8 + p:9 + p], SMALL[:, 10 + p:11 + p]
        for g4 in range(4):
            bank = rot["ps"] % 2
            rot["ps"] += 1
            tp = PS[bank][:, :].bitcast(BF16)[:, 0:512]
            for ti in range(4):
                t = g4 * 4 + ti
                P.op("pe", lambda e, t=t, ti=ti, tp=tp: e.transpose(tp[:, ti * 128:(ti + 1) * 128],
                                                                      kT[:, t * 128:(t + 1) * 128], IDENT),
                     rd=[r_uk[set_], r_const], wr=[r_ps[bank]])
            tp3 = tp.rearrange("p (t c) -> p t c", c=128)
            tt("dve", KDF[:, g4 * 4:(g4 + 1) * 4, :], tp3, DKF.unsqueeze(1).to_broadcast([128, 4, 128]), ALU.mult,
               rd=[r_ps[bank], r_ctab], wr=[r_cder[0]])
            tt("dve", KDB[:, g4 * 4:(g4 + 1) * 4, :], tp3, DKB.unsqueeze(1).to_broadcast([128, 4, 128]), ALU.mult,
               rd=[r_ps[bank], r_ctab], wr=[r_cder[1]])
            q3 = qT[:, g4 * TB:(g4 + 1) * TB].rearrange("p (t c) -> p t c", c=128)
            tt("pool", QDF[:, g4 * TB:(g4 + 1) * TB].rearrange("p (t c) -> p t c", c=128), q3,
               DQF.unsqueeze(1).to_broadcast([128, 4, 128]), ALU.mult, rd=[r_uq[set_], r_ctab], wr=[r_cder[2]])
            tt("pool", QDB[:, g4 * TB:(g4 + 1) * TB].rearrange("p (t c) -> p t c", c=128), q3,
               DQB.unsqueeze(1).to_broadcast([128, 4, 128]), ALU.mult, rd=[r_uq[set_], r_ctab], wr=[r_cder[3]])
            yield
        P.op("pool", lambda e: e.memset(S32[:, :], 0.0), wr=r_s32)
        P.op("pool", lambda e: e.memset(STF[:, 0, :], 0.0), wr=[r_cder[4]])
        P.op("pool", lambda e: e.memset(STB[:, 15, :], 0.0), wr=[r_cder[5]])
        for step in range(15):
            for d in range(2):
                c = step if d == 0 else 15 - step
                KD = KDF if d == 0 else KDB
                ST = STF if d == 0 else STB
                gcol = g128f if d == 0 else g128b
                s32 = S32[:, d * 128:(d + 1) * 128]
                bank = rot["ps"] % 2
                rot["ps"] += 1
                mm(PS[bank][:, 0:256], KD[:, c, :], V[:, c, :], True, True, rd=[r_cder[d], r_uv[set_]],
                   wr=[r_ps[bank]])
                for hh in range(2):
                    rws = slice(64 * hh, 64 * hh + 64)
                    stt("dve", s32[rws, :], s32[rws, :], gcol[rws, :], PS[bank][rws, hh * 128:(hh + 1) * 128],
                        ALU.mult, ALU.add, rd=[r_ps[bank], r_s32[d], r_small], wr=[r_s32[d]])
                cn = c + 1 if d == 0 else c - 1
                cp("act", ST[:, cn, :], s32, rd=[r_s32[d]], wr=[r_cder[4 + d]])
            if step % 4 == 3:
                yield
        yield
        for hh in range(2):
            rws = slice(64 * hh, 64 * hh + 64)
            h = 2 * p + hh
            ch = 7 + h
            for cb in range(4):
                sbank = 4 + cb % 2
                for ci in range(4):
                    c = 4 * cb + ci
                    mm(PS[sbank][:, ci * 128:(ci + 1) * 128], kT[rws, c * 128:(c + 1) * 128],
                       qT[rws, c * 128:(c + 1) * 128], True, True, rd=[r_uq[set_], r_uk[set_]], wr=[r_ps[sbank]])
                pi = cb % 3
                tt("dve", pb(pi).rearrange("p (t c) -> p t c", c=128),
                   PS[sbank][:, :].rearrange("p (t c) -> p t c", c=128),
                   ctab(h).unsqueeze(1).to_broadcast([128, 4, 128]), ALU.mult, rd=[r_ps[sbank], r_ctab],
                   wr=[r_pb[pi]])
                for ci in range(4):
                    c = 4 * cb + ci
                    o = PS[6][:, ci * 128:(ci + 1) * 128]
                    mm(o, V[:, c, hh * 128:(hh + 1) * 128], pb(pi)[:, ci * 128:(ci + 1) * 128], True, False,
                       rd=[r_pb[pi], r_uv[set_]], wr=[r_ps[6]])
                    mm(o, STF[rws, c, :], QDF[rws, c * 128:(c + 1) * 128], False, False,
                       rd=[r_cder[4], r_cder[2]], wr=[r_ps[6]])
                    mm(o, STB[rws, c, :], QDB[rws, c * 128:(c + 1) * 128], False, True,
                       rd=[r_cder[5], r_cder[3]], wr=[r_ps[6]])
                cp("dve", ft(7), PS[6][:, :], rd=[r_ps[6]], wr=[r_ft[7]])
                act(pb(3), PS[6][:, :], AF.Square, rd=[r_ps[6]], wr=[r_pb[3]])
                mm(PS[2][:, :], ONES, pb(3), True, True, rd=[r_pb[3], r_const], wr=[r_ps[2]])
                rstd_from(PS[2][:, :], r_ps[2], ft(6), r_ft[6], 1.0 / 128)
                stt("dve", y_(ch)[:, cb * TB:(cb + 1) * TB], ft(7), PCOL[:, 21:22], ft(6), ALU.mult, ALU.mult,
                    rd=[r_ft[7], r_ft[6], r_pcol], wr=[r_y[ch][cb], r_yall[ch]])
                yield

    def layer_params(l):
        lam_init = 0.8 - 0.6 * math.exp(-0.3 * (l + l0))
        P.dma("sp", "par", lambda e: e.dma_start(out=PCOL[:, :], in_=pcol_d[l]), wr=[r_pcol])
        P.dma("sp", "par", lambda e: e.dma_start(out=CTAB[:, 8 * 128:12 * 128].rearrange("p (c n) -> p c n", n=128),
                                                   in_=arow_d[l].rearrange("c p n -> p c n")), wr=[r_ctab])
        rp = [r_pcol]
        tt("dve", SMALL[:, 2:3], PCOL[:, 22:23], PCOL[:, 23:24], ALU.mult, rd=rp, wr=[r_small])
        tt("dve", SMALL[:, 3:4], PCOL[:, 24:25], PCOL[:, 25:26], ALU.mult, rd=rp, wr=[r_small])
        mm(PS[2][:, 0:2], ONESF[0:64, :], SMALL[0:64, 2:4], True, True, rd=[r_small, r_const], wr=[r_ps[2]])
        act(SMALL[:, 4:6], PS[2][:, 0:2], AF.Exp, rd=[r_ps[2]], wr=[r_small])
        tt("dve", SMALL[:, 6:7], SMALL[:, 5:6], SMALL[:, 4:5], ALU.subtract, rd=[r_small], wr=[r_small])
        ts("dve", SMALL[:, 0:1], SMALL[:, 6:7], -lam_init, None, ALU.add, None, rd=[r_small], wr=[r_small])
        ts("dve", SMALL[:, 1:2], PCOL[:, 20:21], 1.0 - lam_init, None, ALU.mult, None, rd=rp + [r_small],
           wr=[r_small])
        act(SMALL[:, 12:16], PCOL[:, 26:30], AF.Exp, rd=rp + [r_small], wr=[r_small])
        ts("dve", SMALL[:, 12:16], SMALL[:, 12:16], -1.0, None, ALU.mult, None, rd=[r_small], wr=[r_small])
        act(SMALL[:, 8:12], SMALL[:, 12:16], AF.Exp, rd=[r_small], wr=[r_small], scale=128.0)
        act(SMALL[:, 16:24], PCOL[:, 30:38], AF.Exp, rd=rp + [r_small], wr=[r_small])
        ts("dve", SMALL[:, 16:24], SMALL[:, 16:24], -1.0, None, ALU.mult, None, rd=[r_small], wr=[r_small])
        rs = [r_small, r_const]
        for h in range(4):
            act(ctab(h), RELF, AF.Exp, rd=rs, wr=[r_ctab], scale=SMALL[:, 16 + h:17 + h])
            tt("dve", ctab(h), ctab(h), MKF8, ALU.mult, rd=[r_ctab, r_const], wr=[r_ctab])
            act(ctab(12), RELB, AF.Exp, rd=rs + [r_ctab], wr=[r_ctab], scale=SMALL[:, 20 + h:21 + h])
            tt("dve", ctab(12), ctab(12), MKB8, ALU.mult, rd=[r_ctab, r_const], wr=[r_ctab])
            tt("dve", ctab(h), ctab(h), ctab(12), ALU.add, rd=[r_ctab], wr=[r_ctab])
        for p in range(2):
            act(ctab(4 + p), POS1, AF.Exp, rd=rs + [r_ctab], wr=[r_ctab], scale=SMALL[:, 12 + p:13 + p])
            act(ctab(6 + p), POSB, AF.Exp, rd=rs + [r_ctab], wr=[r_ctab], scale=SMALL[:, 14 + p:15 + p])
            ts("dve", ctab(4 + p), ctab(4 + p), 0.125, None, ALU.mult, None, rd=[r_ctab], wr=[r_ctab])
            ts("dve", ctab(6 + p), ctab(6 + p), 0.125, None, ALU.mult, None, rd=[r_ctab], wr=[r_ctab])
            for d in range(2):
                dstt = ctab(8 + 2 * d + p)
                src = dstt
                act(dstt, src, AF.Exp, rd=rp + [r_ctab], wr=[r_ctab])
                ts("dve", dstt, dstt, -1.0, None, ALU.mult, None, rd=[r_ctab], wr=[r_ctab])
                act(dstt, dstt, AF.Exp, rd=[r_ctab, r_const], wr=[r_ctab], scale=(TOKCR if d == 0 else TOKC))

    def interleave(a, b):
        da = db = False
        while not (da and db):
            if not da:
                try:
                    next(a)
                except StopIteration:
                    da = True
            if b is None:
                db = True
            if not db:
                try:
                    next(b)
                except StopIteration:
                    db = True

    def drain(g):
        for _ in g:
            pass

    stop_state = {}
    attn_fn = {"A": attn_A, "B": attn_B, "C": attn_C}

    def ck(name):
        if stop == name:
            raise _Stop()

    def layer(l):
        layer_params(l)
        ck('params')
        for u in (units if stop not in ('xc', 'xc2') else units[7:]):
            push_unit_weights(l, u)
        norm(0)
        ck('norm1')
        for kc in range(KC):
            P.dma("sp", "xsp", lambda e, kc=kc: e.dma_start(out=xsp_d[kc * 128:(kc + 1) * 128, :],
                                                           in_=XY[:, kc * S:(kc + 1) * S]),
                  rd=r_x[kc], wr=[r_xsp])
        P.barrier(dma_keys=("xsp",))
        ck('spill')
        if stop in ('xc', 'xc2'):
            drain(inproj(l, 7))
            ck('xc')
            drain(attn_C(l, 7))
            ck('xc2')
        drain(inproj(l, 0))
        ck('inproj0')
        for ui in range(len(units)):
            a = attn_fn[units[ui]["kind"]](l, ui)
            b = inproj(l, ui + 1) if ui + 1 < len(units) else None
            interleave(a, b)
            ck('u%d' % ui)
        P.barrier()
        ck('mix')
        if dbg and l == 0:
            P.dma("sp", "dbg", lambda e: e.dma_start(out=ydbg_d.rearrange("(c p) s -> p c s", p=128),
                                                       in_=Ybf.rearrange("p (c s) -> p c s", s=S)), rd=[], wr=[])
            P.barrier(dma_keys=("dbg",))
        for h in range(4):
            WS.push(w_in_d[l, C_CG + h])
        for j in range(8):
            for i in range(3):
                WS.push(w_in_d[l, C_GATE + 8 * i + j])
            WS.push(w_br_d[l, j], 11)
        for oc in range(8):
            WS.push(w_o_d[l, oc])

        def nbank():
            b = rot["ps"] % 8
            rot["ps"] += 1
            return b

        for h in range(4):
            slot = WS.pop()
            ch = 7 + h
            for tb in range(NB):
                bank = nbank()
                proj_fm(slot, tb, bank)
                j2 = tb % 2
                act(ub(j2), PS[bank][:, :], AF.Silu, rd=[r_ps[bank]], wr=[r_ub[j2]])
                ysl = y_(ch)[:, tb * TB:(tb + 1) * TB]
                tt("pool", ysl, ysl, ub(j2), ALU.mult, rd=[r_ub[j2], r_y[ch][tb], r_yall[ch]], wr=[r_y[ch][tb]])
            WS.release(slot)
        ybr = ((0, 3), (3, 7), (7, 11))
        for j in range(8):
            gslots = [WS.pop() for _ in range(3)]
            bslot = WS.pop()
            wb = wslot(bslot, 11)
            for tb in range(NB):
                for i in range(3):
                    gb = nbank()
                    proj_fm(gslots[i], tb, gb)
                    act(ft(i), PS[gb][:, :], AF.Sigmoid, rd=[r_ps[gb]], wr=[r_ft[i]])
                    bb = nbank()
                    c0, c1 = ybr[i]
                    for ch in range(c0, c1):
                        mm(PS[bb][:, :], wb[:, ch, :], y_(ch)[:, tb * TB:(tb + 1) * TB], ch == c0, ch == c1 - 1,
                           rd=[r_w[bslot], r_y[ch][tb], r_yall[ch]], wr=[r_ps[bb]])
                    if i == 0:
                        tt("dve", ft(3), PS[bb][:, :], ft(0), ALU.mult, rd=[r_ps[bb], r_ft[0]], wr=[r_ft[3]])
                    elif i == 1:
                        tt("dve", ft(4), PS[bb][:, :], ft(1), ALU.mult, rd=[r_ps[bb], r_ft[1]], wr=[r_ft[4]])
                        tt("pool", ft(3), ft(3), ft(4), ALU.add, rd=[r_ft[3], r_ft[4]], wr=[r_ft[3]])
                    else:
                        tt("dve", ft(5), PS[bb][:, :], ft(2), ALU.mult, rd=[r_ps[bb], r_ft[2]], wr=[r_ft[5]])
                        tt("pool", m_(j, tb), ft(3), ft(5), ALU.add, rd=[r_ft[3], r_ft[5]], wr=[r_m[j][tb]])
            for s_ in gslots + [bslot]:
                WS.release(s_)
        P.barrier()
        if stop == '4a':
            for kc in range(KC):
                P.dma('pool', 'dbg', lambda e, kc=kc: e.dma_start(out=out_d[kc * 128:(kc + 1) * 128, :], in_=M[:, kc * S:(kc + 1) * S]), rd=[], wr=[])
            P.barrier(dma_keys=('dbg',))
            stop_state['skip_out'] = True
        ck('4a')
        for kc in range(KC):
            P.dma("sp", "xld", lambda e, kc=kc: e.dma_start(out=XY[:, kc * S:(kc + 1) * S],
                                                           in_=xsp_d[kc * 128:(kc + 1) * 128, :]),
                  rd=[r_xsp], wr=r_x[kc])
        for hg in range(4):
            for c in range(8):
                WS.push(w_m1_d[l, hg * 8 + c])
            for oc in range(8):
                WS.push(w_m2_d[l, hg * 8 + oc])
        for oc in range(8):
            slot = WS.pop()
            w = wslot(slot)
            for tb in range(NB):
                bank = nbank()
                for kc in range(KC):
                    mm(PS[bank][:, :], w[:, kc, :], m_(kc, tb), kc == 0, kc == KC - 1,
                       rd=[r_w[slot], r_m[kc][tb]], wr=[r_ps[bank]])
                tt("dve", x_(oc, tb), x_(oc, tb), PS[bank][:, :], ALU.add, rd=[r_ps[bank], r_x[oc][tb]],
                   wr=[r_x[oc][tb]])
            WS.release(slot)
        ck('4b')
        norm(8)
        P.barrier()
        ck('norm2')
        for hg in range(4):
            for c in range(8):
                slot = WS.pop()
                for tb in range(NB):
                    bank = nbank()
                    proj_fm(slot, tb, bank)
                    j2 = (c * NB + tb) % 2
                    act(ft(j2), PS[bank][:, :], AF.Relu, rd=[r_ps[bank]], wr=[r_ft[j2]])
                    tt("pool", m_(c, tb), ft(j2), ft(j2), ALU.mult, rd=[r_ft[j2]], wr=[r_m[c][tb]])
                WS.release(slot)
            for oc in range(8):
                slot = WS.pop()
                w = wslot(slot)
                for tb in range(NB):
                    bank = nbank()
                    for c in range(8):
                        mm(PS[bank][:, :], w[:, c, :], m_(c, tb), c == 0, c == 7,
                           rd=[r_w[slot], r_m[c][tb]], wr=[r_ps[bank]])
                    tt("dve", x_(oc, tb), x_(oc, tb), PS[bank][:, :], ALU.add, rd=[r_ps[bank], r_x[oc][tb]],
                       wr=[r_x[oc][tb]])
                WS.release(slot)
        P.barrier()
        if dbg and l == 0:
            for kc in range(KC):
                P.dma("sp", "dbg", lambda e, kc=kc: e.dma_start(out=xdbg_d[kc * 128:(kc + 1) * 128, :],
                                                               in_=XY[:, kc * S:(kc + 1) * S]), rd=r_x[kc], wr=[])

    try:
        for l in range(nl):
            layer(l)
    except _Stop:
        pass

    for kc in range(KC if not stop_state.get('skip_out') else 0):
        P.dma("sp", "out", lambda e, kc=kc: e.dma_start(out=out_d[kc * 128:(kc + 1) * 128, :],
                                                       in_=XY[:, kc * S:(kc + 1) * S]), rd=r_x[kc], wr=[])
    final_waits = tuple((k, n) for k, n in P.dcnt.items() if k in ("out", "dbg", "xsp"))
    P.q["sp"].append((final_waits, None, None))

    keys = list(P.ENG) + sorted(P.dcnt.keys())
    sems = {k: es.enter_context(nc.semaphore("s_" + k)) for k in keys}
    stats = {e: len(P.q[e]) for e in P.ENG}

    def run(e, name):
        for waits, fn, inc in P.q[name]:
            for k, n in waits:
                e.wait_ge(sems[k], n)
            if fn is not None:
                fn(e).then_inc(sems[inc[0]], inc[1])

    with nc.Block() as block:
        @block.tensor
        def _(e):
            run(e, "pe")

        @block.scalar
        def _(e):
            run(e, "act")

        @block.vector
        def _(e):
            run(e, "dve")

        @block.gpsimd
        def _(e):
            run(e, "pool")

        @block.sync
        def _(e):
            run(e, "sp")
    es.close()
    return nc, stats


def _rope_tables():
    pos = np.arange(S, dtype=np.float32)

    def tab(rot, theta, hd):
        inv = (1.0 / (np.float32(theta) ** (np.arange(0, rot, 2, dtype=np.float32) / np.float32(rot)))).astype(np.float32)
        ang = (pos[:, None] * inv[None, :]).astype(np.float32)
        c, s = np.cos(ang).astype(np.float32), np.sin(ang).astype(np.float32)
        r2 = rot // 2
        cosF = np.ones((hd, S), np.float32)
        sinF = np.zeros((hd, S), np.float32)
        cosF[0:r2] = c.T
        cosF[r2:2 * r2] = c.T
        sinF[0:r2] = s.T
        sinF[r2:2 * r2] = s.T
        return np.tile(cosF, (128 // hd, 1)), np.tile(sinF, (128 // hd, 1))

    cab, sab = tab(16, 500000.0, 64)
    cc, sc = tab(64, 10000.0, 64)
    return np.ascontiguousarray(np.stack([cab, sab, cc, sc]).astype(np.float32))


def _rot_mat(r2, hd=64):
    R = np.zeros((128, 128), np.float32)
    for b in range(128 // hd):
        for d in range(r2):
            R[b * hd + d + r2, b * hd + d] = -1.0
            R[b * hd + d, b * hd + d + r2] = 1.0
    return R


def _constants():
    ones = np.ones((128, 128), np.float32)
    bones = np.zeros((128, 128), np.float32)
    bones[0:64, 0:64] = 1.0
    bones[64:128, 64:128] = 1.0
    ident = np.eye(128, dtype=np.float32)
    cmats = np.stack([ones, bones, _rot_mat(8), _rot_mat(32), ident]).astype(np.float32)
    k = np.arange(128)[:, None]
    q = np.arange(512)[None, :]
    masks = []
    for rel in range(-1, 5):
        masks.append((np.abs(q - (128 * rel + k)) <= 64).astype(np.float32))
    band = (np.abs(np.arange(128)[None, :] - k) <= 64).astype(np.float32)
    masks.append(np.tile(band, (1, 4)))
    cmasks = np.stack(masks).astype(np.float32)
    m = np.arange(128)[:, None].astype(np.float32)
    n = np.arange(128)[None, :].astype(np.float32)
    relf = np.maximum(n - m, 0.0)
    relb = np.maximum(m - n, 0.0)
    mkf8 = (n >= m).astype(np.float32) * 0.125
    mkb8 = (m >= n).astype(np.float32) * 0.125
    pos1 = np.broadcast_to(n + 1.0, (128, 128))
    posb = np.broadcast_to(128.0 - n, (128, 128))
    tok = np.arange(128, dtype=np.float32)[:, None]
    cf = np.concatenate([relf, relb, mkf8, mkb8, pos1, posb, ones, tok, 127.0 - tok], axis=1).astype(np.float32)
    return cmats, cmasks, np.ascontiguousarray(cf)


def _prep_layers(inp, layers):
    f = lambda a: np.asarray(a, dtype=np.float32)
    nl = len(layers)
    w_in = f(inp["w_in"])[layers].reshape(nl, 8, 128, 57, 128).transpose(0, 3, 2, 1, 4)
    wbr = np.concatenate([f(inp["w_br_a"])[layers], f(inp["w_br_b"])[layers], f(inp["w_br_c"])[layers]], axis=1)
    w_br = wbr.reshape(nl, 11, 128, 8, 128).transpose(0, 3, 2, 1, 4)
    w_o = f(inp["w_o"])[layers].reshape(nl, 8, 128, 8, 128).transpose(0, 3, 2, 1, 4)
    w_m1 = f(inp["w_mlp1"])[layers].reshape(nl, 8, 128, 32, 128).transpose(0, 3, 2, 1, 4)
    w_m2 = f(inp["w_mlp2"])[layers].reshape(nl, 4, 8, 128, 8, 128).transpose(0, 1, 4, 3, 2, 5).reshape(nl, 32, 128, 8, 128)
    pcol = np.zeros((nl, 128, NPC), np.float32)
    arow = np.zeros((nl, 4, 128, 128), np.float32)
    for i, l in enumerate(layers):
        pcol[i, :, 0:8] = f(inp["norm1_g"])[l].reshape(8, 128).T
        pcol[i, :, 8:16] = f(inp["norm2_g"])[l].reshape(8, 128).T
        pcol[i, :, 16] = np.tile(f(inp["a_q_norm_g"])[l], 2)
        pcol[i, :, 17] = np.tile(f(inp["a_k_norm_g"])[l], 2)
        pcol[i, :, 18] = np.tile(f(inp["b_q_norm_g"])[l], 2)
        pcol[i, :, 19] = np.tile(f(inp["b_k_norm_g"])[l], 2)
        pcol[i, :, 20] = f(inp["b_out_norm_g"])[l]
        pcol[i, :, 21] = f(inp["c_out_norm_g"])[l]
        pcol[i, :, 22] = np.tile(f(inp["b_lambda_q1"])[l], 2)
        pcol[i, :, 23] = np.tile(f(inp["b_lambda_k1"])[l], 2)
        pcol[i, :, 24] = np.tile(f(inp["b_lambda_q2"])[l], 2)
        pcol[i, :, 25] = np.tile(f(inp["b_lambda_k2"])[l], 2)
        af, ab = f(inp["c_decay_f"])[l], f(inp["c_decay_b"])[l]
        for p in range(2):
            pcol[i, :, 26 + p] = np.repeat(af[2 * p:2 * p + 2], 64)
            pcol[i, :, 28 + p] = np.repeat(ab[2 * p:2 * p + 2], 64)
            arow[i, p] = np.repeat(af[2 * p:2 * p + 2], 64)[None, :]
            arow[i, 2 + p] = np.repeat(ab[2 * p:2 * p + 2], 64)[None, :]
        pcol[i, :, 30:34] = af[None, :]
        pcol[i, :, 34:38] = ab[None, :]
    c = np.ascontiguousarray
    return dict(w_in=c(w_in), w_br=c(w_br), w_o=c(w_o), w_m1=c(w_m1), w_m2=c(w_m2), pcol=pcol, arow=arow)


_CACHE = {}


def _get_prog(nl, l0):
    if (nl, l0) not in _CACHE:
        _CACHE[(nl, l0)] = build(nl, l0=l0)[0]
    return _CACHE[(nl, l0)]


FUSED = False


def kernel(**inputs):
    x = np.asarray(inputs["x"], dtype=np.float32)
    B = x.shape[0]
    cmats, cmasks, cf = _constants()
    rope = _rope_tables()
    consts = dict(c_mats=cmats, c_masks=cmasks, c_f32=cf, rope=rope)
    xT = [np.ascontiguousarray(x[b].T) for b in range(B)]
    groups = [list(range(DEPTH))] if FUSED else [[l] for l in range(DEPTH)]
    for layers in groups:
        nc = _get_prog(len(layers), layers[0])
        shared = _prep_layers(inputs, layers)
        shared.update(consts)
        in_maps = []
        for b in range(B):
            d = dict(shared)
            d["xT"] = xT[b]
            in_maps.append(d)
        res = run_bass_kernel_spmd(nc, in_maps, core_ids=list(range(B)))
        xT = [np.ascontiguousarray(np.asarray(r["outT"], dtype=np.float32)) for r in res.results]
    out = np.stack([t.T for t in xT], axis=0)
    return np.ascontiguousarray(out.astype(np.float32))
```

```python
import math
from contextlib import ExitStack
import numpy as np
import ml_dtypes
import concourse.bass as bass
import concourse.mybir as mybir
from concourse.bass_utils import run_bass_kernel_spmd

F32, BF16 = mybir.dt.float32, mybir.dt.bfloat16
ALU, AF = mybir.AluOpType, mybir.ActivationFunctionType

S = 2048
D = 1024
KC = 8
NB = 4
TB = 512
DEPTH = 4
EPS = 1e-6
NPC = 38
C_AQ, C_AK, C_AV = 0, 3, 6
C_BQ, C_BK, C_BV = 9, 13, 17
C_CQ, C_CK, C_CV, C_CG = 21, 23, 25, 29
C_GATE = 33
NWSLOT = 5


class Res:
    __slots__ = ("w", "r", "x")

    def __init__(self, x=False):
        self.w = None
        self.r = {}
        self.x = x


class Prog:
    ENG = ("pe", "act", "dve", "pool", "sp")

    def __init__(self):
        self.q = {e: [] for e in self.ENG}
        self.cnt = {e: 0 for e in self.ENG}
        self.seen = {e: {} for e in self.ENG}
        self.dcnt = {}

    def _need(self, eng, dep, waits, raw):
        if dep is None:
            return
        k, n = dep
        if k == eng:
            if eng == "pe" or not raw:
                return
            if n <= self.cnt[eng] - 2:
                return
        if k in self.dcnt:
            n = max(n, self.dcnt[k])
        if self.seen[eng].get(k, 0) >= n:
            return
        waits[k] = max(waits.get(k, 0), n)

    def _deps(self, eng, rd, wr):
        waits = {}
        for r in rd:
            self._need(eng, r.w, waits, True)
        for w in wr:
            self._need(eng, w.w, waits, False)
            for k, n in w.r.items():
                self._need(eng, (k, n), waits, False)
        for k, n in waits.items():
            self.seen[eng][k] = n
        return tuple(waits.items())

    def op(self, eng, fn, rd=(), wr=()):
        wr = list(wr) + [r for r in rd if r.x]
        rd = [r for r in rd if not r.x]
        waits = self._deps(eng, rd, wr)
        self.cnt[eng] += 1
        n = self.cnt[eng]
        self.q[eng].append((waits, fn, (eng, 1)))
        for r in rd:
            r.r[eng] = max(r.r.get(eng, 0), n)
        for w in wr:
            w.w = (eng, n)
            w.r = {}

    def dma(self, qeng, semkey, fn, rd=(), wr=()):
        waits = self._deps(qeng, rd, wr)
        self.dcnt[semkey] = self.dcnt.get(semkey, 0) + 16
        n = self.dcnt[semkey]
        self.q[qeng].append((waits, fn, (semkey, 16)))
        for r in rd:
            r.r[semkey] = max(r.r.get(semkey, 0), n)
        for w in wr:
            w.w = (semkey, n)
            w.r = {}

    def barrier(self, dma_keys=()):
        for e in self.ENG:
            waits = {}
            for f in self.ENG:
                if f != e and self.cnt[f] > self.seen[e].get(f, 0):
                    waits[f] = self.cnt[f]
            for k in dma_keys:
                if self.dcnt.get(k, 0) > self.seen[e].get(k, 0):
                    waits[k] = self.dcnt[k]
            for k, n in waits.items():
                self.seen[e][k] = n
            if waits:
                self.q[e].append((tuple(waits.items()), None, None))


class _Stop(Exception):
    pass


def build(nl, dbg=False, stop=None, l0=0):
    nc = bass.Bass("TRN2", target_bir_lowering=False)

    def din(name, shape, dt=F32):
        return nc.dram_tensor(name, list(shape), dt, kind="ExternalInput").ap()

    xT_d = din("xT", [D, S])
    w_in_d = din("w_in", [nl, 57, 128, 8, 128])
    w_br_d = din("w_br", [nl, 8, 128, 11, 128])
    w_o_d = din("w_o", [nl, 8, 128, 8, 128])
    w_m1_d = din("w_m1", [nl, 32, 128, 8, 128])
    w_m2_d = din("w_m2", [nl, 32, 128, 8, 128])
    pcol_d = din("pcol", [nl, 128, NPC])
    arow_d = din("arow", [nl, 4, 128, 128])
    cmat_d = din("c_mats", [5, 128, 128])
    cmask_d = din("c_masks", [7, 128, 512])
    cf32_d = din("c_f32", [128, 7 * 128 + 2])
    rope_d = din("rope", [4, 128, S])
    out_d = nc.dram_tensor("outT", [D, S], F32, kind="ExternalOutput").ap()
    xsp_d = nc.dram_tensor("xspill", [D, S], F32, kind="Internal").ap()
    if dbg:
        ydbg_d = nc.dram_tensor("ydbg", [11 * 128, S], BF16, kind="ExternalOutput").ap()
        xdbg_d = nc.dram_tensor("xdbg", [D, S], F32, kind="ExternalOutput").ap()

    P = Prog()
    es = ExitStack()

    def sb(name, shape, dt):
        return es.enter_context(nc.sbuf_tensor(name, list(shape), dt))

    XN = sb("XN", [128, KC * S], BF16)
    XY = sb("XY", [128, KC * S], F32)
    M = sb("M", [128, KC * S], BF16)
    WR = sb("WR", [128, NWSLOT * 11 * 128], BF16)
    CMB = sb("CMB", [128, 5 * 128], BF16)
    MASKS = sb("MASKS", [128, 7 * 512], BF16)
    CF = sb("CF", [128, 7 * 128 + 2], F32)
    PCOL = sb("PCOL", [128, NPC], F32)
    SMALL = sb("SMALL", [128, 64], F32)
    CTAB = sb("CTAB", [128, 13 * 128], F32)
    CDER = sb("CDER", [128, 6 * 2048], BF16)
    S32 = sb("S32", [128, 2 * 128], F32)
    SQB = sb("SQB", [128, 2 * 512], BF16)
    UB = sb("UB", [128, 2 * 512], BF16)
    FT = sb("FT", [128, 8 * 512], F32)

    Ybf = XY[:, 0:11 * 1024].bitcast(BF16)
    ROPE = XY[:, 11 * 1024:11 * 1024 + 4096]
    PB = XY[:, 15 * 1024:16 * 1024].bitcast(BF16)
    DTOT = CDER[:, 4 * 2048:6 * 2048].bitcast(F32)

    def x_(kc, tb):
        return XY[:, kc * S + tb * TB: kc * S + (tb + 1) * TB]

    def xn_(kc, tb):
        return XN[:, kc * S + tb * TB: kc * S + (tb + 1) * TB]

    def xnfull(kc):
        return XN[:, kc * S:(kc + 1) * S]

    def y_(ch):
        return Ybf[:, ch * S:(ch + 1) * S]

    def m_(kc, tb):
        return M[:, kc * S + tb * TB: kc * S + (tb + 1) * TB]

    def uq(s):
        return M[:, s * 8192: s * 8192 + 2048]

    def uk(s):
        return M[:, s * 8192 + 2048: s * 8192 + 4096]

    def uv(s):
        return M[:, s * 8192 + 4096: s * 8192 + 8192].rearrange("p (t c) -> p t c", c=256)

    def wslot(s, nk=8):
        return WR[:, s * 1408: s * 1408 + nk * 128].rearrange("p (k n) -> p k n", n=128)

    ONES = CMB[:, 0:128]
    BONES = CMB[:, 128:256]
    RAB = CMB[:, 256:384]
    RC = CMB[:, 384:512]
    IDENT = CMB[:, 512:640]

    def mask_(i):
        return MASKS[:, i * 512:(i + 1) * 512]

    RELF, RELB = CF[:, 0:128], CF[:, 128:256]
    MKF8, MKB8 = CF[:, 256:384], CF[:, 384:512]
    POS1, POSB = CF[:, 512:640], CF[:, 640:768]
    ONESF = CF[:, 768:896]
    TOKC, TOKCR = CF[:, 896:897], CF[:, 897:898]

    def ctab(i):
        return CTAB[:, i * 128:(i + 1) * 128]

    def cder(i):
        return CDER[:, i * 2048:(i + 1) * 2048]

    KDF, KDB = cder(0).rearrange("p (t c) -> p t c", c=128), cder(1).rearrange("p (t c) -> p t c", c=128)
    QDF, QDB = cder(2), cder(3)
    STF, STB = cder(4).rearrange("p (t c) -> p t c", c=128), cder(5).rearrange("p (t c) -> p t c", c=128)

    def pb(i):
        return PB[:, i * 512:(i + 1) * 512]

    def sqb(i):
        return SQB[:, i * 512:(i + 1) * 512]

    def ub(i):
        return UB[:, i * 512:(i + 1) * 512]

    def ft(i):
        return FT[:, i * 512:(i + 1) * 512]

    PS = [es.enter_context(nc.psum_tensor("ps%d" % i, [128, 512], F32)) for i in range(8)]

    r_x = [[Res() for _ in range(NB)] for _ in range(KC)]
    r_xn = [Res() for _ in range(NB)]
    r_y = [[Res() for _ in range(NB)] for _ in range(11)]
    r_yall = [Res() for _ in range(11)]
    r_m = [[Res() for _ in range(NB)] for _ in range(KC)]
    r_uq = [Res(), Res()]
    r_uk = [Res(), Res()]
    r_uv = [Res(), Res()]
    r_w = [Res() for _ in range(NWSLOT)]
    r_ps = [Res(True) for _ in range(8)]
    r_pb = [Res() for _ in range(4)]
    r_sqb = [Res(), Res()]
    r_ub = [Res(), Res()]
    r_ft = [Res() for _ in range(8)]
    r_rope = Res()
    r_const = Res()
    r_pcol = Res()
    r_small = Res()
    r_ctab = Res()
    r_cder = [Res() for _ in range(6)]
    r_s32 = [Res(), Res()]
    r_xsp = Res()

    wstate = {"n": 0}

    def load_w(dram_ap, nk=8):
        s = wstate["n"] % NWSLOT
        wstate["n"] += 1
        dst = wslot(s, nk)
        P.dma("pool", "w%d" % s, lambda e, dst=dst, src=dram_ap: e.dma_start(out=dst, in_=src), wr=[r_w[s]])
        return s

    class WStream:
        def __init__(self):
            self.pending = []
            self.loaded = []
            self.inuse = 0

        def push(self, ap, nk=8):
            self.pending.append((ap, nk))

        def prefetch(self):
            while self.pending and len(self.loaded) + self.inuse < NWSLOT:
                ap, nk = self.pending.pop(0)
                self.loaded.append(load_w(ap, nk))

        def pop(self):
            self.prefetch()
            s = self.loaded.pop(0)
            self.inuse += 1
            return s

        def release(self, s):
            self.inuse -= 1
            self.prefetch()

    WS = WStream()

    def mm(out, lhsT, rhs, start, stop, rd, wr):
        P.op("pe", lambda e: e.matmul(out, lhsT=lhsT, rhs=rhs, start=start, stop=stop), rd=rd, wr=wr)

    def act(out, in_, func, rd, wr, scale=None, bias=None):
        kw = {}
        if scale is not None:
            kw["scale"] = scale
        if bias is not None:
            kw["bias"] = bias
        P.op("act", lambda e: e.activation(out=out, in_=in_, func=func, **kw), rd=rd, wr=wr)

    def tt(eng, out, in0, in1, op, rd, wr):
        P.op(eng, lambda e: e.tensor_tensor(out=out, in0=in0, in1=in1, op=op), rd=rd, wr=wr)

    def stt(eng, out, in0, scalar, in1, op0, op1, rd, wr):
        P.op(eng, lambda e: e.scalar_tensor_tensor(out=out, in0=in0, scalar=scalar, in1=in1, op0=op0, op1=op1),
             rd=rd, wr=wr)

    def ts(eng, out, in0, s1, s2, op0, op1, rd, wr):
        if op1 is None:
            P.op(eng, lambda e: e.tensor_scalar(out=out, in0=in0, scalar1=s1, scalar2=None, op0=op0), rd=rd, wr=wr)
        else:
            P.op(eng, lambda e: e.tensor_scalar(out=out, in0=in0, scalar1=s1, scalar2=s2, op0=op0, op1=op1),
                 rd=rd, wr=wr)

    def cp(eng, out, in_, rd, wr):
        if eng == "act":
            act(out, in_, AF.Copy, rd, wr)
        else:
            P.op(eng, lambda e: e.tensor_copy(out=out, in_=in_), rd=rd, wr=wr)

    def recip(out, in_, rd, wr):
        P.op("dve", lambda e: e.reciprocal(out=out, in_=in_), rd=rd, wr=wr)

    def rstd_from(ps_ap, r_in, ftile, r_ftile, inv_n):
        act(ftile, ps_ap, AF.Ln, rd=[r_in], wr=[r_ftile], scale=inv_n, bias=EPS)
        act(ftile, ftile, AF.Exp, rd=[r_ftile], wr=[r_ftile], scale=-0.5)

    rot = {"ps": 0, "alt": 0}

    P.dma("pool", "cst", lambda e: e.dma_start(out=CMB[:, :].rearrange("p (c n) -> p c n", n=128),
                                                  in_=cmat_d.rearrange("c p n -> p c n")), wr=[r_const])
    P.dma("pool", "cst", lambda e: e.dma_start(out=MASKS[:, :].rearrange("p (c n) -> p c n", n=512),
                                                  in_=cmask_d.rearrange("c p n -> p c n")), wr=[r_const])
    P.dma("sp", "cst", lambda e: e.dma_start(out=CF[:, :], in_=cf32_d), wr=[r_const])
    for kc in range(KC):
        P.dma("sp", "xld", lambda e, kc=kc: e.dma_start(out=XY[:, kc * S:(kc + 1) * S],
                                                       in_=xT_d[kc * 128:(kc + 1) * 128, :]),
              wr=r_x[kc])

    def norm(gcol0):
        for tb in range(NB):
            for kc in range(KC):
                i = kc % 2
                act(sqb(i), x_(kc, tb), AF.Square, rd=[r_x[kc][tb]], wr=[r_sqb[i]])
                mm(PS[2][:, :], ONES, sqb(i), kc == 0, kc == KC - 1, rd=[r_sqb[i], r_const], wr=[r_ps[2]])
            rstd_from(PS[2][:, :], r_ps[2], ft(0), r_ft[0], 1.0 / D)
            for kc in range(KC):
                stt("dve", xn_(kc, tb), x_(kc, tb), PCOL[:, gcol0 + kc: gcol0 + kc + 1], ft(0), ALU.mult, ALU.mult,
                    rd=[r_x[kc][tb], r_ft[0], r_pcol], wr=[r_xn[tb]])

    def proj_fm(slot, tb, bank):
        w = wslot(slot)
        for kc in range(KC):
            mm(PS[bank][:, :], w[:, kc, :], xn_(kc, tb), kc == 0, kc == KC - 1,
               rd=[r_w[slot], r_xn[tb]], wr=[r_ps[bank]])

    def perm_dst(buf, tb, delta):
        if delta == 1:
            return buf[:, tb * TB:(tb + 1) * TB], None
        n0 = tb * TB // delta
        return buf.rearrange("p (r n) -> p n r", r=delta)[:, n0:n0 + TB // delta, :], delta

    def srcv(ap, delta):
        if delta is None:
            return ap
        return ap.rearrange("p (n r) -> p n r", r=delta)

    def qk_post(bank, tb, gcol, rmat, normed, dst_buf, r_dst, delta):
        j = rot["alt"] % 2
        rot["alt"] += 1
        pj = PS[bank][:, :]
        cos = ROPE[:, tb * TB:(tb + 1) * TB]
        sin = ROPE[:, S + tb * TB: S + (tb + 1) * TB]
        dst, dl = perm_dst(dst_buf, tb, delta)
        if normed:
            act(sqb(j), pj, AF.Square, rd=[r_ps[bank]], wr=[r_sqb[j]])
            mm(PS[2][:, :], BONES, sqb(j), True, True, rd=[r_sqb[j], r_const], wr=[r_ps[2]])
            rstd_from(PS[2][:, :], r_ps[2], ft(j), r_ft[j], 1.0 / 64)
            stt("dve", ub(j), pj, PCOL[:, gcol:gcol + 1], ft(j), ALU.mult, ALU.mult,
                rd=[r_ps[bank], r_ft[j], r_pcol], wr=[r_ub[j]])
            mm(PS[3][:, :], rmat, ub(j), True, True, rd=[r_ub[j], r_const], wr=[r_ps[3]])
            tt("pool", ft(2 + j), ub(j), cos, ALU.mult, rd=[r_ub[j], r_rope], wr=[r_ft[2 + j]])
        else:
            cp("act", ub(j), pj, rd=[r_ps[bank]], wr=[r_ub[j]])
            mm(PS[3][:, :], rmat, ub(j), True, True, rd=[r_ub[j], r_const], wr=[r_ps[3]])
            tt("dve", ft(2 + j), pj, cos, ALU.mult, rd=[r_ps[bank], r_rope], wr=[r_ft[2 + j]])
        tt("dve", ft(4 + j), PS[3][:, :], sin, ALU.mult, rd=[r_ps[3], r_rope], wr=[r_ft[4 + j]])
        tt("pool", dst, srcv(ft(2 + j), dl), srcv(ft(4 + j), dl), ALU.add,
           rd=[r_ft[2 + j], r_ft[4 + j]], wr=[r_dst])

    def v_proj(slot, set_, coloff, delta):
        w = wslot(slot)
        V = uv(set_)
        for g4 in range(4):
            bank = rot["ps"] % 2
            rot["ps"] += 1
            for ti in range(4):
                t = g4 * 4 + ti
                for kc in range(KC):
                    if delta == 1:
                        lhs = xnfull(kc)[:, t * 128:(t + 1) * 128]
                    elif delta == 4:
                        lhs = xnfull(kc).rearrange("p (n r) -> p r n", r=4)[:, t // 4, (t % 4) * 128:(t % 4 + 1) * 128]
                    else:
                        lhs = xnfull(kc).rearrange("p (n r) -> p r n", r=16)[:, t, :]
                    mm(PS[bank][:, ti * 128:(ti + 1) * 128], lhs, w[:, kc, :], kc == 0, kc == KC - 1,
                       rd=[r_w[slot]] + r_xn, wr=[r_ps[bank]])
            cp("act", V[:, g4 * 4:(g4 + 1) * 4, coloff:coloff + 128],
               PS[bank][:, :].rearrange("p (t c) -> p t c", c=128), rd=[r_ps[bank]], wr=[r_uv[set_]])
            yield

    units = []
    for g in range(3):
        units.append(dict(kind="A", idx=g, q=C_AQ + g, k=C_AK + g, v=[C_AV + g], gq=16, gk=17, delta=(1, 4, 16)[g]))
    for h in range(4):
        units.append(dict(kind="B", idx=h, q=C_BQ + h, k=C_BK + h, v=[C_BV + h], gq=18, gk=19, delta=1))
    for p in range(2):
        units.append(dict(kind="C", idx=p, q=C_CQ + p, k=C_CK + p, v=[C_CV + 2 * p, C_CV + 2 * p + 1], gq=0, gk=0,
                          delta=1))

    def push_unit_weights(l, u):
        WS.push(w_in_d[l, u["q"]])
        WS.push(w_in_d[l, u["k"]])
        for c in u["v"]:
            WS.push(w_in_d[l, c])

    def rope_load(which):
        P.dma("sp", "rope", lambda e: e.dma_start(out=ROPE[:, 0:S], in_=rope_d[2 * which]), wr=[r_rope])
        P.dma("sp", "rope", lambda e: e.dma_start(out=ROPE[:, S:2 * S], in_=rope_d[2 * which + 1]), wr=[r_rope])

    def inproj(l, ui):
        u = units[ui]
        set_ = ui % 2
        normed = u["kind"] != "C"
        rmat = RC if u["kind"] == "C" else RAB
        if ui == 0:
            rope_load(0)
        if ui == 7:
            rope_load(1)
        for which, buf, rres, gcol in (("q", uq(set_), r_uq[set_], u["gq"]), ("k", uk(set_), r_uk[set_], u["gk"])):
            slot = WS.pop()
            for tb in range(NB):
                bank = rot["ps"] % 2
                rot["ps"] += 1
                proj_fm(slot, tb, bank)
                qk_post(bank, tb, gcol, rmat, normed, buf, rres, u["delta"])
                yield
            WS.release(slot)
        for vi, c in enumerate(u["v"]):
            slot = WS.pop()
            yield from v_proj(slot, set_, vi * 128, u["delta"])
            WS.release(slot)

    def attn_B(l, ui):
        u = units[ui]
        h = u["idx"]
        set_ = ui % 2
        qT, kT, V = uq(set_), uk(set_), uv(set_)
        rdq = [r_uq[set_], r_uk[set_]]
        ch = 3 + h
        for qb in range(NB):
            for m in range(2):
                rows = slice(64 * m, 64 * m + 64)

                def smm(kt):
                    bank = 4 + kt % 2
                    mm(PS[bank][:, :], kT[rows, kt * 128:(kt + 1) * 128], qT[rows, qb * TB:(qb + 1) * TB], True, True,
                       rd=rdq, wr=[r_ps[bank]])

                smm(0)
                for kt in range(16):
                    if kt + 1 < 16:
                        smm(kt + 1)
                    bank = 4 + kt % 2
                    pi = kt % 3
                    act(pb(pi), PS[bank][:, :], AF.Exp, rd=[r_ps[bank]], wr=[r_pb[pi]], scale=0.125)
                    mm(PS[6][:, :], V[:, kt, 0:128], pb(pi), kt == 0, kt == 15, rd=[r_pb[pi], r_uv[set_]],
                       wr=[r_ps[6]])
                    mm(PS[7][:, :], ONES, pb(pi), kt == 0, kt == 15, rd=[r_pb[pi], r_const], wr=[r_ps[7]])
                recip(ft(6), PS[7][:, :], rd=[r_ps[7]], wr=[r_ft[6]])
                if m == 0:
                    tt("dve", ft(7), PS[6][:, :], ft(6), ALU.mult, rd=[r_ps[6], r_ft[6]], wr=[r_ft[7]])
                else:
                    tt("dve", ft(6), PS[6][:, :], ft(6), ALU.mult, rd=[r_ps[6], r_ft[6]], wr=[r_ft[6]])
                    stt("dve", ft(7), ft(6), SMALL[:, 0:1], ft(7), ALU.mult, ALU.add,
                        rd=[r_ft[6], r_ft[7], r_small], wr=[r_ft[7]])
                    act(pb(3), ft(7), AF.Square, rd=[r_ft[7]], wr=[r_pb[3]])
                    mm(PS[2][:, :], ONES, pb(3), True, True, rd=[r_pb[3], r_const], wr=[r_ps[2]])
                    rstd_from(PS[2][:, :], r_ps[2], ft(6), r_ft[6], 1.0 / 128)
                    stt("dve", y_(ch)[:, qb * TB:(qb + 1) * TB], ft(7), SMALL[:, 1:2], ft(6), ALU.mult, ALU.mult,
                        rd=[r_ft[7], r_ft[6], r_small], wr=[r_y[ch][qb], r_yall[ch]])
                yield

    def attn_A(l, ui):
        u = units[ui]
        g = u["idx"]
        delta = u["delta"]
        set_ = ui % 2
        qT, kT, V = uq(set_), uk(set_), uv(set_)
        rdq = [r_uq[set_], r_uk[set_]]
        for s2 in range(2):
            rows = slice(64 * s2, 64 * s2 + 64)
            for qb in range(NB):
                if g < 2:
                    rels = list(range(-1, 5)) if g == 0 else list(range(4))
                    kts = [(4 * qb + r, r) for r in rels if 0 <= 4 * qb + r < 16]
                    n = len(kts)

                    def smm(i):
                        kt, rel = kts[i]
                        bank = 4 + i % 2
                        mm(PS[bank][:, :], kT[rows, kt * 128:(kt + 1) * 128], qT[rows, qb * TB:(qb + 1) * TB], True,
                           True, rd=rdq, wr=[r_ps[bank]])

                    smm(0)
                    for i in range(n):
                        if i + 1 < n:
                            smm(i + 1)
                        kt, rel = kts[i]
                        bank = 4 + i % 2
                        pi = i % 3
                        act(pb(pi), PS[bank][:, :], AF.Exp, rd=[r_ps[bank]], wr=[r_pb[pi]], scale=0.125)
                        tt("pool", pb(pi), pb(pi), mask_(rel + 1), ALU.mult, rd=[r_pb[pi], r_const], wr=[r_pb[pi]])
                        mm(PS[6][:, :], V[:, kt, 0:128], pb(pi), i == 0, i == n - 1, rd=[r_pb[pi], r_uv[set_]],
                           wr=[r_ps[6]])
                        mm(PS[7][:, :], ONES, pb(pi), i == 0, i == n - 1, rd=[r_pb[pi], r_const], wr=[r_ps[7]])
                else:
                    bank = 4 + qb % 2
                    for rel in range(4):
                        kt = 4 * qb + rel
                        mm(PS[bank][:, rel * 128:(rel + 1) * 128], kT[rows, kt * 128:(kt + 1) * 128],
                           qT[rows, kt * 128:(kt + 1) * 128], True, True, rd=rdq, wr=[r_ps[bank]])
                    pi = qb % 3
                    act(pb(pi), PS[bank][:, :], AF.Exp, rd=[r_ps[bank]], wr=[r_pb[pi]], scale=0.125)
                    tt("pool", pb(pi), pb(pi), mask_(6), ALU.mult, rd=[r_pb[pi], r_const], wr=[r_pb[pi]])
                    for rel in range(4):
                        kt = 4 * qb + rel
                        mm(PS[6][:, rel * 128:(rel + 1) * 128], V[:, kt, 0:128], pb(pi)[:, rel * 128:(rel + 1) * 128],
                           True, True, rd=[r_pb[pi], r_uv[set_]], wr=[r_ps[6]])
                        mm(PS[7][:, rel * 128:(rel + 1) * 128], ONES, pb(pi)[:, rel * 128:(rel + 1) * 128],
                           True, True, rd=[r_pb[pi], r_const], wr=[r_ps[7]])
                if delta == 1:
                    ydst = y_(g)[rows, qb * TB:(qb + 1) * TB]
                    ddst = DTOT[rows, qb * TB:(qb + 1) * TB]
                    nsrc, dsrc = PS[6][rows, :], PS[7][rows, :]
                elif delta == 4:
                    ydst = y_(g).rearrange("p (n r) -> p r n", r=4)[rows, qb, :]
                    ddst = DTOT[:, :].rearrange("p (n r) -> p r n", r=4)[rows, qb, :]
                    nsrc, dsrc = PS[6][rows, :], PS[7][rows, :]
                else:
                    ydst = y_(g).rearrange("p (n r) -> p r n", r=16)[rows, 4 * qb:4 * qb + 4, :]
                    ddst = DTOT[:, :].rearrange("p (n r) -> p r n", r=16)[rows, 4 * qb:4 * qb + 4, :]
                    nsrc = PS[6][rows, :].rearrange("p (a b) -> p a b", a=4)
                    dsrc = PS[7][rows, :].rearrange("p (a b) -> p a b", a=4)
                cp("act", ydst, nsrc, rd=[r_ps[6]], wr=[r_yall[g]] + r_y[g])
                if g == 0:
                    cp("dve", ddst, dsrc, rd=[r_ps[7]], wr=[r_cder[4], r_cder[5]])
                else:
                    tt("dve", ddst, ddst, dsrc, ALU.add, rd=[r_ps[7], r_cder[4], r_cder[5]], wr=[r_cder[4], r_cder[5]])
                yield
        if g == 2:
            for tb in range(NB):
                recip(ft(6), DTOT[:, tb * TB:(tb + 1) * TB], rd=[r_cder[4], r_cder[5]], wr=[r_ft[6]])
                for gg in range(3):
                    ysl = y_(gg)[:, tb * TB:(tb + 1) * TB]
                    tt("pool", ysl, ysl, ft(6), ALU.mult, rd=[r_ft[6], r_yall[gg]] + r_y[gg], wr=[r_y[gg][tb], r_yall[gg]])
                yield

    def attn_C(l, ui):
        u = units[ui]
        p = u["idx"]
        set_ = ui % 2
        qT, kT, V = uq(set_), uk(set_), uv(set_)
        DQF, DQB, DKF, DKB = ctab(4 + p), ctab(6 + p), ctab(8 + p), ctab(10 + p)
        g128f, g128b = SMALL[:, 8 + p:9 + p], SMALL[:, 10 + p:11 + p]
        for g4 in range(4):
            bank = rot["ps"] % 2
            rot["ps"] += 1
            tp = PS[bank][:, :].bitcast(BF16)[:, 0:512]
            for ti in range(4):
                t = g4 * 4 + ti
                P.op("pe", lambda e, t=t, ti=ti, tp=tp: e.transpose(tp[:, ti * 128:(ti + 1) * 128],
                                                                      kT[:, t * 128:(t + 1) * 128], IDENT),
                     rd=[r_uk[set_], r_const], wr=[r_ps[bank]])
            tp3 = tp.rearrange("p (t c) -> p t c", c=128)
            tt("dve", KDF[:, g4 * 4:(g4 + 1) * 4, :], tp3, DKF.unsqueeze(1).to_broadcast([128, 4, 128]), ALU.mult,
               rd=[r_ps[bank], r_ctab], wr=[r_cder[0]])
            tt("dve", KDB[:, g4 * 4:(g4 + 1) * 4, :], tp3, DKB.unsqueeze(1).to_broadcast([128, 4, 128]), ALU.mult,
               rd=[r_ps[bank], r_ctab], wr=[r_cder[1]])
            q3 = qT[:, g4 * TB:(g4 + 1) * TB].rearrange("p (t c) -> p t c", c=128)
            tt("pool", QDF[:, g4 * TB:(g4 + 1) * TB].rearrange("p (t c) -> p t c", c=128), q3,
               DQF.unsqueeze(1).to_broadcast([128, 4, 128]), ALU.mult, rd=[r_uq[set_], r_ctab], wr=[r_cder[2]])
            tt("pool", QDB[:, g4 * TB:(g4 + 1) * TB].rearrange("p (t c) -> p t c", c=128), q3,
               DQB.unsqueeze(1).to_broadcast([128, 4, 128]), ALU.mult, rd=[r_uq[set_], r_ctab], wr=[r_cder[3]])
            yield
        P.op("pool", lambda e: e.memset(S32[:, :], 0.0), wr=r_s32)
        P.op("pool", lambda e: e.memset(STF[:, 0, :], 0.0), wr=[r_cder[4]])
        P.op("pool", lambda e: e.memset(STB[:, 15, :], 0.0), wr=[r_cder[5]])
        for step in range(15):
            for d in range(2):
                c = step if d == 0 else 15 - step
                KD = KDF if d == 0 else KDB
                ST = STF if d == 0 else STB
                gcol = g128f if d == 0 else g128b
                s32 = S32[:, d * 128:(d + 1) * 128]
                bank = rot["ps"] % 2
                rot["ps"] += 1
                mm(PS[bank][:, 0:256], KD[:, c, :], V[:, c, :], True, True, rd=[r_cder[d], r_uv[set_]],
                   wr=[r_ps[bank]])
                for hh in range(2):
                    rws = slice(64 * hh, 64 * hh + 64)
                    stt("dve", s32[rws, :], s32[rws, :], gcol[rws, :], PS[bank][rws, hh * 128:(hh + 1) * 128],
                        ALU.mult, ALU.add, rd=[r_ps[bank], r_s32[d], r_small], wr=[r_s32[d]])
                cn = c + 1 if d == 0 else c - 1
                cp("act", ST[:, cn, :], s32, rd=[r_s32[d]], wr=[r_cder[4 + d]])
            if step % 4 == 3:
                yield
        yield
        for hh in range(2):
            rws = slice(64 * hh, 64 * hh + 64)
            h = 2 * p + hh
            ch = 7 + h
            for cb in range(4):
                sbank = 4 + cb % 2
                for ci in range(4):
                    c = 4 * cb + ci
                    mm(PS[sbank][:, ci * 128:(ci + 1) * 128], kT[rws, c * 128:(c + 1) * 128],
                       qT[rws, c * 128:(c + 1) * 128], True, True, rd=[r_uq[set_], r_uk[set_]], wr=[r_ps[sbank]])
                pi = cb % 3
                tt("dve", pb(pi).rearrange("p (t c) -> p t c", c=128),
                   PS[sbank][:, :].rearrange("p (t c) -> p t c", c=128),
                   ctab(h).unsqueeze(1).to_broadcast([128, 4, 128]), ALU.mult, rd=[r_ps[sbank], r_ctab],
                   wr=[r_pb[pi]])
                for ci in range(4):
                    c = 4 * cb + ci
                    o = PS[6][:, ci * 128:(ci + 1) * 128]
                    mm(o, V[:, c, hh * 128:(hh + 1) * 128], pb(pi)[:, ci * 128:(ci + 1) * 128], True, False,
                       rd=[r_pb[pi], r_uv[set_]], wr=[r_ps[6]])
                    mm(o, STF[rws, c, :], QDF[rws, c * 128:(c + 1) * 128], False, False,
                       rd=[r_cder[4], r_cder[2]], wr=[r_ps[6]])
                    mm(o, STB[rws, c, :], QDB[rws, c * 128:(c + 1) * 128], False, True,
                       rd=[r_cder[5], r_cder[3]], wr=[r_ps[6]])
                cp("dve", ft(7), PS[6][:, :], rd=[r_ps[6]], wr=[r_ft[7]])
                act(pb(3), PS[6][:, :], AF.Square, rd=[r_ps[6]], wr=[r_pb[3]])
                mm(PS[2][:, :], ONES, pb(3), True, True, rd=[r_pb[3], r_const], wr=[r_ps[2]])
                rstd_from(PS[2][:, :], r_ps[2], ft(6), r_ft[6], 1.0 / 128)
                stt("dve", y_(ch)[:, cb * TB:(cb + 1) * TB], ft(7), PCOL[:, 21:22], ft(6), ALU.mult, ALU.mult,
                    rd=[r_ft[7], r_ft[6], r_pcol], wr=[r_y[ch][cb], r_yall[ch]])
                yield

    def layer_params(l):
        lam_init = 0.8 - 0.6 * math.exp(-0.3 * (l + l0))
        P.dma("sp", "par", lambda e: e.dma_start(out=PCOL[:, :], in_=pcol_d[l]), wr=[r_pcol])
        P.dma("sp", "par", lambda e: e.dma_start(out=CTAB[:, 8 * 128:12 * 128].rearrange("p (c n) -> p c n", n=128),
                                                   in_=arow_d[l].rearrange("c p n -> p c n")), wr=[r_ctab])
        rp = [r_pcol]
        tt("dve", SMALL[:, 2:3], PCOL[:, 22:23], PCOL[:, 23:24], ALU.mult, rd=rp, wr=[r_small])
        tt("dve", SMALL[:, 3:4], PCOL[:, 24:25], PCOL[:, 25:26], ALU.mult, rd=rp, wr=[r_small])
        mm(PS[2][:, 0:2], ONESF[0:64, :], SMALL[0:64, 2:4], True, True, rd=[r_small, r_const], wr=[r_ps[2]])
        act(SMALL[:, 4:6], PS[2][:, 0:2], AF.Exp, rd=[r_ps[2]], wr=[r_small])
        tt("dve", SMALL[:, 6:7], SMALL[:, 5:6], SMALL[:, 4:5], ALU.subtract, rd=[r_small], wr=[r_small])
        ts("dve", SMALL[:, 0:1], SMALL[:, 6:7], -lam_init, None, ALU.add, None, rd=[r_small], wr=[r_small])
        ts("dve", SMALL[:, 1:2], PCOL[:, 20:21], 1.0 - lam_init, None, ALU.mult, None, rd=rp + [r_small],
           wr=[r_small])
        act(SMALL[:, 12:16], PCOL[:, 26:30], AF.Exp, rd=rp + [r_small], wr=[r_small])
        ts("dve", SMALL[:, 12:16], SMALL[:, 12:16], -1.0, None, ALU.mult, None, rd=[r_small], wr=[r_small])
        act(SMALL[:, 8:12], SMALL[:, 12:16], AF.Exp, rd=[r_small], wr=[r_small], scale=128.0)
        act(SMALL[:, 16:24], PCOL[:, 30:38], AF.Exp, rd=rp + [r_small], wr=[r_small])
        ts("dve", SMALL[:, 16:24], SMALL[:, 16:24], -1.0, None, ALU.mult, None, rd=[r_small], wr=[r_small])
        rs = [r_small, r_const]
        for h in range(4):
            act(ctab(h), RELF, AF.Exp, rd=rs, wr=[r_ctab], scale=SMALL[:, 16 + h:17 + h])
            tt("dve", ctab(h), ctab(h), MKF8, ALU.mult, rd=[r_ctab, r_const], wr=[r_ctab])
            act(ctab(12), RELB, AF.Exp, rd=rs + [r_ctab], wr=[r_ctab], scale=SMALL[:, 20 + h:21 + h])
            tt("dve", ctab(12), ctab(12), MKB8, ALU.mult, rd=[r_ctab, r_const], wr=[r_ctab])
            tt("dve", ctab(h), ctab(h), ctab(12), ALU.add, rd=[r_ctab], wr=[r_ctab])
        for p in range(2):
            act(ctab(4 + p), POS1, AF.Exp, rd=rs + [r_ctab], wr=[r_ctab], scale=SMALL[:, 12 + p:13 + p])
            act(ctab(6 + p), POSB, AF.Exp, rd=rs + [r_ctab], wr=[r_ctab], scale=SMALL[:, 14 + p:15 + p])
            ts("dve", ctab(4 + p), ctab(4 + p), 0.125, None, ALU.mult, None, rd=[r_ctab], wr=[r_ctab])
            ts("dve", ctab(6 + p), ctab(6 + p), 0.125, None, ALU.mult, None, rd=[r_ctab], wr=[r_ctab])
            for d in range(2):
                dstt = ctab(8 + 2 * d + p)
                src = dstt
                act(dstt, src, AF.Exp, rd=rp + [r_ctab], wr=[r_ctab])
                ts("dve", dstt, dstt, -1.0, None, ALU.mult, None, rd=[r_ctab], wr=[r_ctab])
                act(dstt, dstt, AF.Exp, rd=[r_ctab, r_const], wr=[r_ctab], scale=(TOKCR if d == 0 else TOKC))

    def interleave(a, b):
        da = db = False
        while not (da and db):
            if not da:
                try:
                    next(a)
                except StopIteration:
                    da = True
            if b is None:
                db = True
            if not db:
                try:
                    next(b)
                except StopIteration:
                    db = True

    def drain(g):
        for _ in g:
            pass

    stop_state = {}
    attn_fn = {"A": attn_A, "B": attn_B, "C": attn_C}

    def ck(name):
        if stop == name:
            raise _Stop()

    def layer(l):
        layer_params(l)
        ck('params')
        for u in (units if stop not in ('xc', 'xc2') else units[7:]):
            push_unit_weights(l, u)
        norm(0)
        ck('norm1')
        for kc in range(KC):
            P.dma("sp", "xsp", lambda e, kc=kc: e.dma_start(out=xsp_d[kc * 128:(kc + 1) * 128, :],
                                                           in_=XY[:, kc * S:(kc + 1) * S]),
                  rd=r_x[kc], wr=[r_xsp])
        P.barrier(dma_keys=("xsp",))
        ck('spill')
        if stop in ('xc', 'xc2'):
            drain(inproj(l, 7))
            ck('xc')
            drain(attn_C(l, 7))
            ck('xc2')
        drain(inproj(l, 0))
        ck('inproj0')
        for ui in range(len(units)):
            a = attn_fn[units[ui]["kind"]](l, ui)
            b = inproj(l, ui + 1) if ui + 1 < len(units) else None
            interleave(a, b)
            ck('u%d' % ui)
        P.barrier()
        ck('mix')
        if dbg and l == 0:
            P.dma("sp", "dbg", lambda e: e.dma_start(out=ydbg_d.rearrange("(c p) s -> p c s", p=128),
                                                       in_=Ybf.rearrange("p (c s) -> p c s", s=S)), rd=[], wr=[])
            P.barrier(dma_keys=("dbg",))
        for h in range(4):
            WS.push(w_in_d[l, C_CG + h])
        for j in range(8):
            for i in range(3):
                WS.push(w_in_d[l, C_GATE + 8 * i + j])
            WS.push(w_br_d[l, j], 11)
        for oc in range(8):
            WS.push(w_o_d[l, oc])

        def nbank():
            b = rot["ps"] % 8
            rot["ps"] += 1
            return b

        for h in range(4):
            slot = WS.pop()
            ch = 7 + h
            for tb in range(NB):
                bank = nbank()
                proj_fm(slot, tb, bank)
                j2 = tb % 2
                act(ub(j2), PS[bank][:, :], AF.Silu, rd=[r_ps[bank]], wr=[r_ub[j2]])
                ysl = y_(ch)[:, tb * TB:(tb + 1) * TB]
                tt("pool", ysl, ysl, ub(j2), ALU.mult, rd=[r_ub[j2], r_y[ch][tb], r_yall[ch]], wr=[r_y[ch][tb]])
            WS.release(slot)
        ybr = ((0, 3), (3, 7), (7, 11))
        for j in range(8):
            gslots = [WS.pop() for _ in range(3)]
            bslot = WS.pop()
            wb = wslot(bslot, 11)
            for tb in range(NB):
                for i in range(3):
                    gb = nbank()
                    proj_fm(gslots[i], tb, gb)
                    act(ft(i), PS[gb][:, :], AF.Sigmoid, rd=[r_ps[gb]], wr=[r_ft[i]])
                    bb = nbank()
                    c0, c1 = ybr[i]
                    for ch in range(c0, c1):
                        mm(PS[bb][:, :], wb[:, ch, :], y_(ch)[:, tb * TB:(tb + 1) * TB], ch == c0, ch == c1 - 1,
                           rd=[r_w[bslot], r_y[ch][tb], r_yall[ch]], wr=[r_ps[bb]])
                    if i == 0:
                        tt("dve", ft(3), PS[bb][:, :], ft(0), ALU.mult, rd=[r_ps[bb], r_ft[0]], wr=[r_ft[3]])
                    elif i == 1:
                        tt("dve", ft(4), PS[bb][:, :], ft(1), ALU.mult, rd=[r_ps[bb], r_ft[1]], wr=[r_ft[4]])
                        tt("pool", ft(3), ft(3), ft(4), ALU.add, rd=[r_ft[3], r_ft[4]], wr=[r_ft[3]])
                    else:
                        tt("dve", ft(5), PS[bb][:, :], ft(2), ALU.mult, rd=[r_ps[bb], r_ft[2]], wr=[r_ft[5]])
                        tt("pool", m_(j, tb), ft(3), ft(5), ALU.add, rd=[r_ft[3], r_ft[5]], wr=[r_m[j][tb]])
            for s_ in gslots + [bslot]:
                WS.release(s_)
        P.barrier()
        if stop == '4a':
            for kc in range(KC):
                P.dma('pool', 'dbg', lambda e, kc=kc: e.dma_start(out=out_d[kc * 128:(kc + 1) * 128, :], in_=M[:, kc * S:(kc + 1) * S]), rd=[], wr=[])
            P.barrier(dma_keys=('dbg',))
            stop_state['skip_out'] = True
        ck('4a')
        for kc in range(KC):
            P.dma("sp", "xld", lambda e, kc=kc: e.dma_start(out=XY[:, kc * S:(kc + 1) * S],
                                                           in_=xsp_d[kc * 128:(kc + 1) * 128, :]),
                  rd=[r_xsp], wr=r_x[kc])
        for hg in range(4):
            for c in range(8):
                WS.push(w_m1_d[l, hg * 8 + c])
            for oc in range(8):
                WS.push(w_m2_d[l, hg * 8 + oc])
        for oc in range(8):
            slot = WS.pop()
            w = wslot(slot)
            for tb in range(NB):
                bank = nbank()
                for kc in range(KC):
                    mm(PS[bank][:, :], w[:, kc, :], m_(kc, tb), kc == 0, kc == KC - 1,
                       rd=[r_w[slot], r_m[kc][tb]], wr=[r_ps[bank]])
                tt("dve", x_(oc, tb), x_(oc, tb), PS[bank][:, :], ALU.add, rd=[r_ps[bank], r_x[oc][tb]],
                   wr=[r_x[oc][tb]])
            WS.release(slot)
        ck('4b')
        norm(8)
        P.barrier()
        ck('norm2')
        for hg in range(4):
            for c in range(8):
                slot = WS.pop()
                for tb in range(NB):
                    bank = nbank()
                    proj_fm(slot, tb, bank)
                    j2 = (c * NB + tb) % 2
                    act(ft(j2), PS[bank][:, :], AF.Relu, rd=[r_ps[bank]], wr=[r_ft[j2]])
                    tt("pool", m_(c, tb), ft(j2), ft(j2), ALU.mult, rd=[r_ft[j2]], wr=[r_m[c][tb]])
                WS.release(slot)
            for oc in range(8):
                slot = WS.pop()
                w = wslot(slot)
                for tb in range(NB):
                    bank = nbank()
                    for c in range(8):
                        mm(PS[bank][:, :], w[:, c, :], m_(c, tb), c == 0, c == 7,
                           rd=[r_w[slot], r_m[c][tb]], wr=[r_ps[bank]])
                    tt("dve", x_(oc, tb), x_(oc, tb), PS[bank][:, :], ALU.add, rd=[r_ps[bank], r_x[oc][tb]],
                       wr=[r_x[oc][tb]])
                WS.release(slot)
        P.barrier()
        if dbg and l == 0:
            for kc in range(KC):
                P.dma("sp", "dbg", lambda e, kc=kc: e.dma_start(out=xdbg_d[kc * 128:(kc + 1) * 128, :],
                                                               in_=XY[:, kc * S:(kc + 1) * S]), rd=r_x[kc], wr=[])

    try:
        for l in range(nl):
            layer(l)
    except _Stop:
        pass

    for kc in range(KC if not stop_state.get('skip_out') else 0):
        P.dma("sp", "out", lambda e, kc=kc: e.dma_start(out=out_d[kc * 128:(kc + 1) * 128, :],
                                                       in_=XY[:, kc * S:(kc + 1) * S]), rd=r_x[kc], wr=[])
    final_waits = tuple((k, n) for k, n in P.dcnt.items() if k in ("out", "dbg", "xsp"))
    P.q["sp"].append((final_waits, None, None))

    keys = list(P.ENG) + sorted(P.dcnt.keys())
    sems = {k: es.enter_context(nc.semaphore("s_" + k)) for k in keys}
    stats = {e: len(P.q[e]) for e in P.ENG}

    def run(e, name):
        for waits, fn, inc in P.q[name]:
            for k, n in waits:
                e.wait_ge(sems[k], n)
            if fn is not None:
                fn(e).then_inc(sems[inc[0]], inc[1])

    with nc.Block() as block:
        @block.tensor
        def _(e):
            run(e, "pe")

        @block.scalar
        def _(e):
            run(e, "act")

        @block.vector
        def _(e):
            run(e, "dve")

        @block.gpsimd
        def _(e):
            run(e, "pool")

        @block.sync
        def _(e):
            run(e, "sp")
    es.close()
    return nc, stats


def _rope_tables():
    pos = np.arange(S, dtype=np.float32)

    def tab(rot, theta, hd):
        inv = (1.0 / (np.float32(theta) ** (np.arange(0, rot, 2, dtype=np.float32) / np.float32(rot)))).astype(np.float32)
        ang = (pos[:, None] * inv[None, :]).astype(np.float32)
        c, s = np.cos(ang).astype(np.float32), np.sin(ang).astype(np.float32)
        r2 = rot // 2
        cosF = np.ones((hd, S), np.float32)
        sinF = np.zeros((hd, S), np.float32)
        cosF[0:r2] = c.T
        cosF[r2:2 * r2] = c.T
        sinF[0:r2] = s.T
        sinF[r2:2 * r2] = s.T
        return np.tile(cosF, (128 // hd, 1)), np.tile(sinF, (128 // hd, 1))

    cab, sab = tab(16, 500000.0, 64)
    cc, sc = tab(64, 10000.0, 64)
    return np.ascontiguousarray(np.stack([cab, sab, cc, sc]).astype(np.float32))


def _rot_mat(r2, hd=64):
    R = np.zeros((128, 128), np.float32)
    for b in range(128 // hd):
        for d in range(r2):
            R[b * hd + d + r2, b * hd + d] = -1.0
            R[b * hd + d, b * hd + d + r2] = 1.0
    return R


def _constants():
    ones = np.ones((128, 128), np.float32)
    bones = np.zeros((128, 128), np.float32)
    bones[0:64, 0:64] = 1.0
    bones[64:128, 64:128] = 1.0
    ident = np.eye(128, dtype=np.float32)
    cmats = np.stack([ones, bones, _rot_mat(8), _rot_mat(32), ident]).astype(np.float32)
    k = np.arange(128)[:, None]
    q = np.arange(512)[None, :]
    masks = []
    for rel in range(-1, 5):
        masks.append((np.abs(q - (128 * rel + k)) <= 64).astype(np.float32))
    band = (np.abs(np.arange(128)[None, :] - k) <= 64).astype(np.float32)
    masks.append(np.tile(band, (1, 4)))
    cmasks = np.stack(masks).astype(np.float32)
    m = np.arange(128)[:, None].astype(np.float32)
    n = np.arange(128)[None, :].astype(np.float32)
    relf = np.maximum(n - m, 0.0)
    relb = np.maximum(m - n, 0.0)
    mkf8 = (n >= m).astype(np.float32) * 0.125
    mkb8 = (m >= n).astype(np.float32) * 0.125
    pos1 = np.broadcast_to(n + 1.0, (128, 128))
    posb = np.broadcast_to(128.0 - n, (128, 128))
    tok = np.arange(128, dtype=np.float32)[:, None]
    cf = np.concatenate([relf, relb, mkf8, mkb8, pos1, posb, ones, tok, 127.0 - tok], axis=1).astype(np.float32)
    return cmats, cmasks, np.ascontiguousarray(cf)


def _prep_layers(inp, layers):
    f = lambda a: np.asarray(a, dtype=np.float32)
    nl = len(layers)
    w_in = f(inp["w_in"])[layers].reshape(nl, 8, 128, 57, 128).transpose(0, 3, 2, 1, 4)
    wbr = np.concatenate([f(inp["w_br_a"])[layers], f(inp["w_br_b"])[layers], f(inp["w_br_c"])[layers]], axis=1)
    w_br = wbr.reshape(nl, 11, 128, 8, 128).transpose(0, 3, 2, 1, 4)
    w_o = f(inp["w_o"])[layers].reshape(nl, 8, 128, 8, 128).transpose(0, 3, 2, 1, 4)
    w_m1 = f(inp["w_mlp1"])[layers].reshape(nl, 8, 128, 32, 128).transpose(0, 3, 2, 1, 4)
    w_m2 = f(inp["w_mlp2"])[layers].reshape(nl, 4, 8, 128, 8, 128).transpose(0, 1, 4, 3, 2, 5).reshape(nl, 32, 128, 8, 128)
    pcol = np.zeros((nl, 128, NPC), np.float32)
    arow = np.zeros((nl, 4, 128, 128), np.float32)
    for i, l in enumerate(layers):
        pcol[i, :, 0:8] = f(inp["norm1_g"])[l].reshape(8, 128).T
        pcol[i, :, 8:16] = f(inp["norm2_g"])[l].reshape(8, 128).T
        pcol[i, :, 16] = np.tile(f(inp["a_q_norm_g"])[l], 2)
        pcol[i, :, 17] = np.tile(f(inp["a_k_norm_g"])[l], 2)
        pcol[i, :, 18] = np.tile(f(inp["b_q_norm_g"])[l], 2)
        pcol[i, :, 19] = np.tile(f(inp["b_k_norm_g"])[l], 2)
        pcol[i, :, 20] = f(inp["b_out_norm_g"])[l]
        pcol[i, :, 21] = f(inp["c_out_norm_g"])[l]
        pcol[i, :, 22] = np.tile(f(inp["b_lambda_q1"])[l], 2)
        pcol[i, :, 23] = np.tile(f(inp["b_lambda_k1"])[l], 2)
        pcol[i, :, 24] = np.tile(f(inp["b_lambda_q2"])[l], 2)
        pcol[i, :, 25] = np.tile(f(inp["b_lambda_k2"])[l], 2)
        af, ab = f(inp["c_decay_f"])[l], f(inp["c_decay_b"])[l]
        for p in range(2):
            pcol[i, :, 26 + p] = np.repeat(af[2 * p:2 * p + 2], 64)
            pcol[i, :, 28 + p] = np.repeat(ab[2 * p:2 * p + 2], 64)
            arow[i, p] = np.repeat(af[2 * p:2 * p + 2], 64)[None, :]
            arow[i, 2 + p] = np.repeat(ab[2 * p:2 * p + 2], 64)[None, :]
        pcol[i, :, 30:34] = af[None, :]
        pcol[i, :, 34:38] = ab[None, :]
    c = np.ascontiguousarray
    return dict(w_in=c(w_in), w_br=c(w_br), w_o=c(w_o), w_m1=c(w_m1), w_m2=c(w_m2), pcol=pcol, arow=arow)


_CACHE = {}


def _get_prog(nl, l0):
    if (nl, l0) not in _CACHE:
        _CACHE[(nl, l0)] = build(nl, l0=l0)[0]
    return _CACHE[(nl, l0)]


FUSED = True


def kernel(**inputs):
    x = np.asarray(inputs["x"], dtype=np.float32)
    B = x.shape[0]
    cmats, cmasks, cf = _constants()
    rope = _rope_tables()
    consts = dict(c_mats=cmats, c_masks=cmasks, c_f32=cf, rope=rope)
    xT = [np.ascontiguousarray(x[b].T) for b in range(B)]
    groups = [list(range(DEPTH))] if FUSED else [[l] for l in range(DEPTH)]
    for layers in groups:
        nc = _get_prog(len(layers), layers[0])
        shared = _prep_layers(inputs, layers)
        shared.update(consts)
        in_maps = []
        for b in range(B):
            d = dict(shared)
            d["xT"] = xT[b]
            in_maps.append(d)
        res = run_bass_kernel_spmd(nc, in_maps, core_ids=list(range(B)))
        xT = [np.ascontiguousarray(np.asarray(r["outT"], dtype=np.float32)) for r in res.results]
    out = np.stack([t.T for t in xT], axis=0)
    return np.ascontiguousarray(out.astype(np.float32))
```

```python
import math
from contextlib import ExitStack
import numpy as np
import ml_dtypes
import concourse.bass as bass
import concourse.mybir as mybir
from concourse.bass_utils import run_bass_kernel_spmd

F32, BF16 = mybir.dt.float32, mybir.dt.bfloat16
ALU, AF = mybir.AluOpType, mybir.ActivationFunctionType

S = 2048
D = 1024
KC = 8
NB = 4
TB = 512
DEPTH = 4
EPS = 1e-6
NPC = 38
C_AQ, C_AK, C_AV = 0, 3, 6
C_BQ, C_BK, C_BV = 9, 13, 17
C_CQ, C_CK, C_CV, C_CG = 21, 23, 25, 29
C_GATE = 33
NWSLOT = 5


class Res:
    __slots__ = ("w", "r", "x")

    def __init__(self, x=False):
        self.w = None
        self.r = {}
        self.x = x


class Prog:
    ENG = ("pe", "act", "dve", "pool", "sp")

    def __init__(self):
        self.q = {e: [] for e in self.ENG}
        self.cnt = {e: 0 for e in self.ENG}
        self.seen = {e: {} for e in self.ENG}
        self.dcnt = {}

    def _need(self, eng, dep, waits, raw):
        if dep is None:
            return
        k, n = dep
        if k == eng:
            if eng == "pe" or not raw:
                return
            if n <= self.cnt[eng] - 2:
                return
        if k in self.dcnt:
            n = max(n, self.dcnt[k])
        if self.seen[eng].get(k, 0) >= n:
            return
        waits[k] = max(waits.get(k, 0), n)

    def _deps(self, eng, rd, wr):
        waits = {}
        for r in rd:
            self._need(eng, r.w, waits, True)
        for w in wr:
            self._need(eng, w.w, waits, False)
            for k, n in w.r.items():
                self._need(eng, (k, n), waits, False)
        for k, n in waits.items():
            self.seen[eng][k] = n
        return tuple(waits.items())

    def op(self, eng, fn, rd=(), wr=()):
        wr = list(wr) + [r for r in rd if r.x]
        rd = [r for r in rd if not r.x]
        waits = self._deps(eng, rd, wr)
        self.cnt[eng] += 1
        n = self.cnt[eng]
        self.q[eng].append((waits, fn, (eng, 1)))
        for r in rd:
            r.r[eng] = max(r.r.get(eng, 0), n)
        for w in wr:
            w.w = (eng, n)
            w.r = {}

    def dma(self, qeng, semkey, fn, rd=(), wr=()):
        waits = self._deps(qeng, rd, wr)
        self.dcnt[semkey] = self.dcnt.get(semkey, 0) + 16
        n = self.dcnt[semkey]
        self.q[qeng].append((waits, fn, (semkey, 16)))
        for r in rd:
            r.r[semkey] = max(r.r.get(semkey, 0), n)
        for w in wr:
            w.w = (semkey, n)
            w.r = {}

    def barrier(self, dma_keys=()):
        for e in self.ENG:
            waits = {}
            for f in self.ENG:
                if f != e and self.cnt[f] > self.seen[e].get(f, 0):
                    waits[f] = self.cnt[f]
            for k in dma_keys:
                if self.dcnt.get(k, 0) > self.seen[e].get(k, 0):
                    waits[k] = self.dcnt[k]
            for k, n in waits.items():
                self.seen[e][k] = n
            if waits:
                self.q[e].append((tuple(waits.items()), None, None))


class _Stop(Exception):
    pass


def build(nl, dbg=False, stop=None, l0=0):
    nc = bass.Bass("TRN2", target_bir_lowering=False)

    def din(name, shape, dt=F32):
        return nc.dram_tensor(name, list(shape), dt, kind="ExternalInput").ap()

    xT_d = din("xT", [D, S])
    w_in_d = din("w_in", [nl, 57, 128, 8, 128])
    w_br_d = din("w_br", [nl, 8, 128, 11, 128])
    w_o_d = din("w_o", [nl, 8, 128, 8, 128])
    w_m1_d = din("w_m1", [nl, 32, 128, 8, 128])
    w_m2_d = din("w_m2", [nl, 32, 128, 8, 128])
    pcol_d = din("pcol", [nl, 128, NPC])
    arow_d = din("arow", [nl, 4, 128, 128])
    cmat_d = din("c_mats", [5, 128, 128])
    cmask_d = din("c_masks", [7, 128, 512])
    cf32_d = din("c_f32", [128, 7 * 128 + 2])
    rope_d = din("rope", [4, 128, S])
    out_d = nc.dram_tensor("outT", [D, S], F32, kind="ExternalOutput").ap()
    xsp_d = nc.dram_tensor("xspill", [D, S], F32, kind="Internal").ap()
    if dbg:
        ydbg_d = nc.dram_tensor("ydbg", [11 * 128, S], BF16, kind="ExternalOutput").ap()
        xdbg_d = nc.dram_tensor("xdbg", [D, S], F32, kind="ExternalOutput").ap()

    P = Prog()
    es = ExitStack()

    def sb(name, shape, dt):
        return es.enter_context(nc.sbuf_tensor(name, list(shape), dt))

    XN = sb("XN", [128, KC * S], BF16)
    XY = sb("XY", [128, KC * S], F32)
    M = sb("M", [128, KC * S], BF16)
    WR = sb("WR", [128, NWSLOT * 11 * 128], BF16)
    CMB = sb("CMB", [128, 5 * 128], BF16)
    MASKS = sb("MASKS", [128, 7 * 512], BF16)
    CF = sb("CF", [128, 7 * 128 + 2], F32)
    PCOL = sb("PCOL", [128, NPC], F32)
    SMALL = sb("SMALL", [128, 64], F32)
    CTAB = sb("CTAB", [128, 13 * 128], F32)
    CDER = sb("CDER", [128, 6 * 2048], BF16)
    S32 = sb("S32", [128, 2 * 128], F32)
    SQB = sb("SQB", [128, 2 * 512], BF16)
    UB = sb("UB", [128, 2 * 512], BF16)
    FT = sb("FT", [128, 8 * 512], F32)

    Ybf = XY[:, 0:11 * 1024].bitcast(BF16)
    ROPE = XY[:, 11 * 1024:11 * 1024 + 4096]
    PB = XY[:, 15 * 1024:16 * 1024].bitcast(BF16)
    DTOT = CDER[:, 4 * 2048:6 * 2048].bitcast(F32)

    def x_(kc, tb):
        return XY[:, kc * S + tb * TB: kc * S + (tb + 1) * TB]

    def xn_(kc, tb):
        return XN[:, kc * S + tb * TB: kc * S + (tb + 1) * TB]

    def xnfull(kc):
        return XN[:, kc * S:(kc + 1) * S]

    def y_(ch):
        return Ybf[:, ch * S:(ch + 1) * S]

    def m_(kc, tb):
        return M[:, kc * S + tb * TB: kc * S + (tb + 1) * TB]

    def uq(s):
        return M[:, s * 8192: s * 8192 + 2048]

    def uk(s):
        return M[:, s * 8192 + 2048: s * 8192 + 4096]

    def uv(s):
        return M[:, s * 8192 + 4096: s * 8192 + 8192].rearrange("p (h t c) -> p h t c", h=2, c=128)

    def kz(s, m):
        return uk(s) if m == 0 else M[:, s * 8192 + 6144: s * 8192 + 8192]

    def wslot(s, nk=8):
        return WR[:, s * 1408: s * 1408 + nk * 128].rearrange("p (k n) -> p k n", n=128)

    ONES = CMB[:, 0:128]
    BONES = CMB[:, 128:256]
    RAB = CMB[:, 256:384]
    RC = CMB[:, 384:512]
    IDENT = CMB[:, 512:640]

    def mask_(i):
        return MASKS[:, i * 512:(i + 1) * 512]

    RELF, RELB = CF[:, 0:128], CF[:, 128:256]
    MKF8, MKB8 = CF[:, 256:384], CF[:, 384:512]
    POS1, POSB = CF[:, 512:640], CF[:, 640:768]
    ONESF = CF[:, 768:896]
    TOKC, TOKCR = CF[:, 896:897], CF[:, 897:898]

    def ctab(i):
        return CTAB[:, i * 128:(i + 1) * 128]

    def cder(i):
        return CDER[:, i * 2048:(i + 1) * 2048]

    KDF, KDB = cder(0).rearrange("p (t c) -> p t c", c=128), cder(1).rearrange("p (t c) -> p t c", c=128)
    QDF, QDB = cder(2), cder(3)
    STF, STB = cder(4).rearrange("p (t c) -> p t c", c=128), cder(5).rearrange("p (t c) -> p t c", c=128)

    def pb(i):
        return PB[:, i * 512:(i + 1) * 512]

    def sqb(i):
        return SQB[:, i * 512:(i + 1) * 512]

    def ub(i):
        return UB[:, i * 512:(i + 1) * 512]

    def ft(i):
        return FT[:, i * 512:(i + 1) * 512]

    PS = [es.enter_context(nc.psum_tensor("ps%d" % i, [128, 512], F32)) for i in range(8)]

    r_x = [[Res() for _ in range(NB)] for _ in range(KC)]
    r_xn = [Res() for _ in range(NB)]
    r_y = [[Res() for _ in range(NB)] for _ in range(11)]
    r_yall = [Res() for _ in range(11)]
    r_m = [[Res() for _ in range(NB)] for _ in range(KC)]
    r_uq = [Res(), Res()]
    r_uk = [Res(), Res()]
    r_uv = [Res(), Res()]
    r_w = [Res() for _ in range(NWSLOT)]
    r_ps = [Res(True) for _ in range(8)]
    r_pb = [Res() for _ in range(4)]
    r_sqb = [Res(), Res()]
    r_ub = [Res(), Res()]
    r_ft = [Res() for _ in range(8)]
    r_rope = Res()
    r_acc = Res()
    r_const = Res()
    r_pcol = Res()
    r_small = Res()
    r_ctab = Res()
    r_cder = [Res() for _ in range(6)]
    r_s32 = [Res(), Res()]
    r_xsp = Res()

    wstate = {"n": 0}

    def load_w(dram_ap, nk=8):
        s = wstate["n"] % NWSLOT
        wstate["n"] += 1
        dst = wslot(s, nk)
        P.dma("pool", "w%d" % s, lambda e, dst=dst, src=dram_ap: e.dma_start(out=dst, in_=src), wr=[r_w[s]])
        return s

    class WStream:
        def __init__(self):
            self.pending = []
            self.loaded = []
            self.inuse = 0

        def push(self, ap, nk=8):
            self.pending.append((ap, nk))

        def prefetch(self):
            while self.pending and len(self.loaded) + self.inuse < NWSLOT:
                ap, nk = self.pending.pop(0)
                self.loaded.append(load_w(ap, nk))

        def pop(self):
            self.prefetch()
            s = self.loaded.pop(0)
            self.inuse += 1
            return s

        def release(self, s):
            self.inuse -= 1
            self.prefetch()

    WS = WStream()

    def mm(out, lhsT, rhs, start, stop, rd, wr):
        P.op("pe", lambda e: e.matmul(out, lhsT=lhsT, rhs=rhs, start=start, stop=stop), rd=rd, wr=wr)

    def act(out, in_, func, rd, wr, scale=None, bias=None):
        kw = {}
        if scale is not None:
            kw["scale"] = scale
        if bias is not None:
            kw["bias"] = bias
        P.op("act", lambda e: e.activation(out=out, in_=in_, func=func, **kw), rd=rd, wr=wr)

    def tt(eng, out, in0, in1, op, rd, wr):
        P.op(eng, lambda e: e.tensor_tensor(out=out, in0=in0, in1=in1, op=op), rd=rd, wr=wr)

    def stt(eng, out, in0, scalar, in1, op0, op1, rd, wr):
        P.op(eng, lambda e: e.scalar_tensor_tensor(out=out, in0=in0, scalar=scalar, in1=in1, op0=op0, op1=op1),
             rd=rd, wr=wr)

    def ts(eng, out, in0, s1, s2, op0, op1, rd, wr):
        if op1 is None:
            P.op(eng, lambda e: e.tensor_scalar(out=out, in0=in0, scalar1=s1, scalar2=None, op0=op0), rd=rd, wr=wr)
        else:
            P.op(eng, lambda e: e.tensor_scalar(out=out, in0=in0, scalar1=s1, scalar2=s2, op0=op0, op1=op1),
                 rd=rd, wr=wr)

    def cp(eng, out, in_, rd, wr):
        if eng == "act":
            act(out, in_, AF.Copy, rd, wr)
        else:
            P.op(eng, lambda e: e.tensor_copy(out=out, in_=in_), rd=rd, wr=wr)

    def recip(out, in_, rd, wr):
        P.op("dve", lambda e: e.reciprocal(out=out, in_=in_), rd=rd, wr=wr)

    def rstd_from(ps_ap, r_in, ftile, r_ftile, inv_n):
        act(ftile, ps_ap, AF.Ln, rd=[r_in], wr=[r_ftile], scale=inv_n, bias=EPS)
        act(ftile, ftile, AF.Exp, rd=[r_ftile], wr=[r_ftile], scale=-0.5)

    rot = {"ps": 0, "alt": 0}
    SBANK = (4, 5, 3)

    P.dma("pool", "cst", lambda e: e.dma_start(out=CMB[:, :].rearrange("p (c n) -> p c n", n=128),
                                                  in_=cmat_d.rearrange("c p n -> p c n")), wr=[r_const])
    P.dma("pool", "cst", lambda e: e.dma_start(out=MASKS[:, :].rearrange("p (c n) -> p c n", n=512),
                                                  in_=cmask_d.rearrange("c p n -> p c n")), wr=[r_const])
    P.dma("sp", "cst", lambda e: e.dma_start(out=CF[:, :], in_=cf32_d), wr=[r_const])
    for kc in range(KC):
        P.dma("sp", "xld", lambda e, kc=kc: e.dma_start(out=XY[:, kc * S:(kc + 1) * S],
                                                       in_=xT_d[kc * 128:(kc + 1) * 128, :]),
              wr=r_x[kc])

    def norm(gcol0):
        for tb in range(NB):
            for kc in range(KC):
                i = kc % 2
                act(sqb(i), x_(kc, tb), AF.Square, rd=[r_x[kc][tb]], wr=[r_sqb[i]])
                mm(PS[2][:, :], ONES, sqb(i), kc == 0, kc == KC - 1, rd=[r_sqb[i], r_const], wr=[r_ps[2]])
            rstd_from(PS[2][:, :], r_ps[2], ft(0), r_ft[0], 1.0 / D)
            for kc in range(KC):
                stt("dve", xn_(kc, tb), x_(kc, tb), PCOL[:, gcol0 + kc: gcol0 + kc + 1], ft(0), ALU.mult, ALU.mult,
                    rd=[r_x[kc][tb], r_ft[0], r_pcol], wr=[r_xn[tb]])

    def proj_fm(slot, tb, bank):
        w = wslot(slot)
        for kc in range(KC):
            mm(PS[bank][:, :], w[:, kc, :], xn_(kc, tb), kc == 0, kc == KC - 1,
               rd=[r_w[slot], r_xn[tb]], wr=[r_ps[bank]])

    def perm_dst(buf, tb, delta):
        if delta == 1:
            return buf[:, tb * TB:(tb + 1) * TB], None
        n0 = tb * TB // delta
        return buf.rearrange("p (r n) -> p n r", r=delta)[:, n0:n0 + TB // delta, :], delta

    def srcv(ap, delta):
        if delta is None:
            return ap
        return ap.rearrange("p (n r) -> p n r", r=delta)

    def qk_post(bank, tb, gcol, rmat, normed, dsts, delta):
        j = rot["alt"] % 2
        rot["alt"] += 1
        pj = PS[bank][:, :]
        cos = ROPE[:, tb * TB:(tb + 1) * TB]
        sin = ROPE[:, S + tb * TB: S + (tb + 1) * TB]
        if normed:
            act(sqb(j), pj, AF.Square, rd=[r_ps[bank]], wr=[r_sqb[j]])
            mm(PS[2][:, :], BONES, sqb(j), True, True, rd=[r_sqb[j], r_const], wr=[r_ps[2]])
            yield
            rstd_from(PS[2][:, :], r_ps[2], ft(j), r_ft[j], 1.0 / 64)
            stt("dve", ub(j), pj, PCOL[:, gcol:gcol + 1], ft(j), ALU.mult, ALU.mult,
                rd=[r_ps[bank], r_ft[j], r_pcol], wr=[r_ub[j]])
            yield
            mm(PS[2][:, :], rmat, ub(j), True, True, rd=[r_ub[j], r_const], wr=[r_ps[2]])
            tt("pool", ft(2 + j), ub(j), cos, ALU.mult, rd=[r_ub[j], r_rope], wr=[r_ft[2 + j]])
        else:
            cp("act", ub(j), pj, rd=[r_ps[bank]], wr=[r_ub[j]])
            tt("dve", ft(2 + j), pj, cos, ALU.mult, rd=[r_ps[bank], r_rope], wr=[r_ft[2 + j]])
            yield
            mm(PS[2][:, :], rmat, ub(j), True, True, rd=[r_ub[j], r_const], wr=[r_ps[2]])
        yield
        tt("dve", ft(4 + j), PS[2][:, :], sin, ALU.mult, rd=[r_ps[2], r_rope], wr=[r_ft[4 + j]])
        for buf, rows, res in dsts:
            dst, dl = perm_dst(buf, tb, delta)
            a0, a1 = srcv(ft(2 + j), dl), srcv(ft(4 + j), dl)
            if rows is not None:
                dst, a0, a1 = dst[rows[0]:rows[1]], a0[rows[0]:rows[1]], a1[rows[0]:rows[1]]
            tt("pool", dst, a0, a1, ALU.add, rd=[r_ft[2 + j], r_ft[4 + j]], wr=[res])

    def v_proj(slot, set_, coloff, delta):
        w = wslot(slot)
        V = uv(set_)
        for g4 in range(4):
            bank = rot["ps"] % 2
            rot["ps"] += 1
            for ti in range(4):
                t = g4 * 4 + ti
                for kc in range(KC):
                    if delta == 1:
                        lhs = xnfull(kc)[:, t * 128:(t + 1) * 128]
                    elif delta == 4:
                        lhs = xnfull(kc).rearrange("p (n r) -> p r n", r=4)[:, t // 4, (t % 4) * 128:(t % 4 + 1) * 128]
                    else:
                        lhs = xnfull(kc).rearrange("p (n r) -> p r n", r=16)[:, t, :]
                    mm(PS[bank][:, ti * 128:(ti + 1) * 128], lhs, w[:, kc, :], kc == 0, kc == KC - 1,
                       rd=[r_w[slot]] + r_xn, wr=[r_ps[bank]])
            cp("act", V[:, coloff // 128, g4 * 4:(g4 + 1) * 4, :],
               PS[bank][:, :].rearrange("p (t c) -> p t c", c=128), rd=[r_ps[bank]], wr=[r_uv[set_]])
            yield

    units = []
    for g in range(3):
        units.append(dict(kind="A", idx=g, q=C_AQ + g, k=C_AK + g, v=[C_AV + g], gq=16, gk=17, delta=(1, 4, 16)[g]))
    for h in range(4):
        units.append(dict(kind="B", idx=h, q=C_BQ + h, k=C_BK + h, v=[C_BV + h], gq=18, gk=19, delta=1))
    for p in range(2):
        units.append(dict(kind="C", idx=p, q=C_CQ + p, k=C_CK + p, v=[C_CV + 2 * p, C_CV + 2 * p + 1], gq=0, gk=0,
                          delta=1))

    def push_unit_weights(l, u):
        WS.push(w_in_d[l, u["q"]])
        WS.push(w_in_d[l, u["k"]])
        for c in u["v"]:
            WS.push(w_in_d[l, c])

    def rope_load(which):
        P.dma("sp", "rope", lambda e: e.dma_start(out=ROPE[:, 0:S], in_=rope_d[2 * which]), wr=[r_rope])
        P.dma("sp", "rope", lambda e: e.dma_start(out=ROPE[:, S:2 * S], in_=rope_d[2 * which + 1]), wr=[r_rope])

    def inproj(l, ui):
        u = units[ui]
        set_ = ui % 2
        normed = u["kind"] != "C"
        rmat = RC if u["kind"] == "C" else RAB
        if ui == 0:
            rope_load(0)
        if ui == 7:
            rope_load(1)
        for which in ("q", "k"):
            if which == "q":
                dsts, gcol = [(uq(set_), None, r_uq[set_])], u["gq"]
            elif u["kind"] == "C":
                dsts, gcol = [(uk(set_), None, r_uk[set_])], u["gk"]
            else:
                k0, k1 = kz(set_, 0), kz(set_, 1)
                P.op("pool", lambda e, k0=k0: e.memset(k0[64:128, :], 0.0), wr=[r_uk[set_]])
                P.op("pool", lambda e, k1=k1: e.memset(k1[0:64, :], 0.0), wr=[r_uv[set_]])
                dsts, gcol = [(k0, (0, 64), r_uk[set_]), (k1, (64, 128), r_uv[set_])], u["gk"]
            slot = WS.pop()
            for tb in range(NB):
                bank = rot["ps"] % 2
                rot["ps"] += 1
                proj_fm(slot, tb, bank)
                yield
                yield from qk_post(bank, tb, gcol, rmat, normed, dsts, u["delta"])
                yield
            WS.release(slot)
        for vi, c in enumerate(u["v"]):
            slot = WS.pop()
            yield from v_proj(slot, set_, vi * 128, u["delta"])
            WS.release(slot)

    def attn_B(l, ui):
        u = units[ui]
        h = u["idx"]
        set_ = ui % 2
        qT, kT, V = uq(set_), uk(set_), uv(set_)
        rdq = [r_uq[set_], r_uk[set_], r_uv[set_]]
        ch = 3 + h
        for qb in range(NB):
            for m in range(2):
                rows = slice(64 * m, 64 * m + 64)

                def smm(kt):
                    bank = SBANK[kt % 3]
                    mm(PS[bank][:, :], kz(set_, m)[:, kt * 128:(kt + 1) * 128], qT[:, qb * TB:(qb + 1) * TB], True, True,
                       rd=rdq, wr=[r_ps[bank]])

                smm(0)
                smm(1)
                for kt in range(16):
                    if kt + 2 < 16:
                        smm(kt + 2)
                    bank = SBANK[kt % 3]
                    pi = kt % 4
                    act(pb(pi), PS[bank][:, :], AF.Exp, rd=[r_ps[bank]], wr=[r_pb[pi]], scale=0.125)
                    mm(PS[6][:, :], V[:, 0, kt, :], pb(pi), kt == 0, kt == 15, rd=[r_pb[pi], r_uv[set_]],
                       wr=[r_ps[6]])
                    mm(PS[7][:, :], ONES, pb(pi), kt == 0, kt == 15, rd=[r_pb[pi], r_const], wr=[r_ps[7]])
                    if kt % 4 == 3 and kt < 15:
                        yield
                recip(ft(6), PS[7][:, :], rd=[r_ps[7]], wr=[r_ft[6]])
                if m == 0:
                    tt("dve", ft(7), PS[6][:, :], ft(6), ALU.mult, rd=[r_ps[6], r_ft[6]], wr=[r_ft[7]])
                else:
                    tt("dve", ft(6), PS[6][:, :], ft(6), ALU.mult, rd=[r_ps[6], r_ft[6]], wr=[r_ft[6]])
                    stt("dve", ft(7), ft(6), SMALL[:, 0:1], ft(7), ALU.mult, ALU.add,
                        rd=[r_ft[6], r_ft[7], r_small], wr=[r_ft[7]])
                    act(sqb(0), ft(7), AF.Square, rd=[r_ft[7]], wr=[r_sqb[0]])
                    mm(PS[7][:, :], ONES, sqb(0), True, True, rd=[r_sqb[0], r_const], wr=[r_ps[7]])
                    rstd_from(PS[7][:, :], r_ps[7], ft(6), r_ft[6], 1.0 / 128)
                    stt("dve", y_(ch)[:, qb * TB:(qb + 1) * TB], ft(7), SMALL[:, 1:2], ft(6), ALU.mult, ALU.mult,
                        rd=[r_ft[7], r_ft[6], r_small], wr=[r_y[ch][qb], r_yall[ch]])
                yield

    def attn_A(l, ui):
        u = units[ui]
        g = u["idx"]
        delta = u["delta"]
        set_ = ui % 2
        qT, kT, V = uq(set_), uk(set_), uv(set_)
        rdq = [r_uq[set_], r_uk[set_], r_uv[set_]]
        for s2 in range(2):
            rows = slice(64 * s2, 64 * s2 + 64)
            for qb in range(NB):
                if g < 2:
                    rels = list(range(-1, 5)) if g == 0 else list(range(4))
                    kts = [(4 * qb + r, r) for r in rels if 0 <= 4 * qb + r < 16]
                    n = len(kts)

                    def smm(i):
                        kt, rel = kts[i]
                        bank = 4 + i % 2
                        mm(PS[bank][:, :], kz(set_, s2)[:, kt * 128:(kt + 1) * 128], qT[:, qb * TB:(qb + 1) * TB], True,
                           True, rd=rdq, wr=[r_ps[bank]])

                    smm(0)
                    for i in range(n):
                        if i + 1 < n:
                            smm(i + 1)
                        kt, rel = kts[i]
                        bank = 4 + i % 2
                        pi = i % 3
                        act(pb(pi), PS[bank][:, :], AF.Exp, rd=[r_ps[bank]], wr=[r_pb[pi]], scale=0.125)
                        tt("dve", pb(pi), pb(pi), mask_(rel + 1), ALU.mult, rd=[r_pb[pi], r_const], wr=[r_pb[pi]])
                        mm(PS[6][:, :], V[:, 0, kt, :], pb(pi), i == 0, i == n - 1, rd=[r_pb[pi], r_uv[set_]],
                           wr=[r_ps[6]])
                        mm(PS[7][:, :], ONES, pb(pi), i == 0, i == n - 1, rd=[r_pb[pi], r_const], wr=[r_ps[7]])
                        if i < n - 1:
                            yield
                else:
                    bank = 4 + qb % 2
                    for rel in range(4):
                        kt = 4 * qb + rel
                        mm(PS[bank][:, rel * 128:(rel + 1) * 128], kz(set_, s2)[:, kt * 128:(kt + 1) * 128],
                           qT[:, kt * 128:(kt + 1) * 128], True, True, rd=rdq, wr=[r_ps[bank]])
                    pi = qb % 3
                    act(pb(pi), PS[bank][:, :], AF.Exp, rd=[r_ps[bank]], wr=[r_pb[pi]], scale=0.125)
                    tt("dve", pb(pi), pb(pi), mask_(6), ALU.mult, rd=[r_pb[pi], r_const], wr=[r_pb[pi]])
                    for rel in range(4):
                        kt = 4 * qb + rel
                        mm(PS[6][:, rel * 128:(rel + 1) * 128], V[:, 0, kt, :], pb(pi)[:, rel * 128:(rel + 1) * 128],
                           True, True, rd=[r_pb[pi], r_uv[set_]], wr=[r_ps[6]])
                        mm(PS[7][:, rel * 128:(rel + 1) * 128], ONES, pb(pi)[:, rel * 128:(rel + 1) * 128],
                           True, True, rd=[r_pb[pi], r_const], wr=[r_ps[7]])
                if delta == 1:
                    ydst = y_(g)[rows, qb * TB:(qb + 1) * TB]
                    ddst = DTOT[rows, qb * TB:(qb + 1) * TB]
                    nsrc, dsrc = PS[6][rows, :], PS[7][rows, :]
                elif delta == 4:
                    ydst = y_(g).rearrange("p (n r) -> p r n", r=4)[rows, qb, :]
                    ddst = DTOT[:, :].rearrange("p (n r) -> p r n", r=4)[rows, qb, :]
                    nsrc, dsrc = PS[6][rows, :], PS[7][rows, :]
                else:
                    ydst = y_(g).rearrange("p (n r) -> p r n", r=16)[rows, 4 * qb:4 * qb + 4, :]
                    ddst = DTOT[:, :].rearrange("p (n r) -> p r n", r=16)[rows, 4 * qb:4 * qb + 4, :]
                    nsrc = PS[6][rows, :].rearrange("p (a b) -> p a b", a=4)
                    dsrc = PS[7][rows, :].rearrange("p (a b) -> p a b", a=4)
                cp("act", ydst, nsrc, rd=[r_ps[6]], wr=[r_yall[g]] + r_y[g])
                if g == 0:
                    cp("dve", ddst, dsrc, rd=[r_ps[7]], wr=[r_cder[4], r_cder[5]])
                else:
                    tt("dve", ddst, ddst, dsrc, ALU.add, rd=[r_ps[7], r_cder[4], r_cder[5]], wr=[r_cder[4], r_cder[5]])
                yield
        if g == 2:
            for tb in range(NB):
                recip(ft(6), DTOT[:, tb * TB:(tb + 1) * TB], rd=[r_cder[4], r_cder[5]], wr=[r_ft[6]])
                for gg in range(3):
                    ysl = y_(gg)[:, tb * TB:(tb + 1) * TB]
                    tt("pool", ysl, ysl, ft(6), ALU.mult, rd=[r_ft[6], r_yall[gg]] + r_y[gg], wr=[r_y[gg][tb], r_yall[gg]])
                yield

    def attn_C(l, ui):
        u = units[ui]
        p = u["idx"]
        set_ = ui % 2
        qT, kT, V = uq(set_), uk(set_), uv(set_)
        DQF, DQB, DKF, DKB = ctab(4 + p), ctab(6 + p), ctab(8 + p), ctab(10 + p)
        g128f, g128b = SMALL[:, 8 + p:9 + p], SMALL[:, 10 + p:11 + p]
        for g4 in range(4):
            bank = 3
            tp = PS[bank][:, :].bitcast(BF16)[:, 0:512]
            for ti in range(4):
                t = g4 * 4 + ti
                P.op("pe", lambda e, t=t, ti=ti, tp=tp: e.transpose(tp[:, ti * 128:(ti + 1) * 128],
                                                                      kT[:, t * 128:(t + 1) * 128], IDENT),
                     rd=[r_uk[set_], r_const], wr=[r_ps[bank]])
            tp3 = tp.rearrange("p (t c) -> p t c", c=128)
            tt("dve", KDF[:, g4 * 4:(g4 + 1) * 4, :], tp3, DKF.unsqueeze(1).to_broadcast([128, 4, 128]), ALU.mult,
               rd=[r_ps[bank], r_ctab], wr=[r_cder[0]])
            tt("dve", KDB[:, g4 * 4:(g4 + 1) * 4, :], tp3, DKB.unsqueeze(1).to_broadcast([128, 4, 128]), ALU.mult,
               rd=[r_ps[bank], r_ctab], wr=[r_cder[1]])
            q3 = qT[:, g4 * TB:(g4 + 1) * TB].rearrange("p (t c) -> p t c", c=128)
            tt("pool", QDF[:, g4 * TB:(g4 + 1) * TB].rearrange("p (t c) -> p t c", c=128), q3,
               DQF.unsqueeze(1).to_broadcast([128, 4, 128]), ALU.mult, rd=[r_uq[set_], r_ctab], wr=[r_cder[2]])
            tt("pool", QDB[:, g4 * TB:(g4 + 1) * TB].rearrange("p (t c) -> p t c", c=128), q3,
               DQB.unsqueeze(1).to_broadcast([128, 4, 128]), ALU.mult, rd=[r_uq[set_], r_ctab], wr=[r_cder[3]])
            yield
        P.op("pool", lambda e: e.memset(S32[:, :], 0.0), wr=r_s32)
        P.op("pool", lambda e: e.memset(STF[:, 0, :], 0.0), wr=[r_cder[4]])
        P.op("pool", lambda e: e.memset(STB[:, 15, :], 0.0), wr=[r_cder[5]])
        for step in range(15):
            for d in range(2):
                c = step if d == 0 else 15 - step
                KD = KDF if d == 0 else KDB
                ST = STF if d == 0 else STB
                gcol = g128f if d == 0 else g128b
                s32 = S32[:, d * 128:(d + 1) * 128]
                bank = 4 + d
                for hh in range(2):
                    mm(PS[bank][:, hh * 128:(hh + 1) * 128], KD[:, c, :], V[:, hh, c, :], True, True,
                       rd=[r_cder[d], r_uv[set_]], wr=[r_ps[bank]])
                for hh in range(2):
                    rws = slice(64 * hh, 64 * hh + 64)
                    stt("dve", s32[rws, :], s32[rws, :], gcol[rws, :], PS[bank][rws, hh * 128:(hh + 1) * 128],
                        ALU.mult, ALU.add, rd=[r_ps[bank], r_s32[d], r_small], wr=[r_s32[d]])
                cn = c + 1 if d == 0 else c - 1
                cp("act", ST[:, cn, :], s32, rd=[r_s32[d]], wr=[r_cder[4 + d]])
            if step % 4 == 3:
                yield
        yield
        for hh in range(2):
            rws = slice(64 * hh, 64 * hh + 64)
            h = 2 * p + hh
            ch = 7 + h
            for cb in range(4):
                sbank = 4 + cb % 2
                for ci in range(4):
                    c = 4 * cb + ci
                    mm(PS[sbank][:, ci * 128:(ci + 1) * 128], kT[rws, c * 128:(c + 1) * 128],
                       qT[rws, c * 128:(c + 1) * 128], True, True, rd=[r_uq[set_], r_uk[set_]], wr=[r_ps[sbank]])
                pi = cb % 3
                tt("dve", pb(pi).rearrange("p (t c) -> p t c", c=128),
                   PS[sbank][:, :].rearrange("p (t c) -> p t c", c=128),
                   ctab(h).unsqueeze(1).to_broadcast([128, 4, 128]), ALU.mult, rd=[r_ps[sbank], r_ctab],
                   wr=[r_pb[pi]])
                for ci in range(4):
                    c = 4 * cb + ci
                    o = PS[6][:, ci * 128:(ci + 1) * 128]
                    mm(o, V[:, hh, c, :], pb(pi)[:, ci * 128:(ci + 1) * 128], True, False,
                       rd=[r_pb[pi], r_uv[set_]], wr=[r_ps[6]])
                    mm(o, STF[rws, c, :], QDF[rws, c * 128:(c + 1) * 128], False, False,
                       rd=[r_cder[4], r_cder[2]], wr=[r_ps[6]])
                    mm(o, STB[rws, c, :], QDB[rws, c * 128:(c + 1) * 128], False, True,
                       rd=[r_cder[5], r_cder[3]], wr=[r_ps[6]])
                cp("dve", ft(7), PS[6][:, :], rd=[r_ps[6]], wr=[r_ft[7]])
                act(pb(3), PS[6][:, :], AF.Square, rd=[r_ps[6]], wr=[r_pb[3]])
                mm(PS[7][:, :], ONES, pb(3), True, True, rd=[r_pb[3], r_const], wr=[r_ps[7]])
                rstd_from(PS[7][:, :], r_ps[7], ft(6), r_ft[6], 1.0 / 128)
                stt("dve", y_(ch)[:, cb * TB:(cb + 1) * TB], ft(7), PCOL[:, 21:22], ft(6), ALU.mult, ALU.mult,
                    rd=[r_ft[7], r_ft[6], r_pcol], wr=[r_y[ch][cb], r_yall[ch]])
                yield

    def layer_params(l):
        lam_init = 0.8 - 0.6 * math.exp(-0.3 * (l + l0))
        P.dma("sp", "par", lambda e: e.dma_start(out=PCOL[:, :], in_=pcol_d[l]), wr=[r_pcol])
        P.dma("sp", "par", lambda e: e.dma_start(out=CTAB[:, 8 * 128:12 * 128].rearrange("p (c n) -> p c n", n=128),
                                                   in_=arow_d[l].rearrange("c p n -> p c n")), wr=[r_ctab])
        rp = [r_pcol]
        tt("dve", SMALL[:, 2:3], PCOL[:, 22:23], PCOL[:, 23:24], ALU.mult, rd=rp, wr=[r_small])
        tt("dve", SMALL[:, 3:4], PCOL[:, 24:25], PCOL[:, 25:26], ALU.mult, rd=rp, wr=[r_small])
        mm(PS[2][:, 0:2], ONESF[0:64, :], SMALL[0:64, 2:4], True, True, rd=[r_small, r_const], wr=[r_ps[2]])
        act(SMALL[:, 4:6], PS[2][:, 0:2], AF.Exp, rd=[r_ps[2]], wr=[r_small])
        tt("dve", SMALL[:, 6:7], SMALL[:, 5:6], SMALL[:, 4:5], ALU.subtract, rd=[r_small], wr=[r_small])
        ts("dve", SMALL[:, 0:1], SMALL[:, 6:7], -lam_init, None, ALU.add, None, rd=[r_small], wr=[r_small])
        ts("dve", SMALL[:, 1:2], PCOL[:, 20:21], 1.0 - lam_init, None, ALU.mult, None, rd=rp + [r_small],
           wr=[r_small])
        act(SMALL[:, 12:16], PCOL[:, 26:30], AF.Exp, rd=rp + [r_small], wr=[r_small])
        ts("dve", SMALL[:, 12:16], SMALL[:, 12:16], -1.0, None, ALU.mult, None, rd=[r_small], wr=[r_small])
        act(SMALL[:, 8:12], SMALL[:, 12:16], AF.Exp, rd=[r_small], wr=[r_small], scale=128.0)
        act(SMALL[:, 16:24], PCOL[:, 30:38], AF.Exp, rd=rp + [r_small], wr=[r_small])
        ts("dve", SMALL[:, 16:24], SMALL[:, 16:24], -1.0, None, ALU.mult, None, rd=[r_small], wr=[r_small])
        rs = [r_small, r_const]
        for h in range(4):
            act(ctab(h), RELF, AF.Exp, rd=rs, wr=[r_ctab], scale=SMALL[:, 16 + h:17 + h])
            tt("dve", ctab(h), ctab(h), MKF8, ALU.mult, rd=[r_ctab, r_const], wr=[r_ctab])
            act(ctab(12), RELB, AF.Exp, rd=rs + [r_ctab], wr=[r_ctab], scale=SMALL[:, 20 + h:21 + h])
            tt("dve", ctab(12), ctab(12), MKB8, ALU.mult, rd=[r_ctab, r_const], wr=[r_ctab])
            tt("dve", ctab(h), ctab(h), ctab(12), ALU.add, rd=[r_ctab], wr=[r_ctab])
        for p in range(2):
            act(ctab(4 + p), POS1, AF.Exp, rd=rs + [r_ctab], wr=[r_ctab], scale=SMALL[:, 12 + p:13 + p])
            act(ctab(6 + p), POSB, AF.Exp, rd=rs + [r_ctab], wr=[r_ctab], scale=SMALL[:, 14 + p:15 + p])
            ts("dve", ctab(4 + p), ctab(4 + p), 0.125, None, ALU.mult, None, rd=[r_ctab], wr=[r_ctab])
            ts("dve", ctab(6 + p), ctab(6 + p), 0.125, None, ALU.mult, None, rd=[r_ctab], wr=[r_ctab])
            for d in range(2):
                dstt = ctab(8 + 2 * d + p)
                src = dstt
                act(dstt, src, AF.Exp, rd=rp + [r_ctab], wr=[r_ctab])
                ts("dve", dstt, dstt, -1.0, None, ALU.mult, None, rd=[r_ctab], wr=[r_ctab])
                act(dstt, dstt, AF.Exp, rd=[r_ctab, r_const], wr=[r_ctab], scale=(TOKCR if d == 0 else TOKC))

    def interleave(a, b):
        da = db = False
        while not (da and db):
            if not da:
                try:
                    next(a)
                except StopIteration:
                    da = True
            if b is None:
                db = True
            if not db:
                try:
                    next(b)
                except StopIteration:
                    db = True

    def drain(g):
        for _ in g:
            pass

    stop_state = {}
    attn_fn = {"A": attn_A, "B": attn_B, "C": attn_C}

    def ck(name):
        if stop == name:
            raise _Stop()

    def layer(l):
        layer_params(l)
        ck('params')
        for u in (units if stop not in ('xc', 'xc2') else units[7:]):
            push_unit_weights(l, u)
        norm(0)
        ck('norm1')
        for kc in range(KC):
            P.dma("sp", "xsp", lambda e, kc=kc: e.dma_start(out=xsp_d[kc * 128:(kc + 1) * 128, :],
                                                           in_=XY[:, kc * S:(kc + 1) * S]),
                  rd=r_x[kc], wr=[r_xsp])
        P.barrier(dma_keys=("xsp",))
        ck('spill')
        if stop in ('xc', 'xc2'):
            drain(inproj(l, 7))
            ck('xc')
            drain(attn_C(l, 7))
            ck('xc2')
        drain(inproj(l, 0))
        ck('inproj0')
        for ui in range(len(units)):
            a = attn_fn[units[ui]["kind"]](l, ui)
            b = inproj(l, ui + 1) if ui + 1 < len(units) else None
            interleave(a, b)
            ck('u%d' % ui)
        P.barrier()
        ck('mix')
        if dbg and l == 0:
            P.dma("sp", "dbg", lambda e: e.dma_start(out=ydbg_d.rearrange("(c p) s -> p c s", p=128),
                                                       in_=Ybf.rearrange("p (c s) -> p c s", s=S)), rd=[], wr=[])
            P.barrier(dma_keys=("dbg",))
        for h in range(4):
            WS.push(w_in_d[l, C_CG + h])
        for j in range(8):
            for i in range(3):
                WS.push(w_in_d[l, C_GATE + 8 * i + j])
            WS.push(w_br_d[l, j], 11)
        for oc in range(8):
            WS.push(w_o_d[l, oc])

        def nbank():
            b = rot["ps"] % 8
            rot["ps"] += 1
            return b

        for h in range(4):
            slot = WS.pop()
            ch = 7 + h
            for tb in range(NB):
                bank = nbank()
                proj_fm(slot, tb, bank)
                j2 = tb % 2
                act(ub(j2), PS[bank][:, :], AF.Silu, rd=[r_ps[bank]], wr=[r_ub[j2]])
                ysl = y_(ch)[:, tb * TB:(tb + 1) * TB]
                tt("pool", ysl, ysl, ub(j2), ALU.mult, rd=[r_ub[j2], r_y[ch][tb], r_yall[ch]], wr=[r_y[ch][tb]])
            WS.release(slot)
        ybr = ((0, 3), (3, 7), (7, 11))
        for j in range(8):
            gslots = [WS.pop() for _ in range(3)]
            bslot = WS.pop()
            wb = wslot(bslot, 11)
            for tb in range(NB):
                for i in range(3):
                    gb = nbank()
                    proj_fm(gslots[i], tb, gb)
                    act(ft(i), PS[gb][:, :], AF.Sigmoid, rd=[r_ps[gb]], wr=[r_ft[i]])
                    bb = nbank()
                    c0, c1 = ybr[i]
                    for ch in range(c0, c1):
                        mm(PS[bb][:, :], wb[:, ch, :], y_(ch)[:, tb * TB:(tb + 1) * TB], ch == c0, ch == c1 - 1,
                           rd=[r_w[bslot], r_y[ch][tb], r_yall[ch]], wr=[r_ps[bb]])
                    if i == 0:
                        tt("dve", ft(3), PS[bb][:, :], ft(0), ALU.mult, rd=[r_ps[bb], r_ft[0]], wr=[r_ft[3]])
                    elif i == 1:
                        tt("dve", ft(4), PS[bb][:, :], ft(1), ALU.mult, rd=[r_ps[bb], r_ft[1]], wr=[r_ft[4]])
                        tt("pool", ft(3), ft(3), ft(4), ALU.add, rd=[r_ft[3], r_ft[4]], wr=[r_ft[3]])
                    else:
                        tt("dve", ft(5), PS[bb][:, :], ft(2), ALU.mult, rd=[r_ps[bb], r_ft[2]], wr=[r_ft[5]])
                        tt("pool", m_(j, tb), ft(3), ft(5), ALU.add, rd=[r_ft[3], r_ft[5]], wr=[r_m[j][tb]])
            for s_ in gslots + [bslot]:
                WS.release(s_)
        P.barrier()
        if stop == '4a':
            for kc in range(KC):
                P.dma('pool', 'dbg', lambda e, kc=kc: e.dma_start(out=out_d[kc * 128:(kc + 1) * 128, :], in_=M[:, kc * S:(kc + 1) * S]), rd=[], wr=[])
            P.barrier(dma_keys=('dbg',))
            stop_state['skip_out'] = True
        ck('4a')
        for kc in range(KC):
            P.dma("sp", "xld", lambda e, kc=kc: e.dma_start(out=XY[:, kc * S:(kc + 1) * S],
                                                           in_=xsp_d[kc * 128:(kc + 1) * 128, :]),
                  rd=[r_xsp], wr=r_x[kc])
        for hg in range(4):
            for c in range(8):
                WS.push(w_m1_d[l, hg * 8 + c])
            for oc in range(8):
                WS.push(w_m2_d[l, hg * 8 + oc])
        for oc in range(8):
            slot = WS.pop()
            w = wslot(slot)
            for tb in range(NB):
                bank = nbank()
                for kc in range(KC):
                    mm(PS[bank][:, :], w[:, kc, :], m_(kc, tb), kc == 0, kc == KC - 1,
                       rd=[r_w[slot], r_m[kc][tb]], wr=[r_ps[bank]])
                tt("dve", x_(oc, tb), x_(oc, tb), PS[bank][:, :], ALU.add, rd=[r_ps[bank], r_x[oc][tb]],
                   wr=[r_x[oc][tb]])
            WS.release(slot)
        ck('4b')
        norm(8)
        P.barrier()
        ck('norm2')
        for hg in range(4):
            for c in range(8):
                slot = WS.pop()
                for tb in range(NB):
                    bank = nbank()
                    proj_fm(slot, tb, bank)
                    j2 = (c * NB + tb) % 2
                    act(ft(j2), PS[bank][:, :], AF.Relu, rd=[r_ps[bank]], wr=[r_ft[j2]])
                    tt("pool", m_(c, tb), ft(j2), ft(j2), ALU.mult, rd=[r_ft[j2]], wr=[r_m[c][tb]])
                WS.release(slot)
            for oc in range(8):
                slot = WS.pop()
                w = wslot(slot)
                for tb in range(NB):
                    bank = nbank()
                    for c in range(8):
                        mm(PS[bank][:, :], w[:, c, :], m_(c, tb), c == 0, c == 7,
                           rd=[r_w[slot], r_m[c][tb]], wr=[r_ps[bank]])
                    tt("dve", x_(oc, tb), x_(oc, tb), PS[bank][:, :], ALU.add, rd=[r_ps[bank], r_x[oc][tb]],
                       wr=[r_x[oc][tb]])
                WS.release(slot)
        P.barrier()
        if dbg and l == 0:
            for kc in range(KC):
                P.dma("sp", "dbg", lambda e, kc=kc: e.dma_start(out=xdbg_d[kc * 128:(kc + 1) * 128, :],
                                                               in_=XY[:, kc * S:(kc + 1) * S]), rd=r_x[kc], wr=[])

    try:
        for l in range(nl):
            layer(l)
    except _Stop:
        pass

    for kc in range(KC if not stop_state.get('skip_out') else 0):
        P.dma("sp", "out", lambda e, kc=kc: e.dma_start(out=out_d[kc * 128:(kc + 1) * 128, :],
                                                       in_=XY[:, kc * S:(kc + 1) * S]), rd=r_x[kc], wr=[])
    final_waits = tuple((k, n) for k, n in P.dcnt.items() if k in ("out", "dbg", "xsp"))
    P.q["sp"].append((final_waits, None, None))

    keys = list(P.ENG) + sorted(P.dcnt.keys())
    sems = {k: es.enter_context(nc.semaphore("s_" + k)) for k in keys}
    stats = {e: len(P.q[e]) for e in P.ENG}

    def run(e, name):
        for waits, fn, inc in P.q[name]:
            for k, n in waits:
                e.wait_ge(sems[k], n)
            if fn is not None:
                fn(e).then_inc(sems[inc[0]], inc[1])

    with nc.Block() as block:
        @block.tensor
        def _(e):
            run(e, "pe")

        @block.scalar
        def _(e):
            run(e, "act")

        @block.vector
        def _(e):
            run(e, "dve")

        @block.gpsimd
        def _(e):
            run(e, "pool")

        @block.sync
        def _(e):
            run(e, "sp")
    es.close()
    return nc, stats


def _rope_tables():
    pos = np.arange(S, dtype=np.float32)

    def tab(rot, theta, hd):
        inv = (1.0 / (np.float32(theta) ** (np.arange(0, rot, 2, dtype=np.float32) / np.float32(rot)))).astype(np.float32)
        ang = (pos[:, None] * inv[None, :]).astype(np.float32)
        c, s = np.cos(ang).astype(np.float32), np.sin(ang).astype(np.float32)
        r2 = rot // 2
        cosF = np.ones((hd, S), np.float32)
        sinF = np.zeros((hd, S), np.float32)
        cosF[0:r2] = c.T
        cosF[r2:2 * r2] = c.T
        sinF[0:r2] = s.T
        sinF[r2:2 * r2] = s.T
        return np.tile(cosF, (128 // hd, 1)), np.tile(sinF, (128 // hd, 1))

    cab, sab = tab(16, 500000.0, 64)
    cc, sc = tab(64, 10000.0, 64)
    return np.ascontiguousarray(np.stack([cab, sab, cc, sc]).astype(np.float32))


def _rot_mat(r2, hd=64):
    R = np.zeros((128, 128), np.float32)
    for b in range(128 // hd):
        for d in range(r2):
            R[b * hd + d + r2, b * hd + d] = -1.0
            R[b * hd + d, b * hd + d + r2] = 1.0
    return R


def _constants():
    ones = np.ones((128, 128), np.float32)
    bones = np.zeros((128, 128), np.float32)
    bones[0:64, 0:64] = 1.0
    bones[64:128, 64:128] = 1.0
    ident = np.eye(128, dtype=np.float32)
    cmats = np.stack([ones, bones, _rot_mat(8), _rot_mat(32), ident]).astype(np.float32)
    k = np.arange(128)[:, None]
    q = np.arange(512)[None, :]
    masks = []
    for rel in range(-1, 5):
        masks.append((np.abs(q - (128 * rel + k)) <= 64).astype(np.float32))
    band = (np.abs(np.arange(128)[None, :] - k) <= 64).astype(np.float32)
    masks.append(np.tile(band, (1, 4)))
    cmasks = np.stack(masks).astype(np.float32)
    m = np.arange(128)[:, None].astype(np.float32)
    n = np.arange(128)[None, :].astype(np.float32)
    relf = np.maximum(n - m, 0.0)
    relb = np.maximum(m - n, 0.0)
    mkf8 = (n >= m).astype(np.float32) * 0.125
    mkb8 = (m >= n).astype(np.float32) * 0.125
    pos1 = np.broadcast_to(n + 1.0, (128, 128))
    posb = np.broadcast_to(128.0 - n, (128, 128))
    tok = np.arange(128, dtype=np.float32)[:, None]
    cf = np.concatenate([relf, relb, mkf8, mkb8, pos1, posb, ones, tok, 127.0 - tok], axis=1).astype(np.float32)
    return cmats, cmasks, np.ascontiguousarray(cf)


def _prep_layers(inp, layers):
    f = lambda a: np.asarray(a, dtype=np.float32)
    nl = len(layers)
    w_in = f(inp["w_in"])[layers].reshape(nl, 8, 128, 57, 128).transpose(0, 3, 2, 1, 4)
    wbr = np.concatenate([f(inp["w_br_a"])[layers], f(inp["w_br_b"])[layers], f(inp["w_br_c"])[layers]], axis=1)
    w_br = wbr.reshape(nl, 11, 128, 8, 128).transpose(0, 3, 2, 1, 4)
    w_o = f(inp["w_o"])[layers].reshape(nl, 8, 128, 8, 128).transpose(0, 3, 2, 1, 4)
    w_m1 = f(inp["w_mlp1"])[layers].reshape(nl, 8, 128, 32, 128).transpose(0, 3, 2, 1, 4)
    w_m2 = f(inp["w_mlp2"])[layers].reshape(nl, 4, 8, 128, 8, 128).transpose(0, 1, 4, 3, 2, 5).reshape(nl, 32, 128, 8, 128)
    pcol = np.zeros((nl, 128, NPC), np.float32)
    arow = np.zeros((nl, 4, 128, 128), np.float32)
    for i, l in enumerate(layers):
        pcol[i, :, 0:8] = f(inp["norm1_g"])[l].reshape(8, 128).T
        pcol[i, :, 8:16] = f(inp["norm2_g"])[l].reshape(8, 128).T
        pcol[i, :, 16] = np.tile(f(inp["a_q_norm_g"])[l], 2)
        pcol[i, :, 17] = np.tile(f(inp["a_k_norm_g"])[l], 2)
        pcol[i, :, 18] = np.tile(f(inp["b_q_norm_g"])[l], 2)
        pcol[i, :, 19] = np.tile(f(inp["b_k_norm_g"])[l], 2)
        pcol[i, :, 20] = f(inp["b_out_norm_g"])[l]
        pcol[i, :, 21] = f(inp["c_out_norm_g"])[l]
        pcol[i, :, 22] = np.tile(f(inp["b_lambda_q1"])[l], 2)
        pcol[i, :, 23] = np.tile(f(inp["b_lambda_k1"])[l], 2)
        pcol[i, :, 24] = np.tile(f(inp["b_lambda_q2"])[l], 2)
        pcol[i, :, 25] = np.tile(f(inp["b_lambda_k2"])[l], 2)
        af, ab = f(inp["c_decay_f"])[l], f(inp["c_decay_b"])[l]
        for p in range(2):
            pcol[i, :, 26 + p] = np.repeat(af[2 * p:2 * p + 2], 64)
            pcol[i, :, 28 + p] = np.repeat(ab[2 * p:2 * p + 2], 64)
            arow[i, p] = np.repeat(af[2 * p:2 * p + 2], 64)[None, :]
            arow[i, 2 + p] = np.repeat(ab[2 * p:2 * p + 2], 64)[None, :]
        pcol[i, :, 30:34] = af[None, :]
        pcol[i, :, 34:38] = ab[None, :]
    c = np.ascontiguousarray
    return dict(w_in=c(w_in), w_br=c(w_br), w_o=c(w_o), w_m1=c(w_m1), w_m2=c(w_m2), pcol=pcol, arow=arow)


_CACHE = {}


def _get_prog(nl, l0):
    if (nl, l0) not in _CACHE:
        _CACHE[(nl, l0)] = build(nl, l0=l0)[0]
    return _CACHE[(nl, l0)]


FUSED = True


def kernel(**inputs):
    x = np.asarray(inputs["x"], dtype=np.float32)
    B = x.shape[0]
    cmats, cmasks, cf = _constants()
    rope = _rope_tables()
    consts = dict(c_mats=cmats, c_masks=cmasks, c_f32=cf, rope=rope)
    xT = [np.ascontiguousarray(x[b].T) for b in range(B)]
    groups = [list(range(DEPTH))] if FUSED else [[l] for l in range(DEPTH)]
    for layers in groups:
        nc = _get_prog(len(layers), layers[0])
        shared = _prep_layers(inputs, layers)
        shared.update(consts)
        in_maps = []
        for b in range(B):
            d = dict(shared)
            d["xT"] = xT[b]
            in_maps.append(d)
        res = run_bass_kernel_spmd(nc, in_maps, core_ids=list(range(B)))
        xT = [np.ascontiguousarray(np.asarray(r["outT"], dtype=np.float32)) for r in res.results]
    out = np.stack([t.T for t in xT], axis=0)
    return np.ascontiguousarray(out.astype(np.float32))
```

```python
import math
from contextlib import ExitStack
import numpy as np
import ml_dtypes
import concourse.bass as bass
import concourse.mybir as mybir
from concourse.bass_utils import run_bass_kernel_spmd

F32, BF16 = mybir.dt.float32, mybir.dt.bfloat16
ALU, AF = mybir.AluOpType, mybir.ActivationFunctionType

S = 2048
D = 1024
KC = 8
NB = 4
TB = 512
DEPTH = 4
EPS = 1e-6
NPC = 38
C_AQ, C_AK, C_AV = 0, 3, 6
C_BQ, C_BK, C_BV = 9, 13, 17
C_CQ, C_CK, C_CV, C_CG = 21, 23, 25, 29
C_GATE = 33
NWSLOT = 5


class Res:
    __slots__ = ("w", "r", "x")

    def __init__(self, x=False):
        self.w = None
        self.r = {}
        self.x = x


class Prog:
    ENG = ("pe", "act", "dve", "pool", "sp")

    def __init__(self):
        self.q = {e: [] for e in self.ENG}
        self.cnt = {e: 0 for e in self.ENG}
        self.seen = {e: {} for e in self.ENG}
        self.dcnt = {}

    def _need(self, eng, dep, waits, raw):
        if dep is None:
            return
        k, n = dep
        if k == eng:
            if eng == "pe" or not raw:
                return
            if n <= self.cnt[eng] - 2:
                return
        if k in self.dcnt:
            n = max(n, self.dcnt[k])
        if self.seen[eng].get(k, 0) >= n:
            return
        waits[k] = max(waits.get(k, 0), n)

    def _deps(self, eng, rd, wr):
        waits = {}
        for r in rd:
            self._need(eng, r.w, waits, True)
        for w in wr:
            self._need(eng, w.w, waits, False)
            for k, n in w.r.items():
                self._need(eng, (k, n), waits, False)
        for k, n in waits.items():
            self.seen[eng][k] = n
        return tuple(waits.items())

    def op(self, eng, fn, rd=(), wr=()):
        wr = list(wr) + [r for r in rd if r.x]
        rd = [r for r in rd if not r.x]
        waits = self._deps(eng, rd, wr)
        self.cnt[eng] += 1
        n = self.cnt[eng]
        self.q[eng].append((waits, fn, (eng, 1)))
        for r in rd:
            r.r[eng] = max(r.r.get(eng, 0), n)
        for w in wr:
            w.w = (eng, n)
            w.r = {}

    def dma(self, qeng, semkey, fn, rd=(), wr=()):
        waits = self._deps(qeng, rd, wr)
        self.dcnt[semkey] = self.dcnt.get(semkey, 0) + 16
        n = self.dcnt[semkey]
        self.q[qeng].append((waits, fn, (semkey, 16)))
        for r in rd:
            r.r[semkey] = max(r.r.get(semkey, 0), n)
        for w in wr:
            w.w = (semkey, n)
            w.r = {}

    def barrier(self, dma_keys=()):
        for e in self.ENG:
            waits = {}
            for f in self.ENG:
                if f != e and self.cnt[f] > self.seen[e].get(f, 0):
                    waits[f] = self.cnt[f]
            for k in dma_keys:
                if self.dcnt.get(k, 0) > self.seen[e].get(k, 0):
                    waits[k] = self.dcnt[k]
            for k, n in waits.items():
                self.seen[e][k] = n
            if waits:
                self.q[e].append((tuple(waits.items()), None, None))


class _Stop(Exception):
    pass


def build(nl, dbg=False, stop=None, l0=0):
    nc = bass.Bass("TRN2", target_bir_lowering=False)

    def din(name, shape, dt=F32):
        return nc.dram_tensor(name, list(shape), dt, kind="ExternalInput").ap()

    xT_d = din("xT", [D, S])
    w_in_d = din("w_in", [nl, 57, 128, 8, 128])
    w_br_d = din("w_br", [nl, 8, 128, 11, 128])
    w_o_d = din("w_o", [nl, 8, 128, 8, 128])
    w_m1_d = din("w_m1", [nl, 32, 128, 8, 128])
    w_m2_d = din("w_m2", [nl, 32, 128, 8, 128])
    pcol_d = din("pcol", [nl, 128, NPC])
    arow_d = din("arow", [nl, 4, 128, 128])
    cmat_d = din("c_mats", [5, 128, 128])
    cmask_d = din("c_masks", [7, 128, 512])
    cf32_d = din("c_f32", [128, 7 * 128 + 2])
    rope_d = din("rope", [4, 128, S])
    out_d = nc.dram_tensor("outT", [D, S], F32, kind="ExternalOutput").ap()
    xsp_d = nc.dram_tensor("xspill", [D, S], F32, kind="Internal").ap()
    if dbg:
        ydbg_d = nc.dram_tensor("ydbg", [11 * 128, S], BF16, kind="ExternalOutput").ap()
        xdbg_d = nc.dram_tensor("xdbg", [D, S], F32, kind="ExternalOutput").ap()

    P = Prog()
    es = ExitStack()

    def sb(name, shape, dt):
        return es.enter_context(nc.sbuf_tensor(name, list(shape), dt))

    XN = sb("XN", [128, KC * S], BF16)
    XY = sb("XY", [128, KC * S], F32)
    M = sb("M", [128, KC * S], BF16)
    WR = sb("WR", [128, NWSLOT * 11 * 128], BF16)
    CMB = sb("CMB", [128, 5 * 128], BF16)
    MASKS = sb("MASKS", [128, 7 * 512], BF16)
    CF = sb("CF", [128, 7 * 128 + 2], F32)
    PCOL = sb("PCOL", [128, NPC], F32)
    SMALL = sb("SMALL", [128, 64], F32)
    CTAB = sb("CTAB", [128, 13 * 128], F32)
    CDER = sb("CDER", [128, 6 * 2048], BF16)
    S32 = sb("S32", [128, 2 * 128], F32)
    SQB = sb("SQB", [128, 2 * 512], BF16)
    UB = sb("UB", [128, 2 * 512], BF16)
    FT = sb("FT", [128, 9 * 512], F32)

    Ybf = XY[:, 0:11 * 1024].bitcast(BF16)
    ROPE = XY[:, 11 * 1024:11 * 1024 + 4096]
    PB = XY[:, 15 * 1024:16 * 1024].bitcast(BF16)
    DTOT = CDER[:, 4 * 2048:6 * 2048].bitcast(F32)

    def x_(kc, tb):
        return XY[:, kc * S + tb * TB: kc * S + (tb + 1) * TB]

    def xn_(kc, tb):
        return XN[:, kc * S + tb * TB: kc * S + (tb + 1) * TB]

    def xnfull(kc):
        return XN[:, kc * S:(kc + 1) * S]

    def y_(ch):
        return Ybf[:, ch * S:(ch + 1) * S]

    def m_(kc, tb):
        return M[:, kc * S + tb * TB: kc * S + (tb + 1) * TB]

    def uq(s):
        return M[:, s * 8192: s * 8192 + 2048]

    def uk(s):
        return M[:, s * 8192 + 2048: s * 8192 + 4096]

    def uv(s):
        return M[:, s * 8192 + 4096: s * 8192 + 8192].rearrange("p (h t c) -> p h t c", h=2, c=128)

    def kz(s, m):
        return uk(s) if m == 0 else M[:, s * 8192 + 6144: s * 8192 + 8192]

    def wslot(s, nk=8):
        return WR[:, s * 1408: s * 1408 + nk * 128].rearrange("p (k n) -> p k n", n=128)

    ONES = CMB[:, 0:128]
    BONES = CMB[:, 128:256]
    RAB = CMB[:, 256:384]
    RC = CMB[:, 384:512]
    IDENT = CMB[:, 512:640]

    def mask_(i):
        return MASKS[:, i * 512:(i + 1) * 512]

    RELF, RELB = CF[:, 0:128], CF[:, 128:256]
    MKF8, MKB8 = CF[:, 256:384], CF[:, 384:512]
    POS1, POSB = CF[:, 512:640], CF[:, 640:768]
    ONESF = CF[:, 768:896]
    TOKC, TOKCR = CF[:, 896:897], CF[:, 897:898]

    def ctab(i):
        return CTAB[:, i * 128:(i + 1) * 128]

    def cder(i):
        return CDER[:, i * 2048:(i + 1) * 2048]

    KDF, KDB = cder(0).rearrange("p (t c) -> p t c", c=128), cder(1).rearrange("p (t c) -> p t c", c=128)
    QDF, QDB = cder(2), cder(3)
    STF, STB = cder(4).rearrange("p (t c) -> p t c", c=128), cder(5).rearrange("p (t c) -> p t c", c=128)

    def pb(i):
        return PB[:, i * 512:(i + 1) * 512]

    def sqb(i):
        return SQB[:, i * 512:(i + 1) * 512]

    def ub(i):
        return UB[:, i * 512:(i + 1) * 512]

    def ft(i):
        return FT[:, i * 512:(i + 1) * 512]

    PS = [es.enter_context(nc.psum_tensor("ps%d" % i, [128, 512], F32)) for i in range(8)]

    r_x = [[Res() for _ in range(NB)] for _ in range(KC)]
    r_xn = [Res() for _ in range(NB)]
    r_y = [[Res() for _ in range(NB)] for _ in range(11)]
    r_yall = [Res() for _ in range(11)]
    r_m = [[Res() for _ in range(NB)] for _ in range(KC)]
    r_uq = [Res(), Res()]
    r_uk = [Res(), Res()]
    r_uv = [Res(), Res()]
    r_w = [Res() for _ in range(NWSLOT)]
    r_ps = [Res(True) for _ in range(8)]
    r_pb = [Res() for _ in range(4)]
    r_sqb = [Res(), Res()]
    r_ub = [Res(), Res()]
    r_ft = [Res() for _ in range(9)]
    r_rope = Res()
    r_acc = Res()
    r_const = Res()
    r_pcol = Res()
    r_small = Res()
    r_ctab = Res()
    r_cder = [Res() for _ in range(6)]
    r_s32 = [Res(), Res()]
    r_xsp = Res()

    wstate = {"n": 0}

    def load_w(dram_ap, nk=8):
        s = wstate["n"] % NWSLOT
        wstate["n"] += 1
        dst = wslot(s, nk)
        P.dma("pool", "w%d" % s, lambda e, dst=dst, src=dram_ap: e.dma_start(out=dst, in_=src), wr=[r_w[s]])
        return s

    class WStream:
        def __init__(self):
            self.pending = []
            self.loaded = []
            self.inuse = 0

        def push(self, ap, nk=8):
            self.pending.append((ap, nk))

        def prefetch(self):
            while self.pending and len(self.loaded) + self.inuse < NWSLOT:
                ap, nk = self.pending.pop(0)
                self.loaded.append(load_w(ap, nk))

        def pop(self):
            self.prefetch()
            s = self.loaded.pop(0)
            self.inuse += 1
            return s

        def release(self, s):
            self.inuse -= 1
            self.prefetch()

    WS = WStream()

    def mm(out, lhsT, rhs, start, stop, rd, wr):
        P.op("pe", lambda e: e.matmul(out, lhsT=lhsT, rhs=rhs, start=start, stop=stop), rd=rd, wr=wr)

    def act(out, in_, func, rd, wr, scale=None, bias=None):
        kw = {}
        if scale is not None:
            kw["scale"] = scale
        if bias is not None:
            kw["bias"] = bias
        P.op("act", lambda e: e.activation(out=out, in_=in_, func=func, **kw), rd=rd, wr=wr)

    def tt(eng, out, in0, in1, op, rd, wr):
        P.op(eng, lambda e: e.tensor_tensor(out=out, in0=in0, in1=in1, op=op), rd=rd, wr=wr)

    def stt(eng, out, in0, scalar, in1, op0, op1, rd, wr):
        P.op(eng, lambda e: e.scalar_tensor_tensor(out=out, in0=in0, scalar=scalar, in1=in1, op0=op0, op1=op1),
             rd=rd, wr=wr)

    def ts(eng, out, in0, s1, s2, op0, op1, rd, wr):
        if op1 is None:
            P.op(eng, lambda e: e.tensor_scalar(out=out, in0=in0, scalar1=s1, scalar2=None, op0=op0), rd=rd, wr=wr)
        else:
            P.op(eng, lambda e: e.tensor_scalar(out=out, in0=in0, scalar1=s1, scalar2=s2, op0=op0, op1=op1),
                 rd=rd, wr=wr)

    def cp(eng, out, in_, rd, wr):
        if eng == "act":
            act(out, in_, AF.Copy, rd, wr)
        else:
            P.op(eng, lambda e: e.tensor_copy(out=out, in_=in_), rd=rd, wr=wr)

    def recip(out, in_, rd, wr):
        P.op("dve", lambda e: e.reciprocal(out=out, in_=in_), rd=rd, wr=wr)

    def rstd_from(ps_ap, r_in, ftile, r_ftile, inv_n):
        act(ftile, ps_ap, AF.Ln, rd=[r_in], wr=[r_ftile], scale=inv_n, bias=EPS)
        act(ftile, ftile, AF.Exp, rd=[r_ftile], wr=[r_ftile], scale=-0.5)

    rot = {"ps": 0, "alt": 0}
    SBANK = (4, 5, 3)

    P.dma("pool", "cst", lambda e: e.dma_start(out=CMB[:, :].rearrange("p (c n) -> p c n", n=128),
                                                  in_=cmat_d.rearrange("c p n -> p c n")), wr=[r_const])
    P.dma("pool", "cst", lambda e: e.dma_start(out=MASKS[:, :].rearrange("p (c n) -> p c n", n=512),
                                                  in_=cmask_d.rearrange("c p n -> p c n")), wr=[r_const])
    P.dma("sp", "cst", lambda e: e.dma_start(out=CF[:, :], in_=cf32_d), wr=[r_const])
    for kc in range(KC):
        P.dma("sp", "xld", lambda e, kc=kc: e.dma_start(out=XY[:, kc * S:(kc + 1) * S],
                                                       in_=xT_d[kc * 128:(kc + 1) * 128, :]),
              wr=r_x[kc])

    def norm(gcol0):
        for tb in range(NB):
            for kc in range(KC):
                i = kc % 2
                act(sqb(i), x_(kc, tb), AF.Square, rd=[r_x[kc][tb]], wr=[r_sqb[i]])
                mm(PS[2][:, :], ONES, sqb(i), kc == 0, kc == KC - 1, rd=[r_sqb[i], r_const], wr=[r_ps[2]])
            rstd_from(PS[2][:, :], r_ps[2], ft(0), r_ft[0], 1.0 / D)
            for kc in range(KC):
                stt("dve", xn_(kc, tb), x_(kc, tb), PCOL[:, gcol0 + kc: gcol0 + kc + 1], ft(0), ALU.mult, ALU.mult,
                    rd=[r_x[kc][tb], r_ft[0], r_pcol], wr=[r_xn[tb]])

    def proj_fm(slot, tb, bank):
        w = wslot(slot)
        for kc in range(KC):
            mm(PS[bank][:, :], w[:, kc, :], xn_(kc, tb), kc == 0, kc == KC - 1,
               rd=[r_w[slot], r_xn[tb]], wr=[r_ps[bank]])

    def perm_dst(buf, tb, delta):
        if delta == 1:
            return buf[:, tb * TB:(tb + 1) * TB], None
        n0 = tb * TB // delta
        return buf.rearrange("p (r n) -> p n r", r=delta)[:, n0:n0 + TB // delta, :], delta

    def srcv(ap, delta):
        if delta is None:
            return ap
        return ap.rearrange("p (n r) -> p n r", r=delta)

    def qk_post(bank, tb, gcol, rmat, normed, dsts, delta):
        j = rot["alt"] % 2
        rot["alt"] += 1
        pj = PS[bank][:, :]
        cos = ROPE[:, tb * TB:(tb + 1) * TB]
        sin = ROPE[:, S + tb * TB: S + (tb + 1) * TB]
        if normed:
            act(sqb(j), pj, AF.Square, rd=[r_ps[bank]], wr=[r_sqb[j]])
            mm(PS[2][:, :], BONES, sqb(j), True, True, rd=[r_sqb[j], r_const], wr=[r_ps[2]])
            yield
            rstd_from(PS[2][:, :], r_ps[2], ft(j), r_ft[j], 1.0 / 64)
            stt("dve", ub(j), pj, PCOL[:, gcol:gcol + 1], ft(j), ALU.mult, ALU.mult,
                rd=[r_ps[bank], r_ft[j], r_pcol], wr=[r_ub[j]])
            yield
            mm(PS[2][:, :], rmat, ub(j), True, True, rd=[r_ub[j], r_const], wr=[r_ps[2]])
            tt("pool", ft(2 + j), ub(j), cos, ALU.mult, rd=[r_ub[j], r_rope], wr=[r_ft[2 + j]])
        else:
            cp("act", ub(j), pj, rd=[r_ps[bank]], wr=[r_ub[j]])
            tt("dve", ft(2 + j), pj, cos, ALU.mult, rd=[r_ps[bank], r_rope], wr=[r_ft[2 + j]])
            yield
            mm(PS[2][:, :], rmat, ub(j), True, True, rd=[r_ub[j], r_const], wr=[r_ps[2]])
        yield
        tt("dve", ft(4 + j), PS[2][:, :], sin, ALU.mult, rd=[r_ps[2], r_rope], wr=[r_ft[4 + j]])
        for buf, rows, res in dsts:
            dst, dl = perm_dst(buf, tb, delta)
            a0, a1 = srcv(ft(2 + j), dl), srcv(ft(4 + j), dl)
            if rows is not None:
                dst, a0, a1 = dst[rows[0]:rows[1]], a0[rows[0]:rows[1]], a1[rows[0]:rows[1]]
            tt("pool", dst, a0, a1, ALU.add, rd=[r_ft[2 + j], r_ft[4 + j]], wr=[res])

    def v_proj(slot, set_, coloff, delta):
        w = wslot(slot)
        V = uv(set_)
        for g4 in range(4):
            bank = 0
            for ti in range(4):
                t = g4 * 4 + ti
                for kc in range(KC):
                    if delta == 1:
                        lhs = xnfull(kc)[:, t * 128:(t + 1) * 128]
                    elif delta == 4:
                        lhs = xnfull(kc).rearrange("p (n r) -> p r n", r=4)[:, t // 4, (t % 4) * 128:(t % 4 + 1) * 128]
                    else:
                        lhs = xnfull(kc).rearrange("p (n r) -> p r n", r=16)[:, t, :]
                    mm(PS[bank][:, ti * 128:(ti + 1) * 128], lhs, w[:, kc, :], kc == 0, kc == KC - 1,
                       rd=[r_w[slot]] + r_xn, wr=[r_ps[bank]])
            cp("act", V[:, coloff // 128, g4 * 4:(g4 + 1) * 4, :],
               PS[bank][:, :].rearrange("p (t c) -> p t c", c=128), rd=[r_ps[bank]], wr=[r_uv[set_]])
            yield

    units = []
    for g in range(3):
        units.append(dict(kind="A", idx=g, q=C_AQ + g, k=C_AK + g, v=[C_AV + g], gq=16, gk=17, delta=(1, 4, 16)[g]))
    for h in range(4):
        units.append(dict(kind="B", idx=h, q=C_BQ + h, k=C_BK + h, v=[C_BV + h], gq=18, gk=19, delta=1))
    for p in range(2):
        units.append(dict(kind="C", idx=p, q=C_CQ + p, k=C_CK + p, v=[C_CV + 2 * p, C_CV + 2 * p + 1], gq=0, gk=0,
                          delta=1))

    def push_unit_weights(l, u):
        WS.push(w_in_d[l, u["q"]])
        WS.push(w_in_d[l, u["k"]])
        for c in u["v"]:
            WS.push(w_in_d[l, c])

    def rope_load(which):
        P.dma("sp", "rope", lambda e: e.dma_start(out=ROPE[:, 0:S], in_=rope_d[2 * which]), wr=[r_rope])
        P.dma("sp", "rope", lambda e: e.dma_start(out=ROPE[:, S:2 * S], in_=rope_d[2 * which + 1]), wr=[r_rope])

    def inproj(l, ui):
        u = units[ui]
        set_ = ui % 2
        normed = u["kind"] != "C"
        rmat = RC if u["kind"] == "C" else RAB
        if ui == 0:
            rope_load(0)
        if ui == 7:
            rope_load(1)
        for which in ("q", "k"):
            if which == "q":
                dsts, gcol = [(uq(set_), None, r_uq[set_])], u["gq"]
            elif u["kind"] == "C":
                dsts, gcol = [(uk(set_), None, r_uk[set_])], u["gk"]
            else:
                k0, k1 = kz(set_, 0), kz(set_, 1)
                P.op("pool", lambda e, k0=k0: e.memset(k0[64:128, :], 0.0), wr=[r_uk[set_]])
                P.op("pool", lambda e, k1=k1: e.memset(k1[0:64, :], 0.0), wr=[r_uv[set_]])
                dsts, gcol = [(k0, (0, 64), r_uk[set_]), (k1, (64, 128), r_uv[set_])], u["gk"]
            slot = WS.pop()
            for tb in range(NB):
                bank = 0
                proj_fm(slot, tb, bank)
                yield
                yield from qk_post(bank, tb, gcol, rmat, normed, dsts, u["delta"])
                yield
            WS.release(slot)
        for vi, c in enumerate(u["v"]):
            slot = WS.pop()
            yield from v_proj(slot, set_, vi * 128, u["delta"])
            WS.release(slot)

    def attn_B(l, ui):
        u = units[ui]
        h = u["idx"]
        set_ = ui % 2
        qT, V = uq(set_), uv(set_)
        rdq = [r_uq[set_], r_uk[set_], r_uv[set_]]
        ch = 3 + h
        pending = []

        def make_epilogue(qb):
            def s1():
                act(sqb(0), ft(7), AF.Square, rd=[r_ft[7]], wr=[r_sqb[0]])

            def s2():
                mm(PS[1][:, :], ONES, sqb(0), True, True, rd=[r_sqb[0], r_const], wr=[r_ps[1]])

            def s3():
                rstd_from(PS[1][:, :], r_ps[1], ft(6), r_ft[6], 1.0 / 128)
                stt("dve", y_(ch)[:, qb * TB:(qb + 1) * TB], ft(7), SMALL[:, 1:2], ft(6), ALU.mult, ALU.mult,
                    rd=[r_ft[7], r_ft[6], r_small], wr=[r_y[ch][qb], r_yall[ch]])

            return [(9, s1), (11, s2), (13, s3)]

        for qb in range(NB):
            for m in range(2):
                def smm(kt):
                    bank = SBANK[kt % 3]
                    mm(PS[bank][:, :], kz(set_, m)[:, kt * 128:(kt + 1) * 128], qT[:, qb * TB:(qb + 1) * TB], True, True,
                       rd=rdq, wr=[r_ps[bank]])

                smm(0)
                smm(1)
                for kt in range(16):
                    if kt + 2 < 16:
                        smm(kt + 2)
                    bank = SBANK[kt % 3]
                    pi = kt % 4
                    act(pb(pi), PS[bank][:, :], AF.Exp, rd=[r_ps[bank]], wr=[r_pb[pi]], scale=0.125)
                    mm(PS[6][:, :], V[:, 0, kt, :], pb(pi), kt == 0, kt == 15, rd=[r_pb[pi], r_uv[set_]],
                       wr=[r_ps[6]])
                    mm(PS[7][:, :], ONES, pb(pi), kt == 0, kt == 15, rd=[r_pb[pi], r_const], wr=[r_ps[7]])
                    while pending and pending[0][0] <= kt:
                        pending.pop(0)[1]()
                    if kt % 4 == 3 and kt < 15:
                        yield
                cp("dve", ft(6), PS[7][:, :], rd=[r_ps[7]], wr=[r_ft[6]])
                if m == 0:
                    cp("dve", ft(7), PS[6][:, :], rd=[r_ps[6]], wr=[r_ft[7]])
                    recip(ft(6), ft(6), rd=[r_ft[6]], wr=[r_ft[6]])
                    tt("dve", ft(7), ft(7), ft(6), ALU.mult, rd=[r_ft[7], r_ft[6]], wr=[r_ft[7]])
                else:
                    cp("dve", ft(8), PS[6][:, :], rd=[r_ps[6]], wr=[r_ft[8]])
                    recip(ft(6), ft(6), rd=[r_ft[6]], wr=[r_ft[6]])
                    tt("dve", ft(8), ft(8), ft(6), ALU.mult, rd=[r_ft[8], r_ft[6]], wr=[r_ft[8]])
                    stt("dve", ft(7), ft(8), SMALL[:, 0:1], ft(7), ALU.mult, ALU.add,
                        rd=[r_ft[8], r_ft[7], r_small], wr=[r_ft[7]])
                    pending = make_epilogue(qb)
                yield
        for _, fn in pending:
            fn()
        yield

    def attn_A(l, ui):
        u = units[ui]
        g = u["idx"]
        delta = u["delta"]
        set_ = ui % 2
        qT, kT, V = uq(set_), uk(set_), uv(set_)
        rdq = [r_uq[set_], r_uk[set_], r_uv[set_]]
        for s2 in range(2):
            rows = slice(64 * s2, 64 * s2 + 64)
            for qb in range(NB):
                if g < 2:
                    rels = list(range(-1, 5)) if g == 0 else list(range(4))
                    kts = [(4 * qb + r, r) for r in rels if 0 <= 4 * qb + r < 16]
                    n = len(kts)

                    def smm(i):
                        kt, rel = kts[i]
                        bank = 4 + i % 2
                        mm(PS[bank][:, :], kz(set_, s2)[:, kt * 128:(kt + 1) * 128], qT[:, qb * TB:(qb + 1) * TB], True,
                           True, rd=rdq, wr=[r_ps[bank]])

                    smm(0)
                    for i in range(n):
                        if i + 1 < n:
                            smm(i + 1)
                        kt, rel = kts[i]
                        bank = 4 + i % 2
                        pi = i % 3
                        act(pb(pi), PS[bank][:, :], AF.Exp, rd=[r_ps[bank]], wr=[r_pb[pi]], scale=0.125)
                        tt("dve", pb(pi), pb(pi), mask_(rel + 1), ALU.mult, rd=[r_pb[pi], r_const], wr=[r_pb[pi]])
                        mm(PS[6][:, :], V[:, 0, kt, :], pb(pi), i == 0, i == n - 1, rd=[r_pb[pi], r_uv[set_]],
                           wr=[r_ps[6]])
                        mm(PS[7][:, :], ONES, pb(pi), i == 0, i == n - 1, rd=[r_pb[pi], r_const], wr=[r_ps[7]])
                        if i < n - 1:
                            yield
                else:
                    bank = 4 + qb % 2
                    for rel in range(4):
                        kt = 4 * qb + rel
                        mm(PS[bank][:, rel * 128:(rel + 1) * 128], kz(set_, s2)[:, kt * 128:(kt + 1) * 128],
                           qT[:, kt * 128:(kt + 1) * 128], True, True, rd=rdq, wr=[r_ps[bank]])
                    pi = qb % 3
                    act(pb(pi), PS[bank][:, :], AF.Exp, rd=[r_ps[bank]], wr=[r_pb[pi]], scale=0.125)
                    tt("dve", pb(pi), pb(pi), mask_(6), ALU.mult, rd=[r_pb[pi], r_const], wr=[r_pb[pi]])
                    for rel in range(4):
                        kt = 4 * qb + rel
                        mm(PS[6][:, rel * 128:(rel + 1) * 128], V[:, 0, kt, :], pb(pi)[:, rel * 128:(rel + 1) * 128],
                           True, True, rd=[r_pb[pi], r_uv[set_]], wr=[r_ps[6]])
                        mm(PS[7][:, rel * 128:(rel + 1) * 128], ONES, pb(pi)[:, rel * 128:(rel + 1) * 128],
                           True, True, rd=[r_pb[pi], r_const], wr=[r_ps[7]])
                if delta == 1:
                    ydst = y_(g)[rows, qb * TB:(qb + 1) * TB]
                    ddst = DTOT[rows, qb * TB:(qb + 1) * TB]
                    nsrc, dsrc = PS[6][rows, :], PS[7][rows, :]
                elif delta == 4:
                    ydst = y_(g).rearrange("p (n r) -> p r n", r=4)[rows, qb, :]
                    ddst = DTOT[:, :].rearrange("p (n r) -> p r n", r=4)[rows, qb, :]
                    nsrc, dsrc = PS[6][rows, :], PS[7][rows, :]
                else:
                    ydst = y_(g).rearrange("p (n r) -> p r n", r=16)[rows, 4 * qb:4 * qb + 4, :]
                    ddst = DTOT[:, :].rearrange("p (n r) -> p r n", r=16)[rows, 4 * qb:4 * qb + 4, :]
                    nsrc = PS[6][rows, :].rearrange("p (a b) -> p a b", a=4)
                    dsrc = PS[7][rows, :].rearrange("p (a b) -> p a b", a=4)
                cp("act", ydst, nsrc, rd=[r_ps[6]], wr=[r_yall[g]] + r_y[g])
                if g == 0:
                    cp("dve", ddst, dsrc, rd=[r_ps[7]], wr=[r_cder[4], r_cder[5]])
                else:
                    tt("dve", ddst, ddst, dsrc, ALU.add, rd=[r_ps[7], r_cder[4], r_cder[5]], wr=[r_cder[4], r_cder[5]])
                yield
        if g == 2:
            for tb in range(NB):
                recip(ft(6), DTOT[:, tb * TB:(tb + 1) * TB], rd=[r_cder[4], r_cder[5]], wr=[r_ft[6]])
                for gg in range(3):
                    ysl = y_(gg)[:, tb * TB:(tb + 1) * TB]
                    tt("pool", ysl, ysl, ft(6), ALU.mult, rd=[r_ft[6], r_yall[gg]] + r_y[gg], wr=[r_y[gg][tb], r_yall[gg]])
                yield

    def attn_C(l, ui):
        u = units[ui]
        p = u["idx"]
        set_ = ui % 2
        qT, kT, V = uq(set_), uk(set_), uv(set_)
        DQF, DQB, DKF, DKB = ctab(4 + p), ctab(6 + p), ctab(8 + p), ctab(10 + p)
        g128f, g128b = SMALL[:, 8 + p:9 + p], SMALL[:, 10 + p:11 + p]
        for g4 in range(4):
            bank = 3
            tp = PS[bank][:, :].bitcast(BF16)[:, 0:512]
            for ti in range(4):
                t = g4 * 4 + ti
                P.op("pe", lambda e, t=t, ti=ti, tp=tp: e.transpose(tp[:, ti * 128:(ti + 1) * 128],
                                                                      kT[:, t * 128:(t + 1) * 128], IDENT),
                     rd=[r_uk[set_], r_const], wr=[r_ps[bank]])
            tp3 = tp.rearrange("p (t c) -> p t c", c=128)
            tt("dve", KDF[:, g4 * 4:(g4 + 1) * 4, :], tp3, DKF.unsqueeze(1).to_broadcast([128, 4, 128]), ALU.mult,
               rd=[r_ps[bank], r_ctab], wr=[r_cder[0]])
            tt("dve", KDB[:, g4 * 4:(g4 + 1) * 4, :], tp3, DKB.unsqueeze(1).to_broadcast([128, 4, 128]), ALU.mult,
               rd=[r_ps[bank], r_ctab], wr=[r_cder[1]])
            q3 = qT[:, g4 * TB:(g4 + 1) * TB].rearrange("p (t c) -> p t c", c=128)
            tt("pool", QDF[:, g4 * TB:(g4 + 1) * TB].rearrange("p (t c) -> p t c", c=128), q3,
               DQF.unsqueeze(1).to_broadcast([128, 4, 128]), ALU.mult, rd=[r_uq[set_], r_ctab], wr=[r_cder[2]])
            tt("pool", QDB[:, g4 * TB:(g4 + 1) * TB].rearrange("p (t c) -> p t c", c=128), q3,
               DQB.unsqueeze(1).to_broadcast([128, 4, 128]), ALU.mult, rd=[r_uq[set_], r_ctab], wr=[r_cder[3]])
            yield
        P.op("pool", lambda e: e.memset(S32[:, :], 0.0), wr=r_s32)
        P.op("pool", lambda e: e.memset(STF[:, 0, :], 0.0), wr=[r_cder[4]])
        P.op("pool", lambda e: e.memset(STB[:, 15, :], 0.0), wr=[r_cder[5]])
        for step in range(15):
            for d in range(2):
                c = step if d == 0 else 15 - step
                KD = KDF if d == 0 else KDB
                ST = STF if d == 0 else STB
                gcol = g128f if d == 0 else g128b
                s32 = S32[:, d * 128:(d + 1) * 128]
                bank = 4 + d
                for hh in range(2):
                    mm(PS[bank][:, hh * 128:(hh + 1) * 128], KD[:, c, :], V[:, hh, c, :], True, True,
                       rd=[r_cder[d], r_uv[set_]], wr=[r_ps[bank]])
                for hh in range(2):
                    rws = slice(64 * hh, 64 * hh + 64)
                    stt("dve", s32[rws, :], s32[rws, :], gcol[rws, :], PS[bank][rws, hh * 128:(hh + 1) * 128],
                        ALU.mult, ALU.add, rd=[r_ps[bank], r_s32[d], r_small], wr=[r_s32[d]])
                cn = c + 1 if d == 0 else c - 1
                cp("act", ST[:, cn, :], s32, rd=[r_s32[d]], wr=[r_cder[4 + d]])
            if step % 4 == 3:
                yield
        yield
        for hh in range(2):
            rws = slice(64 * hh, 64 * hh + 64)
            h = 2 * p + hh
            ch = 7 + h
            for cb in range(4):
                sbank = 4 + cb % 2
                for ci in range(4):
                    c = 4 * cb + ci
                    mm(PS[sbank][:, ci * 128:(ci + 1) * 128], kT[rws, c * 128:(c + 1) * 128],
                       qT[rws, c * 128:(c + 1) * 128], True, True, rd=[r_uq[set_], r_uk[set_]], wr=[r_ps[sbank]])
                pi = cb % 3
                tt("dve", pb(pi).rearrange("p (t c) -> p t c", c=128),
                   PS[sbank][:, :].rearrange("p (t c) -> p t c", c=128),
                   ctab(h).unsqueeze(1).to_broadcast([128, 4, 128]), ALU.mult, rd=[r_ps[sbank], r_ctab],
                   wr=[r_pb[pi]])
                for ci in range(4):
                    c = 4 * cb + ci
                    o = PS[6][:, ci * 128:(ci + 1) * 128]
                    mm(o, V[:, hh, c, :], pb(pi)[:, ci * 128:(ci + 1) * 128], True, False,
                       rd=[r_pb[pi], r_uv[set_]], wr=[r_ps[6]])
                    mm(o, STF[rws, c, :], QDF[rws, c * 128:(c + 1) * 128], False, False,
                       rd=[r_cder[4], r_cder[2]], wr=[r_ps[6]])
                    mm(o, STB[rws, c, :], QDB[rws, c * 128:(c + 1) * 128], False, True,
                       rd=[r_cder[5], r_cder[3]], wr=[r_ps[6]])
                cp("dve", ft(7), PS[6][:, :], rd=[r_ps[6]], wr=[r_ft[7]])
                act(pb(3), PS[6][:, :], AF.Square, rd=[r_ps[6]], wr=[r_pb[3]])
                mm(PS[1][:, :], ONES, pb(3), True, True, rd=[r_pb[3], r_const], wr=[r_ps[1]])
                rstd_from(PS[1][:, :], r_ps[1], ft(6), r_ft[6], 1.0 / 128)
                stt("dve", y_(ch)[:, cb * TB:(cb + 1) * TB], ft(7), PCOL[:, 21:22], ft(6), ALU.mult, ALU.mult,
                    rd=[r_ft[7], r_ft[6], r_pcol], wr=[r_y[ch][cb], r_yall[ch]])
                yield

    def layer_params(l):
        lam_init = 0.8 - 0.6 * math.exp(-0.3 * (l + l0))
        P.dma("sp", "par", lambda e: e.dma_start(out=PCOL[:, :], in_=pcol_d[l]), wr=[r_pcol])
        P.dma("sp", "par", lambda e: e.dma_start(out=CTAB[:, 8 * 128:12 * 128].rearrange("p (c n) -> p c n", n=128),
                                                   in_=arow_d[l].rearrange("c p n -> p c n")), wr=[r_ctab])
        rp = [r_pcol]
        tt("dve", SMALL[:, 2:3], PCOL[:, 22:23], PCOL[:, 23:24], ALU.mult, rd=rp, wr=[r_small])
        tt("dve", SMALL[:, 3:4], PCOL[:, 24:25], PCOL[:, 25:26], ALU.mult, rd=rp, wr=[r_small])
        mm(PS[2][:, 0:2], ONESF[0:64, :], SMALL[0:64, 2:4], True, True, rd=[r_small, r_const], wr=[r_ps[2]])
        act(SMALL[:, 4:6], PS[2][:, 0:2], AF.Exp, rd=[r_ps[2]], wr=[r_small])
        tt("dve", SMALL[:, 6:7], SMALL[:, 5:6], SMALL[:, 4:5], ALU.subtract, rd=[r_small], wr=[r_small])
        ts("dve", SMALL[:, 0:1], SMALL[:, 6:7], -lam_init, None, ALU.add, None, rd=[r_small], wr=[r_small])
        ts("dve", SMALL[:, 1:2], PCOL[:, 20:21], 1.0 - lam_init, None, ALU.mult, None, rd=rp + [r_small],
           wr=[r_small])
        act(SMALL[:, 12:16], PCOL[:, 26:30], AF.Exp, rd=rp + [r_small], wr=[r_small])
        ts("dve", SMALL[:, 12:16], SMALL[:, 12:16], -1.0, None, ALU.mult, None, rd=[r_small], wr=[r_small])
        act(SMALL[:, 8:12], SMALL[:, 12:16], AF.Exp, rd=[r_small], wr=[r_small], scale=128.0)
        act(SMALL[:, 16:24], PCOL[:, 30:38], AF.Exp, rd=rp + [r_small], wr=[r_small])
        ts("dve", SMALL[:, 16:24], SMALL[:, 16:24], -1.0, None, ALU.mult, None, rd=[r_small], wr=[r_small])
        rs = [r_small, r_const]
        for h in range(4):
            act(ctab(h), RELF, AF.Exp, rd=rs, wr=[r_ctab], scale=SMALL[:, 16 + h:17 + h])
            tt("dve", ctab(h), ctab(h), MKF8, ALU.mult, rd=[r_ctab, r_const], wr=[r_ctab])
            act(ctab(12), RELB, AF.Exp, rd=rs + [r_ctab], wr=[r_ctab], scale=SMALL[:, 20 + h:21 + h])
            tt("dve", ctab(12), ctab(12), MKB8, ALU.mult, rd=[r_ctab, r_const], wr=[r_ctab])
            tt("dve", ctab(h), ctab(h), ctab(12), ALU.add, rd=[r_ctab], wr=[r_ctab])
        for p in range(2):
            act(ctab(4 + p), POS1, AF.Exp, rd=rs + [r_ctab], wr=[r_ctab], scale=SMALL[:, 12 + p:13 + p])
            act(ctab(6 + p), POSB, AF.Exp, rd=rs + [r_ctab], wr=[r_ctab], scale=SMALL[:, 14 + p:15 + p])
            ts("dve", ctab(4 + p), ctab(4 + p), 0.125, None, ALU.mult, None, rd=[r_ctab], wr=[r_ctab])
            ts("dve", ctab(6 + p), ctab(6 + p), 0.125, None, ALU.mult, None, rd=[r_ctab], wr=[r_ctab])
            for d in range(2):
                dstt = ctab(8 + 2 * d + p)
                src = dstt
                act(dstt, src, AF.Exp, rd=rp + [r_ctab], wr=[r_ctab])
                ts("dve", dstt, dstt, -1.0, None, ALU.mult, None, rd=[r_ctab], wr=[r_ctab])
                act(dstt, dstt, AF.Exp, rd=[r_ctab, r_const], wr=[r_ctab], scale=(TOKCR if d == 0 else TOKC))

    def interleave(a, b):
        da = db = False
        while not (da and db):
            if not da:
                try:
                    next(a)
                except StopIteration:
                    da = True
            if b is None:
                db = True
            if not db:
                try:
                    next(b)
                except StopIteration:
                    db = True

    def drain(g):
        for _ in g:
            pass

    stop_state = {}
    attn_fn = {"A": attn_A, "B": attn_B, "C": attn_C}

    def ck(name):
        if stop == name:
            raise _Stop()

    def layer(l):
        layer_params(l)
        ck('params')
        for u in (units if stop not in ('xc', 'xc2') else units[7:]):
            push_unit_weights(l, u)
        norm(0)
        ck('norm1')
        for kc in range(KC):
            P.dma("sp", "xsp", lambda e, kc=kc: e.dma_start(out=xsp_d[kc * 128:(kc + 1) * 128, :],
                                                           in_=XY[:, kc * S:(kc + 1) * S]),
                  rd=r_x[kc], wr=[r_xsp])
        P.barrier(dma_keys=("xsp",))
        ck('spill')
        if stop in ('xc', 'xc2'):
            drain(inproj(l, 7))
            ck('xc')
            drain(attn_C(l, 7))
            ck('xc2')
        drain(inproj(l, 0))
        ck('inproj0')
        for ui in range(len(units)):
            a = attn_fn[units[ui]["kind"]](l, ui)
            b = inproj(l, ui + 1) if ui + 1 < len(units) else None
            interleave(a, b)
            ck('u%d' % ui)
        P.barrier()
        ck('mix')
        if dbg and l == 0:
            P.dma("sp", "dbg", lambda e: e.dma_start(out=ydbg_d.rearrange("(c p) s -> p c s", p=128),
                                                       in_=Ybf.rearrange("p (c s) -> p c s", s=S)), rd=[], wr=[])
            P.barrier(dma_keys=("dbg",))
        for h in range(4):
            WS.push(w_in_d[l, C_CG + h])
        for j in range(8):
            for i in range(3):
                WS.push(w_in_d[l, C_GATE + 8 * i + j])
            WS.push(w_br_d[l, j], 11)
        for oc in range(8):
            WS.push(w_o_d[l, oc])

        def nbank():
            b = rot["ps"] % 8
            rot["ps"] += 1
            return b

        for h in range(4):
            slot = WS.pop()
            ch = 7 + h
            for tb in range(NB):
                bank = nbank()
                proj_fm(slot, tb, bank)
                j2 = tb % 2
                act(ub(j2), PS[bank][:, :], AF.Silu, rd=[r_ps[bank]], wr=[r_ub[j2]])
                ysl = y_(ch)[:, tb * TB:(tb + 1) * TB]
                tt("pool", ysl, ysl, ub(j2), ALU.mult, rd=[r_ub[j2], r_y[ch][tb], r_yall[ch]], wr=[r_y[ch][tb]])
            WS.release(slot)
        ybr = ((0, 3), (3, 7), (7, 11))
        for j in range(8):
            gslots = [WS.pop() for _ in range(3)]
            bslot = WS.pop()
            wb = wslot(bslot, 11)
            for tb in range(NB):
                for i in range(3):
                    gb = nbank()
                    proj_fm(gslots[i], tb, gb)
                    act(ft(i), PS[gb][:, :], AF.Sigmoid, rd=[r_ps[gb]], wr=[r_ft[i]])
                    bb = nbank()
                    c0, c1 = ybr[i]
                    for ch in range(c0, c1):
                        mm(PS[bb][:, :], wb[:, ch, :], y_(ch)[:, tb * TB:(tb + 1) * TB], ch == c0, ch == c1 - 1,
                           rd=[r_w[bslot], r_y[ch][tb], r_yall[ch]], wr=[r_ps[bb]])
                    if i == 0:
                        tt("dve", ft(3), PS[bb][:, :], ft(0), ALU.mult, rd=[r_ps[bb], r_ft[0]], wr=[r_ft[3]])
                    elif i == 1:
                        tt("dve", ft(4), PS[bb][:, :], ft(1), ALU.mult, rd=[r_ps[bb], r_ft[1]], wr=[r_ft[4]])
                        tt("pool", ft(3), ft(3), ft(4), ALU.add, rd=[r_ft[3], r_ft[4]], wr=[r_ft[3]])
                    else:
                        tt("dve", ft(5), PS[bb][:, :], ft(2), ALU.mult, rd=[r_ps[bb], r_ft[2]], wr=[r_ft[5]])
                        tt("pool", m_(j, tb), ft(3), ft(5), ALU.add, rd=[r_ft[3], r_ft[5]], wr=[r_m[j][tb]])
            for s_ in gslots + [bslot]:
                WS.release(s_)
        P.barrier()
        if stop == '4a':
            for kc in range(KC):
                P.dma('pool', 'dbg', lambda e, kc=kc: e.dma_start(out=out_d[kc * 128:(kc + 1) * 128, :], in_=M[:, kc * S:(kc + 1) * S]), rd=[], wr=[])
            P.barrier(dma_keys=('dbg',))
            stop_state['skip_out'] = True
        ck('4a')
        for kc in range(KC):
            P.dma("sp", "xld", lambda e, kc=kc: e.dma_start(out=XY[:, kc * S:(kc + 1) * S],
                                                           in_=xsp_d[kc * 128:(kc + 1) * 128, :]),
                  rd=[r_xsp], wr=r_x[kc])
        for hg in range(4):
            for c in range(8):
                WS.push(w_m1_d[l, hg * 8 + c])
            for oc in range(8):
                WS.push(w_m2_d[l, hg * 8 + oc])
        for oc in range(8):
            slot = WS.pop()
            w = wslot(slot)
            for tb in range(NB):
                bank = nbank()
                for kc in range(KC):
                    mm(PS[bank][:, :], w[:, kc, :], m_(kc, tb), kc == 0, kc == KC - 1,
                       rd=[r_w[slot], r_m[kc][tb]], wr=[r_ps[bank]])
                tt("dve", x_(oc, tb), x_(oc, tb), PS[bank][:, :], ALU.add, rd=[r_ps[bank], r_x[oc][tb]],
                   wr=[r_x[oc][tb]])
            WS.release(slot)
        ck('4b')
        norm(8)
        P.barrier()
        ck('norm2')
        for hg in range(4):
            for c in range(8):
                slot = WS.pop()
                for tb in range(NB):
                    bank = nbank()
                    proj_fm(slot, tb, bank)
                    j2 = (c * NB + tb) % 2
                    act(ft(j2), PS[bank][:, :], AF.Relu, rd=[r_ps[bank]], wr=[r_ft[j2]])
                    tt("pool", m_(c, tb), ft(j2), ft(j2), ALU.mult, rd=[r_ft[j2]], wr=[r_m[c][tb]])
                WS.release(slot)
            for oc in range(8):
                slot = WS.pop()
                w = wslot(slot)
                for tb in range(NB):
                    bank = nbank()
                    for c in range(8):
                        mm(PS[bank][:, :], w[:, c, :], m_(c, tb), c == 0, c == 7,
                           rd=[r_w[slot], r_m[c][tb]], wr=[r_ps[bank]])
                    tt("dve", x_(oc, tb), x_(oc, tb), PS[bank][:, :], ALU.add, rd=[r_ps[bank], r_x[oc][tb]],
                       wr=[r_x[oc][tb]])
                WS.release(slot)
        P.barrier()
        if dbg and l == 0:
            for kc in range(KC):
                P.dma("sp", "dbg", lambda e, kc=kc: e.dma_start(out=xdbg_d[kc * 128:(kc + 1) * 128, :],
                                                               in_=XY[:, kc * S:(kc + 1) * S]), rd=r_x[kc], wr=[])

    try:
        for l in range(nl):
            layer(l)
    except _Stop:
        pass

    for kc in range(KC if not stop_state.get('skip_out') else 0):
        P.dma("sp", "out", lambda e, kc=kc: e.dma_start(out=out_d[kc * 128:(kc + 1) * 128, :],
                                                       in_=XY[:, kc * S:(kc + 1) * S]), rd=r_x[kc], wr=[])
    final_waits = tuple((k, n) for k, n in P.dcnt.items() if k in ("out", "dbg", "xsp"))
    P.q["sp"].append((final_waits, None, None))

    keys = list(P.ENG) + sorted(P.dcnt.keys())
    sems = {k: es.enter_context(nc.semaphore("s_" + k)) for k in keys}
    stats = {e: len(P.q[e]) for e in P.ENG}

    def run(e, name):
        for waits, fn, inc in P.q[name]:
            for k, n in waits:
                e.wait_ge(sems[k], n)
            if fn is not None:
                fn(e).then_inc(sems[inc[0]], inc[1])

    with nc.Block() as block:
        @block.tensor
        def _(e):
            run(e, "pe")

        @block.scalar
        def _(e):
            run(e, "act")

        @block.vector
        def _(e):
            run(e, "dve")

        @block.gpsimd
        def _(e):
            run(e, "pool")

        @block.sync
        def _(e):
            run(e, "sp")
    es.close()
    return nc, stats


def _rope_tables():
    pos = np.arange(S, dtype=np.float32)

    def tab(rot, theta, hd):
        inv = (1.0 / (np.float32(theta) ** (np.arange(0, rot, 2, dtype=np.float32) / np.float32(rot)))).astype(np.float32)
        ang = (pos[:, None] * inv[None, :]).astype(np.float32)
        c, s = np.cos(ang).astype(np.float32), np.sin(ang).astype(np.float32)
        r2 = rot // 2
        cosF = np.ones((hd, S), np.float32)
        sinF = np.zeros((hd, S), np.float32)
        cosF[0:r2] = c.T
        cosF[r2:2 * r2] = c.T
        sinF[0:r2] = s.T
        sinF[r2:2 * r2] = s.T
        return np.tile(cosF, (128 // hd, 1)), np.tile(sinF, (128 // hd, 1))

    cab, sab = tab(16, 500000.0, 64)
    cc, sc = tab(64, 10000.0, 64)
    return np.ascontiguousarray(np.stack([cab, sab, cc, sc]).astype(np.float32))


def _rot_mat(r2, hd=64):
    R = np.zeros((128, 128), np.float32)
    for b in range(128 // hd):
        for d in range(r2):
            R[b * hd + d + r2, b * hd + d] = -1.0
            R[b * hd + d, b * hd + d + r2] = 1.0
    return R


def _constants():
    ones = np.ones((128, 128), np.float32)
    bones = np.zeros((128, 128), np.float32)
    bones[0:64, 0:64] = 1.0
    bones[64:128, 64:128] = 1.0
    ident = np.eye(128, dtype=np.float32)
    cmats = np.stack([ones, bones, _rot_mat(8), _rot_mat(32), ident]).astype(np.float32)
    k = np.arange(128)[:, None]
    q = np.arange(512)[None, :]
    masks = []
    for rel in range(-1, 5):
        masks.append((np.abs(q - (128 * rel + k)) <= 64).astype(np.float32))
    band = (np.abs(np.arange(128)[None, :] - k) <= 64).astype(np.float32)
    masks.append(np.tile(band, (1, 4)))
    cmasks = np.stack(masks).astype(np.float32)
    m = np.arange(128)[:, None].astype(np.float32)
    n = np.arange(128)[None, :].astype(np.float32)
    relf = np.maximum(n - m, 0.0)
    relb = np.maximum(m - n, 0.0)
    mkf8 = (n >= m).astype(np.float32) * 0.125
    mkb8 = (m >= n).astype(np.float32) * 0.125
    pos1 = np.broadcast_to(n + 1.0, (128, 128))
    posb = np.broadcast_to(128.0 - n, (128, 128))
    tok = np.arange(128, dtype=np.float32)[:, None]
    cf = np.concatenate([relf, relb, mkf8, mkb8, pos1, posb, ones, tok, 127.0 - tok], axis=1).astype(np.float32)
    return cmats, cmasks, np.ascontiguousarray(cf)


def _prep_layers(inp, layers):
    f = lambda a: np.asarray(a, dtype=np.float32)
    nl = len(layers)
    w_in = f(inp["w_in"])[layers].reshape(nl, 8, 128, 57, 128).transpose(0, 3, 2, 1, 4)
    wbr = np.concatenate([f(inp["w_br_a"])[layers], f(inp["w_br_b"])[layers], f(inp["w_br_c"])[layers]], axis=1)
    w_br = wbr.reshape(nl, 11, 128, 8, 128).transpose(0, 3, 2, 1, 4)
    w_o = f(inp["w_o"])[layers].reshape(nl, 8, 128, 8, 128).transpose(0, 3, 2, 1, 4)
    w_m1 = f(inp["w_mlp1"])[layers].reshape(nl, 8, 128, 32, 128).transpose(0, 3, 2, 1, 4)
    w_m2 = f(inp["w_mlp2"])[layers].reshape(nl, 4, 8, 128, 8, 128).transpose(0, 1, 4, 3, 2, 5).reshape(nl, 32, 128, 8, 128)
    pcol = np.zeros((nl, 128, NPC), np.float32)
    arow = np.zeros((nl, 4, 128, 128), np.float32)
    for i, l in enumerate(layers):
        pcol[i, :, 0:8] = f(inp["norm1_g"])[l].reshape(8, 128).T
        pcol[i, :, 8:16] = f(inp["norm2_g"])[l].reshape(8, 128).T
        pcol[i, :, 16] = np.tile(f(inp["a_q_norm_g"])[l], 2)
        pcol[i, :, 17] = np.tile(f(inp["a_k_norm_g"])[l], 2)
        pcol[i, :, 18] = np.tile(f(inp["b_q_norm_g"])[l], 2)
        pcol[i, :, 19] = np.tile(f(inp["b_k_norm_g"])[l], 2)
        pcol[i, :, 20] = f(inp["b_out_norm_g"])[l]
        pcol[i, :, 21] = f(inp["c_out_norm_g"])[l]
        pcol[i, :, 22] = np.tile(f(inp["b_lambda_q1"])[l], 2)
        pcol[i, :, 23] = np.tile(f(inp["b_lambda_k1"])[l], 2)
        pcol[i, :, 24] = np.tile(f(inp["b_lambda_q2"])[l], 2)
        pcol[i, :, 25] = np.tile(f(inp["b_lambda_k2"])[l], 2)
        af, ab = f(inp["c_decay_f"])[l], f(inp["c_decay_b"])[l]
        for p in range(2):
            pcol[i, :, 26 + p] = np.repeat(af[2 * p:2 * p + 2], 64)
            pcol[i, :, 28 + p] = np.repeat(ab[2 * p:2 * p + 2], 64)
            arow[i, p] = np.repeat(af[2 * p:2 * p + 2], 64)[None, :]
            arow[i, 2 + p] = np.repeat(ab[2 * p:2 * p + 2], 64)[None, :]
        pcol[i, :, 30:34] = af[None, :]
        pcol[i, :, 34:38] = ab[None, :]
    c = np.ascontiguousarray
    return dict(w_in=c(w_in), w_br=c(w_br), w_o=c(w_o), w_m1=c(w_m1), w_m2=c(w_m2), pcol=pcol, arow=arow)


_CACHE = {}


def _get_prog(nl, l0):
    if (nl, l0) not in _CACHE:
        _CACHE[(nl, l0)] = build(nl, l0=l0)[0]
    return _CACHE[(nl, l0)]


FUSED = True


def kernel(**inputs):
    x = np.asarray(inputs["x"], dtype=np.float32)
    B = x.shape[0]
    cmats, cmasks, cf = _constants()
    rope = _rope_tables()
    consts = dict(c_mats=cmats, c_masks=cmasks, c_f32=cf, rope=rope)
    xT = [np.ascontiguousarray(x[b].T) for b in range(B)]
    groups = [list(range(DEPTH))] if FUSED else [[l] for l in range(DEPTH)]
    for layers in groups:
        nc = _get_prog(len(layers), layers[0])
        shared = _prep_layers(inputs, layers)
        shared.update(consts)
        in_maps = []
        for b in range(B):
            d = dict(shared)
            d["xT"] = xT[b]
            in_maps.append(d)
        res = run_bass_kernel_spmd(nc, in_maps, core_ids=list(range(B)))
        xT = [np.ascontiguousarray(np.asarray(r["outT"], dtype=np.float32)) for r in res.results]
    out = np.stack([t.T for t in xT], axis=0)
    return np.ascontiguousarray(out.astype(np.float32))
```

```python
import math
from contextlib import ExitStack
import numpy as np
import ml_dtypes
import concourse.bass as bass
import concourse.mybir as mybir
from concourse.bass_utils import run_bass_kernel_spmd

F32, BF16 = mybir.dt.float32, mybir.dt.bfloat16
ALU, AF = mybir.AluOpType, mybir.ActivationFunctionType

S = 2048
D = 1024
KC = 8
NB = 4
TB = 512
DEPTH = 4
EPS = 1e-6
NPC = 38
C_AQ, C_AK, C_AV = 0, 3, 6
C_BQ, C_BK, C_BV = 9, 13, 17
C_CQ, C_CK, C_CV, C_CG = 21, 23, 25, 29
C_GATE = 33
NWSLOT = 5


class Res:
    __slots__ = ("w", "r", "x")

    def __init__(self, x=False):
        self.w = None
        self.r = {}
        self.x = x


class Prog:
    ENG = ("pe", "act", "dve", "pool", "sp")

    def __init__(self):
        self.q = {e: [] for e in self.ENG}
        self.cnt = {e: 0 for e in self.ENG}
        self.seen = {e: {} for e in self.ENG}
        self.dcnt = {}

    def _need(self, eng, dep, waits, raw):
        if dep is None:
            return
        k, n = dep
        if k == eng:
            if eng == "pe" or not raw:
                return
            if n <= self.cnt[eng] - 2:
                return
        if k in self.dcnt:
            n = max(n, self.dcnt[k])
        if self.seen[eng].get(k, 0) >= n:
            return
        waits[k] = max(waits.get(k, 0), n)

    def _deps(self, eng, rd, wr):
        waits = {}
        for r in rd:
            self._need(eng, r.w, waits, True)
        for w in wr:
            self._need(eng, w.w, waits, False)
            for k, n in w.r.items():
                self._need(eng, (k, n), waits, False)
        for k, n in waits.items():
            self.seen[eng][k] = n
        return tuple(waits.items())

    def op(self, eng, fn, rd=(), wr=()):
        wr = list(wr) + [r for r in rd if r.x]
        rd = [r for r in rd if not r.x]
        waits = self._deps(eng, rd, wr)
        self.cnt[eng] += 1
        n = self.cnt[eng]
        self.q[eng].append((waits, fn, (eng, 1)))
        for r in rd:
            r.r[eng] = max(r.r.get(eng, 0), n)
        for w in wr:
            w.w = (eng, n)
            w.r = {}

    def dma(self, qeng, semkey, fn, rd=(), wr=()):
        waits = self._deps(qeng, rd, wr)
        self.dcnt[semkey] = self.dcnt.get(semkey, 0) + 16
        n = self.dcnt[semkey]
        self.q[qeng].append((waits, fn, (semkey, 16)))
        for r in rd:
            r.r[semkey] = max(r.r.get(semkey, 0), n)
        for w in wr:
            w.w = (semkey, n)
            w.r = {}

    def barrier(self, dma_keys=()):
        for e in self.ENG:
            waits = {}
            for f in self.ENG:
                if f != e and self.cnt[f] > self.seen[e].get(f, 0):
                    waits[f] = self.cnt[f]
            for k in dma_keys:
                if self.dcnt.get(k, 0) > self.seen[e].get(k, 0):
                    waits[k] = self.dcnt[k]
            for k, n in waits.items():
                self.seen[e][k] = n
            if waits:
                self.q[e].append((tuple(waits.items()), None, None))


class _Stop(Exception):
    pass


def build(nl, dbg=False, stop=None, l0=0):
    nc = bass.Bass("TRN2", target_bir_lowering=False)

    def din(name, shape, dt=F32):
        return nc.dram_tensor(name, list(shape), dt, kind="ExternalInput").ap()

    xT_d = din("xT", [D, S])
    w_in_d = din("w_in", [nl, 57, 128, 8, 128])
    w_br_d = din("w_br", [nl, 8, 128, 11, 128])
    w_o_d = din("w_o", [nl, 8, 128, 8, 128])
    w_m1_d = din("w_m1", [nl, 32, 128, 8, 128])
    w_m2_d = din("w_m2", [nl, 32, 128, 8, 128])
    pcol_d = din("pcol", [nl, 128, NPC])
    arow_d = din("arow", [nl, 4, 128, 128])
    cmat_d = din("c_mats", [5, 128, 128])
    cmask_d = din("c_masks", [7, 128, 512])
    cf32_d = din("c_f32", [128, 7 * 128 + 2])
    rope_d = din("rope", [4, 128, S])
    out_d = nc.dram_tensor("outT", [D, S], F32, kind="ExternalOutput").ap()
    xsp_d = nc.dram_tensor("xspill", [D, S], F32, kind="Internal").ap()
    if dbg:
        ydbg_d = nc.dram_tensor("ydbg", [11 * 128, S], BF16, kind="ExternalOutput").ap()
        xdbg_d = nc.dram_tensor("xdbg", [D, S], F32, kind="ExternalOutput").ap()

    P = Prog()
    es = ExitStack()

    def sb(name, shape, dt):
        return es.enter_context(nc.sbuf_tensor(name, list(shape), dt))

    XN = sb("XN", [128, KC * S], BF16)
    XY = sb("XY", [128, KC * S], F32)
    M = sb("M", [128, KC * S], BF16)
    WR = sb("WR", [128, NWSLOT * 11 * 128], BF16)
    CMB = sb("CMB", [128, 5 * 128], BF16)
    MASKS = sb("MASKS", [128, 7 * 512], BF16)
    CF = sb("CF", [128, 7 * 128 + 2], F32)
    PCOL = sb("PCOL", [128, NPC], F32)
    SMALL = sb("SMALL", [128, 64], F32)
    CTAB = sb("CTAB", [128, 13 * 128], F32)
    CDER = sb("CDER", [128, 6 * 2048], BF16)
    S32 = sb("S32", [128, 2 * 128], F32)
    SQB = sb("SQB", [128, 2 * 512], BF16)
    UB = sb("UB", [128, 2 * 512], BF16)
    FT = sb("FT", [128, 9 * 512], F32)

    Ybf = XY[:, 0:11 * 1024].bitcast(BF16)
    ROPE = XY[:, 11 * 1024:11 * 1024 + 4096]
    PB = XY[:, 15 * 1024:16 * 1024].bitcast(BF16)
    DTOT = CDER[:, 4 * 2048:6 * 2048].bitcast(F32)

    def x_(kc, tb):
        return XY[:, kc * S + tb * TB: kc * S + (tb + 1) * TB]

    def xn_(kc, tb):
        return XN[:, kc * S + tb * TB: kc * S + (tb + 1) * TB]

    def xnfull(kc):
        return XN[:, kc * S:(kc + 1) * S]

    def y_(ch):
        return Ybf[:, ch * S:(ch + 1) * S]

    def m_(kc, tb):
        return M[:, kc * S + tb * TB: kc * S + (tb + 1) * TB]

    def uq(s):
        return M[:, s * 8192: s * 8192 + 2048]

    def uk(s):
        return M[:, s * 8192 + 2048: s * 8192 + 4096]

    def uv(s):
        return M[:, s * 8192 + 4096: s * 8192 + 8192].rearrange("p (h t c) -> p h t c", h=2, c=128)

    def kz(s, m):
        return uk(s) if m == 0 else M[:, s * 8192 + 6144: s * 8192 + 8192]

    def wslot(s, nk=8):
        return WR[:, s * 1408: s * 1408 + nk * 128].rearrange("p (k n) -> p k n", n=128)

    ONES = CMB[:, 0:128]
    BONES = CMB[:, 128:256]
    RAB = CMB[:, 256:384]
    RC = CMB[:, 384:512]
    IDENT = CMB[:, 512:640]

    def mask_(i):
        return MASKS[:, i * 512:(i + 1) * 512]

    RELF, RELB = CF[:, 0:128], CF[:, 128:256]
    MKF8, MKB8 = CF[:, 256:384], CF[:, 384:512]
    POS1, POSB = CF[:, 512:640], CF[:, 640:768]
    ONESF = CF[:, 768:896]
    TOKC, TOKCR = CF[:, 896:897], CF[:, 897:898]

    def ctab(i):
        return CTAB[:, i * 128:(i + 1) * 128]

    def cder(i):
        return CDER[:, i * 2048:(i + 1) * 2048]

    KDF, KDB = cder(0).rearrange("p (t c) -> p t c", c=128), cder(1).rearrange("p (t c) -> p t c", c=128)
    QDF, QDB = cder(2), cder(3)
    STF, STB = cder(4).rearrange("p (t c) -> p t c", c=128), cder(5).rearrange("p (t c) -> p t c", c=128)

    def pb(i):
        return PB[:, i * 512:(i + 1) * 512]

    def sqb(i):
        return SQB[:, i * 512:(i + 1) * 512]

    def ub(i):
        return UB[:, i * 512:(i + 1) * 512]

    def ft(i):
        return FT[:, i * 512:(i + 1) * 512]

    PS = [es.enter_context(nc.psum_tensor("ps%d" % i, [128, 512], F32)) for i in range(8)]

    r_x = [[Res() for _ in range(NB)] for _ in range(KC)]
    r_xn = [Res() for _ in range(NB)]
    r_y = [[Res() for _ in range(NB)] for _ in range(11)]
    r_yall = [Res() for _ in range(11)]
    r_m = [[Res() for _ in range(NB)] for _ in range(KC)]
    r_uq = [Res(), Res()]
    r_uk = [Res(), Res()]
    r_uv = [Res(), Res()]
    r_w = [Res() for _ in range(NWSLOT)]
    r_ps = [Res(True) for _ in range(8)]
    r_pb = [Res() for _ in range(4)]
    r_sqb = [Res(), Res()]
    r_ub = [Res(), Res()]
    r_ft = [Res() for _ in range(9)]
    r_rope = Res()
    r_acc = Res()
    r_const = Res()
    r_pcol = Res()
    r_small = Res()
    r_ctab = Res()
    r_cder = [Res() for _ in range(6)]
    r_s32 = [Res(), Res()]
    r_xsp = Res()

    wstate = {"n": 0}

    def load_w(dram_ap, nk, s):
        dst = wslot(s, nk)
        P.dma("pool", "w%d" % s, lambda e, dst=dst, src=dram_ap: e.dma_start(out=dst, in_=src), wr=[r_w[s]])
        return s

    class WStream:
        def __init__(self):
            self.pending = []
            self.loaded = []
            self.free = list(range(NWSLOT))

        def push(self, ap, nk=8):
            self.pending.append((ap, nk))

        def prefetch(self):
            while self.pending and self.free:
                ap, nk = self.pending.pop(0)
                self.loaded.append(load_w(ap, nk, self.free.pop(0)))

        def pop(self):
            self.prefetch()
            return self.loaded.pop(0)

        def release(self, s):
            self.free.append(s)
            self.prefetch()

    WS = WStream()

    def mm(out, lhsT, rhs, start, stop, rd, wr):
        P.op("pe", lambda e: e.matmul(out, lhsT=lhsT, rhs=rhs, start=start, stop=stop), rd=rd, wr=wr)

    def act(out, in_, func, rd, wr, scale=None, bias=None):
        kw = {}
        if scale is not None:
            kw["scale"] = scale
        if bias is not None:
            kw["bias"] = bias
        P.op("act", lambda e: e.activation(out=out, in_=in_, func=func, **kw), rd=rd, wr=wr)

    def tt(eng, out, in0, in1, op, rd, wr):
        P.op(eng, lambda e: e.tensor_tensor(out=out, in0=in0, in1=in1, op=op), rd=rd, wr=wr)

    def stt(eng, out, in0, scalar, in1, op0, op1, rd, wr):
        P.op(eng, lambda e: e.scalar_tensor_tensor(out=out, in0=in0, scalar=scalar, in1=in1, op0=op0, op1=op1),
             rd=rd, wr=wr)

    def ts(eng, out, in0, s1, s2, op0, op1, rd, wr):
        if op1 is None:
            P.op(eng, lambda e: e.tensor_scalar(out=out, in0=in0, scalar1=s1, scalar2=None, op0=op0), rd=rd, wr=wr)
        else:
            P.op(eng, lambda e: e.tensor_scalar(out=out, in0=in0, scalar1=s1, scalar2=s2, op0=op0, op1=op1),
                 rd=rd, wr=wr)

    def cp(eng, out, in_, rd, wr):
        if eng == "act":
            act(out, in_, AF.Copy, rd, wr)
        else:
            P.op(eng, lambda e: e.tensor_copy(out=out, in_=in_), rd=rd, wr=wr)

    def recip(out, in_, rd, wr):
        P.op("dve", lambda e: e.reciprocal(out=out, in_=in_), rd=rd, wr=wr)

    def rstd_from(ps_ap, r_in, ftile, r_ftile, inv_n):
        act(ftile, ps_ap, AF.Ln, rd=[r_in], wr=[r_ftile], scale=inv_n, bias=EPS)
        act(ftile, ftile, AF.Exp, rd=[r_ftile], wr=[r_ftile], scale=-0.5)

    rot = {"ps": 0, "alt": 0}
    SBANK = (4, 5, 3)

    P.dma("pool", "cst", lambda e: e.dma_start(out=CMB[:, :].rearrange("p (c n) -> p c n", n=128),
                                                  in_=cmat_d.rearrange("c p n -> p c n")), wr=[r_const])
    P.dma("pool", "cst", lambda e: e.dma_start(out=MASKS[:, :].rearrange("p (c n) -> p c n", n=512),
                                                  in_=cmask_d.rearrange("c p n -> p c n")), wr=[r_const])
    P.dma("sp", "cst", lambda e: e.dma_start(out=CF[:, :], in_=cf32_d), wr=[r_const])
    for kc in range(KC):
        P.dma("sp", "xld", lambda e, kc=kc: e.dma_start(out=XY[:, kc * S:(kc + 1) * S],
                                                       in_=xT_d[kc * 128:(kc + 1) * 128, :]),
              wr=r_x[kc])

    def norm(gcol0):
        for tb in range(NB):
            for kc in range(KC):
                i = kc % 2
                act(sqb(i), x_(kc, tb), AF.Square, rd=[r_x[kc][tb]], wr=[r_sqb[i]])
                mm(PS[2][:, :], ONES, sqb(i), kc == 0, kc == KC - 1, rd=[r_sqb[i], r_const], wr=[r_ps[2]])
            rstd_from(PS[2][:, :], r_ps[2], ft(0), r_ft[0], 1.0 / D)
            for kc in range(KC):
                stt("dve", xn_(kc, tb), x_(kc, tb), PCOL[:, gcol0 + kc: gcol0 + kc + 1], ft(0), ALU.mult, ALU.mult,
                    rd=[r_x[kc][tb], r_ft[0], r_pcol], wr=[r_xn[tb]])

    def proj_fm(slot, tb, bank):
        w = wslot(slot)
        for kc in range(KC):
            mm(PS[bank][:, :], w[:, kc, :], xn_(kc, tb), kc == 0, kc == KC - 1,
               rd=[r_w[slot], r_xn[tb]], wr=[r_ps[bank]])

    def perm_dst(buf, tb, delta):
        if delta == 1:
            return buf[:, tb * TB:(tb + 1) * TB], None
        n0 = tb * TB // delta
        return buf.rearrange("p (r n) -> p n r", r=delta)[:, n0:n0 + TB // delta, :], delta

    def srcv(ap, delta):
        if delta is None:
            return ap
        return ap.rearrange("p (n r) -> p n r", r=delta)

    def qk_post(bank, tb, gcol, rmat, normed, dsts, delta):
        j = rot["alt"] % 2
        rot["alt"] += 1
        pj = PS[bank][:, :]
        cos = ROPE[:, tb * TB:(tb + 1) * TB]
        sin = ROPE[:, S + tb * TB: S + (tb + 1) * TB]
        if normed:
            act(sqb(j), pj, AF.Square, rd=[r_ps[bank]], wr=[r_sqb[j]])
            mm(PS[2][:, :], BONES, sqb(j), True, True, rd=[r_sqb[j], r_const], wr=[r_ps[2]])
            yield
            rstd_from(PS[2][:, :], r_ps[2], ft(j), r_ft[j], 1.0 / 64)
            stt("dve", ub(j), pj, PCOL[:, gcol:gcol + 1], ft(j), ALU.mult, ALU.mult,
                rd=[r_ps[bank], r_ft[j], r_pcol], wr=[r_ub[j]])
            yield
            mm(PS[2][:, :], rmat, ub(j), True, True, rd=[r_ub[j], r_const], wr=[r_ps[2]])
            tt("pool", ft(2 + j), ub(j), cos, ALU.mult, rd=[r_ub[j], r_rope], wr=[r_ft[2 + j]])
        else:
            cp("act", ub(j), pj, rd=[r_ps[bank]], wr=[r_ub[j]])
            tt("dve", ft(2 + j), pj, cos, ALU.mult, rd=[r_ps[bank], r_rope], wr=[r_ft[2 + j]])
            yield
            mm(PS[2][:, :], rmat, ub(j), True, True, rd=[r_ub[j], r_const], wr=[r_ps[2]])
        yield
        tt("dve", ft(4 + j), PS[2][:, :], sin, ALU.mult, rd=[r_ps[2], r_rope], wr=[r_ft[4 + j]])
        for buf, rows, res in dsts:
            dst, dl = perm_dst(buf, tb, delta)
            a0, a1 = srcv(ft(2 + j), dl), srcv(ft(4 + j), dl)
            if rows is not None:
                dst, a0, a1 = dst[rows[0]:rows[1]], a0[rows[0]:rows[1]], a1[rows[0]:rows[1]]
            tt("pool", dst, a0, a1, ALU.add, rd=[r_ft[2 + j], r_ft[4 + j]], wr=[res])

    def v_proj(slot, set_, coloff, delta):
        w = wslot(slot)
        V = uv(set_)
        for g4 in range(4):
            bank = 0
            for ti in range(4):
                t = g4 * 4 + ti
                for kc in range(KC):
                    if delta == 1:
                        lhs = xnfull(kc)[:, t * 128:(t + 1) * 128]
                    elif delta == 4:
                        lhs = xnfull(kc).rearrange("p (n r) -> p r n", r=4)[:, t // 4, (t % 4) * 128:(t % 4 + 1) * 128]
                    else:
                        lhs = xnfull(kc).rearrange("p (n r) -> p r n", r=16)[:, t, :]
                    mm(PS[bank][:, ti * 128:(ti + 1) * 128], lhs, w[:, kc, :], kc == 0, kc == KC - 1,
                       rd=[r_w[slot]] + r_xn, wr=[r_ps[bank]])
            cp("act", V[:, coloff // 128, g4 * 4:(g4 + 1) * 4, :],
               PS[bank][:, :].rearrange("p (t c) -> p t c", c=128), rd=[r_ps[bank]], wr=[r_uv[set_]])
            yield

    units = []
    for g in range(3):
        units.append(dict(kind="A", idx=g, q=C_AQ + g, k=C_AK + g, v=[C_AV + g], gq=16, gk=17, delta=(1, 4, 16)[g]))
    for h in range(4):
        units.append(dict(kind="B", idx=h, q=C_BQ + h, k=C_BK + h, v=[C_BV + h], gq=18, gk=19, delta=1))
    for p in range(2):
        units.append(dict(kind="C", idx=p, q=C_CQ + p, k=C_CK + p, v=[C_CV + 2 * p, C_CV + 2 * p + 1], gq=0, gk=0,
                          delta=1))

    def push_unit_weights(l, u):
        WS.push(w_in_d[l, u["q"]])
        WS.push(w_in_d[l, u["k"]])
        for c in u["v"]:
            WS.push(w_in_d[l, c])

    def rope_load(which):
        P.dma("sp", "rope", lambda e: e.dma_start(out=ROPE[:, 0:S], in_=rope_d[2 * which]), wr=[r_rope])
        P.dma("sp", "rope", lambda e: e.dma_start(out=ROPE[:, S:2 * S], in_=rope_d[2 * which + 1]), wr=[r_rope])

    def inproj(l, ui):
        u = units[ui]
        set_ = ui % 2
        normed = u["kind"] != "C"
        rmat = RC if u["kind"] == "C" else RAB
        if ui == 0:
            rope_load(0)
        if ui == 7:
            rope_load(1)
        for which in ("q", "k"):
            if which == "q":
                dsts, gcol = [(uq(set_), None, r_uq[set_])], u["gq"]
            elif u["kind"] == "C":
                dsts, gcol = [(uk(set_), None, r_uk[set_])], u["gk"]
            else:
                k0, k1 = kz(set_, 0), kz(set_, 1)
                P.op("pool", lambda e, k0=k0: e.memset(k0[64:128, :], 0.0), wr=[r_uk[set_]])
                P.op("pool", lambda e, k1=k1: e.memset(k1[0:64, :], 0.0), wr=[r_uv[set_]])
                dsts, gcol = [(k0, (0, 64), r_uk[set_]), (k1, (64, 128), r_uv[set_])], u["gk"]
            slot = WS.pop()
            for tb in range(NB):
                bank = 0
                proj_fm(slot, tb, bank)
                yield
                yield from qk_post(bank, tb, gcol, rmat, normed, dsts, u["delta"])
                yield
            WS.release(slot)
        for vi, c in enumerate(u["v"]):
            slot = WS.pop()
            yield from v_proj(slot, set_, vi * 128, u["delta"])
            WS.release(slot)

    def attn_B(l, ui):
        u = units[ui]
        h = u["idx"]
        set_ = ui % 2
        qT, V = uq(set_), uv(set_)
        rdq = [r_uq[set_], r_uk[set_], r_uv[set_]]
        ch = 3 + h
        pending = []

        def make_epilogue(qb):
            def s1():
                act(sqb(0), ft(7), AF.Square, rd=[r_ft[7]], wr=[r_sqb[0]])

            def s2():
                mm(PS[1][:, :], ONES, sqb(0), True, True, rd=[r_sqb[0], r_const], wr=[r_ps[1]])

            def s3():
                rstd_from(PS[1][:, :], r_ps[1], ft(6), r_ft[6], 1.0 / 128)
                stt("dve", y_(ch)[:, qb * TB:(qb + 1) * TB], ft(7), SMALL[:, 1:2], ft(6), ALU.mult, ALU.mult,
                    rd=[r_ft[7], r_ft[6], r_small], wr=[r_y[ch][qb], r_yall[ch]])

            return [(9, s1), (11, s2), (13, s3)]

        for qb in range(NB):
            for m in range(2):
                def smm(kt):
                    bank = SBANK[kt % 3]
                    mm(PS[bank][:, :], kz(set_, m)[:, kt * 128:(kt + 1) * 128], qT[:, qb * TB:(qb + 1) * TB], True, True,
                       rd=rdq, wr=[r_ps[bank]])

                smm(0)
                smm(1)
                for kt in range(16):
                    if kt + 2 < 16:
                        smm(kt + 2)
                    bank = SBANK[kt % 3]
                    pi = kt % 4
                    act(pb(pi), PS[bank][:, :], AF.Exp, rd=[r_ps[bank]], wr=[r_pb[pi]], scale=0.125)
                    mm(PS[6][:, :], V[:, 0, kt, :], pb(pi), kt == 0, kt == 15, rd=[r_pb[pi], r_uv[set_]],
                       wr=[r_ps[6]])
                    mm(PS[7][:, :], ONES, pb(pi), kt == 0, kt == 15, rd=[r_pb[pi], r_const], wr=[r_ps[7]])
                    while pending and pending[0][0] <= kt:
                        pending.pop(0)[1]()
                    if kt % 4 == 3 and kt < 15:
                        yield
                cp("dve", ft(6), PS[7][:, :], rd=[r_ps[7]], wr=[r_ft[6]])
                if m == 0:
                    cp("dve", ft(7), PS[6][:, :], rd=[r_ps[6]], wr=[r_ft[7]])
                    recip(ft(6), ft(6), rd=[r_ft[6]], wr=[r_ft[6]])
                    tt("dve", ft(7), ft(7), ft(6), ALU.mult, rd=[r_ft[7], r_ft[6]], wr=[r_ft[7]])
                else:
                    cp("dve", ft(8), PS[6][:, :], rd=[r_ps[6]], wr=[r_ft[8]])
                    recip(ft(6), ft(6), rd=[r_ft[6]], wr=[r_ft[6]])
                    tt("dve", ft(8), ft(8), ft(6), ALU.mult, rd=[r_ft[8], r_ft[6]], wr=[r_ft[8]])
                    stt("dve", ft(7), ft(8), SMALL[:, 0:1], ft(7), ALU.mult, ALU.add,
                        rd=[r_ft[8], r_ft[7], r_small], wr=[r_ft[7]])
                    pending = make_epilogue(qb)
                yield
        for _, fn in pending:
            fn()
        yield

    def attn_A(l, ui):
        u = units[ui]
        g = u["idx"]
        delta = u["delta"]
        set_ = ui % 2
        qT, kT, V = uq(set_), uk(set_), uv(set_)
        rdq = [r_uq[set_], r_uk[set_], r_uv[set_]]
        for s2 in range(2):
            rows = slice(64 * s2, 64 * s2 + 64)
            for qb in range(NB):
                if g < 2:
                    rels = list(range(-1, 5)) if g == 0 else list(range(4))
                    kts = [(4 * qb + r, r) for r in rels if 0 <= 4 * qb + r < 16]
                    n = len(kts)

                    def smm(i):
                        kt, rel = kts[i]
                        bank = 4 + i % 2
                        mm(PS[bank][:, :], kz(set_, s2)[:, kt * 128:(kt + 1) * 128], qT[:, qb * TB:(qb + 1) * TB], True,
                           True, rd=rdq, wr=[r_ps[bank]])

                    smm(0)
                    for i in range(n):
                        if i + 1 < n:
                            smm(i + 1)
                        kt, rel = kts[i]
                        bank = 4 + i % 2
                        pi = i % 3
                        act(pb(pi), PS[bank][:, :], AF.Exp, rd=[r_ps[bank]], wr=[r_pb[pi]], scale=0.125)
                        tt("dve", pb(pi), pb(pi), mask_(rel + 1), ALU.mult, rd=[r_pb[pi], r_const], wr=[r_pb[pi]])
                        mm(PS[6][:, :], V[:, 0, kt, :], pb(pi), i == 0, i == n - 1, rd=[r_pb[pi], r_uv[set_]],
                           wr=[r_ps[6]])
                        mm(PS[7][:, :], ONES, pb(pi), i == 0, i == n - 1, rd=[r_pb[pi], r_const], wr=[r_ps[7]])
                        if i < n - 1:
                            yield
                else:
                    bank = 4 + qb % 2
                    for rel in range(4):
                        kt = 4 * qb + rel
                        mm(PS[bank][:, rel * 128:(rel + 1) * 128], kz(set_, s2)[:, kt * 128:(kt + 1) * 128],
                           qT[:, kt * 128:(kt + 1) * 128], True, True, rd=rdq, wr=[r_ps[bank]])
                    pi = qb % 3
                    act(pb(pi), PS[bank][:, :], AF.Exp, rd=[r_ps[bank]], wr=[r_pb[pi]], scale=0.125)
                    tt("dve", pb(pi), pb(pi), mask_(6), ALU.mult, rd=[r_pb[pi], r_const], wr=[r_pb[pi]])
                    for rel in range(4):
                        kt = 4 * qb + rel
                        mm(PS[6][:, rel * 128:(rel + 1) * 128], V[:, 0, kt, :], pb(pi)[:, rel * 128:(rel + 1) * 128],
                           True, True, rd=[r_pb[pi], r_uv[set_]], wr=[r_ps[6]])
                        mm(PS[7][:, rel * 128:(rel + 1) * 128], ONES, pb(pi)[:, rel * 128:(rel + 1) * 128],
                           True, True, rd=[r_pb[pi], r_const], wr=[r_ps[7]])
                if delta == 1:
                    ydst = y_(g)[rows, qb * TB:(qb + 1) * TB]
                    ddst = DTOT[rows, qb * TB:(qb + 1) * TB]
                    nsrc, dsrc = PS[6][rows, :], PS[7][rows, :]
                elif delta == 4:
                    ydst = y_(g).rearrange("p (n r) -> p r n", r=4)[rows, qb, :]
                    ddst = DTOT[:, :].rearrange("p (n r) -> p r n", r=4)[rows, qb, :]
                    nsrc, dsrc = PS[6][rows, :], PS[7][rows, :]
                else:
                    ydst = y_(g).rearrange("p (n r) -> p r n", r=16)[rows, 4 * qb:4 * qb + 4, :]
                    ddst = DTOT[:, :].rearrange("p (n r) -> p r n", r=16)[rows, 4 * qb:4 * qb + 4, :]
                    nsrc = PS[6][rows, :].rearrange("p (a b) -> p a b", a=4)
                    dsrc = PS[7][rows, :].rearrange("p (a b) -> p a b", a=4)
                cp("act", ydst, nsrc, rd=[r_ps[6]], wr=[r_yall[g]] + r_y[g])
                if g == 0:
                    cp("dve", ddst, dsrc, rd=[r_ps[7]], wr=[r_cder[4], r_cder[5]])
                else:
                    tt("dve", ddst, ddst, dsrc, ALU.add, rd=[r_ps[7], r_cder[4], r_cder[5]], wr=[r_cder[4], r_cder[5]])
                yield
        if g == 2:
            for tb in range(NB):
                recip(ft(6), DTOT[:, tb * TB:(tb + 1) * TB], rd=[r_cder[4], r_cder[5]], wr=[r_ft[6]])
                for gg in range(3):
                    ysl = y_(gg)[:, tb * TB:(tb + 1) * TB]
                    tt("pool", ysl, ysl, ft(6), ALU.mult, rd=[r_ft[6], r_yall[gg]] + r_y[gg], wr=[r_y[gg][tb], r_yall[gg]])
                yield

    def attn_C(l, ui):
        u = units[ui]
        p = u["idx"]
        set_ = ui % 2
        qT, kT, V = uq(set_), uk(set_), uv(set_)
        DQF, DQB, DKF, DKB = ctab(4 + p), ctab(6 + p), ctab(8 + p), ctab(10 + p)
        g128f, g128b = SMALL[:, 8 + p:9 + p], SMALL[:, 10 + p:11 + p]
        for g4 in range(4):
            bank = 3
            tp = PS[bank][:, :].bitcast(BF16)[:, 0:512]
            for ti in range(4):
                t = g4 * 4 + ti
                P.op("pe", lambda e, t=t, ti=ti, tp=tp: e.transpose(tp[:, ti * 128:(ti + 1) * 128],
                                                                      kT[:, t * 128:(t + 1) * 128], IDENT),
                     rd=[r_uk[set_], r_const], wr=[r_ps[bank]])
            tp3 = tp.rearrange("p (t c) -> p t c", c=128)
            tt("dve", KDF[:, g4 * 4:(g4 + 1) * 4, :], tp3, DKF.unsqueeze(1).to_broadcast([128, 4, 128]), ALU.mult,
               rd=[r_ps[bank], r_ctab], wr=[r_cder[0]])
            tt("dve", KDB[:, g4 * 4:(g4 + 1) * 4, :], tp3, DKB.unsqueeze(1).to_broadcast([128, 4, 128]), ALU.mult,
               rd=[r_ps[bank], r_ctab], wr=[r_cder[1]])
            q3 = qT[:, g4 * TB:(g4 + 1) * TB].rearrange("p (t c) -> p t c", c=128)
            tt("pool", QDF[:, g4 * TB:(g4 + 1) * TB].rearrange("p (t c) -> p t c", c=128), q3,
               DQF.unsqueeze(1).to_broadcast([128, 4, 128]), ALU.mult, rd=[r_uq[set_], r_ctab], wr=[r_cder[2]])
            tt("pool", QDB[:, g4 * TB:(g4 + 1) * TB].rearrange("p (t c) -> p t c", c=128), q3,
               DQB.unsqueeze(1).to_broadcast([128, 4, 128]), ALU.mult, rd=[r_uq[set_], r_ctab], wr=[r_cder[3]])
            yield
        P.op("pool", lambda e: e.memset(S32[:, :], 0.0), wr=r_s32)
        P.op("pool", lambda e: e.memset(STF[:, 0, :], 0.0), wr=[r_cder[4]])
        P.op("pool", lambda e: e.memset(STB[:, 15, :], 0.0), wr=[r_cder[5]])
        for step in range(15):
            for d in range(2):
                c = step if d == 0 else 15 - step
                KD = KDF if d == 0 else KDB
                ST = STF if d == 0 else STB
                gcol = g128f if d == 0 else g128b
                s32 = S32[:, d * 128:(d + 1) * 128]
                bank = 4 + d
                for hh in range(2):
                    mm(PS[bank][:, hh * 128:(hh + 1) * 128], KD[:, c, :], V[:, hh, c, :], True, True,
                       rd=[r_cder[d], r_uv[set_]], wr=[r_ps[bank]])
                for hh in range(2):
                    rws = slice(64 * hh, 64 * hh + 64)
                    stt("dve", s32[rws, :], s32[rws, :], gcol[rws, :], PS[bank][rws, hh * 128:(hh + 1) * 128],
                        ALU.mult, ALU.add, rd=[r_ps[bank], r_s32[d], r_small], wr=[r_s32[d]])
                cn = c + 1 if d == 0 else c - 1
                cp("act", ST[:, cn, :], s32, rd=[r_s32[d]], wr=[r_cder[4 + d]])
            if step % 4 == 3:
                yield
        yield
        pend = [None]
        for hh in range(2):
            rws = slice(64 * hh, 64 * hh + 64)
            h = 2 * p + hh
            ch = 7 + h
            for cb in range(4):
                sbank = 4 + cb % 2
                for ci in range(4):
                    c = 4 * cb + ci
                    mm(PS[sbank][:, ci * 128:(ci + 1) * 128], kT[rws, c * 128:(c + 1) * 128],
                       qT[rws, c * 128:(c + 1) * 128], True, True, rd=[r_uq[set_], r_uk[set_]], wr=[r_ps[sbank]])
                pi = cb % 3
                tt("dve", pb(pi).rearrange("p (t c) -> p t c", c=128),
                   PS[sbank][:, :].rearrange("p (t c) -> p t c", c=128),
                   ctab(h).unsqueeze(1).to_broadcast([128, 4, 128]), ALU.mult, rd=[r_ps[sbank], r_ctab],
                   wr=[r_pb[pi]])
                for ci in range(4):
                    c = 4 * cb + ci
                    o = PS[6][:, ci * 128:(ci + 1) * 128]
                    mm(o, V[:, hh, c, :], pb(pi)[:, ci * 128:(ci + 1) * 128], True, False,
                       rd=[r_pb[pi], r_uv[set_]], wr=[r_ps[6]])
                    mm(o, STF[rws, c, :], QDF[rws, c * 128:(c + 1) * 128], False, False,
                       rd=[r_cder[4], r_cder[2]], wr=[r_ps[6]])
                    mm(o, STB[rws, c, :], QDB[rws, c * 128:(c + 1) * 128], False, True,
                       rd=[r_cder[5], r_cder[3]], wr=[r_ps[6]])
                if pend[0] is not None:
                    pend[0]()
                fo = 7 + (hh * 4 + cb) % 2
                cp("dve", ft(fo), PS[6][:, :], rd=[r_ps[6]], wr=[r_ft[fo]])

                def epi(fo=fo, ch=ch, cb=cb):
                    act(pb(3), ft(fo), AF.Square, rd=[r_ft[fo]], wr=[r_pb[3]])
                    mm(PS[1][:, :], ONES, pb(3), True, True, rd=[r_pb[3], r_const], wr=[r_ps[1]])
                    rstd_from(PS[1][:, :], r_ps[1], ft(6), r_ft[6], 1.0 / 128)
                    stt("dve", y_(ch)[:, cb * TB:(cb + 1) * TB], ft(fo), PCOL[:, 21:22], ft(6), ALU.mult, ALU.mult,
                        rd=[r_ft[fo], r_ft[6], r_pcol], wr=[r_y[ch][cb], r_yall[ch]])

                pend[0] = epi
                yield
        if pend[0] is not None:
            pend[0]()
        yield

    def layer_params(l):
        lam_init = 0.8 - 0.6 * math.exp(-0.3 * (l + l0))
        P.dma("sp", "par", lambda e: e.dma_start(out=PCOL[:, :], in_=pcol_d[l]), wr=[r_pcol])
        P.dma("sp", "par", lambda e: e.dma_start(out=CTAB[:, 8 * 128:12 * 128].rearrange("p (c n) -> p c n", n=128),
                                                   in_=arow_d[l].rearrange("c p n -> p c n")), wr=[r_ctab])
        rp = [r_pcol]
        tt("dve", SMALL[:, 2:3], PCOL[:, 22:23], PCOL[:, 23:24], ALU.mult, rd=rp, wr=[r_small])
        tt("dve", SMALL[:, 3:4], PCOL[:, 24:25], PCOL[:, 25:26], ALU.mult, rd=rp, wr=[r_small])
        mm(PS[2][:, 0:2], ONESF[0:64, :], SMALL[0:64, 2:4], True, True, rd=[r_small, r_const], wr=[r_ps[2]])
        act(SMALL[:, 4:6], PS[2][:, 0:2], AF.Exp, rd=[r_ps[2]], wr=[r_small])
        tt("dve", SMALL[:, 6:7], SMALL[:, 5:6], SMALL[:, 4:5], ALU.subtract, rd=[r_small], wr=[r_small])
        ts("dve", SMALL[:, 0:1], SMALL[:, 6:7], -lam_init, None, ALU.add, None, rd=[r_small], wr=[r_small])
        ts("dve", SMALL[:, 1:2], PCOL[:, 20:21], 1.0 - lam_init, None, ALU.mult, None, rd=rp + [r_small],
           wr=[r_small])
        act(SMALL[:, 12:16], PCOL[:, 26:30], AF.Exp, rd=rp + [r_small], wr=[r_small])
        ts("dve", SMALL[:, 12:16], SMALL[:, 12:16], -1.0, None, ALU.mult, None, rd=[r_small], wr=[r_small])
        act(SMALL[:, 8:12], SMALL[:, 12:16], AF.Exp, rd=[r_small], wr=[r_small], scale=128.0)
        act(SMALL[:, 16:24], PCOL[:, 30:38], AF.Exp, rd=rp + [r_small], wr=[r_small])
        ts("dve", SMALL[:, 16:24], SMALL[:, 16:24], -1.0, None, ALU.mult, None, rd=[r_small], wr=[r_small])
        rs = [r_small, r_const]
        for h in range(4):
            act(ctab(h), RELF, AF.Exp, rd=rs, wr=[r_ctab], scale=SMALL[:, 16 + h:17 + h])
            tt("dve", ctab(h), ctab(h), MKF8, ALU.mult, rd=[r_ctab, r_const], wr=[r_ctab])
            act(ctab(12), RELB, AF.Exp, rd=rs + [r_ctab], wr=[r_ctab], scale=SMALL[:, 20 + h:21 + h])
            tt("dve", ctab(12), ctab(12), MKB8, ALU.mult, rd=[r_ctab, r_const], wr=[r_ctab])
            tt("dve", ctab(h), ctab(h), ctab(12), ALU.add, rd=[r_ctab], wr=[r_ctab])
        for p in range(2):
            act(ctab(4 + p), POS1, AF.Exp, rd=rs + [r_ctab], wr=[r_ctab], scale=SMALL[:, 12 + p:13 + p])
            act(ctab(6 + p), POSB, AF.Exp, rd=rs + [r_ctab], wr=[r_ctab], scale=SMALL[:, 14 + p:15 + p])
            ts("dve", ctab(4 + p), ctab(4 + p), 0.125, None, ALU.mult, None, rd=[r_ctab], wr=[r_ctab])
            ts("dve", ctab(6 + p), ctab(6 + p), 0.125, None, ALU.mult, None, rd=[r_ctab], wr=[r_ctab])
            for d in range(2):
                dstt = ctab(8 + 2 * d + p)
                src = dstt
                act(dstt, src, AF.Exp, rd=rp + [r_ctab], wr=[r_ctab])
                ts("dve", dstt, dstt, -1.0, None, ALU.mult, None, rd=[r_ctab], wr=[r_ctab])
                act(dstt, dstt, AF.Exp, rd=[r_ctab, r_const], wr=[r_ctab], scale=(TOKCR if d == 0 else TOKC))

    def interleave(a, b):
        da = db = False
        while not (da and db):
            if not da:
                try:
                    next(a)
                except StopIteration:
                    da = True
            if b is None:
                db = True
            if not db:
                try:
                    next(b)
                except StopIteration:
                    db = True

    def drain(g):
        for _ in g:
            pass

    stop_state = {}
    attn_fn = {"A": attn_A, "B": attn_B, "C": attn_C}

    def ck(name):
        if stop == name:
            raise _Stop()

    def layer(l):
        layer_params(l)
        ck('params')
        for u in (units if stop not in ('xc', 'xc2') else units[7:]):
            push_unit_weights(l, u)
        norm(0)
        ck('norm1')
        for kc in range(KC):
            P.dma("sp", "xsp", lambda e, kc=kc: e.dma_start(out=xsp_d[kc * 128:(kc + 1) * 128, :],
                                                           in_=XY[:, kc * S:(kc + 1) * S]),
                  rd=r_x[kc], wr=[r_xsp])
        P.barrier(dma_keys=("xsp",))
        ck('spill')
        if stop in ('xc', 'xc2'):
            drain(inproj(l, 7))
            ck('xc')
            drain(attn_C(l, 7))
            ck('xc2')
        drain(inproj(l, 0))
        ck('inproj0')
        for ui in range(len(units)):
            a = attn_fn[units[ui]["kind"]](l, ui)
            b = inproj(l, ui + 1) if ui + 1 < len(units) else None
            interleave(a, b)
            ck('u%d' % ui)
        P.barrier()
        ck('mix')
        if dbg and l == 0:
            P.dma("sp", "dbg", lambda e: e.dma_start(out=ydbg_d.rearrange("(c p) s -> p c s", p=128),
                                                       in_=Ybf.rearrange("p (c s) -> p c s", s=S)), rd=[], wr=[])
            P.barrier(dma_keys=("dbg",))
        for h in range(4):
            WS.push(w_in_d[l, C_CG + h])
        for j in range(8):
            WS.push(w_br_d[l, j], 11)
            for i in range(3):
                WS.push(w_in_d[l, C_GATE + 8 * i + j])
        for oc in range(8):
            WS.push(w_o_d[l, oc])

        def nbank():
            b = rot["ps"] % 8
            rot["ps"] += 1
            return b

        for h in range(4):
            slot = WS.pop()
            ch = 7 + h
            for tb in range(NB):
                bank = nbank()
                proj_fm(slot, tb, bank)
                j2 = tb % 2
                act(ub(j2), PS[bank][:, :], AF.Silu, rd=[r_ps[bank]], wr=[r_ub[j2]])
                ysl = y_(ch)[:, tb * TB:(tb + 1) * TB]
                tt("pool", ysl, ysl, ub(j2), ALU.mult, rd=[r_ub[j2], r_y[ch][tb], r_yall[ch]], wr=[r_y[ch][tb]])
            WS.release(slot)
        ybr = ((0, 3), (3, 7), (7, 11))
        cnt4 = 0
        for j in range(8):
            bslot = WS.pop()
            wb = wslot(bslot, 11)
            for i in range(3):
                gslot = WS.pop()
                c0, c1 = ybr[i]
                for tb in range(NB):
                    gb = nbank()
                    proj_fm(gslot, tb, gb)
                    sg = 4 + cnt4 % 2
                    tmp = 6 + cnt4 % 2
                    cnt4 += 1
                    act(ft(sg), PS[gb][:, :], AF.Sigmoid, rd=[r_ps[gb]], wr=[r_ft[sg]])
                    bb = nbank()
                    for ch in range(c0, c1):
                        mm(PS[bb][:, :], wb[:, ch, :], y_(ch)[:, tb * TB:(tb + 1) * TB], ch == c0, ch == c1 - 1,
                           rd=[r_w[bslot], r_y[ch][tb], r_yall[ch]], wr=[r_ps[bb]])
                    if i == 0:
                        tt("dve", ft(tb), PS[bb][:, :], ft(sg), ALU.mult, rd=[r_ps[bb], r_ft[sg]], wr=[r_ft[tb]])
                    else:
                        tt("dve", ft(tmp), PS[bb][:, :], ft(sg), ALU.mult, rd=[r_ps[bb], r_ft[sg]], wr=[r_ft[tmp]])
                        if i == 1:
                            tt("pool", ft(tb), ft(tb), ft(tmp), ALU.add, rd=[r_ft[tb], r_ft[tmp]], wr=[r_ft[tb]])
                        else:
                            tt("pool", m_(j, tb), ft(tb), ft(tmp), ALU.add, rd=[r_ft[tb], r_ft[tmp]],
                               wr=[r_m[j][tb]])
                WS.release(gslot)
            WS.release(bslot)
        P.barrier()
        if stop == '4a':
            for kc in range(KC):
                P.dma('pool', 'dbg', lambda e, kc=kc: e.dma_start(out=out_d[kc * 128:(kc + 1) * 128, :], in_=M[:, kc * S:(kc + 1) * S]), rd=[], wr=[])
            P.barrier(dma_keys=('dbg',))
            stop_state['skip_out'] = True
        ck('4a')
        for kc in range(KC):
            P.dma("sp", "xld", lambda e, kc=kc: e.dma_start(out=XY[:, kc * S:(kc + 1) * S],
                                                           in_=xsp_d[kc * 128:(kc + 1) * 128, :]),
                  rd=[r_xsp], wr=r_x[kc])
        for hg in range(4):
            for c in range(8):
                WS.push(w_m1_d[l, hg * 8 + c])
            for oc in range(8):
                WS.push(w_m2_d[l, hg * 8 + oc])
        for oc in range(8):
            slot = WS.pop()
            w = wslot(slot)
            for tb in range(NB):
                bank = nbank()
                for kc in range(KC):
                    mm(PS[bank][:, :], w[:, kc, :], m_(kc, tb), kc == 0, kc == KC - 1,
                       rd=[r_w[slot], r_m[kc][tb]], wr=[r_ps[bank]])
                tt("dve", x_(oc, tb), x_(oc, tb), PS[bank][:, :], ALU.add, rd=[r_ps[bank], r_x[oc][tb]],
                   wr=[r_x[oc][tb]])
            WS.release(slot)
        ck('4b')
        norm(8)
        P.barrier()
        ck('norm2')
        for hg in range(4):
            for c in range(8):
                slot = WS.pop()
                for tb in range(NB):
                    bank = nbank()
                    proj_fm(slot, tb, bank)
                    j2 = (c * NB + tb) % 2
                    act(ft(j2), PS[bank][:, :], AF.Relu, rd=[r_ps[bank]], wr=[r_ft[j2]])
                    tt("pool", m_(c, tb), ft(j2), ft(j2), ALU.mult, rd=[r_ft[j2]], wr=[r_m[c][tb]])
                WS.release(slot)
            for oc in range(8):
                slot = WS.pop()
                w = wslot(slot)
                for tb in range(NB):
                    bank = nbank()
                    for c in range(8):
                        mm(PS[bank][:, :], w[:, c, :], m_(c, tb), c == 0, c == 7,
                           rd=[r_w[slot], r_m[c][tb]], wr=[r_ps[bank]])
                    tt("dve", x_(oc, tb), x_(oc, tb), PS[bank][:, :], ALU.add, rd=[r_ps[bank], r_x[oc][tb]],
                       wr=[r_x[oc][tb]])
                WS.release(slot)
        P.barrier()
        if dbg and l == 0:
            for kc in range(KC):
                P.dma("sp", "dbg", lambda e, kc=kc: e.dma_start(out=xdbg_d[kc * 128:(kc + 1) * 128, :],
                                                               in_=XY[:, kc * S:(kc + 1) * S]), rd=r_x[kc], wr=[])

    try:
        for l in range(nl):
            layer(l)
    except _Stop:
        pass

    for kc in range(KC if not stop_state.get('skip_out') else 0):
        P.dma("sp", "out", lambda e, kc=kc: e.dma_start(out=out_d[kc * 128:(kc + 1) * 128, :],
                                                       in_=XY[:, kc * S:(kc + 1) * S]), rd=r_x[kc], wr=[])
    final_waits = tuple((k, n) for k, n in P.dcnt.items() if k in ("out", "dbg", "xsp"))
    P.q["sp"].append((final_waits, None, None))

    keys = list(P.ENG) + sorted(P.dcnt.keys())
    sems = {k: es.enter_context(nc.semaphore("s_" + k)) for k in keys}
    stats = {e: len(P.q[e]) for e in P.ENG}

    def run(e, name):
        for waits, fn, inc in P.q[name]:
            for k, n in waits:
                e.wait_ge(sems[k], n)
            if fn is not None:
                fn(e).then_inc(sems[inc[0]], inc[1])

    with nc.Block() as block:
        @block.tensor
        def _(e):
            run(e, "pe")

        @block.scalar
        def _(e):
            run(e, "act")

        @block.vector
        def _(e):
            run(e, "dve")

        @block.gpsimd
        def _(e):
            run(e, "pool")

        @block.sync
        def _(e):
            run(e, "sp")
    es.close()
    return nc, stats


def _rope_tables():
    pos = np.arange(S, dtype=np.float32)

    def tab(rot, theta, hd):
        inv = (1.0 / (np.float32(theta) ** (np.arange(0, rot, 2, dtype=np.float32) / np.float32(rot)))).astype(np.float32)
        ang = (pos[:, None] * inv[None, :]).astype(np.float32)
        c, s = np.cos(ang).astype(np.float32), np.sin(ang).astype(np.float32)
        r2 = rot // 2
        cosF = np.ones((hd, S), np.float32)
        sinF = np.zeros((hd, S), np.float32)
        cosF[0:r2] = c.T
        cosF[r2:2 * r2] = c.T
        sinF[0:r2] = s.T
        sinF[r2:2 * r2] = s.T
        return np.tile(cosF, (128 // hd, 1)), np.tile(sinF, (128 // hd, 1))

    cab, sab = tab(16, 500000.0, 64)
    cc, sc = tab(64, 10000.0, 64)
    return np.ascontiguousarray(np.stack([cab, sab, cc, sc]).astype(np.float32))


def _rot_mat(r2, hd=64):
    R = np.zeros((128, 128), np.float32)
    for b in range(128 // hd):
        for d in range(r2):
            R[b * hd + d + r2, b * hd + d] = -1.0
            R[b * hd + d, b * hd + d + r2] = 1.0
    return R


def _constants():
    ones = np.ones((128, 128), np.float32)
    bones = np.zeros((128, 128), np.float32)
    bones[0:64, 0:64] = 1.0
    bones[64:128, 64:128] = 1.0
    ident = np.eye(128, dtype=np.float32)
    cmats = np.stack([ones, bones, _rot_mat(8), _rot_mat(32), ident]).astype(np.float32)
    k = np.arange(128)[:, None]
    q = np.arange(512)[None, :]
    masks = []
    for rel in range(-1, 5):
        masks.append((np.abs(q - (128 * rel + k)) <= 64).astype(np.float32))
    band = (np.abs(np.arange(128)[None, :] - k) <= 64).astype(np.float32)
    masks.append(np.tile(band, (1, 4)))
    cmasks = np.stack(masks).astype(np.float32)
    m = np.arange(128)[:, None].astype(np.float32)
    n = np.arange(128)[None, :].astype(np.float32)
    relf = np.maximum(n - m, 0.0)
    relb = np.maximum(m - n, 0.0)
    mkf8 = (n >= m).astype(np.float32) * 0.125
    mkb8 = (m >= n).astype(np.float32) * 0.125
    pos1 = np.broadcast_to(n + 1.0, (128, 128))
    posb = np.broadcast_to(128.0 - n, (128, 128))
    tok = np.arange(128, dtype=np.float32)[:, None]
    cf = np.concatenate([relf, relb, mkf8, mkb8, pos1, posb, ones, tok, 127.0 - tok], axis=1).astype(np.float32)
    return cmats, cmasks, np.ascontiguousarray(cf)


def _prep_layers(inp, layers):
    f = lambda a: np.asarray(a, dtype=np.float32)
    nl = len(layers)
    w_in = f(inp["w_in"])[layers].reshape(nl, 8, 128, 57, 128).transpose(0, 3, 2, 1, 4)
    wbr = np.concatenate([f(inp["w_br_a"])[layers], f(inp["w_br_b"])[layers], f(inp["w_br_c"])[layers]], axis=1)
    w_br = wbr.reshape(nl, 11, 128, 8, 128).transpose(0, 3, 2, 1, 4)
    w_o = f(inp["w_o"])[layers].reshape(nl, 8, 128, 8, 128).transpose(0, 3, 2, 1, 4)
    w_m1 = f(inp["w_mlp1"])[layers].reshape(nl, 8, 128, 32, 128).transpose(0, 3, 2, 1, 4)
    w_m2 = f(inp["w_mlp2"])[layers].reshape(nl, 4, 8, 128, 8, 128).transpose(0, 1, 4, 3, 2, 5).reshape(nl, 32, 128, 8, 128)
    pcol = np.zeros((nl, 128, NPC), np.float32)
    arow = np.zeros((nl, 4, 128, 128), np.float32)
    for i, l in enumerate(layers):
        pcol[i, :, 0:8] = f(inp["norm1_g"])[l].reshape(8, 128).T
        pcol[i, :, 8:16] = f(inp["norm2_g"])[l].reshape(8, 128).T
        pcol[i, :, 16] = np.tile(f(inp["a_q_norm_g"])[l], 2)
        pcol[i, :, 17] = np.tile(f(inp["a_k_norm_g"])[l], 2)
        pcol[i, :, 18] = np.tile(f(inp["b_q_norm_g"])[l], 2)
        pcol[i, :, 19] = np.tile(f(inp["b_k_norm_g"])[l], 2)
        pcol[i, :, 20] = f(inp["b_out_norm_g"])[l]
        pcol[i, :, 21] = f(inp["c_out_norm_g"])[l]
        pcol[i, :, 22] = np.tile(f(inp["b_lambda_q1"])[l], 2)
        pcol[i, :, 23] = np.tile(f(inp["b_lambda_k1"])[l], 2)
        pcol[i, :, 24] = np.tile(f(inp["b_lambda_q2"])[l], 2)
        pcol[i, :, 25] = np.tile(f(inp["b_lambda_k2"])[l], 2)
        af, ab = f(inp["c_decay_f"])[l], f(inp["c_decay_b"])[l]
        for p in range(2):
            pcol[i, :, 26 + p] = np.repeat(af[2 * p:2 * p + 2], 64)
            pcol[i, :, 28 + p] = np.repeat(ab[2 * p:2 * p + 2], 64)
            arow[i, p] = np.repeat(af[2 * p:2 * p + 2], 64)[None, :]
            arow[i, 2 + p] = np.repeat(ab[2 * p:2 * p + 2], 64)[None, :]
        pcol[i, :, 30:34] = af[None, :]
        pcol[i, :, 34:38] = ab[None, :]
    c = np.ascontiguousarray
    return dict(w_in=c(w_in), w_br=c(w_br), w_o=c(w_o), w_m1=c(w_m1), w_m2=c(w_m2), pcol=pcol, arow=arow)


_CACHE = {}


def _get_prog(nl, l0):
    if (nl, l0) not in _CACHE:
        _CACHE[(nl, l0)] = build(nl, l0=l0)[0]
    return _CACHE[(nl, l0)]


FUSED = True


def kernel(**inputs):
    x = np.asarray(inputs["x"], dtype=np.float32)
    B = x.shape[0]
    cmats, cmasks, cf = _constants()
    rope = _rope_tables()
    consts = dict(c_mats=cmats, c_masks=cmasks, c_f32=cf, rope=rope)
    xT = [np.ascontiguousarray(x[b].T) for b in range(B)]
    groups = [list(range(DEPTH))] if FUSED else [[l] for l in range(DEPTH)]
    for layers in groups:
        nc = _get_prog(len(layers), layers[0])
        shared = _prep_layers(inputs, layers)
        shared.update(consts)
        in_maps = []
        for b in range(B):
            d = dict(shared)
            d["xT"] = xT[b]
            in_maps.append(d)
        res = run_bass_kernel_spmd(nc, in_maps, core_ids=list(range(B)))
        xT = [np.ascontiguousarray(np.asarray(r["outT"], dtype=np.float32)) for r in res.results]
    out = np.stack([t.T for t in xT], axis=0)
    return np.ascontiguousarray(out.astype(np.float32))
```

```python
import math
from contextlib import ExitStack
import numpy as np
import ml_dtypes
import concourse.bass as bass
import concourse.mybir as mybir
from concourse.bass_utils import run_bass_kernel_spmd

F32, BF16 = mybir.dt.float32, mybir.dt.bfloat16
ALU, AF = mybir.AluOpType, mybir.ActivationFunctionType

S = 2048
D = 1024
KC = 8
NB = 4
TB = 512
DEPTH = 4
EPS = 1e-6
NPC = 38
C_AQ, C_AK, C_AV = 0, 3, 6
C_BQ, C_BK, C_BV = 9, 13, 17
C_CQ, C_CK, C_CV, C_CG = 21, 23, 25, 29
C_GATE = 33
NWSLOT = 5


class Res:
    __slots__ = ("w", "r", "x")

    def __init__(self, x=False):
        self.w = None
        self.r = {}
        self.x = x


class Prog:
    ENG = ("pe", "act", "dve", "pool", "sp")

    def __init__(self):
        self.q = {e: [] for e in self.ENG}
        self.cnt = {e: 0 for e in self.ENG}
        self.seen = {e: {} for e in self.ENG}
        self.dcnt = {}

    def _need(self, eng, dep, waits, raw):
        if dep is None:
            return
        k, n = dep
        if k == eng:
            if eng == "pe" or not raw:
                return
            if n <= self.cnt[eng] - 2:
                return
        if k in self.dcnt:
            n = max(n, self.dcnt[k])
        if self.seen[eng].get(k, 0) >= n:
            return
        waits[k] = max(waits.get(k, 0), n)

    def _deps(self, eng, rd, wr):
        waits = {}
        for r in rd:
            self._need(eng, r.w, waits, True)
        for w in wr:
            self._need(eng, w.w, waits, False)
            for k, n in w.r.items():
                self._need(eng, (k, n), waits, False)
        for k, n in waits.items():
            self.seen[eng][k] = n
        return tuple(waits.items())

    def op(self, eng, fn, rd=(), wr=()):
        wr = list(wr) + [r for r in rd if r.x]
        rd = [r for r in rd if not r.x]
        waits = self._deps(eng, rd, wr)
        self.cnt[eng] += 1
        n = self.cnt[eng]
        self.q[eng].append((waits, fn, (eng, 1)))
        for r in rd:
            r.r[eng] = max(r.r.get(eng, 0), n)
        for w in wr:
            w.w = (eng, n)
            w.r = {}

    def dma(self, qeng, semkey, fn, rd=(), wr=()):
        waits = self._deps(qeng, rd, wr)
        self.dcnt[semkey] = self.dcnt.get(semkey, 0) + 16
        n = self.dcnt[semkey]
        self.q[qeng].append((waits, fn, (semkey, 16)))
        for r in rd:
            r.r[semkey] = max(r.r.get(semkey, 0), n)
        for w in wr:
            w.w = (semkey, n)
            w.r = {}

    def barrier(self, dma_keys=()):
        for e in self.ENG:
            waits = {}
            for f in self.ENG:
                if f != e and self.cnt[f] > self.seen[e].get(f, 0):
                    waits[f] = self.cnt[f]
            for k in dma_keys:
                if self.dcnt.get(k, 0) > self.seen[e].get(k, 0):
                    waits[k] = self.dcnt[k]
            for k, n in waits.items():
                self.seen[e][k] = n
            if waits:
                self.q[e].append((tuple(waits.items()), None, None))


class _Stop(Exception):
    pass


def build(nl, dbg=False, stop=None, l0=0):
    nc = bass.Bass("TRN2", target_bir_lowering=False)

    def din(name, shape, dt=F32):
        return nc.dram_tensor(name, list(shape), dt, kind="ExternalInput").ap()

    xT_d = din("xT", [D, S])
    w_in_d = din("w_in", [nl, 57, 128, 8, 128])
    w_br_d = din("w_br", [nl, 8, 128, 11, 128])
    w_o_d = din("w_o", [nl, 8, 128, 8, 128])
    w_m1_d = din("w_m1", [nl, 32, 128, 8, 128])
    w_m2_d = din("w_m2", [nl, 32, 128, 8, 128])
    pcol_d = din("pcol", [nl, 128, NPC])
    arow_d = din("arow", [nl, 4, 128, 128])
    cmat_d = din("c_mats", [5, 128, 128])
    cmask_d = din("c_masks", [7, 128, 512])
    cf32_d = din("c_f32", [128, 7 * 128 + 2])
    rope_d = din("rope", [4, 128, S])
    out_d = nc.dram_tensor("outT", [D, S], F32, kind="ExternalOutput").ap()
    xsp_d = nc.dram_tensor("xspill", [D, S], F32, kind="Internal").ap()
    if dbg:
        ydbg_d = nc.dram_tensor("ydbg", [11 * 128, S], BF16, kind="ExternalOutput").ap()
        xdbg_d = nc.dram_tensor("xdbg", [D, S], F32, kind="ExternalOutput").ap()

    P = Prog()
    es = ExitStack()

    def sb(name, shape, dt):
        return es.enter_context(nc.sbuf_tensor(name, list(shape), dt))

    XN = sb("XN", [128, KC * S], BF16)
    XY = sb("XY", [128, KC * S], F32)
    M = sb("M", [128, KC * S], BF16)
    WR = sb("WR", [128, NWSLOT * 11 * 128], BF16)
    CMB = sb("CMB", [128, 5 * 128], BF16)
    MASKS = sb("MASKS", [128, 7 * 512], BF16)
    CF = sb("CF", [128, 7 * 128 + 2], F32)
    PCOL = sb("PCOL", [128, NPC], F32)
    SMALL = sb("SMALL", [128, 64], F32)
    CTAB = sb("CTAB", [128, 13 * 128], F32)
    CDER = sb("CDER", [128, 6 * 2048], BF16)
    S32 = sb("S32", [128, 2 * 128], F32)
    SQB = sb("SQB", [128, 2 * 512], BF16)
    UB = sb("UB", [128, 2 * 512], BF16)
    FT = sb("FT", [128, 9 * 512], F32)

    Ybf = XY[:, 0:11 * 1024].bitcast(BF16)
    ROPE = XY[:, 11 * 1024:11 * 1024 + 4096]
    PB = XY[:, 15 * 1024:16 * 1024].bitcast(BF16)
    DTOT = CDER[:, 4 * 2048:6 * 2048].bitcast(F32)

    def x_(kc, tb):
        return XY[:, kc * S + tb * TB: kc * S + (tb + 1) * TB]

    def xn_(kc, tb):
        return XN[:, kc * S + tb * TB: kc * S + (tb + 1) * TB]

    def xnfull(kc):
        return XN[:, kc * S:(kc + 1) * S]

    def y_(ch):
        return Ybf[:, ch * S:(ch + 1) * S]

    def m_(kc, tb):
        return M[:, kc * S + tb * TB: kc * S + (tb + 1) * TB]

    def uq(s):
        return M[:, s * 8192: s * 8192 + 2048]

    def uk(s):
        return M[:, s * 8192 + 2048: s * 8192 + 4096]

    def uv(s):
        return M[:, s * 8192 + 4096: s * 8192 + 8192].rearrange("p (h t c) -> p h t c", h=2, c=128)

    def kz(s, m):
        return uk(s) if m == 0 else M[:, s * 8192 + 6144: s * 8192 + 8192]

    def wslot(s, nk=8):
        return WR[:, s * 1408: s * 1408 + nk * 128].rearrange("p (k n) -> p k n", n=128)

    ONES = CMB[:, 0:128]
    BONES = CMB[:, 128:256]
    RAB = CMB[:, 256:384]
    RC = CMB[:, 384:512]
    IDENT = CMB[:, 512:640]

    def mask_(i):
        return MASKS[:, i * 512:(i + 1) * 512]

    RELF, RELB = CF[:, 0:128], CF[:, 128:256]
    MKF8, MKB8 = CF[:, 256:384], CF[:, 384:512]
    POS1, POSB = CF[:, 512:640], CF[:, 640:768]
    ONESF = CF[:, 768:896]
    TOKC, TOKCR = CF[:, 896:897], CF[:, 897:898]

    def ctab(i):
        return CTAB[:, i * 128:(i + 1) * 128]

    def cder(i):
        return CDER[:, i * 2048:(i + 1) * 2048]

    KDF, KDB = cder(0).rearrange("p (t c) -> p t c", c=128), cder(1).rearrange("p (t c) -> p t c", c=128)
    QDF, QDB = cder(2), cder(3)
    STF, STB = cder(4).rearrange("p (t c) -> p t c", c=128), cder(5).rearrange("p (t c) -> p t c", c=128)

    def pb(i):
        return PB[:, i * 512:(i + 1) * 512]

    def sqb(i):
        return SQB[:, i * 512:(i + 1) * 512]

    def ub(i):
        return UB[:, i * 512:(i + 1) * 512]

    def ft(i):
        return FT[:, i * 512:(i + 1) * 512]

    PS = [es.enter_context(nc.psum_tensor("ps%d" % i, [128, 512], F32)) for i in range(8)]

    r_x = [[Res() for _ in range(NB)] for _ in range(KC)]
    r_xn = [Res() for _ in range(NB)]
    r_y = [[Res() for _ in range(NB)] for _ in range(11)]
    r_yall = [Res() for _ in range(11)]
    r_m = [[Res() for _ in range(NB)] for _ in range(KC)]
    r_uq = [Res(), Res()]
    r_uk = [Res(), Res()]
    r_uv = [Res(), Res()]
    r_w = [Res() for _ in range(NWSLOT)]
    r_ps = [Res(True) for _ in range(8)]
    r_pb = [Res() for _ in range(4)]
    r_sqb = [Res(), Res()]
    r_ub = [Res(), Res()]
    r_ft = [Res() for _ in range(9)]
    r_rope = Res()
    r_acc = Res()
    r_const = Res()
    r_pcol = Res()
    r_small = Res()
    r_ctab = Res()
    r_cder = [Res() for _ in range(6)]
    r_s32 = [Res(), Res()]
    r_xsp = Res()

    wstate = {"n": 0}

    def load_w(dram_ap, nk, s):
        dst = wslot(s, nk)
        P.dma("pool", "w%d" % s, lambda e, dst=dst, src=dram_ap: e.dma_start(out=dst, in_=src), wr=[r_w[s]])
        return s

    class WStream:
        def __init__(self):
            self.pending = []
            self.loaded = []
            self.free = list(range(NWSLOT))

        def push(self, ap, nk=8):
            self.pending.append((ap, nk))

        def prefetch(self):
            while self.pending and self.free:
                ap, nk = self.pending.pop(0)
                self.loaded.append(load_w(ap, nk, self.free.pop(0)))

        def pop(self):
            self.prefetch()
            return self.loaded.pop(0)

        def release(self, s):
            self.free.append(s)
            self.prefetch()

    WS = WStream()

    def mm(out, lhsT, rhs, start, stop, rd, wr):
        P.op("pe", lambda e: e.matmul(out, lhsT=lhsT, rhs=rhs, start=start, stop=stop), rd=rd, wr=wr)

    def act(out, in_, func, rd, wr, scale=None, bias=None):
        kw = {}
        if scale is not None:
            kw["scale"] = scale
        if bias is not None:
            kw["bias"] = bias
        P.op("act", lambda e: e.activation(out=out, in_=in_, func=func, **kw), rd=rd, wr=wr)

    def tt(eng, out, in0, in1, op, rd, wr):
        P.op(eng, lambda e: e.tensor_tensor(out=out, in0=in0, in1=in1, op=op), rd=rd, wr=wr)

    def stt(eng, out, in0, scalar, in1, op0, op1, rd, wr):
        P.op(eng, lambda e: e.scalar_tensor_tensor(out=out, in0=in0, scalar=scalar, in1=in1, op0=op0, op1=op1),
             rd=rd, wr=wr)

    def ts(eng, out, in0, s1, s2, op0, op1, rd, wr):
        if op1 is None:
            P.op(eng, lambda e: e.tensor_scalar(out=out, in0=in0, scalar1=s1, scalar2=None, op0=op0), rd=rd, wr=wr)
        else:
            P.op(eng, lambda e: e.tensor_scalar(out=out, in0=in0, scalar1=s1, scalar2=s2, op0=op0, op1=op1),
                 rd=rd, wr=wr)

    def cp(eng, out, in_, rd, wr):
        if eng == "act":
            act(out, in_, AF.Copy, rd, wr)
        else:
            P.op(eng, lambda e: e.tensor_copy(out=out, in_=in_), rd=rd, wr=wr)

    def recip(out, in_, rd, wr):
        P.op("dve", lambda e: e.reciprocal(out=out, in_=in_), rd=rd, wr=wr)

    def rstd_from(ps_ap, r_in, ftile, r_ftile, inv_n):
        act(ftile, ps_ap, AF.Ln, rd=[r_in], wr=[r_ftile], scale=inv_n, bias=EPS)
        act(ftile, ftile, AF.Exp, rd=[r_ftile], wr=[r_ftile], scale=-0.5)

    rot = {"ps": 0, "alt": 0}
    SBANK = (4, 5, 3)

    P.dma("pool", "cst", lambda e: e.dma_start(out=CMB[:, :].rearrange("p (c n) -> p c n", n=128),
                                                  in_=cmat_d.rearrange("c p n -> p c n")), wr=[r_const])
    P.dma("pool", "cst", lambda e: e.dma_start(out=MASKS[:, :].rearrange("p (c n) -> p c n", n=512),
                                                  in_=cmask_d.rearrange("c p n -> p c n")), wr=[r_const])
    P.dma("sp", "cstf", lambda e: e.dma_start(out=CF[:, :], in_=cf32_d), wr=[r_const])
    P.barrier(dma_keys=("cst", "cstf"))
    for kc in range(KC):
        P.dma("sp", "xld", lambda e, kc=kc: e.dma_start(out=XY[:, kc * S:(kc + 1) * S],
                                                       in_=xT_d[kc * 128:(kc + 1) * 128, :]),
              wr=r_x[kc])

    def norm(gcol0):
        for tb in range(NB):
            for kc in range(KC):
                i = kc % 2
                act(sqb(i), x_(kc, tb), AF.Square, rd=[r_x[kc][tb]], wr=[r_sqb[i]])
                mm(PS[2][:, :], ONES, sqb(i), kc == 0, kc == KC - 1, rd=[r_sqb[i], r_const], wr=[r_ps[2]])
            rstd_from(PS[2][:, :], r_ps[2], ft(0), r_ft[0], 1.0 / D)
            for kc in range(KC):
                stt("dve", xn_(kc, tb), x_(kc, tb), PCOL[:, gcol0 + kc: gcol0 + kc + 1], ft(0), ALU.mult, ALU.mult,
                    rd=[r_x[kc][tb], r_ft[0], r_pcol], wr=[r_xn[tb]])

    def proj_fm(slot, tb, bank):
        w = wslot(slot)
        for kc in range(KC):
            mm(PS[bank][:, :], w[:, kc, :], xn_(kc, tb), kc == 0, kc == KC - 1,
               rd=[r_w[slot], r_xn[tb]], wr=[r_ps[bank]])

    def perm_dst(buf, tb, delta):
        if delta == 1:
            return buf[:, tb * TB:(tb + 1) * TB], None
        n0 = tb * TB // delta
        return buf.rearrange("p (r n) -> p n r", r=delta)[:, n0:n0 + TB // delta, :], delta

    def srcv(ap, delta):
        if delta is None:
            return ap
        return ap.rearrange("p (n r) -> p n r", r=delta)

    def qk_post(bank, tb, gcol, rmat, normed, dsts, delta, feng="pool", aux=2):
        j = rot["alt"] % 2
        rot["alt"] += 1
        pj = PS[bank][:, :]
        cos = ROPE[:, tb * TB:(tb + 1) * TB]
        sin = ROPE[:, S + tb * TB: S + (tb + 1) * TB]
        if normed:
            act(sqb(j), pj, AF.Square, rd=[r_ps[bank]], wr=[r_sqb[j]])
            mm(PS[aux][:, :], BONES, sqb(j), True, True, rd=[r_sqb[j], r_const], wr=[r_ps[aux]])
            yield
            rstd_from(PS[aux][:, :], r_ps[aux], ft(j), r_ft[j], 1.0 / 64)
            stt("dve", ub(j), pj, PCOL[:, gcol:gcol + 1], ft(j), ALU.mult, ALU.mult,
                rd=[r_ps[bank], r_ft[j], r_pcol], wr=[r_ub[j]])
            yield
            mm(PS[aux][:, :], rmat, ub(j), True, True, rd=[r_ub[j], r_const], wr=[r_ps[aux]])
            tt(feng, ft(2 + j), ub(j), cos, ALU.mult, rd=[r_ub[j], r_rope], wr=[r_ft[2 + j]])
        else:
            cp("act", ub(j), pj, rd=[r_ps[bank]], wr=[r_ub[j]])
            tt("dve", ft(2 + j), pj, cos, ALU.mult, rd=[r_ps[bank], r_rope], wr=[r_ft[2 + j]])
            yield
            mm(PS[aux][:, :], rmat, ub(j), True, True, rd=[r_ub[j], r_const], wr=[r_ps[aux]])
        yield
        tt("dve", ft(4 + j), PS[aux][:, :], sin, ALU.mult, rd=[r_ps[aux], r_rope], wr=[r_ft[4 + j]])
        for buf, rows, res in dsts:
            dst, dl = perm_dst(buf, tb, delta)
            a0, a1 = srcv(ft(2 + j), dl), srcv(ft(4 + j), dl)
            if rows is not None:
                dst, a0, a1 = dst[rows[0]:rows[1]], a0[rows[0]:rows[1]], a1[rows[0]:rows[1]]
            tt(feng, dst, a0, a1, ALU.add, rd=[r_ft[2 + j], r_ft[4 + j]], wr=[res])

    def v_proj(slot, set_, coloff, delta):
        w = wslot(slot)
        V = uv(set_)
        for g4 in range(4):
            bank = 0
            for ti in range(4):
                t = g4 * 4 + ti
                for kc in range(KC):
                    if delta == 1:
                        lhs = xnfull(kc)[:, t * 128:(t + 1) * 128]
                    elif delta == 4:
                        lhs = xnfull(kc).rearrange("p (n r) -> p r n", r=4)[:, t // 4, (t % 4) * 128:(t % 4 + 1) * 128]
                    else:
                        lhs = xnfull(kc).rearrange("p (n r) -> p r n", r=16)[:, t, :]
                    mm(PS[bank][:, ti * 128:(ti + 1) * 128], lhs, w[:, kc, :], kc == 0, kc == KC - 1,
                       rd=[r_w[slot]] + r_xn, wr=[r_ps[bank]])
            cp("act", V[:, coloff // 128, g4 * 4:(g4 + 1) * 4, :],
               PS[bank][:, :].rearrange("p (t c) -> p t c", c=128), rd=[r_ps[bank]], wr=[r_uv[set_]])
            yield

    units = []
    for g in range(3):
        units.append(dict(kind="A", idx=g, q=C_AQ + g, k=C_AK + g, v=[C_AV + g], gq=16, gk=17, delta=(1, 4, 16)[g]))
    for h in range(4):
        units.append(dict(kind="B", idx=h, q=C_BQ + h, k=C_BK + h, v=[C_BV + h], gq=18, gk=19, delta=1))
    for p in range(2):
        units.append(dict(kind="C", idx=p, q=C_CQ + p, k=C_CK + p, v=[C_CV + 2 * p, C_CV + 2 * p + 1], gq=0, gk=0,
                          delta=1))

    def push_unit_weights(l, u):
        WS.push(w_in_d[l, u["q"]])
        WS.push(w_in_d[l, u["k"]])
        for c in u["v"]:
            WS.push(w_in_d[l, c])

    def rope_load(which):
        P.dma("sp", "rope", lambda e: e.dma_start(out=ROPE[:, 0:S], in_=rope_d[2 * which]), wr=[r_rope])
        P.dma("sp", "rope", lambda e: e.dma_start(out=ROPE[:, S:2 * S], in_=rope_d[2 * which + 1]), wr=[r_rope])

    def inproj(l, ui):
        u = units[ui]
        set_ = ui % 2
        normed = u["kind"] != "C"
        rmat = RC if u["kind"] == "C" else RAB
        if ui == 0:
            rope_load(0)
        if ui == 7:
            rope_load(1)
        feng = "dve" if ui == 0 else "pool"
        width = 2 if ui <= 3 else 1
        qslot = WS.pop()
        kslot = WS.pop()
        gens = []

        def cb_gen(slot, tb, gcol, dsts, k):
            bank = k % width
            proj_fm(slot, tb, bank)
            yield
            yield from qk_post(bank, tb, gcol, rmat, normed, dsts, u["delta"], feng, 2 + k % width)

        k = 0
        for which in ("q", "k"):
            if which == "q":
                dsts, gcol, slot = [(uq(set_), None, r_uq[set_])], u["gq"], qslot
            elif u["kind"] == "C":
                dsts, gcol, slot = [(uk(set_), None, r_uk[set_])], u["gk"], kslot
            else:
                k0, k1 = kz(set_, 0), kz(set_, 1)
                P.op("pool", lambda e, k0=k0: e.memset(k0[64:128, :], 0.0), wr=[r_uk[set_]])
                P.op("pool", lambda e, k1=k1: e.memset(k1[0:64, :], 0.0), wr=[r_uv[set_]])
                dsts, gcol, slot = [(k0, (0, 64), r_uk[set_]), (k1, (64, 128), r_uv[set_])], u["gk"], kslot
            for tb in range(NB):
                gens.append(cb_gen(slot, tb, gcol, dsts, k))
                k += 1
        active = []
        while gens or active:
            while gens and len(active) < width:
                active.append(gens.pop(0))
            for g_ in list(active):
                try:
                    next(g_)
                except StopIteration:
                    active.remove(g_)
            yield
        WS.release(qslot)
        WS.release(kslot)
        for vi, c in enumerate(u["v"]):
            slot = WS.pop()
            yield from v_proj(slot, set_, vi * 128, u["delta"])
            WS.release(slot)

    def attn_B(l, ui):
        u = units[ui]
        h = u["idx"]
        set_ = ui % 2
        qT, V = uq(set_), uv(set_)
        rdq = [r_uq[set_], r_uk[set_], r_uv[set_]]
        ch = 3 + h
        pending = []

        def make_epilogue(qb):
            def s1():
                act(sqb(0), ft(7), AF.Square, rd=[r_ft[7]], wr=[r_sqb[0]])

            def s2():
                mm(PS[1][:, :], ONES, sqb(0), True, True, rd=[r_sqb[0], r_const], wr=[r_ps[1]])

            def s3():
                rstd_from(PS[1][:, :], r_ps[1], ft(6), r_ft[6], 1.0 / 128)
                stt("dve", y_(ch)[:, qb * TB:(qb + 1) * TB], ft(7), SMALL[:, 1:2], ft(6), ALU.mult, ALU.mult,
                    rd=[r_ft[7], r_ft[6], r_small], wr=[r_y[ch][qb], r_yall[ch]])

            return [(9, s1), (11, s2), (13, s3)]

        for qb in range(NB):
            for m in range(2):
                def smm(kt):
                    bank = SBANK[kt % 3]
                    mm(PS[bank][:, :], kz(set_, m)[:, kt * 128:(kt + 1) * 128], qT[:, qb * TB:(qb + 1) * TB], True, True,
                       rd=rdq, wr=[r_ps[bank]])

                smm(0)
                smm(1)
                for kt in range(16):
                    if kt + 2 < 16:
                        smm(kt + 2)
                    bank = SBANK[kt % 3]
                    pi = kt % 4
                    act(pb(pi), PS[bank][:, :], AF.Exp, rd=[r_ps[bank]], wr=[r_pb[pi]], scale=0.125)
                    mm(PS[6][:, :], V[:, 0, kt, :], pb(pi), kt == 0, kt == 15, rd=[r_pb[pi], r_uv[set_]],
                       wr=[r_ps[6]])
                    mm(PS[7][:, :], ONES, pb(pi), kt == 0, kt == 15, rd=[r_pb[pi], r_const], wr=[r_ps[7]])
                    while pending and pending[0][0] <= kt:
                        pending.pop(0)[1]()
                    if kt % 4 == 3 and kt < 15:
                        yield
                cp("dve", ft(6), PS[7][:, :], rd=[r_ps[7]], wr=[r_ft[6]])
                if m == 0:
                    cp("dve", ft(7), PS[6][:, :], rd=[r_ps[6]], wr=[r_ft[7]])
                    recip(ft(6), ft(6), rd=[r_ft[6]], wr=[r_ft[6]])
                    tt("dve", ft(7), ft(7), ft(6), ALU.mult, rd=[r_ft[7], r_ft[6]], wr=[r_ft[7]])
                else:
                    cp("dve", ft(8), PS[6][:, :], rd=[r_ps[6]], wr=[r_ft[8]])
                    recip(ft(6), ft(6), rd=[r_ft[6]], wr=[r_ft[6]])
                    tt("dve", ft(8), ft(8), ft(6), ALU.mult, rd=[r_ft[8], r_ft[6]], wr=[r_ft[8]])
                    stt("dve", ft(7), ft(8), SMALL[:, 0:1], ft(7), ALU.mult, ALU.add,
                        rd=[r_ft[8], r_ft[7], r_small], wr=[r_ft[7]])
                    pending = make_epilogue(qb)
                yield
        for _, fn in pending:
            fn()
        yield

    def attn_A(l, ui):
        u = units[ui]
        g = u["idx"]
        delta = u["delta"]
        set_ = ui % 2
        qT, kT, V = uq(set_), uk(set_), uv(set_)
        rdq = [r_uq[set_], r_uk[set_], r_uv[set_]]
        for s2 in range(2):
            rows = slice(64 * s2, 64 * s2 + 64)
            for qb in range(NB):
                if g < 2:
                    rels = list(range(-1, 5)) if g == 0 else list(range(4))
                    kts = [(4 * qb + r, r) for r in rels if 0 <= 4 * qb + r < 16]
                    n = len(kts)

                    def win(i):
                        rel = kts[i][1]
                        if i == 0:
                            return 0, TB
                        return max(0, 128 * rel - 64), min(TB, 128 * rel + 192)

                    def smm(i):
                        kt, rel = kts[i]
                        bank = 4 + i % 2
                        lo, hi = win(i)
                        mm(PS[bank][:, lo:hi], kz(set_, s2)[:, kt * 128:(kt + 1) * 128],
                           qT[:, qb * TB + lo:qb * TB + hi], True, True, rd=rdq, wr=[r_ps[bank]])

                    smm(0)
                    for i in range(n):
                        if i + 1 < n:
                            smm(i + 1)
                        kt, rel = kts[i]
                        bank = 4 + i % 2
                        pi = i % 4
                        lo, hi = win(i)
                        act(pb(pi)[:, lo:hi], PS[bank][:, lo:hi], AF.Exp, rd=[r_ps[bank]], wr=[r_pb[pi]], scale=0.125)
                        tt("dve", pb(pi)[:, lo:hi], pb(pi)[:, lo:hi], mask_(rel + 1)[:, lo:hi], ALU.mult,
                           rd=[r_pb[pi], r_const], wr=[r_pb[pi]])
                        mm(PS[6][:, lo:hi], V[:, 0, kt, :], pb(pi)[:, lo:hi], i == 0, i == n - 1,
                           rd=[r_pb[pi], r_uv[set_]], wr=[r_ps[6]])
                        mm(PS[7][:, lo:hi], ONES, pb(pi)[:, lo:hi], i == 0, i == n - 1, rd=[r_pb[pi], r_const],
                           wr=[r_ps[7]])
                        if i < n - 1:
                            yield
                else:
                    bank = 4 + qb % 2
                    for rel in range(4):
                        kt = 4 * qb + rel
                        mm(PS[bank][:, rel * 128:(rel + 1) * 128], kz(set_, s2)[:, kt * 128:(kt + 1) * 128],
                           qT[:, kt * 128:(kt + 1) * 128], True, True, rd=rdq, wr=[r_ps[bank]])
                    pi = qb % 3
                    act(pb(pi), PS[bank][:, :], AF.Exp, rd=[r_ps[bank]], wr=[r_pb[pi]], scale=0.125)
                    tt("dve", pb(pi), pb(pi), mask_(6), ALU.mult, rd=[r_pb[pi], r_const], wr=[r_pb[pi]])
                    for rel in range(4):
                        kt = 4 * qb + rel
                        mm(PS[6][:, rel * 128:(rel + 1) * 128], V[:, 0, kt, :], pb(pi)[:, rel * 128:(rel + 1) * 128],
                           True, True, rd=[r_pb[pi], r_uv[set_]], wr=[r_ps[6]])
                        mm(PS[7][:, rel * 128:(rel + 1) * 128], ONES, pb(pi)[:, rel * 128:(rel + 1) * 128],
                           True, True, rd=[r_pb[pi], r_const], wr=[r_ps[7]])
                if delta == 1:
                    ydst = y_(g)[rows, qb * TB:(qb + 1) * TB]
                    ddst = DTOT[rows, qb * TB:(qb + 1) * TB]
                    nsrc, dsrc = PS[6][rows, :], PS[7][rows, :]
                elif delta == 4:
                    ydst = y_(g).rearrange("p (n r) -> p r n", r=4)[rows, qb, :]
                    ddst = DTOT[:, :].rearrange("p (n r) -> p r n", r=4)[rows, qb, :]
                    nsrc, dsrc = PS[6][rows, :], PS[7][rows, :]
                else:
                    ydst = y_(g).rearrange("p (n r) -> p r n", r=16)[rows, 4 * qb:4 * qb + 4, :]
                    ddst = DTOT[:, :].rearrange("p (n r) -> p r n", r=16)[rows, 4 * qb:4 * qb + 4, :]
                    nsrc = PS[6][rows, :].rearrange("p (a b) -> p a b", a=4)
                    dsrc = PS[7][rows, :].rearrange("p (a b) -> p a b", a=4)
                cp("act", ydst, nsrc, rd=[r_ps[6]], wr=[r_yall[g]] + r_y[g])
                if g == 0:
                    cp("dve", ddst, dsrc, rd=[r_ps[7]], wr=[r_cder[4], r_cder[5]])
                else:
                    tt("dve", ddst, ddst, dsrc, ALU.add, rd=[r_ps[7], r_cder[4], r_cder[5]], wr=[r_cder[4], r_cder[5]])
                yield
        if g == 2:
            for tb in range(NB):
                recip(ft(6), DTOT[:, tb * TB:(tb + 1) * TB], rd=[r_cder[4], r_cder[5]], wr=[r_ft[6]])
                for gg in range(3):
                    ysl = y_(gg)[:, tb * TB:(tb + 1) * TB]
                    tt("pool", ysl, ysl, ft(6), ALU.mult, rd=[r_ft[6], r_yall[gg]] + r_y[gg], wr=[r_y[gg][tb], r_yall[gg]])
                yield

    def attn_C(l, ui):
        u = units[ui]
        p = u["idx"]
        set_ = ui % 2
        qT, kT, V = uq(set_), uk(set_), uv(set_)
        DQF, DQB, DKF, DKB = ctab(4 + p), ctab(6 + p), ctab(8 + p), ctab(10 + p)
        g128f, g128b = SMALL[:, 8 + p:9 + p], SMALL[:, 10 + p:11 + p]
        for g4 in range(4):
            bank = 3
            tp = PS[bank][:, :].bitcast(BF16)[:, 0:512]
            for ti in range(4):
                t = g4 * 4 + ti
                P.op("pe", lambda e, t=t, ti=ti, tp=tp: e.transpose(tp[:, ti * 128:(ti + 1) * 128],
                                                                      kT[:, t * 128:(t + 1) * 128], IDENT),
                     rd=[r_uk[set_], r_const], wr=[r_ps[bank]])
            tp3 = tp.rearrange("p (t c) -> p t c", c=128)
            tt("dve", KDF[:, g4 * 4:(g4 + 1) * 4, :], tp3, DKF.unsqueeze(1).to_broadcast([128, 4, 128]), ALU.mult,
               rd=[r_ps[bank], r_ctab], wr=[r_cder[0]])
            tt("dve", KDB[:, g4 * 4:(g4 + 1) * 4, :], tp3, DKB.unsqueeze(1).to_broadcast([128, 4, 128]), ALU.mult,
               rd=[r_ps[bank], r_ctab], wr=[r_cder[1]])
            q3 = qT[:, g4 * TB:(g4 + 1) * TB].rearrange("p (t c) -> p t c", c=128)
            tt("pool", QDF[:, g4 * TB:(g4 + 1) * TB].rearrange("p (t c) -> p t c", c=128), q3,
               DQF.unsqueeze(1).to_broadcast([128, 4, 128]), ALU.mult, rd=[r_uq[set_], r_ctab], wr=[r_cder[2]])
            tt("pool", QDB[:, g4 * TB:(g4 + 1) * TB].rearrange("p (t c) -> p t c", c=128), q3,
               DQB.unsqueeze(1).to_broadcast([128, 4, 128]), ALU.mult, rd=[r_uq[set_], r_ctab], wr=[r_cder[3]])
            yield
        P.op("pool", lambda e: e.memset(S32[:, :], 0.0), wr=r_s32)
        P.op("pool", lambda e: e.memset(STF[:, 0, :], 0.0), wr=[r_cder[4]])
        P.op("pool", lambda e: e.memset(STB[:, 15, :], 0.0), wr=[r_cder[5]])
        for step in range(15):
            for d in range(2):
                c = step if d == 0 else 15 - step
                KD = KDF if d == 0 else KDB
                ST = STF if d == 0 else STB
                gcol = g128f if d == 0 else g128b
                s32 = S32[:, d * 128:(d + 1) * 128]
                bank = 4 + d
                for hh in range(2):
                    mm(PS[bank][:, hh * 128:(hh + 1) * 128], KD[:, c, :], V[:, hh, c, :], True, True,
                       rd=[r_cder[d], r_uv[set_]], wr=[r_ps[bank]])
                for hh in range(2):
                    rws = slice(64 * hh, 64 * hh + 64)
                    stt("dve", s32[rws, :], s32[rws, :], gcol[rws, :], PS[bank][rws, hh * 128:(hh + 1) * 128],
                        ALU.mult, ALU.add, rd=[r_ps[bank], r_s32[d], r_small], wr=[r_s32[d]])
                cn = c + 1 if d == 0 else c - 1
                cp("act", ST[:, cn, :], s32, rd=[r_s32[d]], wr=[r_cder[4 + d]])
            if step % 4 == 3:
                yield
        yield
        pend = [None]
        for hh in range(2):
            rws = slice(64 * hh, 64 * hh + 64)
            h = 2 * p + hh
            ch = 7 + h
            for cb in range(4):
                sbank = 4 + cb % 2
                for ci in range(4):
                    c = 4 * cb + ci
                    mm(PS[sbank][:, ci * 128:(ci + 1) * 128], kT[rws, c * 128:(c + 1) * 128],
                       qT[rws, c * 128:(c + 1) * 128], True, True, rd=[r_uq[set_], r_uk[set_]], wr=[r_ps[sbank]])
                pi = cb % 3
                tt("dve", pb(pi).rearrange("p (t c) -> p t c", c=128),
                   PS[sbank][:, :].rearrange("p (t c) -> p t c", c=128),
                   ctab(h).unsqueeze(1).to_broadcast([128, 4, 128]), ALU.mult, rd=[r_ps[sbank], r_ctab],
                   wr=[r_pb[pi]])
                for ci in range(4):
                    c = 4 * cb + ci
                    o = PS[6][:, ci * 128:(ci + 1) * 128]
                    mm(o, V[:, hh, c, :], pb(pi)[:, ci * 128:(ci + 1) * 128], True, False,
                       rd=[r_pb[pi], r_uv[set_]], wr=[r_ps[6]])
                    mm(o, STF[rws, c, :], QDF[rws, c * 128:(c + 1) * 128], False, False,
                       rd=[r_cder[4], r_cder[2]], wr=[r_ps[6]])
                    mm(o, STB[rws, c, :], QDB[rws, c * 128:(c + 1) * 128], False, True,
                       rd=[r_cder[5], r_cder[3]], wr=[r_ps[6]])
                if pend[0] is not None:
                    pend[0]()
                fo = 7 + (hh * 4 + cb) % 2
                cp("dve", ft(fo), PS[6][:, :], rd=[r_ps[6]], wr=[r_ft[fo]])

                def epi(fo=fo, ch=ch, cb=cb):
                    act(pb(3), ft(fo), AF.Square, rd=[r_ft[fo]], wr=[r_pb[3]])
                    mm(PS[1][:, :], ONES, pb(3), True, True, rd=[r_pb[3], r_const], wr=[r_ps[1]])
                    rstd_from(PS[1][:, :], r_ps[1], ft(6), r_ft[6], 1.0 / 128)
                    stt("dve", y_(ch)[:, cb * TB:(cb + 1) * TB], ft(fo), PCOL[:, 21:22], ft(6), ALU.mult, ALU.mult,
                        rd=[r_ft[fo], r_ft[6], r_pcol], wr=[r_y[ch][cb], r_yall[ch]])

                pend[0] = epi
                yield
        if pend[0] is not None:
            pend[0]()
        yield

    def layer_params(l):
        lam_init = 0.8 - 0.6 * math.exp(-0.3 * (l + l0))
        P.dma("sp", "par", lambda e: e.dma_start(out=PCOL[:, :], in_=pcol_d[l]), wr=[r_pcol])
        P.dma("sp", "par", lambda e: e.dma_start(out=CTAB[:, 8 * 128:12 * 128].rearrange("p (c n) -> p c n", n=128),
                                                   in_=arow_d[l].rearrange("c p n -> p c n")), wr=[r_ctab])
        rp = [r_pcol]
        tt("dve", SMALL[:, 2:3], PCOL[:, 22:23], PCOL[:, 23:24], ALU.mult, rd=rp, wr=[r_small])
        tt("dve", SMALL[:, 3:4], PCOL[:, 24:25], PCOL[:, 25:26], ALU.mult, rd=rp, wr=[r_small])
        mm(PS[2][:, 0:2], ONESF[0:64, :], SMALL[0:64, 2:4], True, True, rd=[r_small, r_const], wr=[r_ps[2]])
        act(SMALL[:, 4:6], PS[2][:, 0:2], AF.Exp, rd=[r_ps[2]], wr=[r_small])
        tt("dve", SMALL[:, 6:7], SMALL[:, 5:6], SMALL[:, 4:5], ALU.subtract, rd=[r_small], wr=[r_small])
        ts("dve", SMALL[:, 0:1], SMALL[:, 6:7], -lam_init, None, ALU.add, None, rd=[r_small], wr=[r_small])
        ts("dve", SMALL[:, 1:2], PCOL[:, 20:21], 1.0 - lam_init, None, ALU.mult, None, rd=rp + [r_small],
           wr=[r_small])
        act(SMALL[:, 12:16], PCOL[:, 26:30], AF.Exp, rd=rp + [r_small], wr=[r_small])
        ts("dve", SMALL[:, 12:16], SMALL[:, 12:16], -1.0, None, ALU.mult, None, rd=[r_small], wr=[r_small])
        act(SMALL[:, 8:12], SMALL[:, 12:16], AF.Exp, rd=[r_small], wr=[r_small], scale=128.0)
        act(SMALL[:, 16:24], PCOL[:, 30:38], AF.Exp, rd=rp + [r_small], wr=[r_small])
        ts("dve", SMALL[:, 16:24], SMALL[:, 16:24], -1.0, None, ALU.mult, None, rd=[r_small], wr=[r_small])
        rs = [r_small, r_const]
        for h in range(4):
            act(ctab(h), RELF, AF.Exp, rd=rs, wr=[r_ctab], scale=SMALL[:, 16 + h:17 + h])
            tt("dve", ctab(h), ctab(h), MKF8, ALU.mult, rd=[r_ctab, r_const], wr=[r_ctab])
            act(ctab(12), RELB, AF.Exp, rd=rs + [r_ctab], wr=[r_ctab], scale=SMALL[:, 20 + h:21 + h])
            tt("dve", ctab(12), ctab(12), MKB8, ALU.mult, rd=[r_ctab, r_const], wr=[r_ctab])
            tt("dve", ctab(h), ctab(h), ctab(12), ALU.add, rd=[r_ctab], wr=[r_ctab])
        for p in range(2):
            act(ctab(4 + p), POS1, AF.Exp, rd=rs + [r_ctab], wr=[r_ctab], scale=SMALL[:, 12 + p:13 + p])
            act(ctab(6 + p), POSB, AF.Exp, rd=rs + [r_ctab], wr=[r_ctab], scale=SMALL[:, 14 + p:15 + p])
            ts("dve", ctab(4 + p), ctab(4 + p), 0.125, None, ALU.mult, None, rd=[r_ctab], wr=[r_ctab])
            ts("dve", ctab(6 + p), ctab(6 + p), 0.125, None, ALU.mult, None, rd=[r_ctab], wr=[r_ctab])
            for d in range(2):
                dstt = ctab(8 + 2 * d + p)
                src = dstt
                act(dstt, src, AF.Exp, rd=rp + [r_ctab], wr=[r_ctab])
                ts("dve", dstt, dstt, -1.0, None, ALU.mult, None, rd=[r_ctab], wr=[r_ctab])
                act(dstt, dstt, AF.Exp, rd=[r_ctab, r_const], wr=[r_ctab], scale=(TOKCR if d == 0 else TOKC))

    def interleave(a, b):
        da = db = False
        while not (da and db):
            if not da:
                try:
                    next(a)
                except StopIteration:
                    da = True
            if b is None:
                db = True
            if not db:
                try:
                    next(b)
                except StopIteration:
                    db = True

    def drain(g):
        for _ in g:
            pass

    stop_state = {}
    attn_fn = {"A": attn_A, "B": attn_B, "C": attn_C}

    def ck(name):
        if stop == name:
            raise _Stop()

    def layer(l):
        layer_params(l)
        ck('params')
        for u in (units if stop not in ('xc', 'xc2') else units[7:]):
            push_unit_weights(l, u)
        norm(0)
        ck('norm1')
        for kc in range(KC):
            P.dma("sp", "xsp", lambda e, kc=kc: e.dma_start(out=xsp_d[kc * 128:(kc + 1) * 128, :],
                                                           in_=XY[:, kc * S:(kc + 1) * S]),
                  rd=r_x[kc], wr=[r_xsp])
        P.barrier(dma_keys=("xsp",))
        ck('spill')
        if stop in ('xc', 'xc2'):
            drain(inproj(l, 7))
            ck('xc')
            drain(attn_C(l, 7))
            ck('xc2')
        drain(inproj(l, 0))
        ck('inproj0')
        for ui in range(len(units)):
            a = attn_fn[units[ui]["kind"]](l, ui)
            b = inproj(l, ui + 1) if ui + 1 < len(units) else None
            interleave(a, b)
            ck('u%d' % ui)
        P.barrier()
        ck('mix')
        if dbg and l == 0:
            P.dma("sp", "dbg", lambda e: e.dma_start(out=ydbg_d.rearrange("(c p) s -> p c s", p=128),
                                                       in_=Ybf.rearrange("p (c s) -> p c s", s=S)), rd=[], wr=[])
            P.barrier(dma_keys=("dbg",))
        for h in range(4):
            WS.push(w_in_d[l, C_CG + h])
        for j in range(8):
            WS.push(w_br_d[l, j], 11)
            for i in range(3):
                WS.push(w_in_d[l, C_GATE + 8 * i + j])
        for oc in range(8):
            WS.push(w_o_d[l, oc])

        def nbank():
            b = rot["ps"] % 8
            rot["ps"] += 1
            return b

        for h in range(4):
            slot = WS.pop()
            ch = 7 + h
            for tb in range(NB):
                bank = nbank()
                proj_fm(slot, tb, bank)
                j2 = tb % 2
                act(ub(j2), PS[bank][:, :], AF.Silu, rd=[r_ps[bank]], wr=[r_ub[j2]])
                ysl = y_(ch)[:, tb * TB:(tb + 1) * TB]
                tt("pool", ysl, ysl, ub(j2), ALU.mult, rd=[r_ub[j2], r_y[ch][tb], r_yall[ch]], wr=[r_y[ch][tb]])
            WS.release(slot)
        ybr = ((0, 3), (3, 7), (7, 11))
        cnt4 = 0
        for j in range(8):
            bslot = WS.pop()
            wb = wslot(bslot, 11)
            for i in range(3):
                gslot = WS.pop()
                c0, c1 = ybr[i]
                for tb in range(NB):
                    gb = nbank()
                    proj_fm(gslot, tb, gb)
                    sg = 4 + cnt4 % 2
                    tmp = 6 + cnt4 % 2
                    cnt4 += 1
                    act(ft(sg), PS[gb][:, :], AF.Sigmoid, rd=[r_ps[gb]], wr=[r_ft[sg]])
                    bb = nbank()
                    for ch in range(c0, c1):
                        mm(PS[bb][:, :], wb[:, ch, :], y_(ch)[:, tb * TB:(tb + 1) * TB], ch == c0, ch == c1 - 1,
                           rd=[r_w[bslot], r_y[ch][tb], r_yall[ch]], wr=[r_ps[bb]])
                    if i == 0:
                        tt("dve", ft(tb), PS[bb][:, :], ft(sg), ALU.mult, rd=[r_ps[bb], r_ft[sg]], wr=[r_ft[tb]])
                    else:
                        tt("dve", ft(tmp), PS[bb][:, :], ft(sg), ALU.mult, rd=[r_ps[bb], r_ft[sg]], wr=[r_ft[tmp]])
                        if i == 1:
                            tt("pool", ft(tb), ft(tb), ft(tmp), ALU.add, rd=[r_ft[tb], r_ft[tmp]], wr=[r_ft[tb]])
                        else:
                            tt("pool", m_(j, tb), ft(tb), ft(tmp), ALU.add, rd=[r_ft[tb], r_ft[tmp]],
                               wr=[r_m[j][tb]])
                WS.release(gslot)
            WS.release(bslot)
        P.barrier()
        if stop == '4a':
            for kc in range(KC):
                P.dma('pool', 'dbg', lambda e, kc=kc: e.dma_start(out=out_d[kc * 128:(kc + 1) * 128, :], in_=M[:, kc * S:(kc + 1) * S]), rd=[], wr=[])
            P.barrier(dma_keys=('dbg',))
            stop_state['skip_out'] = True
        ck('4a')
        for kc in range(KC):
            P.dma("sp", "xld", lambda e, kc=kc: e.dma_start(out=XY[:, kc * S:(kc + 1) * S],
                                                           in_=xsp_d[kc * 128:(kc + 1) * 128, :]),
                  rd=[r_xsp], wr=r_x[kc])
        for hg in range(4):
            for c in range(8):
                WS.push(w_m1_d[l, hg * 8 + c])
            for oc in range(8):
                WS.push(w_m2_d[l, hg * 8 + oc])
        for oc in range(8):
            slot = WS.pop()
            w = wslot(slot)
            for tb in range(NB):
                bank = nbank()
                for kc in range(KC):
                    mm(PS[bank][:, :], w[:, kc, :], m_(kc, tb), kc == 0, kc == KC - 1,
                       rd=[r_w[slot], r_m[kc][tb]], wr=[r_ps[bank]])
                tt("dve", x_(oc, tb), x_(oc, tb), PS[bank][:, :], ALU.add, rd=[r_ps[bank], r_x[oc][tb]],
                   wr=[r_x[oc][tb]])
            WS.release(slot)
        ck('4b')
        norm(8)
        P.barrier()
        ck('norm2')
        for hg in range(4):
            for c in range(8):
                slot = WS.pop()
                for tb in range(NB):
                    bank = nbank()
                    proj_fm(slot, tb, bank)
                    j2 = (c * NB + tb) % 2
                    act(ft(j2), PS[bank][:, :], AF.Relu, rd=[r_ps[bank]], wr=[r_ft[j2]])
                    tt("pool", m_(c, tb), ft(j2), ft(j2), ALU.mult, rd=[r_ft[j2]], wr=[r_m[c][tb]])
                WS.release(slot)
            for oc in range(8):
                slot = WS.pop()
                w = wslot(slot)
                for tb in range(NB):
                    bank = nbank()
                    for c in range(8):
                        mm(PS[bank][:, :], w[:, c, :], m_(c, tb), c == 0, c == 7,
                           rd=[r_w[slot], r_m[c][tb]], wr=[r_ps[bank]])
                    tt("dve", x_(oc, tb), x_(oc, tb), PS[bank][:, :], ALU.add, rd=[r_ps[bank], r_x[oc][tb]],
                       wr=[r_x[oc][tb]])
                WS.release(slot)
        P.barrier()
        if dbg and l == 0:
            for kc in range(KC):
                P.dma("sp", "dbg", lambda e, kc=kc: e.dma_start(out=xdbg_d[kc * 128:(kc + 1) * 128, :],
                                                               in_=XY[:, kc * S:(kc + 1) * S]), rd=r_x[kc], wr=[])

    try:
        for l in range(nl):
            layer(l)
    except _Stop:
        pass

    for kc in range(KC if not stop_state.get('skip_out') else 0):
        P.dma("sp", "out", lambda e, kc=kc: e.dma_start(out=out_d[kc * 128:(kc + 1) * 128, :],
                                                       in_=XY[:, kc * S:(kc + 1) * S]), rd=r_x[kc], wr=[])
    final_waits = tuple((k, n) for k, n in P.dcnt.items() if k in ("out", "dbg", "xsp"))
    P.q["sp"].append((final_waits, None, None))

    keys = list(P.ENG) + sorted(P.dcnt.keys())
    sems = {k: es.enter_context(nc.semaphore("s_" + k)) for k in keys}
    stats = {e: len(P.q[e]) for e in P.ENG}

    def run(e, name):
        for waits, fn, inc in P.q[name]:
            for k, n in waits:
                e.wait_ge(sems[k], n)
            if fn is not None:
                fn(e).then_inc(sems[inc[0]], inc[1])

    with nc.Block() as block:
        @block.tensor
        def _(e):
            run(e, "pe")

        @block.scalar
        def _(e):
            run(e, "act")

        @block.vector
        def _(e):
            run(e, "dve")

        @block.gpsimd
        def _(e):
            run(e, "pool")

        @block.sync
        def _(e):
            run(e, "sp")
    es.close()
    return nc, stats


def _rope_tables():
    pos = np.arange(S, dtype=np.float32)

    def tab(rot, theta, hd):
        inv = (1.0 / (np.float32(theta) ** (np.arange(0, rot, 2, dtype=np.float32) / np.float32(rot)))).astype(np.float32)
        ang = (pos[:, None] * inv[None, :]).astype(np.float32)
        c, s = np.cos(ang).astype(np.float32), np.sin(ang).astype(np.float32)
        r2 = rot // 2
        cosF = np.ones((hd, S), np.float32)
        sinF = np.zeros((hd, S), np.float32)
        cosF[0:r2] = c.T
        cosF[r2:2 * r2] = c.T
        sinF[0:r2] = s.T
        sinF[r2:2 * r2] = s.T
        return np.tile(cosF, (128 // hd, 1)), np.tile(sinF, (128 // hd, 1))

    cab, sab = tab(16, 500000.0, 64)
    cc, sc = tab(64, 10000.0, 64)
    return np.ascontiguousarray(np.stack([cab, sab, cc, sc]).astype(np.float32))


def _rot_mat(r2, hd=64):
    R = np.zeros((128, 128), np.float32)
    for b in range(128 // hd):
        for d in range(r2):
            R[b * hd + d + r2, b * hd + d] = -1.0
            R[b * hd + d, b * hd + d + r2] = 1.0
    return R


def _constants():
    ones = np.ones((128, 128), np.float32)
    bones = np.zeros((128, 128), np.float32)
    bones[0:64, 0:64] = 1.0
    bones[64:128, 64:128] = 1.0
    ident = np.eye(128, dtype=np.float32)
    cmats = np.stack([ones, bones, _rot_mat(8), _rot_mat(32), ident]).astype(np.float32)
    k = np.arange(128)[:, None]
    q = np.arange(512)[None, :]
    masks = []
    for rel in range(-1, 5):
        masks.append((np.abs(q - (128 * rel + k)) <= 64).astype(np.float32))
    band = (np.abs(np.arange(128)[None, :] - k) <= 64).astype(np.float32)
    masks.append(np.tile(band, (1, 4)))
    cmasks = np.stack(masks).astype(np.float32)
    m = np.arange(128)[:, None].astype(np.float32)
    n = np.arange(128)[None, :].astype(np.float32)
    relf = np.maximum(n - m, 0.0)
    relb = np.maximum(m - n, 0.0)
    mkf8 = (n >= m).astype(np.float32) * 0.125
    mkb8 = (m >= n).astype(np.float32) * 0.125
    pos1 = np.broadcast_to(n + 1.0, (128, 128))
    posb = np.broadcast_to(128.0 - n, (128, 128))
    tok = np.arange(128, dtype=np.float32)[:, None]
    cf = np.concatenate([relf, relb, mkf8, mkb8, pos1, posb, ones, tok, 127.0 - tok], axis=1).astype(np.float32)
    return cmats, cmasks, np.ascontiguousarray(cf)


def _prep_layers(inp, layers):
    f = lambda a: np.asarray(a, dtype=np.float32)
    nl = len(layers)
    w_in = f(inp["w_in"])[layers].reshape(nl, 8, 128, 57, 128).transpose(0, 3, 2, 1, 4)
    wbr = np.concatenate([f(inp["w_br_a"])[layers], f(inp["w_br_b"])[layers], f(inp["w_br_c"])[layers]], axis=1)
    w_br = wbr.reshape(nl, 11, 128, 8, 128).transpose(0, 3, 2, 1, 4)
    w_o = f(inp["w_o"])[layers].reshape(nl, 8, 128, 8, 128).transpose(0, 3, 2, 1, 4)
    w_m1 = f(inp["w_mlp1"])[layers].reshape(nl, 8, 128, 32, 128).transpose(0, 3, 2, 1, 4)
    w_m2 = f(inp["w_mlp2"])[layers].reshape(nl, 4, 8, 128, 8, 128).transpose(0, 1, 4, 3, 2, 5).reshape(nl, 32, 128, 8, 128)
    pcol = np.zeros((nl, 128, NPC), np.float32)
    arow = np.zeros((nl, 4, 128, 128), np.float32)
    for i, l in enumerate(layers):
        pcol[i, :, 0:8] = f(inp["norm1_g"])[l].reshape(8, 128).T
        pcol[i, :, 8:16] = f(inp["norm2_g"])[l].reshape(8, 128).T
        pcol[i, :, 16] = np.tile(f(inp["a_q_norm_g"])[l], 2)
        pcol[i, :, 17] = np.tile(f(inp["a_k_norm_g"])[l], 2)
        pcol[i, :, 18] = np.tile(f(inp["b_q_norm_g"])[l], 2)
        pcol[i, :, 19] = np.tile(f(inp["b_k_norm_g"])[l], 2)
        pcol[i, :, 20] = f(inp["b_out_norm_g"])[l]
        pcol[i, :, 21] = f(inp["c_out_norm_g"])[l]
        pcol[i, :, 22] = np.tile(f(inp["b_lambda_q1"])[l], 2)
        pcol[i, :, 23] = np.tile(f(inp["b_lambda_k1"])[l], 2)
        pcol[i, :, 24] = np.tile(f(inp["b_lambda_q2"])[l], 2)
        pcol[i, :, 25] = np.tile(f(inp["b_lambda_k2"])[l], 2)
        af, ab = f(inp["c_decay_f"])[l], f(inp["c_decay_b"])[l]
        for p in range(2):
            pcol[i, :, 26 + p] = np.repeat(af[2 * p:2 * p + 2], 64)
            pcol[i, :, 28 + p] = np.repeat(ab[2 * p:2 * p + 2], 64)
            arow[i, p] = np.repeat(af[2 * p:2 * p + 2], 64)[None, :]
            arow[i, 2 + p] = np.repeat(ab[2 * p:2 * p + 2], 64)[None, :]
        pcol[i, :, 30:34] = af[None, :]
        pcol[i, :, 34:38] = ab[None, :]
    c = np.ascontiguousarray
    return dict(w_in=c(w_in), w_br=c(w_br), w_o=c(w_o), w_m1=c(w_m1), w_m2=c(w_m2), pcol=pcol, arow=arow)


_CACHE = {}


def _get_prog(nl, l0):
    if (nl, l0) not in _CACHE:
        _CACHE[(nl, l0)] = build(nl, l0=l0)[0]
    return _CACHE[(nl, l0)]


FUSED = True


def kernel(**inputs):
    x = np.asarray(inputs["x"], dtype=np.float32)
    B = x.shape[0]
    cmats, cmasks, cf = _constants()
    rope = _rope_tables()
    consts = dict(c_mats=cmats, c_masks=cmasks, c_f32=cf, rope=rope)
    xT = [np.ascontiguousarray(x[b].T) for b in range(B)]
    groups = [list(range(DEPTH))] if FUSED else [[l] for l in range(DEPTH)]
    for layers in groups:
        nc = _get_prog(len(layers), layers[0])
        shared = _prep_layers(inputs, layers)
        shared.update(consts)
        in_maps = []
        for b in range(B):
            d = dict(shared)
            d["xT"] = xT[b]
            in_maps.append(d)
        res = run_bass_kernel_spmd(nc, in_maps, core_ids=list(range(B)))
        xT = [np.ascontiguousarray(np.asarray(r["outT"], dtype=np.float32)) for r in res.results]
    out = np.stack([t.T for t in xT], axis=0)
    return np.ascontiguousarray(out.astype(np.float32))
```

```python
import math
from contextlib import ExitStack
import numpy as np
import ml_dtypes
import concourse.bass as bass
import concourse.mybir as mybir
from concourse.bass_utils import run_bass_kernel_spmd

F32, BF16 = mybir.dt.float32, mybir.dt.bfloat16
ALU, AF = mybir.AluOpType, mybir.ActivationFunctionType

S = 2048
D = 1024
KC = 8
NB = 4
TB = 512
DEPTH = 4
EPS = 1e-6
NPC = 38
C_AQ, C_AK, C_AV = 0, 3, 6
C_BQ, C_BK, C_BV = 9, 13, 17
C_CQ, C_CK, C_CV, C_CG = 21, 23, 25, 29
C_GATE = 33
NWSLOT = 5


class Res:
    __slots__ = ("w", "r", "x")

    def __init__(self, x=False):
        self.w = None
        self.r = {}
        self.x = x


class Prog:
    ENG = ("pe", "act", "dve", "pool", "sp")

    def __init__(self):
        self.q = {e: [] for e in self.ENG}
        self.cnt = {e: 0 for e in self.ENG}
        self.seen = {e: {} for e in self.ENG}
        self.dcnt = {}

    def _need(self, eng, dep, waits, raw):
        if dep is None:
            return
        k, n = dep
        if k == eng:
            if eng == "pe":
                return
        if k in self.dcnt:
            n = max(n, self.dcnt[k])
        if self.seen[eng].get(k, 0) >= n:
            return
        waits[k] = max(waits.get(k, 0), n)

    def _deps(self, eng, rd, wr):
        waits = {}
        for r in rd:
            self._need(eng, r.w, waits, True)
        for w in wr:
            self._need(eng, w.w, waits, False)
            for k, n in w.r.items():
                self._need(eng, (k, n), waits, False)
        for k, n in waits.items():
            self.seen[eng][k] = n
        return tuple(waits.items())

    def op(self, eng, fn, rd=(), wr=()):
        wr = list(wr) + [r for r in rd if r.x]
        rd = [r for r in rd if not r.x]
        waits = self._deps(eng, rd, wr)
        self.cnt[eng] += 1
        n = self.cnt[eng]
        self.q[eng].append((waits, fn, (eng, 1)))
        for r in rd:
            r.r[eng] = max(r.r.get(eng, 0), n)
        for w in wr:
            w.w = (eng, n)
            w.r = {}

    def dma(self, qeng, semkey, fn, rd=(), wr=()):
        waits = self._deps(qeng, rd, wr)
        self.dcnt[semkey] = self.dcnt.get(semkey, 0) + 16
        n = self.dcnt[semkey]
        self.q[qeng].append((waits, fn, (semkey, 16)))
        for r in rd:
            r.r[semkey] = max(r.r.get(semkey, 0), n)
        for w in wr:
            w.w = (semkey, n)
            w.r = {}

    def barrier(self, dma_keys=()):
        for e in self.ENG:
            waits = {}
            for f in self.ENG:
                if f != e and self.cnt[f] > self.seen[e].get(f, 0):
                    waits[f] = self.cnt[f]
            for k in dma_keys:
                if self.dcnt.get(k, 0) > self.seen[e].get(k, 0):
                    waits[k] = self.dcnt[k]
            for k, n in waits.items():
                self.seen[e][k] = n
            if waits:
                self.q[e].append((tuple(waits.items()), None, None))


class _Stop(Exception):
    pass


def build(nl, dbg=False, stop=None, l0=0):
    nc = bass.Bass("TRN2", target_bir_lowering=False)

    def din(name, shape, dt=F32):
        return nc.dram_tensor(name, list(shape), dt, kind="ExternalInput").ap()

    xT_d = din("xT", [D, S])
    w_in_d = din("w_in", [nl, 57, 128, 8, 128])
    w_br_d = din("w_br", [nl, 8, 128, 11, 128])
    w_o_d = din("w_o", [nl, 8, 128, 8, 128])
    w_m1_d = din("w_m1", [nl, 32, 128, 8, 128])
    w_m2_d = din("w_m2", [nl, 32, 128, 8, 128])
    pcol_d = din("pcol", [nl, 128, NPC])
    arow_d = din("arow", [nl, 4, 128, 128])
    cmat_d = din("c_mats", [5, 128, 128])
    cmask_d = din("c_masks", [7, 128, 512])
    cf32_d = din("c_f32", [128, 7 * 128 + 2])
    rope_d = din("rope", [4, 128, S])
    out_d = nc.dram_tensor("outT", [D, S], F32, kind="ExternalOutput").ap()
    xsp_d = nc.dram_tensor("xspill", [D, S], F32, kind="Internal").ap()
    if dbg:
        ydbg_d = nc.dram_tensor("ydbg", [11 * 128, S], BF16, kind="ExternalOutput").ap()
        xdbg_d = nc.dram_tensor("xdbg", [D, S], F32, kind="ExternalOutput").ap()

    P = Prog()
    es = ExitStack()

    def sb(name, shape, dt):
        return es.enter_context(nc.sbuf_tensor(name, list(shape), dt))

    XN = sb("XN", [128, KC * S], BF16)
    XY = sb("XY", [128, KC * S], F32)
    M = sb("M", [128, KC * S], BF16)
    WR = sb("WR", [128, NWSLOT * 11 * 128], BF16)
    CMB = sb("CMB", [128, 5 * 128], BF16)
    MASKS = sb("MASKS", [128, 7 * 512], BF16)
    CF = sb("CF", [128, 7 * 128 + 2], F32)
    PCOL = sb("PCOL", [128, NPC], F32)
    SMALL = sb("SMALL", [128, 64], F32)
    CTAB = sb("CTAB", [128, 13 * 128], F32)
    CDER = sb("CDER", [128, 6 * 2048], BF16)
    S32 = sb("S32", [128, 2 * 128], F32)
    SQB = sb("SQB", [128, 2 * 512], BF16)
    UB = sb("UB", [128, 2 * 512], BF16)
    FT = sb("FT", [128, 9 * 512], F32)

    Ybf = XY[:, 0:11 * 1024].bitcast(BF16)
    ROPE = XY[:, 11 * 1024:11 * 1024 + 4096]
    PB = XY[:, 15 * 1024:16 * 1024].bitcast(BF16)
    DTOT = CDER[:, 4 * 2048:6 * 2048].bitcast(F32)

    def x_(kc, tb):
        return XY[:, kc * S + tb * TB: kc * S + (tb + 1) * TB]

    def xn_(kc, tb):
        return XN[:, kc * S + tb * TB: kc * S + (tb + 1) * TB]

    def xnfull(kc):
        return XN[:, kc * S:(kc + 1) * S]

    def y_(ch):
        return Ybf[:, ch * S:(ch + 1) * S]

    def m_(kc, tb):
        return M[:, kc * S + tb * TB: kc * S + (tb + 1) * TB]

    def uq(s):
        return M[:, s * 8192: s * 8192 + 2048]

    def uk(s):
        return M[:, s * 8192 + 2048: s * 8192 + 4096]

    def uv(s):
        return M[:, s * 8192 + 4096: s * 8192 + 8192].rearrange("p (h t c) -> p h t c", h=2, c=128)

    def kz(s, m):
        return uk(s) if m == 0 else M[:, s * 8192 + 6144: s * 8192 + 8192]

    def wslot(s, nk=8):
        return WR[:, s * 1408: s * 1408 + nk * 128].rearrange("p (k n) -> p k n", n=128)

    ONES = CMB[:, 0:128]
    BONES = CMB[:, 128:256]
    RAB = CMB[:, 256:384]
    RC = CMB[:, 384:512]
    IDENT = CMB[:, 512:640]

    def mask_(i):
        return MASKS[:, i * 512:(i + 1) * 512]

    RELF, RELB = CF[:, 0:128], CF[:, 128:256]
    MKF8, MKB8 = CF[:, 256:384], CF[:, 384:512]
    POS1, POSB = CF[:, 512:640], CF[:, 640:768]
    ONESF = CF[:, 768:896]
    TOKC, TOKCR = CF[:, 896:897], CF[:, 897:898]

    def ctab(i):
        return CTAB[:, i * 128:(i + 1) * 128]

    def cder(i):
        return CDER[:, i * 2048:(i + 1) * 2048]

    KDF, KDB = cder(0).rearrange("p (t c) -> p t c", c=128), cder(1).rearrange("p (t c) -> p t c", c=128)
    QDF, QDB = cder(2), cder(3)
    STF, STB = cder(4).rearrange("p (t c) -> p t c", c=128), cder(5).rearrange("p (t c) -> p t c", c=128)

    def pb(i):
        return PB[:, i * 512:(i + 1) * 512]

    def sqb(i):
        return SQB[:, i * 512:(i + 1) * 512]

    def ub(i):
        return UB[:, i * 512:(i + 1) * 512]

    def ft(i):
        return FT[:, i * 512:(i + 1) * 512]

    PS = [es.enter_context(nc.psum_tensor("ps%d" % i, [128, 512], F32)) for i in range(8)]

    r_x = [[Res() for _ in range(NB)] for _ in range(KC)]
    r_xn = [Res() for _ in range(NB)]
    r_y = [[Res() for _ in range(NB)] for _ in range(11)]
    r_yall = [Res() for _ in range(11)]
    r_m = [[Res() for _ in range(NB)] for _ in range(KC)]
    r_uq = [Res(), Res()]
    r_uk = [Res(), Res()]
    r_uv = [Res(), Res()]
    r_w = [Res() for _ in range(NWSLOT)]
    r_ps = [Res(True) for _ in range(8)]
    r_pb = [Res() for _ in range(4)]
    r_sqb = [Res(), Res()]
    r_ub = [Res(), Res()]
    r_ft = [Res() for _ in range(9)]
    r_rope = Res()
    r_acc = Res()
    r_const = Res()
    r_pcol = Res()
    r_small = Res()
    r_ctab = Res()
    r_cder = [Res() for _ in range(6)]
    r_s32 = [Res(), Res()]
    r_xsp = Res()

    wstate = {"n": 0}

    def load_w(dram_ap, nk, s):
        dst = wslot(s, nk)
        P.dma("pool", "w%d" % s, lambda e, dst=dst, src=dram_ap: e.dma_start(out=dst, in_=src), wr=[r_w[s]])
        return s

    class WStream:
        def __init__(self):
            self.pending = []
            self.loaded = []
            self.free = list(range(NWSLOT))

        def push(self, ap, nk=8):
            self.pending.append((ap, nk))

        def prefetch(self):
            while self.pending and self.free:
                ap, nk = self.pending.pop(0)
                self.loaded.append(load_w(ap, nk, self.free.pop(0)))

        def pop(self):
            self.prefetch()
            return self.loaded.pop(0)

        def release(self, s):
            self.free.append(s)
            self.prefetch()

    WS = WStream()

    def mm(out, lhsT, rhs, start, stop, rd, wr):
        P.op("pe", lambda e: e.matmul(out, lhsT=lhsT, rhs=rhs, start=start, stop=stop), rd=rd, wr=wr)

    def act(out, in_, func, rd, wr, scale=None, bias=None):
        kw = {}
        if scale is not None:
            kw["scale"] = scale
        if bias is not None:
            kw["bias"] = bias
        P.op("act", lambda e: e.activation(out=out, in_=in_, func=func, **kw), rd=rd, wr=wr)

    def tt(eng, out, in0, in1, op, rd, wr):
        P.op(eng, lambda e: e.tensor_tensor(out=out, in0=in0, in1=in1, op=op), rd=rd, wr=wr)

    def stt(eng, out, in0, scalar, in1, op0, op1, rd, wr):
        P.op(eng, lambda e: e.scalar_tensor_tensor(out=out, in0=in0, scalar=scalar, in1=in1, op0=op0, op1=op1),
             rd=rd, wr=wr)

    def ts(eng, out, in0, s1, s2, op0, op1, rd, wr):
        if op1 is None:
            P.op(eng, lambda e: e.tensor_scalar(out=out, in0=in0, scalar1=s1, scalar2=None, op0=op0), rd=rd, wr=wr)
        else:
            P.op(eng, lambda e: e.tensor_scalar(out=out, in0=in0, scalar1=s1, scalar2=s2, op0=op0, op1=op1),
                 rd=rd, wr=wr)

    def cp(eng, out, in_, rd, wr):
        if eng == "act":
            act(out, in_, AF.Copy, rd, wr)
        else:
            P.op(eng, lambda e: e.tensor_copy(out=out, in_=in_), rd=rd, wr=wr)

    def recip(out, in_, rd, wr):
        P.op("dve", lambda e: e.reciprocal(out=out, in_=in_), rd=rd, wr=wr)

    def rstd_from(ps_ap, r_in, ftile, r_ftile, inv_n):
        act(ftile, ps_ap, AF.Ln, rd=[r_in], wr=[r_ftile], scale=inv_n, bias=EPS)
        act(ftile, ftile, AF.Exp, rd=[r_ftile], wr=[r_ftile], scale=-0.5)

    rot = {"ps": 0, "alt": 0}
    SBANK = (4, 5, 3)

    P.dma("pool", "cst", lambda e: e.dma_start(out=CMB[:, :].rearrange("p (c n) -> p c n", n=128),
                                                  in_=cmat_d.rearrange("c p n -> p c n")), wr=[r_const])
    P.dma("pool", "cst", lambda e: e.dma_start(out=MASKS[:, :].rearrange("p (c n) -> p c n", n=512),
                                                  in_=cmask_d.rearrange("c p n -> p c n")), wr=[r_const])
    P.dma("sp", "cstf", lambda e: e.dma_start(out=CF[:, :], in_=cf32_d), wr=[r_const])
    P.barrier(dma_keys=("cst", "cstf"))
    for kc in range(KC):
        P.dma("sp", "xld", lambda e, kc=kc: e.dma_start(out=XY[:, kc * S:(kc + 1) * S],
                                                       in_=xT_d[kc * 128:(kc + 1) * 128, :]),
              wr=r_x[kc])

    def norm(gcol0):
        for tb in range(NB):
            for kc in range(KC):
                i = kc % 2
                act(sqb(i), x_(kc, tb), AF.Square, rd=[r_x[kc][tb]], wr=[r_sqb[i]])
                mm(PS[2][:, :], ONES, sqb(i), kc == 0, kc == KC - 1, rd=[r_sqb[i], r_const], wr=[r_ps[2]])
            fr = tb % 2
            rstd_from(PS[2][:, :], r_ps[2], ft(fr), r_ft[fr], 1.0 / D)
            for kc in range(KC):
                stt("dve", xn_(kc, tb), x_(kc, tb), PCOL[:, gcol0 + kc: gcol0 + kc + 1], ft(fr), ALU.mult, ALU.mult,
                    rd=[r_x[kc][tb], r_ft[fr], r_pcol], wr=[r_xn[tb]])

    def proj_fm(slot, tb, bank):
        w = wslot(slot)
        for kc in range(KC):
            mm(PS[bank][:, :], w[:, kc, :], xn_(kc, tb), kc == 0, kc == KC - 1,
               rd=[r_w[slot], r_xn[tb]], wr=[r_ps[bank]])

    def perm_dst(buf, tb, delta):
        if delta == 1:
            return buf[:, tb * TB:(tb + 1) * TB], None
        n0 = tb * TB // delta
        return buf.rearrange("p (r n) -> p n r", r=delta)[:, n0:n0 + TB // delta, :], delta

    def srcv(ap, delta):
        if delta is None:
            return ap
        return ap.rearrange("p (n r) -> p n r", r=delta)

    def qk_post(bank, tb, gcol, rmat, normed, dsts, delta, feng="pool", aux=2):
        j = rot["alt"] % 2
        rot["alt"] += 1
        pj = PS[bank][:, :]
        cos = ROPE[:, tb * TB:(tb + 1) * TB]
        sin = ROPE[:, S + tb * TB: S + (tb + 1) * TB]
        if normed:
            act(sqb(j), pj, AF.Square, rd=[r_ps[bank]], wr=[r_sqb[j]])
            mm(PS[aux][:, :], BONES, sqb(j), True, True, rd=[r_sqb[j], r_const], wr=[r_ps[aux]])
            yield
            rstd_from(PS[aux][:, :], r_ps[aux], ft(j), r_ft[j], 1.0 / 64)
            stt("dve", ub(j), pj, PCOL[:, gcol:gcol + 1], ft(j), ALU.mult, ALU.mult,
                rd=[r_ps[bank], r_ft[j], r_pcol], wr=[r_ub[j]])
            yield
            mm(PS[aux][:, :], rmat, ub(j), True, True, rd=[r_ub[j], r_const], wr=[r_ps[aux]])
            tt(feng, ft(2 + j), ub(j), cos, ALU.mult, rd=[r_ub[j], r_rope], wr=[r_ft[2 + j]])
        else:
            cp("act", ub(j), pj, rd=[r_ps[bank]], wr=[r_ub[j]])
            tt("dve", ft(2 + j), pj, cos, ALU.mult, rd=[r_ps[bank], r_rope], wr=[r_ft[2 + j]])
            yield
            mm(PS[aux][:, :], rmat, ub(j), True, True, rd=[r_ub[j], r_const], wr=[r_ps[aux]])
        yield
        tt("dve", ft(4 + j), PS[aux][:, :], sin, ALU.mult, rd=[r_ps[aux], r_rope], wr=[r_ft[4 + j]])
        for buf, rows, res in dsts:
            dst, dl = perm_dst(buf, tb, delta)
            a0, a1 = srcv(ft(2 + j), dl), srcv(ft(4 + j), dl)
            if rows is not None:
                dst, a0, a1 = dst[rows[0]:rows[1]], a0[rows[0]:rows[1]], a1[rows[0]:rows[1]]
            tt(feng, dst, a0, a1, ALU.add, rd=[r_ft[2 + j], r_ft[4 + j]], wr=[res])

    def v_proj(slot, set_, coloff, delta):
        w = wslot(slot)
        V = uv(set_)
        for g4 in range(4):
            bank = 0
            for ti in range(4):
                t = g4 * 4 + ti
                for kc in range(KC):
                    if delta == 1:
                        lhs = xnfull(kc)[:, t * 128:(t + 1) * 128]
                    elif delta == 4:
                        lhs = xnfull(kc).rearrange("p (n r) -> p r n", r=4)[:, t // 4, (t % 4) * 128:(t % 4 + 1) * 128]
                    else:
                        lhs = xnfull(kc).rearrange("p (n r) -> p r n", r=16)[:, t, :]
                    mm(PS[bank][:, ti * 128:(ti + 1) * 128], lhs, w[:, kc, :], kc == 0, kc == KC - 1,
                       rd=[r_w[slot]] + r_xn, wr=[r_ps[bank]])
            cp("act", V[:, coloff // 128, g4 * 4:(g4 + 1) * 4, :],
               PS[bank][:, :].rearrange("p (t c) -> p t c", c=128), rd=[r_ps[bank]], wr=[r_uv[set_]])
            yield

    units = []
    for g in range(3):
        units.append(dict(kind="A", idx=g, q=C_AQ + g, k=C_AK + g, v=[C_AV + g], gq=16, gk=17, delta=(1, 4, 16)[g]))
    for h in range(4):
        units.append(dict(kind="B", idx=h, q=C_BQ + h, k=C_BK + h, v=[C_BV + h], gq=18, gk=19, delta=1))
    for p in range(2):
        units.append(dict(kind="C", idx=p, q=C_CQ + p, k=C_CK + p, v=[C_CV + 2 * p, C_CV + 2 * p + 1], gq=0, gk=0,
                          delta=1))

    def push_unit_weights(l, u):
        WS.push(w_in_d[l, u["q"]])
        WS.push(w_in_d[l, u["k"]])
        for c in u["v"]:
            WS.push(w_in_d[l, c])

    def rope_load(which):
        P.dma("sp", "rope", lambda e: e.dma_start(out=ROPE[:, 0:S], in_=rope_d[2 * which]), wr=[r_rope])
        P.dma("sp", "rope", lambda e: e.dma_start(out=ROPE[:, S:2 * S], in_=rope_d[2 * which + 1]), wr=[r_rope])

    def inproj(l, ui):
        u = units[ui]
        set_ = ui % 2
        normed = u["kind"] != "C"
        rmat = RC if u["kind"] == "C" else RAB
        if ui == 0:
            rope_load(0)
        if ui == 7:
            rope_load(1)
        feng = "dve" if ui == 0 else "pool"
        width = 2 if ui <= 3 else 1
        qslot = WS.pop()
        kslot = WS.pop()
        gens = []

        def cb_gen(slot, tb, gcol, dsts, k):
            bank = k % width
            proj_fm(slot, tb, bank)
            yield
            yield from qk_post(bank, tb, gcol, rmat, normed, dsts, u["delta"], feng, 2 + k % width)

        k = 0
        for which in ("q", "k"):
            if which == "q":
                dsts, gcol, slot = [(uq(set_), None, r_uq[set_])], u["gq"], qslot
            elif u["kind"] == "C":
                dsts, gcol, slot = [(uk(set_), None, r_uk[set_])], u["gk"], kslot
            else:
                k0, k1 = kz(set_, 0), kz(set_, 1)
                P.op("pool", lambda e, k0=k0: e.memset(k0[64:128, :], 0.0), wr=[r_uk[set_]])
                P.op("pool", lambda e, k1=k1: e.memset(k1[0:64, :], 0.0), wr=[r_uv[set_]])
                dsts, gcol, slot = [(k0, (0, 64), r_uk[set_]), (k1, (64, 128), r_uv[set_])], u["gk"], kslot
            for tb in range(NB):
                gens.append(cb_gen(slot, tb, gcol, dsts, k))
                k += 1
        active = []
        while gens or active:
            while gens and len(active) < width:
                active.append(gens.pop(0))
            for g_ in list(active):
                try:
                    next(g_)
                except StopIteration:
                    active.remove(g_)
            yield
        WS.release(qslot)
        WS.release(kslot)
        for vi, c in enumerate(u["v"]):
            slot = WS.pop()
            yield from v_proj(slot, set_, vi * 128, u["delta"])
            WS.release(slot)

    def attn_B(l, ui):
        u = units[ui]
        h = u["idx"]
        set_ = ui % 2
        qT, V = uq(set_), uv(set_)
        rdq = [r_uq[set_], r_uk[set_], r_uv[set_]]
        ch = 3 + h
        pending = []

        def make_epilogue(qb):
            def s1():
                act(sqb(0), ft(7), AF.Square, rd=[r_ft[7]], wr=[r_sqb[0]])

            def s2():
                mm(PS[1][:, :], ONES, sqb(0), True, True, rd=[r_sqb[0], r_const], wr=[r_ps[1]])

            def s3():
                rstd_from(PS[1][:, :], r_ps[1], ft(6), r_ft[6], 1.0 / 128)
                stt("dve", y_(ch)[:, qb * TB:(qb + 1) * TB], ft(7), SMALL[:, 1:2], ft(6), ALU.mult, ALU.mult,
                    rd=[r_ft[7], r_ft[6], r_small], wr=[r_y[ch][qb], r_yall[ch]])

            return [(9, s1), (11, s2), (13, s3)]

        for qb in range(NB):
            for m in range(2):
                def smm(kt):
                    bank = SBANK[kt % 3]
                    mm(PS[bank][:, :], kz(set_, m)[:, kt * 128:(kt + 1) * 128], qT[:, qb * TB:(qb + 1) * TB], True, True,
                       rd=rdq, wr=[r_ps[bank]])

                smm(0)
                smm(1)
                for kt in range(16):
                    if kt + 2 < 16:
                        smm(kt + 2)
                    bank = SBANK[kt % 3]
                    pi = kt % 4
                    act(pb(pi), PS[bank][:, :], AF.Exp, rd=[r_ps[bank]], wr=[r_pb[pi]], scale=0.125)
                    mm(PS[6][:, :], V[:, 0, kt, :], pb(pi), kt == 0, kt == 15, rd=[r_pb[pi], r_uv[set_]],
                       wr=[r_ps[6]])
                    mm(PS[7][:, :], ONES, pb(pi), kt == 0, kt == 15, rd=[r_pb[pi], r_const], wr=[r_ps[7]])
                    while pending and pending[0][0] <= kt:
                        pending.pop(0)[1]()
                    if kt % 4 == 3 and kt < 15:
                        yield
                cp("dve", ft(6), PS[7][:, :], rd=[r_ps[7]], wr=[r_ft[6]])
                if m == 0:
                    cp("dve", ft(7), PS[6][:, :], rd=[r_ps[6]], wr=[r_ft[7]])
                    recip(ft(6), ft(6), rd=[r_ft[6]], wr=[r_ft[6]])
                    tt("dve", ft(7), ft(7), ft(6), ALU.mult, rd=[r_ft[7], r_ft[6]], wr=[r_ft[7]])
                else:
                    cp("dve", ft(8), PS[6][:, :], rd=[r_ps[6]], wr=[r_ft[8]])
                    recip(ft(6), ft(6), rd=[r_ft[6]], wr=[r_ft[6]])
                    tt("dve", ft(8), ft(8), ft(6), ALU.mult, rd=[r_ft[8], r_ft[6]], wr=[r_ft[8]])
                    stt("dve", ft(7), ft(8), SMALL[:, 0:1], ft(7), ALU.mult, ALU.add,
                        rd=[r_ft[8], r_ft[7], r_small], wr=[r_ft[7]])
                    pending = make_epilogue(qb)
                yield
        for _, fn in pending:
            fn()
        yield

    def attn_A(l, ui):
        u = units[ui]
        g = u["idx"]
        delta = u["delta"]
        set_ = ui % 2
        qT, kT, V = uq(set_), uk(set_), uv(set_)
        rdq = [r_uq[set_], r_uk[set_], r_uv[set_]]
        for s2 in range(2):
            rows = slice(64 * s2, 64 * s2 + 64)
            for qb in range(NB):
                if g < 2:
                    rels = list(range(-1, 5)) if g == 0 else list(range(4))
                    kts = [(4 * qb + r, r) for r in rels if 0 <= 4 * qb + r < 16]
                    n = len(kts)

                    def win(i):
                        rel = kts[i][1]
                        if i == 0:
                            return 0, TB
                        return max(0, 128 * rel - 64), min(TB, 128 * rel + 192)

                    def smm(i):
                        kt, rel = kts[i]
                        bank = 4 + i % 2
                        lo, hi = win(i)
                        mm(PS[bank][:, lo:hi], kz(set_, s2)[:, kt * 128:(kt + 1) * 128],
                           qT[:, qb * TB + lo:qb * TB + hi], True, True, rd=rdq, wr=[r_ps[bank]])

                    smm(0)
                    for i in range(n):
                        if i + 1 < n:
                            smm(i + 1)
                        kt, rel = kts[i]
                        bank = 4 + i % 2
                        pi = i % 4
                        lo, hi = win(i)
                        act(pb(pi)[:, lo:hi], PS[bank][:, lo:hi], AF.Exp, rd=[r_ps[bank]], wr=[r_pb[pi]], scale=0.125)
                        tt("dve", pb(pi)[:, lo:hi], pb(pi)[:, lo:hi], mask_(rel + 1)[:, lo:hi], ALU.mult,
                           rd=[r_pb[pi], r_const], wr=[r_pb[pi]])
                        mm(PS[6][:, lo:hi], V[:, 0, kt, :], pb(pi)[:, lo:hi], i == 0, i == n - 1,
                           rd=[r_pb[pi], r_uv[set_]], wr=[r_ps[6]])
                        mm(PS[7][:, lo:hi], ONES, pb(pi)[:, lo:hi], i == 0, i == n - 1, rd=[r_pb[pi], r_const],
                           wr=[r_ps[7]])
                        if i < n - 1:
                            yield
                else:
                    bank = 4 + qb % 2
                    for rel in range(4):
                        kt = 4 * qb + rel
                        mm(PS[bank][:, rel * 128:(rel + 1) * 128], kz(set_, s2)[:, kt * 128:(kt + 1) * 128],
                           qT[:, kt * 128:(kt + 1) * 128], True, True, rd=rdq, wr=[r_ps[bank]])
                    pi = qb % 3
                    act(pb(pi), PS[bank][:, :], AF.Exp, rd=[r_ps[bank]], wr=[r_pb[pi]], scale=0.125)
                    tt("dve", pb(pi), pb(pi), mask_(6), ALU.mult, rd=[r_pb[pi], r_const], wr=[r_pb[pi]])
                    for rel in range(4):
                        kt = 4 * qb + rel
                        mm(PS[6][:, rel * 128:(rel + 1) * 128], V[:, 0, kt, :], pb(pi)[:, rel * 128:(rel + 1) * 128],
                           True, True, rd=[r_pb[pi], r_uv[set_]], wr=[r_ps[6]])
                        mm(PS[7][:, rel * 128:(rel + 1) * 128], ONES, pb(pi)[:, rel * 128:(rel + 1) * 128],
                           True, True, rd=[r_pb[pi], r_const], wr=[r_ps[7]])
                if delta == 1:
                    ydst = y_(g)[rows, qb * TB:(qb + 1) * TB]
                    ddst = DTOT[rows, qb * TB:(qb + 1) * TB]
                    nsrc, dsrc = PS[6][rows, :], PS[7][rows, :]
                elif delta == 4:
                    ydst = y_(g).rearrange("p (n r) -> p r n", r=4)[rows, qb, :]
                    ddst = DTOT[:, :].rearrange("p (n r) -> p r n", r=4)[rows, qb, :]
                    nsrc, dsrc = PS[6][rows, :], PS[7][rows, :]
                else:
                    ydst = y_(g).rearrange("p (n r) -> p r n", r=16)[rows, 4 * qb:4 * qb + 4, :]
                    ddst = DTOT[:, :].rearrange("p (n r) -> p r n", r=16)[rows, 4 * qb:4 * qb + 4, :]
                    nsrc = PS[6][rows, :].rearrange("p (a b) -> p a b", a=4)
                    dsrc = PS[7][rows, :].rearrange("p (a b) -> p a b", a=4)
                cp("act", ydst, nsrc, rd=[r_ps[6]], wr=[r_yall[g]] + r_y[g])
                if g == 0:
                    cp("dve", ddst, dsrc, rd=[r_ps[7]], wr=[r_cder[4], r_cder[5]])
                else:
                    tt("dve", ddst, ddst, dsrc, ALU.add, rd=[r_ps[7], r_cder[4], r_cder[5]], wr=[r_cder[4], r_cder[5]])
                yield
        if g == 2:
            for tb in range(NB):
                recip(ft(6), DTOT[:, tb * TB:(tb + 1) * TB], rd=[r_cder[4], r_cder[5]], wr=[r_ft[6]])
                for gg in range(3):
                    ysl = y_(gg)[:, tb * TB:(tb + 1) * TB]
                    tt("pool", ysl, ysl, ft(6), ALU.mult, rd=[r_ft[6], r_yall[gg]] + r_y[gg], wr=[r_y[gg][tb], r_yall[gg]])
                yield

    def attn_C(l, ui):
        u = units[ui]
        p = u["idx"]
        set_ = ui % 2
        qT, kT, V = uq(set_), uk(set_), uv(set_)
        DQF, DQB, DKF, DKB = ctab(4 + p), ctab(6 + p), ctab(8 + p), ctab(10 + p)
        g128f, g128b = SMALL[:, 8 + p:9 + p], SMALL[:, 10 + p:11 + p]
        for g4 in range(4):
            bank = 3
            tp = PS[bank][:, :].bitcast(BF16)[:, 0:512]
            for ti in range(4):
                t = g4 * 4 + ti
                P.op("pe", lambda e, t=t, ti=ti, tp=tp: e.transpose(tp[:, ti * 128:(ti + 1) * 128],
                                                                      kT[:, t * 128:(t + 1) * 128], IDENT),
                     rd=[r_uk[set_], r_const], wr=[r_ps[bank]])
            tp3 = tp.rearrange("p (t c) -> p t c", c=128)
            tt("dve", KDF[:, g4 * 4:(g4 + 1) * 4, :], tp3, DKF.unsqueeze(1).to_broadcast([128, 4, 128]), ALU.mult,
               rd=[r_ps[bank], r_ctab], wr=[r_cder[0]])
            tt("dve", KDB[:, g4 * 4:(g4 + 1) * 4, :], tp3, DKB.unsqueeze(1).to_broadcast([128, 4, 128]), ALU.mult,
               rd=[r_ps[bank], r_ctab], wr=[r_cder[1]])
            q3 = qT[:, g4 * TB:(g4 + 1) * TB].rearrange("p (t c) -> p t c", c=128)
            tt("pool", QDF[:, g4 * TB:(g4 + 1) * TB].rearrange("p (t c) -> p t c", c=128), q3,
               DQF.unsqueeze(1).to_broadcast([128, 4, 128]), ALU.mult, rd=[r_uq[set_], r_ctab], wr=[r_cder[2]])
            tt("pool", QDB[:, g4 * TB:(g4 + 1) * TB].rearrange("p (t c) -> p t c", c=128), q3,
               DQB.unsqueeze(1).to_broadcast([128, 4, 128]), ALU.mult, rd=[r_uq[set_], r_ctab], wr=[r_cder[3]])
            yield
        P.op("pool", lambda e: e.memset(S32[:, :], 0.0), wr=r_s32)
        P.op("pool", lambda e: e.memset(STF[:, 0, :], 0.0), wr=[r_cder[4]])
        P.op("pool", lambda e: e.memset(STB[:, 15, :], 0.0), wr=[r_cder[5]])
        for step in range(15):
            for d in range(2):
                c = step if d == 0 else 15 - step
                KD = KDF if d == 0 else KDB
                ST = STF if d == 0 else STB
                gcol = g128f if d == 0 else g128b
                s32 = S32[:, d * 128:(d + 1) * 128]
                bank = 4 + d
                for hh in range(2):
                    mm(PS[bank][:, hh * 128:(hh + 1) * 128], KD[:, c, :], V[:, hh, c, :], True, True,
                       rd=[r_cder[d], r_uv[set_]], wr=[r_ps[bank]])
                for hh in range(2):
                    rws = slice(64 * hh, 64 * hh + 64)
                    stt("dve", s32[rws, :], s32[rws, :], gcol[rws, :], PS[bank][rws, hh * 128:(hh + 1) * 128],
                        ALU.mult, ALU.add, rd=[r_ps[bank], r_s32[d], r_small], wr=[r_s32[d]])
                cn = c + 1 if d == 0 else c - 1
                cp("act", ST[:, cn, :], s32, rd=[r_s32[d]], wr=[r_cder[4 + d]])
            if step % 4 == 3:
                yield
        yield
        pend = [None]
        for hh in range(2):
            rws = slice(64 * hh, 64 * hh + 64)
            h = 2 * p + hh
            ch = 7 + h
            for cb in range(4):
                sbank = 4 + cb % 2
                for ci in range(4):
                    c = 4 * cb + ci
                    mm(PS[sbank][:, ci * 128:(ci + 1) * 128], kT[rws, c * 128:(c + 1) * 128],
                       qT[rws, c * 128:(c + 1) * 128], True, True, rd=[r_uq[set_], r_uk[set_]], wr=[r_ps[sbank]])
                pi = cb % 3
                tt("dve", pb(pi).rearrange("p (t c) -> p t c", c=128),
                   PS[sbank][:, :].rearrange("p (t c) -> p t c", c=128),
                   ctab(h).unsqueeze(1).to_broadcast([128, 4, 128]), ALU.mult, rd=[r_ps[sbank], r_ctab],
                   wr=[r_pb[pi]])
                for ci in range(4):
                    c = 4 * cb + ci
                    o = PS[6][:, ci * 128:(ci + 1) * 128]
                    mm(o, V[:, hh, c, :], pb(pi)[:, ci * 128:(ci + 1) * 128], True, False,
                       rd=[r_pb[pi], r_uv[set_]], wr=[r_ps[6]])
                    mm(o, STF[rws, c, :], QDF[rws, c * 128:(c + 1) * 128], False, False,
                       rd=[r_cder[4], r_cder[2]], wr=[r_ps[6]])
                    mm(o, STB[rws, c, :], QDB[rws, c * 128:(c + 1) * 128], False, True,
                       rd=[r_cder[5], r_cder[3]], wr=[r_ps[6]])
                if pend[0] is not None:
                    pend[0]()
                fo = 7 + (hh * 4 + cb) % 2
                cp("dve", ft(fo), PS[6][:, :], rd=[r_ps[6]], wr=[r_ft[fo]])

                def epi(fo=fo, ch=ch, cb=cb):
                    act(pb(3), ft(fo), AF.Square, rd=[r_ft[fo]], wr=[r_pb[3]])
                    mm(PS[1][:, :], ONES, pb(3), True, True, rd=[r_pb[3], r_const], wr=[r_ps[1]])
                    rstd_from(PS[1][:, :], r_ps[1], ft(6), r_ft[6], 1.0 / 128)
                    stt("dve", y_(ch)[:, cb * TB:(cb + 1) * TB], ft(fo), PCOL[:, 21:22], ft(6), ALU.mult, ALU.mult,
                        rd=[r_ft[fo], r_ft[6], r_pcol], wr=[r_y[ch][cb], r_yall[ch]])

                pend[0] = epi
                yield
        if pend[0] is not None:
            pend[0]()
        yield

    def layer_params(l):
        lam_init = 0.8 - 0.6 * math.exp(-0.3 * (l + l0))
        P.dma("sp", "par", lambda e: e.dma_start(out=PCOL[:, :], in_=pcol_d[l]), wr=[r_pcol])
        P.dma("sp", "par", lambda e: e.dma_start(out=CTAB[:, 8 * 128:12 * 128].rearrange("p (c n) -> p c n", n=128),
                                                   in_=arow_d[l].rearrange("c p n -> p c n")), wr=[r_ctab])
        rp = [r_pcol]
        tt("dve", SMALL[:, 2:3], PCOL[:, 22:23], PCOL[:, 23:24], ALU.mult, rd=rp, wr=[r_small])
        tt("dve", SMALL[:, 3:4], PCOL[:, 24:25], PCOL[:, 25:26], ALU.mult, rd=rp, wr=[r_small])
        mm(PS[2][:, 0:2], ONESF[0:64, :], SMALL[0:64, 2:4], True, True, rd=[r_small, r_const], wr=[r_ps[2]])
        act(SMALL[:, 4:6], PS[2][:, 0:2], AF.Exp, rd=[r_ps[2]], wr=[r_small])
        tt("dve", SMALL[:, 6:7], SMALL[:, 5:6], SMALL[:, 4:5], ALU.subtract, rd=[r_small], wr=[r_small])
        ts("dve", SMALL[:, 0:1], SMALL[:, 6:7], -lam_init, None, ALU.add, None, rd=[r_small], wr=[r_small])
        ts("dve", SMALL[:, 1:2], PCOL[:, 20:21], 1.0 - lam_init, None, ALU.mult, None, rd=rp + [r_small],
           wr=[r_small])
        act(SMALL[:, 12:16], PCOL[:, 26:30], AF.Exp, rd=rp + [r_small], wr=[r_small])
        ts("dve", SMALL[:, 12:16], SMALL[:, 12:16], -1.0, None, ALU.mult, None, rd=[r_small], wr=[r_small])
        act(SMALL[:, 8:12], SMALL[:, 12:16], AF.Exp, rd=[r_small], wr=[r_small], scale=128.0)
        act(SMALL[:, 16:24], PCOL[:, 30:38], AF.Exp, rd=rp + [r_small], wr=[r_small])
        ts("dve", SMALL[:, 16:24], SMALL[:, 16:24], -1.0, None, ALU.mult, None, rd=[r_small], wr=[r_small])
        rs = [r_small, r_const]
        for h in range(4):
            act(ctab(h), RELF, AF.Exp, rd=rs, wr=[r_ctab], scale=SMALL[:, 16 + h:17 + h])
            tt("dve", ctab(h), ctab(h), MKF8, ALU.mult, rd=[r_ctab, r_const], wr=[r_ctab])
            act(ctab(12), RELB, AF.Exp, rd=rs + [r_ctab], wr=[r_ctab], scale=SMALL[:, 20 + h:21 + h])
            tt("dve", ctab(12), ctab(12), MKB8, ALU.mult, rd=[r_ctab, r_const], wr=[r_ctab])
            tt("dve", ctab(h), ctab(h), ctab(12), ALU.add, rd=[r_ctab], wr=[r_ctab])
        for p in range(2):
            act(ctab(4 + p), POS1, AF.Exp, rd=rs + [r_ctab], wr=[r_ctab], scale=SMALL[:, 12 + p:13 + p])
            act(ctab(6 + p), POSB, AF.Exp, rd=rs + [r_ctab], wr=[r_ctab], scale=SMALL[:, 14 + p:15 + p])
            ts("dve", ctab(4 + p), ctab(4 + p), 0.125, None, ALU.mult, None, rd=[r_ctab], wr=[r_ctab])
            ts("dve", ctab(6 + p), ctab(6 + p), 0.125, None, ALU.mult, None, rd=[r_ctab], wr=[r_ctab])
            for d in range(2):
                dstt = ctab(8 + 2 * d + p)
                src = dstt
                act(dstt, src, AF.Exp, rd=rp + [r_ctab], wr=[r_ctab])
                ts("dve", dstt, dstt, -1.0, None, ALU.mult, None, rd=[r_ctab], wr=[r_ctab])
                act(dstt, dstt, AF.Exp, rd=[r_ctab, r_const], wr=[r_ctab], scale=(TOKCR if d == 0 else TOKC))

    def interleave(a, b):
        da = db = False
        while not (da and db):
            if not da:
                try:
                    next(a)
                except StopIteration:
                    da = True
            if b is None:
                db = True
            if not db:
                try:
                    next(b)
                except StopIteration:
                    db = True

    def drain(g):
        for _ in g:
            pass

    stop_state = {}
    attn_fn = {"A": attn_A, "B": attn_B, "C": attn_C}

    def ck(name):
        if stop == name:
            raise _Stop()

    def layer(l):
        layer_params(l)
        ck('params')
        for u in (units if stop not in ('xc', 'xc2') else units[7:]):
            push_unit_weights(l, u)
        norm(0)
        ck('norm1')
        P.barrier(dma_keys=("xsp",))
        ck('spill')
        if stop in ('xc', 'xc2'):
            drain(inproj(l, 7))
            ck('xc')
            drain(attn_C(l, 7))
            ck('xc2')
        drain(inproj(l, 0))
        ck('inproj0')
        for ui in range(len(units)):
            a = attn_fn[units[ui]["kind"]](l, ui)
            b = inproj(l, ui + 1) if ui + 1 < len(units) else None
            interleave(a, b)
            ck('u%d' % ui)
        P.barrier()
        ck('mix')
        if dbg and l == 0:
            P.dma("sp", "dbg", lambda e: e.dma_start(out=ydbg_d.rearrange("(c p) s -> p c s", p=128),
                                                       in_=Ybf.rearrange("p (c s) -> p c s", s=S)), rd=[], wr=[])
            P.barrier(dma_keys=("dbg",))
        for h in range(4):
            WS.push(w_in_d[l, C_CG + h])
        for j in range(8):
            WS.push(w_br_d[l, j], 11)
            for i in range(3):
                WS.push(w_in_d[l, C_GATE + 8 * i + j])
        for oc in range(8):
            WS.push(w_o_d[l, oc])

        def nbank():
            b = rot["ps"] % 8
            rot["ps"] += 1
            return b

        for h in range(4):
            slot = WS.pop()
            ch = 7 + h
            for tb in range(NB):
                bank = nbank()
                proj_fm(slot, tb, bank)
                j2 = tb % 2
                act(ub(j2), PS[bank][:, :], AF.Silu, rd=[r_ps[bank]], wr=[r_ub[j2]])
                ysl = y_(ch)[:, tb * TB:(tb + 1) * TB]
                tt("pool", ysl, ysl, ub(j2), ALU.mult, rd=[r_ub[j2], r_y[ch][tb], r_yall[ch]], wr=[r_y[ch][tb]])
            WS.release(slot)
        ybr = ((0, 3), (3, 7), (7, 11))
        cnt4 = 0
        for j in range(8):
            bslot = WS.pop()
            wb = wslot(bslot, 11)
            for i in range(3):
                gslot = WS.pop()
                c0, c1 = ybr[i]
                for tb in range(NB):
                    gb = nbank()
                    proj_fm(gslot, tb, gb)
                    sg = 4 + cnt4 % 2
                    tmp = 6 + cnt4 % 2
                    cnt4 += 1
                    act(ft(sg), PS[gb][:, :], AF.Sigmoid, rd=[r_ps[gb]], wr=[r_ft[sg]])
                    bb = nbank()
                    for ch in range(c0, c1):
                        mm(PS[bb][:, :], wb[:, ch, :], y_(ch)[:, tb * TB:(tb + 1) * TB], ch == c0, ch == c1 - 1,
                           rd=[r_w[bslot], r_y[ch][tb], r_yall[ch]], wr=[r_ps[bb]])
                    if i == 0:
                        tt("dve", ft(tb), PS[bb][:, :], ft(sg), ALU.mult, rd=[r_ps[bb], r_ft[sg]], wr=[r_ft[tb]])
                    else:
                        tt("dve", ft(tmp), PS[bb][:, :], ft(sg), ALU.mult, rd=[r_ps[bb], r_ft[sg]], wr=[r_ft[tmp]])
                        if i == 1:
                            tt("pool", ft(tb), ft(tb), ft(tmp), ALU.add, rd=[r_ft[tb], r_ft[tmp]], wr=[r_ft[tb]])
                        else:
                            tt("pool", m_(j, tb), ft(tb), ft(tmp), ALU.add, rd=[r_ft[tb], r_ft[tmp]],
                               wr=[r_m[j][tb]])
                WS.release(gslot)
            WS.release(bslot)
        P.barrier()
        if stop == '4a':
            for kc in range(KC):
                P.dma('pool', 'dbg', lambda e, kc=kc: e.dma_start(out=out_d[kc * 128:(kc + 1) * 128, :], in_=M[:, kc * S:(kc + 1) * S]), rd=[], wr=[])
            P.barrier(dma_keys=('dbg',))
            stop_state['skip_out'] = True
        ck('4a')
        for kc in range(KC):
            xsrc = xT_d if l == 0 else xsp_d
            P.dma("sp", "xld", lambda e, kc=kc, xsrc=xsrc: e.dma_start(out=XY[:, kc * S:(kc + 1) * S],
                                                                    in_=xsrc[kc * 128:(kc + 1) * 128, :]),
                  rd=[r_xsp], wr=r_x[kc])
        for hg in range(4):
            for c in range(8):
                WS.push(w_m1_d[l, hg * 8 + c])
            for oc in range(8):
                WS.push(w_m2_d[l, hg * 8 + oc])
        for oc in range(8):
            slot = WS.pop()
            w = wslot(slot)
            for tb in range(NB):
                bank = nbank()
                for kc in range(KC):
                    mm(PS[bank][:, :], w[:, kc, :], m_(kc, tb), kc == 0, kc == KC - 1,
                       rd=[r_w[slot], r_m[kc][tb]], wr=[r_ps[bank]])
                tt("dve", x_(oc, tb), x_(oc, tb), PS[bank][:, :], ALU.add, rd=[r_ps[bank], r_x[oc][tb]],
                   wr=[r_x[oc][tb]])
            WS.release(slot)
        ck('4b')
        norm(8)
        P.barrier()
        ck('norm2')
        for hg in range(4):
            for c in range(8):
                slot = WS.pop()
                for tb in range(NB):
                    bank = nbank()
                    proj_fm(slot, tb, bank)
                    j2 = (c * NB + tb) % 2
                    act(ft(j2), PS[bank][:, :], AF.Relu, rd=[r_ps[bank]], wr=[r_ft[j2]])
                    tt("pool", m_(c, tb), ft(j2), ft(j2), ALU.mult, rd=[r_ft[j2]], wr=[r_m[c][tb]])
                WS.release(slot)
            for oc in range(8):
                slot = WS.pop()
                w = wslot(slot)
                for tb in range(NB):
                    bank = nbank()
                    for c in range(8):
                        mm(PS[bank][:, :], w[:, c, :], m_(c, tb), c == 0, c == 7,
                           rd=[r_w[slot], r_m[c][tb]], wr=[r_ps[bank]])
                    tt("dve", x_(oc, tb), x_(oc, tb), PS[bank][:, :], ALU.add, rd=[r_ps[bank], r_x[oc][tb]],
                       wr=[r_x[oc][tb]])
                    if hg == 3 and l < nl - 1:
                        P.dma("sp", "xsp", lambda e, oc=oc, tb=tb: e.dma_start(
                            out=xsp_d[oc * 128:(oc + 1) * 128, tb * TB:(tb + 1) * TB], in_=x_(oc, tb)),
                            rd=[r_x[oc][tb]], wr=[r_xsp])
                WS.release(slot)
        P.barrier()
        if dbg and l == 0:
            for kc in range(KC):
                P.dma("sp", "dbg", lambda e, kc=kc: e.dma_start(out=xdbg_d[kc * 128:(kc + 1) * 128, :],
                                                               in_=XY[:, kc * S:(kc + 1) * S]), rd=r_x[kc], wr=[])

    try:
        for l in range(nl):
            layer(l)
    except _Stop:
        pass

    for kc in range(KC if not stop_state.get('skip_out') else 0):
        P.dma("sp", "out", lambda e, kc=kc: e.dma_start(out=out_d[kc * 128:(kc + 1) * 128, :],
                                                       in_=XY[:, kc * S:(kc + 1) * S]), rd=r_x[kc], wr=[])
    final_waits = tuple((k, n) for k, n in P.dcnt.items() if k in ("out", "dbg", "xsp"))
    P.q["sp"].append((final_waits, None, None))

    keys = list(P.ENG) + sorted(P.dcnt.keys())
    sems = {k: es.enter_context(nc.semaphore("s_" + k)) for k in keys}
    stats = {e: len(P.q[e]) for e in P.ENG}

    def run(e, name):
        for waits, fn, inc in P.q[name]:
            for k, n in waits:
                e.wait_ge(sems[k], n)
            if fn is not None:
                fn(e).then_inc(sems[inc[0]], inc[1])

    with nc.Block() as block:
        @block.tensor
        def _(e):
            run(e, "pe")

        @block.scalar
        def _(e):
            run(e, "act")

        @block.vector
        def _(e):
            run(e, "dve")

        @block.gpsimd
        def _(e):
            run(e, "pool")

        @block.sync
        def _(e):
            run(e, "sp")
    es.close()
    return nc, stats


def _rope_tables():
    pos = np.arange(S, dtype=np.float32)

    def tab(rot, theta, hd):
        inv = (1.0 / (np.float32(theta) ** (np.arange(0, rot, 2, dtype=np.float32) / np.float32(rot)))).astype(np.float32)
        ang = (pos[:, None] * inv[None, :]).astype(np.float32)
        c, s = np.cos(ang).astype(np.float32), np.sin(ang).astype(np.float32)
        r2 = rot // 2
        cosF = np.ones((hd, S), np.float32)
        sinF = np.zeros((hd, S), np.float32)
        cosF[0:r2] = c.T
        cosF[r2:2 * r2] = c.T
        sinF[0:r2] = s.T
        sinF[r2:2 * r2] = s.T
        return np.tile(cosF, (128 // hd, 1)), np.tile(sinF, (128 // hd, 1))

    cab, sab = tab(16, 500000.0, 64)
    cc, sc = tab(64, 10000.0, 64)
    return np.ascontiguousarray(np.stack([cab, sab, cc, sc]).astype(np.float32))


def _rot_mat(r2, hd=64):
    R = np.zeros((128, 128), np.float32)
    for b in range(128 // hd):
        for d in range(r2):
            R[b * hd + d + r2, b * hd + d] = -1.0
            R[b * hd + d, b * hd + d + r2] = 1.0
    return R


def _constants():
    ones = np.ones((128, 128), np.float32)
    bones = np.zeros((128, 128), np.float32)
    bones[0:64, 0:64] = 1.0
    bones[64:128, 64:128] = 1.0
    ident = np.eye(128, dtype=np.float32)
    cmats = np.stack([ones, bones, _rot_mat(8), _rot_mat(32), ident]).astype(np.float32)
    k = np.arange(128)[:, None]
    q = np.arange(512)[None, :]
    masks = []
    for rel in range(-1, 5):
        masks.append((np.abs(q - (128 * rel + k)) <= 64).astype(np.float32))
    band = (np.abs(np.arange(128)[None, :] - k) <= 64).astype(np.float32)
    masks.append(np.tile(band, (1, 4)))
    cmasks = np.stack(masks).astype(np.float32)
    m = np.arange(128)[:, None].astype(np.float32)
    n = np.arange(128)[None, :].astype(np.float32)
    relf = np.maximum(n - m, 0.0)
    relb = np.maximum(m - n, 0.0)
    mkf8 = (n >= m).astype(np.float32) * 0.125
    mkb8 = (m >= n).astype(np.float32) * 0.125
    pos1 = np.broadcast_to(n + 1.0, (128, 128))
    posb = np.broadcast_to(128.0 - n, (128, 128))
    tok = np.arange(128, dtype=np.float32)[:, None]
    cf = np.concatenate([relf, relb, mkf8, mkb8, pos1, posb, ones, tok, 127.0 - tok], axis=1).astype(np.float32)
    return cmats, cmasks, np.ascontiguousarray(cf)


def _prep_layers(inp, layers):
    f = lambda a: np.asarray(a, dtype=np.float32)
    nl = len(layers)
    w_in = f(inp["w_in"])[layers].reshape(nl, 8, 128, 57, 128).transpose(0, 3, 2, 1, 4)
    wbr = np.concatenate([f(inp["w_br_a"])[layers], f(inp["w_br_b"])[layers], f(inp["w_br_c"])[layers]], axis=1)
    w_br = wbr.reshape(nl, 11, 128, 8, 128).transpose(0, 3, 2, 1, 4)
    w_o = f(inp["w_o"])[layers].reshape(nl, 8, 128, 8, 128).transpose(0, 3, 2, 1, 4)
    w_m1 = f(inp["w_mlp1"])[layers].reshape(nl, 8, 128, 32, 128).transpose(0, 3, 2, 1, 4)
    w_m2 = f(inp["w_mlp2"])[layers].reshape(nl, 4, 8, 128, 8, 128).transpose(0, 1, 4, 3, 2, 5).reshape(nl, 32, 128, 8, 128)
    pcol = np.zeros((nl, 128, NPC), np.float32)
    arow = np.zeros((nl, 4, 128, 128), np.float32)
    for i, l in enumerate(layers):
        pcol[i, :, 0:8] = f(inp["norm1_g"])[l].reshape(8, 128).T
        pcol[i, :, 8:16] = f(inp["norm2_g"])[l].reshape(8, 128).T
        pcol[i, :, 16] = np.tile(f(inp["a_q_norm_g"])[l], 2)
        pcol[i, :, 17] = np.tile(f(inp["a_k_norm_g"])[l], 2)
        pcol[i, :, 18] = np.tile(f(inp["b_q_norm_g"])[l], 2)
        pcol[i, :, 19] = np.tile(f(inp["b_k_norm_g"])[l], 2)
        pcol[i, :, 20] = f(inp["b_out_norm_g"])[l]
        pcol[i, :, 21] = f(inp["c_out_norm_g"])[l]
        pcol[i, :, 22] = np.tile(f(inp["b_lambda_q1"])[l], 2)
        pcol[i, :, 23] = np.tile(f(inp["b_lambda_k1"])[l], 2)
        pcol[i, :, 24] = np.tile(f(inp["b_lambda_q2"])[l], 2)
        pcol[i, :, 25] = np.tile(f(inp["b_lambda_k2"])[l], 2)
        af, ab = f(inp["c_decay_f"])[l], f(inp["c_decay_b"])[l]
        for p in range(2):
            pcol[i, :, 26 + p] = np.repeat(af[2 * p:2 * p + 2], 64)
            pcol[i, :, 28 + p] = np.repeat(ab[2 * p:2 * p + 2], 64)
            arow[i, p] = np.repeat(af[2 * p:2 * p + 2], 64)[None, :]
            arow[i, 2 + p] = np.repeat(ab[2 * p:2 * p + 2], 64)[None, :]
        pcol[i, :, 30:34] = af[None, :]
        pcol[i, :, 34:38] = ab[None, :]
    c = np.ascontiguousarray
    return dict(w_in=c(w_in), w_br=c(w_br), w_o=c(w_o), w_m1=c(w_m1), w_m2=c(w_m2), pcol=pcol, arow=arow)


_CACHE = {}


def _get_prog(nl, l0):
    if (nl, l0) not in _CACHE:
        _CACHE[(nl, l0)] = build(nl, l0=l0)[0]
    return _CACHE[(nl, l0)]


FUSED = True


def kernel(**inputs):
    x = np.asarray(inputs["x"], dtype=np.float32)
    B = x.shape[0]
    cmats, cmasks, cf = _constants()
    rope = _rope_tables()
    consts = dict(c_mats=cmats, c_masks=cmasks, c_f32=cf, rope=rope)
    xT = [np.ascontiguousarray(x[b].T) for b in range(B)]
    groups = [list(range(DEPTH))] if FUSED else [[l] for l in range(DEPTH)]
    for layers in groups:
        nc = _get_prog(len(layers), layers[0])
        shared = _prep_layers(inputs, layers)
        shared.update(consts)
        in_maps = []
        for b in range(B):
            d = dict(shared)
            d["xT"] = xT[b]
            in_maps.append(d)
        res = run_bass_kernel_spmd(nc, in_maps, core_ids=list(range(B)))
        xT = [np.ascontiguousarray(np.asarray(r["outT"], dtype=np.float32)) for r in res.results]
    out = np.stack([t.T for t in xT], axis=0)
    return np.ascontiguousarray(out.astype(np.float32))
```

```python
import math
from contextlib import ExitStack
import numpy as np
import ml_dtypes
import concourse.bass as bass
import concourse.mybir as mybir
from concourse.bass_utils import run_bass_kernel_spmd

F32, BF16 = mybir.dt.float32, mybir.dt.bfloat16
ALU, AF = mybir.AluOpType, mybir.ActivationFunctionType

S = 2048
D = 1024
KC = 8
NB = 4
TB = 512
DEPTH = 4
EPS = 1e-6
NPC = 38
C_AQ, C_AK, C_AV = 0, 3, 6
C_BQ, C_BK, C_BV = 9, 13, 17
C_CQ, C_CK, C_CV, C_CG = 21, 23, 25, 29
C_GATE = 33
NWSLOT = 5


class Res:
    __slots__ = ("w", "r", "x")

    def __init__(self, x=False):
        self.w = None
        self.r = {}
        self.x = x


class Prog:
    ENG = ("pe", "act", "dve", "pool", "sp")

    def __init__(self):
        self.q = {e: [] for e in self.ENG}
        self.cnt = {e: 0 for e in self.ENG}
        self.seen = {e: {} for e in self.ENG}
        self.dcnt = {}

    def _need(self, eng, dep, waits, raw):
        if dep is None:
            return
        k, n = dep
        if k == eng:
            if eng == "pe":
                return
        if k in self.dcnt:
            n = max(n, self.dcnt[k])
        if self.seen[eng].get(k, 0) >= n:
            return
        waits[k] = max(waits.get(k, 0), n)

    def _deps(self, eng, rd, wr):
        waits = {}
        for r in rd:
            self._need(eng, r.w, waits, True)
        for w in wr:
            self._need(eng, w.w, waits, False)
            for k, n in w.r.items():
                self._need(eng, (k, n), waits, False)
        for k, n in waits.items():
            self.seen[eng][k] = n
        return tuple(waits.items())

    def op(self, eng, fn, rd=(), wr=()):
        wr = list(wr) + [r for r in rd if r.x]
        rd = [r for r in rd if not r.x]
        waits = self._deps(eng, rd, wr)
        self.cnt[eng] += 1
        n = self.cnt[eng]
        self.q[eng].append((waits, fn, (eng, 1)))
        for r in rd:
            r.r[eng] = max(r.r.get(eng, 0), n)
        for w in wr:
            w.w = (eng, n)
            w.r = {}

    def dma(self, qeng, semkey, fn, rd=(), wr=()):
        waits = self._deps(qeng, rd, wr)
        self.dcnt[semkey] = self.dcnt.get(semkey, 0) + 16
        n = self.dcnt[semkey]
        self.q[qeng].append((waits, fn, (semkey, 16)))
        for r in rd:
            r.r[semkey] = max(r.r.get(semkey, 0), n)
        for w in wr:
            w.w = (semkey, n)
            w.r = {}

    def barrier(self, dma_keys=()):
        for e in self.ENG:
            waits = {}
            for f in self.ENG:
                if f != e and self.cnt[f] > self.seen[e].get(f, 0):
                    waits[f] = self.cnt[f]
            for k in dma_keys:
                if self.dcnt.get(k, 0) > self.seen[e].get(k, 0):
                    waits[k] = self.dcnt[k]
            for k, n in waits.items():
                self.seen[e][k] = n
            if waits:
                self.q[e].append((tuple(waits.items()), None, None))


class _Stop(Exception):
    pass


def build(nl, dbg=False, stop=None, l0=0):
    nc = bass.Bass("TRN2", target_bir_lowering=False)

    def din(name, shape, dt=F32):
        return nc.dram_tensor(name, list(shape), dt, kind="ExternalInput").ap()

    xT_d = din("xT", [D, S])
    w_in_d = din("w_in", [nl, 57, 128, 8, 128])
    w_br_d = din("w_br", [nl, 8, 128, 11, 128])
    w_o_d = din("w_o", [nl, 8, 128, 8, 128])
    w_m1_d = din("w_m1", [nl, 32, 128, 8, 128])
    w_m2_d = din("w_m2", [nl, 32, 128, 8, 128])
    pcol_d = din("pcol", [nl, 128, NPC])
    arow_d = din("arow", [nl, 4, 128, 128])
    cmat_d = din("c_mats", [5, 128, 128])
    cmask_d = din("c_masks", [7, 128, 512])
    cf32_d = din("c_f32", [128, 7 * 128 + 2])
    rope_d = din("rope", [4, 128, S])
    out_d = nc.dram_tensor("outT", [D, S], F32, kind="ExternalOutput").ap()
    xsp_d = nc.dram_tensor("xspill", [D, S], F32, kind="Internal").ap()
    if dbg:
        ydbg_d = nc.dram_tensor("ydbg", [11 * 128, S], BF16, kind="ExternalOutput").ap()
        xdbg_d = nc.dram_tensor("xdbg", [D, S], F32, kind="ExternalOutput").ap()

    P = Prog()
    es = ExitStack()

    def sb(name, shape, dt):
        return es.enter_context(nc.sbuf_tensor(name, list(shape), dt))

    XN = sb("XN", [128, KC * S], BF16)
    XY = sb("XY", [128, KC * S], F32)
    M = sb("M", [128, KC * S], BF16)
    WR = sb("WR", [128, NWSLOT * 11 * 128], BF16)
    CMB = sb("CMB", [128, 5 * 128], BF16)
    MASKS = sb("MASKS", [128, 7 * 512], BF16)
    CF = sb("CF", [128, 7 * 128 + 2], F32)
    PCOL = sb("PCOL", [128, NPC], F32)
    SMALL = sb("SMALL", [128, 64], F32)
    CTAB = sb("CTAB", [128, 13 * 128], F32)
    CDER = sb("CDER", [128, 6 * 2048], BF16)
    S32 = sb("S32", [128, 2 * 128], F32)
    SQB = sb("SQB", [128, 2 * 512], BF16)
    UB = sb("UB", [128, 2 * 512], BF16)
    FT = sb("FT", [128, 9 * 512], F32)

    Ybf = XY[:, 0:11 * 1024].bitcast(BF16)
    ROPE = XY[:, 11 * 1024:11 * 1024 + 4096]
    PB = XY[:, 15 * 1024:16 * 1024].bitcast(BF16)
    DTOT = CDER[:, 4 * 2048:6 * 2048].bitcast(F32)

    def x_(kc, tb):
        return XY[:, kc * S + tb * TB: kc * S + (tb + 1) * TB]

    def xn_(kc, tb):
        return XN[:, kc * S + tb * TB: kc * S + (tb + 1) * TB]

    def xnfull(kc):
        return XN[:, kc * S:(kc + 1) * S]

    def y_(ch):
        return Ybf[:, ch * S:(ch + 1) * S]

    def m_(kc, tb):
        return M[:, kc * S + tb * TB: kc * S + (tb + 1) * TB]

    def uq(s):
        return M[:, s * 8192: s * 8192 + 2048]

    def uk(s):
        return M[:, s * 8192 + 2048: s * 8192 + 4096]

    def uv(s):
        return M[:, s * 8192 + 4096: s * 8192 + 8192].rearrange("p (h t c) -> p h t c", h=2, c=128)

    def kz(s, m):
        return uk(s) if m == 0 else M[:, s * 8192 + 6144: s * 8192 + 8192]

    def wslot(s, nk=8):
        return WR[:, s * 1408: s * 1408 + nk * 128].rearrange("p (k n) -> p k n", n=128)

    ONES = CMB[:, 0:128]
    BONES = CMB[:, 128:256]
    RAB = CMB[:, 256:384]
    RC = CMB[:, 384:512]
    IDENT = CMB[:, 512:640]

    def mask_(i):
        return MASKS[:, i * 512:(i + 1) * 512]

    RELF, RELB = CF[:, 0:128], CF[:, 128:256]
    MKF8, MKB8 = CF[:, 256:384], CF[:, 384:512]
    POS1, POSB = CF[:, 512:640], CF[:, 640:768]
    ONESF = CF[:, 768:896]
    TOKC, TOKCR = CF[:, 896:897], CF[:, 897:898]

    def ctab(i):
        return CTAB[:, i * 128:(i + 1) * 128]

    def cder(i):
        return CDER[:, i * 2048:(i + 1) * 2048]

    KDF, KDB = cder(0).rearrange("p (t c) -> p t c", c=128), cder(1).rearrange("p (t c) -> p t c", c=128)
    QDF, QDB = cder(2), cder(3)
    STF, STB = cder(4).rearrange("p (t c) -> p t c", c=128), cder(5).rearrange("p (t c) -> p t c", c=128)

    def pb(i):
        return PB[:, i * 512:(i + 1) * 512]

    def sqb(i):
        return SQB[:, i * 512:(i + 1) * 512]

    def ub(i):
        return UB[:, i * 512:(i + 1) * 512]

    def ft(i):
        return FT[:, i * 512:(i + 1) * 512]

    PS = [es.enter_context(nc.psum_tensor("ps%d" % i, [128, 512], F32)) for i in range(8)]

    r_x = [[Res() for _ in range(NB)] for _ in range(KC)]
    r_xn = [Res() for _ in range(NB)]
    r_y = [[Res() for _ in range(NB)] for _ in range(11)]
    r_yall = [Res() for _ in range(11)]
    r_m = [[Res() for _ in range(NB)] for _ in range(KC)]
    r_uq = [Res(), Res()]
    r_uk = [Res(), Res()]
    r_uv = [Res(), Res()]
    r_w = [Res() for _ in range(NWSLOT)]
    r_ps = [Res(True) for _ in range(8)]
    r_pb = [Res() for _ in range(4)]
    r_sqb = [Res(), Res()]
    r_ub = [Res(), Res()]
    r_ft = [Res() for _ in range(9)]
    r_rope = Res()
    r_acc = Res()
    r_const = Res()
    r_pcol = Res()
    r_small = Res()
    r_ctab = Res()
    r_cder = [Res() for _ in range(6)]
    r_s32 = [Res(), Res()]
    r_xsp = Res()

    wstate = {"n": 0}

    def load_w(dram_ap, nk, s):
        dst = wslot(s, nk)
        P.dma("pool", "w%d" % s, lambda e, dst=dst, src=dram_ap: e.dma_start(out=dst, in_=src), wr=[r_w[s]])
        return s

    class WStream:
        def __init__(self):
            self.pending = []
            self.loaded = []
            self.free = list(range(NWSLOT))

        def push(self, ap, nk=8):
            self.pending.append((ap, nk))

        def prefetch(self):
            while self.pending and self.free:
                ap, nk = self.pending.pop(0)
                self.loaded.append(load_w(ap, nk, self.free.pop(0)))

        def pop(self):
            self.prefetch()
            return self.loaded.pop(0)

        def release(self, s):
            self.free.append(s)
            self.prefetch()

    WS = WStream()

    def mm(out, lhsT, rhs, start, stop, rd, wr):
        P.op("pe", lambda e: e.matmul(out, lhsT=lhsT, rhs=rhs, start=start, stop=stop), rd=rd, wr=wr)

    def act(out, in_, func, rd, wr, scale=None, bias=None):
        kw = {}
        if scale is not None:
            kw["scale"] = scale
        if bias is not None:
            kw["bias"] = bias
        P.op("act", lambda e: e.activation(out=out, in_=in_, func=func, **kw), rd=rd, wr=wr)

    def tt(eng, out, in0, in1, op, rd, wr):
        P.op(eng, lambda e: e.tensor_tensor(out=out, in0=in0, in1=in1, op=op), rd=rd, wr=wr)

    def stt(eng, out, in0, scalar, in1, op0, op1, rd, wr):
        P.op(eng, lambda e: e.scalar_tensor_tensor(out=out, in0=in0, scalar=scalar, in1=in1, op0=op0, op1=op1),
             rd=rd, wr=wr)

    def ts(eng, out, in0, s1, s2, op0, op1, rd, wr):
        if op1 is None:
            P.op(eng, lambda e: e.tensor_scalar(out=out, in0=in0, scalar1=s1, scalar2=None, op0=op0), rd=rd, wr=wr)
        else:
            P.op(eng, lambda e: e.tensor_scalar(out=out, in0=in0, scalar1=s1, scalar2=s2, op0=op0, op1=op1),
                 rd=rd, wr=wr)

    def cp(eng, out, in_, rd, wr):
        if eng == "act":
            act(out, in_, AF.Copy, rd, wr)
        else:
            P.op(eng, lambda e: e.tensor_copy(out=out, in_=in_), rd=rd, wr=wr)

    def recip(out, in_, rd, wr):
        P.op("dve", lambda e: e.reciprocal(out=out, in_=in_), rd=rd, wr=wr)

    def rstd_from(ps_ap, r_in, ftile, r_ftile, inv_n):
        act(ftile, ps_ap, AF.Ln, rd=[r_in], wr=[r_ftile], scale=inv_n, bias=EPS)
        act(ftile, ftile, AF.Exp, rd=[r_ftile], wr=[r_ftile], scale=-0.5)

    rot = {"ps": 0, "alt": 0}
    SBANK = (4, 5, 3)

    P.dma("pool", "cst", lambda e: e.dma_start(out=CMB[:, :].rearrange("p (c n) -> p c n", n=128),
                                                  in_=cmat_d.rearrange("c p n -> p c n")), wr=[r_const])
    P.dma("pool", "cst", lambda e: e.dma_start(out=MASKS[:, :].rearrange("p (c n) -> p c n", n=512),
                                                  in_=cmask_d.rearrange("c p n -> p c n")), wr=[r_const])
    P.dma("sp", "cstf", lambda e: e.dma_start(out=CF[:, :], in_=cf32_d), wr=[r_const])
    P.barrier(dma_keys=("cst", "cstf"))
    for kc in range(KC):
        P.dma("sp", "xld", lambda e, kc=kc: e.dma_start(out=XY[:, kc * S:(kc + 1) * S],
                                                       in_=xT_d[kc * 128:(kc + 1) * 128, :]),
              wr=r_x[kc])

    def norm(gcol0):
        for tb in range(NB):
            for kc in range(KC):
                i = kc % 2
                act(sqb(i), x_(kc, tb), AF.Square, rd=[r_x[kc][tb]], wr=[r_sqb[i]])
                mm(PS[2][:, :], ONES, sqb(i), kc == 0, kc == KC - 1, rd=[r_sqb[i], r_const], wr=[r_ps[2]])
            fr = tb % 2
            rstd_from(PS[2][:, :], r_ps[2], ft(fr), r_ft[fr], 1.0 / D)
            for kc in range(KC):
                stt("dve", xn_(kc, tb), x_(kc, tb), PCOL[:, gcol0 + kc: gcol0 + kc + 1], ft(fr), ALU.mult, ALU.mult,
                    rd=[r_x[kc][tb], r_ft[fr], r_pcol], wr=[r_xn[tb]])

    def proj_fm(slot, tb, bank):
        w = wslot(slot)
        for kc in range(KC):
            mm(PS[bank][:, :], w[:, kc, :], xn_(kc, tb), kc == 0, kc == KC - 1,
               rd=[r_w[slot], r_xn[tb]], wr=[r_ps[bank]])

    def perm_dst(buf, tb, delta):
        if delta == 1:
            return buf[:, tb * TB:(tb + 1) * TB], None
        n0 = tb * TB // delta
        return buf.rearrange("p (r n) -> p n r", r=delta)[:, n0:n0 + TB // delta, :], delta

    def srcv(ap, delta):
        if delta is None:
            return ap
        return ap.rearrange("p (n r) -> p n r", r=delta)

    def qk_post(bank, tb, gcol, rmat, normed, dsts, delta, feng="pool", aux=2):
        j = rot["alt"] % 2
        rot["alt"] += 1
        pj = PS[bank][:, :]
        cos = ROPE[:, tb * TB:(tb + 1) * TB]
        sin = ROPE[:, S + tb * TB: S + (tb + 1) * TB]
        if normed:
            act(sqb(j), pj, AF.Square, rd=[r_ps[bank]], wr=[r_sqb[j]])
            mm(PS[aux][:, :], BONES, sqb(j), True, True, rd=[r_sqb[j], r_const], wr=[r_ps[aux]])
            yield
            rstd_from(PS[aux][:, :], r_ps[aux], ft(j), r_ft[j], 1.0 / 64)
            stt("dve", ub(j), pj, PCOL[:, gcol:gcol + 1], ft(j), ALU.mult, ALU.mult,
                rd=[r_ps[bank], r_ft[j], r_pcol], wr=[r_ub[j]])
            yield
            mm(PS[aux][:, :], rmat, ub(j), True, True, rd=[r_ub[j], r_const], wr=[r_ps[aux]])
            tt(feng, ft(2 + j), ub(j), cos, ALU.mult, rd=[r_ub[j], r_rope], wr=[r_ft[2 + j]])
        else:
            cp("act", ub(j), pj, rd=[r_ps[bank]], wr=[r_ub[j]])
            tt("dve", ft(2 + j), pj, cos, ALU.mult, rd=[r_ps[bank], r_rope], wr=[r_ft[2 + j]])
            yield
            mm(PS[aux][:, :], rmat, ub(j), True, True, rd=[r_ub[j], r_const], wr=[r_ps[aux]])
        yield
        tt("dve", ft(4 + j), PS[aux][:, :], sin, ALU.mult, rd=[r_ps[aux], r_rope], wr=[r_ft[4 + j]])
        for buf, rows, res in dsts:
            dst, dl = perm_dst(buf, tb, delta)
            a0, a1 = srcv(ft(2 + j), dl), srcv(ft(4 + j), dl)
            if rows is not None:
                dst, a0, a1 = dst[rows[0]:rows[1]], a0[rows[0]:rows[1]], a1[rows[0]:rows[1]]
            tt(feng, dst, a0, a1, ALU.add, rd=[r_ft[2 + j], r_ft[4 + j]], wr=[res])

    def v_proj(slot, set_, coloff, delta):
        w = wslot(slot)
        V = uv(set_)
        for g4 in range(4):
            bank = 0
            for ti in range(4):
                t = g4 * 4 + ti
                for kc in range(KC):
                    if delta == 1:
                        lhs = xnfull(kc)[:, t * 128:(t + 1) * 128]
                    elif delta == 4:
                        lhs = xnfull(kc).rearrange("p (n r) -> p r n", r=4)[:, t // 4, (t % 4) * 128:(t % 4 + 1) * 128]
                    else:
                        lhs = xnfull(kc).rearrange("p (n r) -> p r n", r=16)[:, t, :]
                    mm(PS[bank][:, ti * 128:(ti + 1) * 128], lhs, w[:, kc, :], kc == 0, kc == KC - 1,
                       rd=[r_w[slot]] + r_xn, wr=[r_ps[bank]])
            cp("act", V[:, coloff // 128, g4 * 4:(g4 + 1) * 4, :],
               PS[bank][:, :].rearrange("p (t c) -> p t c", c=128), rd=[r_ps[bank]], wr=[r_uv[set_]])
            yield

    units = []
    for g in range(3):
        units.append(dict(kind="A", idx=g, q=C_AQ + g, k=C_AK + g, v=[C_AV + g], gq=16, gk=17, delta=(1, 4, 16)[g]))
    for h in range(4):
        units.append(dict(kind="B", idx=h, q=C_BQ + h, k=C_BK + h, v=[C_BV + h], gq=18, gk=19, delta=1))
    for p in range(2):
        units.append(dict(kind="C", idx=p, q=C_CQ + p, k=C_CK + p, v=[C_CV + 2 * p, C_CV + 2 * p + 1], gq=0, gk=0,
                          delta=1))

    def push_unit_weights(l, u):
        WS.push(w_in_d[l, u["q"]])
        WS.push(w_in_d[l, u["k"]])
        for c in u["v"]:
            WS.push(w_in_d[l, c])

    def rope_load(which):
        P.dma("sp", "rope", lambda e: e.dma_start(out=ROPE[:, 0:S], in_=rope_d[2 * which]), wr=[r_rope])
        P.dma("sp", "rope", lambda e: e.dma_start(out=ROPE[:, S:2 * S], in_=rope_d[2 * which + 1]), wr=[r_rope])

    def inproj(l, ui):
        u = units[ui]
        set_ = ui % 2
        normed = u["kind"] != "C"
        rmat = RC if u["kind"] == "C" else RAB
        if ui == 0:
            rope_load(0)
        if ui == 7:
            rope_load(1)
        feng = "dve" if ui == 0 else "pool"
        width = 2 if ui <= 3 else 1
        qslot = WS.pop()
        kslot = WS.pop()
        gens = []

        def cb_gen(slot, tb, gcol, dsts, k):
            bank = k % width
            proj_fm(slot, tb, bank)
            yield
            yield from qk_post(bank, tb, gcol, rmat, normed, dsts, u["delta"], feng, 2 + k % width)

        k = 0
        for which in ("q", "k"):
            if which == "q":
                dsts, gcol, slot = [(uq(set_), None, r_uq[set_])], u["gq"], qslot
            elif u["kind"] == "C":
                dsts, gcol, slot = [(uk(set_), None, r_uk[set_])], u["gk"], kslot
            else:
                k0, k1 = kz(set_, 0), kz(set_, 1)
                P.op("pool", lambda e, k0=k0: e.memset(k0[64:128, :], 0.0), wr=[r_uk[set_]])
                P.op("pool", lambda e, k1=k1: e.memset(k1[0:64, :], 0.0), wr=[r_uv[set_]])
                dsts, gcol, slot = [(k0, (0, 64), r_uk[set_]), (k1, (64, 128), r_uv[set_])], u["gk"], kslot
            for tb in range(NB):
                gens.append(cb_gen(slot, tb, gcol, dsts, k))
                k += 1
        active = []
        while gens or active:
            while gens and len(active) < width:
                active.append(gens.pop(0))
            for g_ in list(active):
                try:
                    next(g_)
                except StopIteration:
                    active.remove(g_)
            yield
        WS.release(qslot)
        WS.release(kslot)
        for vi, c in enumerate(u["v"]):
            slot = WS.pop()
            yield from v_proj(slot, set_, vi * 128, u["delta"])
            WS.release(slot)

    def attn_B(l, ui):
        u = units[ui]
        h = u["idx"]
        set_ = ui % 2
        qT, V = uq(set_), uv(set_)
        rdq = [r_uq[set_], r_uk[set_], r_uv[set_]]
        ch = 3 + h
        pending = []

        def make_epilogue(qb):
            def s1():
                act(sqb(0), ft(7), AF.Square, rd=[r_ft[7]], wr=[r_sqb[0]])

            def s2():
                mm(PS[1][:, :], ONES, sqb(0), True, True, rd=[r_sqb[0], r_const], wr=[r_ps[1]])

            def s3():
                rstd_from(PS[1][:, :], r_ps[1], ft(6), r_ft[6], 1.0 / 128)
                stt("dve", y_(ch)[:, qb * TB:(qb + 1) * TB], ft(7), SMALL[:, 1:2], ft(6), ALU.mult, ALU.mult,
                    rd=[r_ft[7], r_ft[6], r_small], wr=[r_y[ch][qb], r_yall[ch]])

            return [(9, s1), (11, s2), (13, s3)]

        for qb in range(NB):
            for m in range(2):
                def smm(kt):
                    bank = SBANK[kt % 3]
                    mm(PS[bank][:, :], kz(set_, m)[:, kt * 128:(kt + 1) * 128], qT[:, qb * TB:(qb + 1) * TB], True, True,
                       rd=rdq, wr=[r_ps[bank]])

                smm(0)
                smm(1)
                for kt in range(16):
                    if kt + 2 < 16:
                        smm(kt + 2)
                    bank = SBANK[kt % 3]
                    pi = kt % 4
                    act(pb(pi), PS[bank][:, :], AF.Exp, rd=[r_ps[bank]], wr=[r_pb[pi]], scale=0.125)
                    mm(PS[6][:, :], V[:, 0, kt, :], pb(pi), kt == 0, kt == 15, rd=[r_pb[pi], r_uv[set_]],
                       wr=[r_ps[6]])
                    mm(PS[7][:, :], ONES, pb(pi), kt == 0, kt == 15, rd=[r_pb[pi], r_const], wr=[r_ps[7]])
                    while pending and pending[0][0] <= kt:
                        pending.pop(0)[1]()
                    if kt % 4 == 3 and kt < 15:
                        yield
                cp("dve", ft(6), PS[7][:, :], rd=[r_ps[7]], wr=[r_ft[6]])
                if m == 0:
                    cp("dve", ft(7), PS[6][:, :], rd=[r_ps[6]], wr=[r_ft[7]])
                    recip(ft(6), ft(6), rd=[r_ft[6]], wr=[r_ft[6]])
                    tt("dve", ft(7), ft(7), ft(6), ALU.mult, rd=[r_ft[7], r_ft[6]], wr=[r_ft[7]])
                else:
                    cp("dve", ft(8), PS[6][:, :], rd=[r_ps[6]], wr=[r_ft[8]])
                    recip(ft(6), ft(6), rd=[r_ft[6]], wr=[r_ft[6]])
                    tt("dve", ft(8), ft(8), ft(6), ALU.mult, rd=[r_ft[8], r_ft[6]], wr=[r_ft[8]])
                    stt("dve", ft(7), ft(8), SMALL[:, 0:1], ft(7), ALU.mult, ALU.add,
                        rd=[r_ft[8], r_ft[7], r_small], wr=[r_ft[7]])
                    pending = make_epilogue(qb)
                yield
        for _, fn in pending:
            fn()
        yield

    def attn_A(l, ui):
        u = units[ui]
        g = u["idx"]
        delta = u["delta"]
        set_ = ui % 2
        qT, kT, V = uq(set_), uk(set_), uv(set_)
        rdq = [r_uq[set_], r_uk[set_], r_uv[set_]]
        for s2 in range(2):
            rows = slice(64 * s2, 64 * s2 + 64)
            for qb in range(NB):
                if g < 2:
                    rels = list(range(-1, 5)) if g == 0 else list(range(4))
                    kts = [(4 * qb + r, r) for r in rels if 0 <= 4 * qb + r < 16]
                    n = len(kts)

                    def win(i):
                        rel = kts[i][1]
                        if i == 0:
                            return 0, TB
                        return max(0, 128 * rel - 64), min(TB, 128 * rel + 192)

                    def smm(i):
                        kt, rel = kts[i]
                        bank = 4 + i % 2
                        lo, hi = win(i)
                        mm(PS[bank][:, lo:hi], kz(set_, s2)[:, kt * 128:(kt + 1) * 128],
                           qT[:, qb * TB + lo:qb * TB + hi], True, True, rd=rdq, wr=[r_ps[bank]])

                    smm(0)
                    for i in range(n):
                        if i + 1 < n:
                            smm(i + 1)
                        kt, rel = kts[i]
                        bank = 4 + i % 2
                        pi = i % 4
                        lo, hi = win(i)
                        act(pb(pi)[:, lo:hi], PS[bank][:, lo:hi], AF.Exp, rd=[r_ps[bank]], wr=[r_pb[pi]], scale=0.125)
                        tt("dve", pb(pi)[:, lo:hi], pb(pi)[:, lo:hi], mask_(rel + 1)[:, lo:hi], ALU.mult,
                           rd=[r_pb[pi], r_const], wr=[r_pb[pi]])
                        mm(PS[6][:, lo:hi], V[:, 0, kt, :], pb(pi)[:, lo:hi], i == 0, i == n - 1,
                           rd=[r_pb[pi], r_uv[set_]], wr=[r_ps[6]])
                        mm(PS[7][:, lo:hi], ONES, pb(pi)[:, lo:hi], i == 0, i == n - 1, rd=[r_pb[pi], r_const],
                           wr=[r_ps[7]])
                        if i < n - 1:
                            yield
                else:
                    bank = 4 + qb % 2
                    for rel in range(4):
                        kt = 4 * qb + rel
                        mm(PS[bank][:, rel * 128:(rel + 1) * 128], kz(set_, s2)[:, kt * 128:(kt + 1) * 128],
                           qT[:, kt * 128:(kt + 1) * 128], True, True, rd=rdq, wr=[r_ps[bank]])
                    pi = qb % 3
                    act(pb(pi), PS[bank][:, :], AF.Exp, rd=[r_ps[bank]], wr=[r_pb[pi]], scale=0.125)
                    tt("dve", pb(pi), pb(pi), mask_(6), ALU.mult, rd=[r_pb[pi], r_const], wr=[r_pb[pi]])
                    for rel in range(4):
                        kt = 4 * qb + rel
                        mm(PS[6][:, rel * 128:(rel + 1) * 128], V[:, 0, kt, :], pb(pi)[:, rel * 128:(rel + 1) * 128],
                           True, True, rd=[r_pb[pi], r_uv[set_]], wr=[r_ps[6]])
                        mm(PS[7][:, rel * 128:(rel + 1) * 128], ONES, pb(pi)[:, rel * 128:(rel + 1) * 128],
                           True, True, rd=[r_pb[pi], r_const], wr=[r_ps[7]])
                if delta == 1:
                    ydst = y_(g)[rows, qb * TB:(qb + 1) * TB]
                    ddst = DTOT[rows, qb * TB:(qb + 1) * TB]
                    nsrc, dsrc = PS[6][rows, :], PS[7][rows, :]
                elif delta == 4:
                    ydst = y_(g).rearrange("p (n r) -> p r n", r=4)[rows, qb, :]
                    ddst = DTOT[:, :].rearrange("p (n r) -> p r n", r=4)[rows, qb, :]
                    nsrc, dsrc = PS[6][rows, :], PS[7][rows, :]
                else:
                    ydst = y_(g).rearrange("p (n r) -> p r n", r=16)[rows, 4 * qb:4 * qb + 4, :]
                    ddst = DTOT[:, :].rearrange("p (n r) -> p r n", r=16)[rows, 4 * qb:4 * qb + 4, :]
                    nsrc = PS[6][rows, :].rearrange("p (a b) -> p a b", a=4)
                    dsrc = PS[7][rows, :].rearrange("p (a b) -> p a b", a=4)
                cp("act", ydst, nsrc, rd=[r_ps[6]], wr=[r_yall[g]] + r_y[g])
                if g == 0:
                    cp("dve", ddst, dsrc, rd=[r_ps[7]], wr=[r_cder[4], r_cder[5]])
                else:
                    tt("dve", ddst, ddst, dsrc, ALU.add, rd=[r_ps[7], r_cder[4], r_cder[5]], wr=[r_cder[4], r_cder[5]])
                yield
        if g == 2:
            for tb in range(NB):
                recip(ft(6), DTOT[:, tb * TB:(tb + 1) * TB], rd=[r_cder[4], r_cder[5]], wr=[r_ft[6]])
                for gg in range(3):
                    ysl = y_(gg)[:, tb * TB:(tb + 1) * TB]
                    tt("pool", ysl, ysl, ft(6), ALU.mult, rd=[r_ft[6], r_yall[gg]] + r_y[gg], wr=[r_y[gg][tb], r_yall[gg]])
                yield

    def attn_C(l, ui):
        u = units[ui]
        p = u["idx"]
        set_ = ui % 2
        qT, kT, V = uq(set_), uk(set_), uv(set_)
        DQF, DQB, DKF, DKB = ctab(4 + p), ctab(6 + p), ctab(8 + p), ctab(10 + p)
        g128f, g128b = SMALL[:, 8 + p:9 + p], SMALL[:, 10 + p:11 + p]
        for g4 in range(4):
            bank = 3
            tp = PS[bank][:, :].bitcast(BF16)[:, 0:512]
            for ti in range(4):
                t = g4 * 4 + ti
                P.op("pe", lambda e, t=t, ti=ti, tp=tp: e.transpose(tp[:, ti * 128:(ti + 1) * 128],
                                                                      kT[:, t * 128:(t + 1) * 128], IDENT),
                     rd=[r_uk[set_], r_const], wr=[r_ps[bank]])
            tp3 = tp.rearrange("p (t c) -> p t c", c=128)
            tt("dve", KDF[:, g4 * 4:(g4 + 1) * 4, :], tp3, DKF.unsqueeze(1).to_broadcast([128, 4, 128]), ALU.mult,
               rd=[r_ps[bank], r_ctab], wr=[r_cder[0]])
            tt("dve", KDB[:, g4 * 4:(g4 + 1) * 4, :], tp3, DKB.unsqueeze(1).to_broadcast([128, 4, 128]), ALU.mult,
               rd=[r_ps[bank], r_ctab], wr=[r_cder[1]])
            q3 = qT[:, g4 * TB:(g4 + 1) * TB].rearrange("p (t c) -> p t c", c=128)
            tt("pool", QDF[:, g4 * TB:(g4 + 1) * TB].rearrange("p (t c) -> p t c", c=128), q3,
               DQF.unsqueeze(1).to_broadcast([128, 4, 128]), ALU.mult, rd=[r_uq[set_], r_ctab], wr=[r_cder[2]])
            tt("pool", QDB[:, g4 * TB:(g4 + 1) * TB].rearrange("p (t c) -> p t c", c=128), q3,
               DQB.unsqueeze(1).to_broadcast([128, 4, 128]), ALU.mult, rd=[r_uq[set_], r_ctab], wr=[r_cder[3]])
            yield
        P.op("pool", lambda e: e.memset(S32[:, :], 0.0), wr=r_s32)
        P.op("pool", lambda e: e.memset(STF[:, 0, :], 0.0), wr=[r_cder[4]])
        P.op("pool", lambda e: e.memset(STB[:, 15, :], 0.0), wr=[r_cder[5]])
        for step in range(15):
            for d in range(2):
                c = step if d == 0 else 15 - step
                KD = KDF if d == 0 else KDB
                ST = STF if d == 0 else STB
                gcol = g128f if d == 0 else g128b
                s32 = S32[:, d * 128:(d + 1) * 128]
                bank = 4 + d
                for hh in range(2):
                    mm(PS[bank][:, hh * 128:(hh + 1) * 128], KD[:, c, :], V[:, hh, c, :], True, True,
                       rd=[r_cder[d], r_uv[set_]], wr=[r_ps[bank]])
                for hh in range(2):
                    rws = slice(64 * hh, 64 * hh + 64)
                    stt("dve", s32[rws, :], s32[rws, :], gcol[rws, :], PS[bank][rws, hh * 128:(hh + 1) * 128],
                        ALU.mult, ALU.add, rd=[r_ps[bank], r_s32[d], r_small], wr=[r_s32[d]])
                cn = c + 1 if d == 0 else c - 1
                cp("act", ST[:, cn, :], s32, rd=[r_s32[d]], wr=[r_cder[4 + d]])
            if step % 4 == 3:
                yield
        yield
        pend = [None]
        for hh in range(2):
            rws = slice(64 * hh, 64 * hh + 64)
            h = 2 * p + hh
            ch = 7 + h
            for cb in range(4):
                sbank = 4 + cb % 2
                for ci in range(4):
                    c = 4 * cb + ci
                    mm(PS[sbank][:, ci * 128:(ci + 1) * 128], kT[rws, c * 128:(c + 1) * 128],
                       qT[rws, c * 128:(c + 1) * 128], True, True, rd=[r_uq[set_], r_uk[set_]], wr=[r_ps[sbank]])
                pi = cb % 3
                tt("dve", pb(pi).rearrange("p (t c) -> p t c", c=128),
                   PS[sbank][:, :].rearrange("p (t c) -> p t c", c=128),
                   ctab(h).unsqueeze(1).to_broadcast([128, 4, 128]), ALU.mult, rd=[r_ps[sbank], r_ctab],
                   wr=[r_pb[pi]])
                for ci in range(4):
                    c = 4 * cb + ci
                    o = PS[6][:, ci * 128:(ci + 1) * 128]
                    mm(o, V[:, hh, c, :], pb(pi)[:, ci * 128:(ci + 1) * 128], True, False,
                       rd=[r_pb[pi], r_uv[set_]], wr=[r_ps[6]])
                    mm(o, STF[rws, c, :], QDF[rws, c * 128:(c + 1) * 128], False, False,
                       rd=[r_cder[4], r_cder[2]], wr=[r_ps[6]])
                    mm(o, STB[rws, c, :], QDB[rws, c * 128:(c + 1) * 128], False, True,
                       rd=[r_cder[5], r_cder[3]], wr=[r_ps[6]])
                if pend[0] is not None:
                    pend[0]()
                fo = 7 + (hh * 4 + cb) % 2
                cp("dve", ft(fo), PS[6][:, :], rd=[r_ps[6]], wr=[r_ft[fo]])

                def epi(fo=fo, ch=ch, cb=cb):
                    act(pb(3), ft(fo), AF.Square, rd=[r_ft[fo]], wr=[r_pb[3]])
                    mm(PS[1][:, :], ONES, pb(3), True, True, rd=[r_pb[3], r_const], wr=[r_ps[1]])
                    rstd_from(PS[1][:, :], r_ps[1], ft(6), r_ft[6], 1.0 / 128)
                    stt("dve", y_(ch)[:, cb * TB:(cb + 1) * TB], ft(fo), PCOL[:, 21:22], ft(6), ALU.mult, ALU.mult,
                        rd=[r_ft[fo], r_ft[6], r_pcol], wr=[r_y[ch][cb], r_yall[ch]])

                pend[0] = epi
                yield
        if pend[0] is not None:
            pend[0]()
        yield

    def layer_params(l):
        lam_init = 0.8 - 0.6 * math.exp(-0.3 * (l + l0))
        P.dma("sp", "par", lambda e: e.dma_start(out=PCOL[:, :], in_=pcol_d[l]), wr=[r_pcol])
        P.dma("sp", "par", lambda e: e.dma_start(out=CTAB[:, 8 * 128:12 * 128].rearrange("p (c n) -> p c n", n=128),
                                                   in_=arow_d[l].rearrange("c p n -> p c n")), wr=[r_ctab])
        rp = [r_pcol]
        tt("dve", SMALL[:, 2:3], PCOL[:, 22:23], PCOL[:, 23:24], ALU.mult, rd=rp, wr=[r_small])
        tt("dve", SMALL[:, 3:4], PCOL[:, 24:25], PCOL[:, 25:26], ALU.mult, rd=rp, wr=[r_small])
        mm(PS[2][:, 0:2], ONESF[0:64, :], SMALL[0:64, 2:4], True, True, rd=[r_small, r_const], wr=[r_ps[2]])
        act(SMALL[:, 4:6], PS[2][:, 0:2], AF.Exp, rd=[r_ps[2]], wr=[r_small])
        tt("dve", SMALL[:, 6:7], SMALL[:, 5:6], SMALL[:, 4:5], ALU.subtract, rd=[r_small], wr=[r_small])
        ts("dve", SMALL[:, 0:1], SMALL[:, 6:7], -lam_init, None, ALU.add, None, rd=[r_small], wr=[r_small])
        ts("dve", SMALL[:, 1:2], PCOL[:, 20:21], 1.0 - lam_init, None, ALU.mult, None, rd=rp + [r_small],
           wr=[r_small])
        act(SMALL[:, 12:16], PCOL[:, 26:30], AF.Exp, rd=rp + [r_small], wr=[r_small])
        ts("dve", SMALL[:, 12:16], SMALL[:, 12:16], -1.0, None, ALU.mult, None, rd=[r_small], wr=[r_small])
        act(SMALL[:, 8:12], SMALL[:, 12:16], AF.Exp, rd=[r_small], wr=[r_small], scale=128.0)
        act(SMALL[:, 16:24], PCOL[:, 30:38], AF.Exp, rd=rp + [r_small], wr=[r_small])
        ts("dve", SMALL[:, 16:24], SMALL[:, 16:24], -1.0, None, ALU.mult, None, rd=[r_small], wr=[r_small])
        rs = [r_small, r_const]
        for h in range(4):
            act(ctab(h), RELF, AF.Exp, rd=rs, wr=[r_ctab], scale=SMALL[:, 16 + h:17 + h])
            tt("dve", ctab(h), ctab(h), MKF8, ALU.mult, rd=[r_ctab, r_const], wr=[r_ctab])
            act(ctab(12), RELB, AF.Exp, rd=rs + [r_ctab], wr=[r_ctab], scale=SMALL[:, 20 + h:21 + h])
            tt("dve", ctab(12), ctab(12), MKB8, ALU.mult, rd=[r_ctab, r_const], wr=[r_ctab])
            tt("dve", ctab(h), ctab(h), ctab(12), ALU.add, rd=[r_ctab], wr=[r_ctab])
        for p in range(2):
            act(ctab(4 + p), POS1, AF.Exp, rd=rs + [r_ctab], wr=[r_ctab], scale=SMALL[:, 12 + p:13 + p])
            act(ctab(6 + p), POSB, AF.Exp, rd=rs + [r_ctab], wr=[r_ctab], scale=SMALL[:, 14 + p:15 + p])
            ts("dve", ctab(4 + p), ctab(4 + p), 0.125, None, ALU.mult, None, rd=[r_ctab], wr=[r_ctab])
            ts("dve", ctab(6 + p), ctab(6 + p), 0.125, None, ALU.mult, None, rd=[r_ctab], wr=[r_ctab])
            for d in range(2):
                dstt = ctab(8 + 2 * d + p)
                src = dstt
                act(dstt, src, AF.Exp, rd=rp + [r_ctab], wr=[r_ctab])
                ts("dve", dstt, dstt, -1.0, None, ALU.mult, None, rd=[r_ctab], wr=[r_ctab])
                act(dstt, dstt, AF.Exp, rd=[r_ctab, r_const], wr=[r_ctab], scale=(TOKCR if d == 0 else TOKC))

    def interleave(a, b):
        da = db = False
        while not (da and db):
            if not da:
                try:
                    next(a)
                except StopIteration:
                    da = True
            if b is None:
                db = True
            if not db:
                try:
                    next(b)
                except StopIteration:
                    db = True

    def drain(g):
        for _ in g:
            pass

    stop_state = {}
    attn_fn = {"A": attn_A, "B": attn_B, "C": attn_C}

    def ck(name):
        if stop == name:
            raise _Stop()

    def layer(l):
        layer_params(l)
        ck('params')
        for u in (units if stop not in ('xc', 'xc2') else units[7:]):
            push_unit_weights(l, u)
        norm(0)
        ck('norm1')
        for kc in range(KC):
            P.dma("sp", "xsp", lambda e, kc=kc: e.dma_start(out=xsp_d[kc * 128:(kc + 1) * 128, :],
                                                           in_=XY[:, kc * S:(kc + 1) * S]),
                  rd=r_x[kc], wr=[r_xsp])
        P.barrier(dma_keys=("xsp",))
        ck('spill')
        if stop in ('xc', 'xc2'):
            drain(inproj(l, 7))
            ck('xc')
            drain(attn_C(l, 7))
            ck('xc2')
        drain(inproj(l, 0))
        ck('inproj0')
        for ui in range(len(units)):
            a = attn_fn[units[ui]["kind"]](l, ui)
            b = inproj(l, ui + 1) if ui + 1 < len(units) else None
            interleave(a, b)
            ck('u%d' % ui)
        P.barrier()
        ck('mix')
        if dbg and l == 0:
            P.dma("sp", "dbg", lambda e: e.dma_start(out=ydbg_d.rearrange("(c p) s -> p c s", p=128),
                                                       in_=Ybf.rearrange("p (c s) -> p c s", s=S)), rd=[], wr=[])
            P.barrier(dma_keys=("dbg",))
        for h in range(4):
            WS.push(w_in_d[l, C_CG + h])
        for j in range(8):
            WS.push(w_br_d[l, j], 11)
            for i in range(3):
                WS.push(w_in_d[l, C_GATE + 8 * i + j])
        for oc in range(8):
            WS.push(w_o_d[l, oc])

        def nbank():
            b = rot["ps"] % 8
            rot["ps"] += 1
            return b

        for h in range(4):
            slot = WS.pop()
            ch = 7 + h
            for tb in range(NB):
                bank = nbank()
                proj_fm(slot, tb, bank)
                j2 = tb % 2
                act(ub(j2), PS[bank][:, :], AF.Silu, rd=[r_ps[bank]], wr=[r_ub[j2]])
                ysl = y_(ch)[:, tb * TB:(tb + 1) * TB]
                tt("pool", ysl, ysl, ub(j2), ALU.mult, rd=[r_ub[j2], r_y[ch][tb], r_yall[ch]], wr=[r_y[ch][tb]])
            WS.release(slot)
        ybr = ((0, 3), (3, 7), (7, 11))
        cnt4 = 0
        for j in range(8):
            bslot = WS.pop()
            wb = wslot(bslot, 11)
            for i in range(3):
                gslot = WS.pop()
                c0, c1 = ybr[i]
                for tb in range(NB):
                    gb = nbank()
                    proj_fm(gslot, tb, gb)
                    sg = 4 + cnt4 % 2
                    tmp = 6 + cnt4 % 2
                    cnt4 += 1
                    act(ft(sg), PS[gb][:, :], AF.Sigmoid, rd=[r_ps[gb]], wr=[r_ft[sg]])
                    bb = nbank()
                    for ch in range(c0, c1):
                        mm(PS[bb][:, :], wb[:, ch, :], y_(ch)[:, tb * TB:(tb + 1) * TB], ch == c0, ch == c1 - 1,
                           rd=[r_w[bslot], r_y[ch][tb], r_yall[ch]], wr=[r_ps[bb]])
                    if i == 0:
                        tt("dve", ft(tb), PS[bb][:, :], ft(sg), ALU.mult, rd=[r_ps[bb], r_ft[sg]], wr=[r_ft[tb]])
                    else:
                        tt("dve", ft(tmp), PS[bb][:, :], ft(sg), ALU.mult, rd=[r_ps[bb], r_ft[sg]], wr=[r_ft[tmp]])
                        if i == 1:
                            tt("pool", ft(tb), ft(tb), ft(tmp), ALU.add, rd=[r_ft[tb], r_ft[tmp]], wr=[r_ft[tb]])
                        else:
                            tt("pool", m_(j, tb), ft(tb), ft(tmp), ALU.add, rd=[r_ft[tb], r_ft[tmp]],
                               wr=[r_m[j][tb]])
                WS.release(gslot)
            WS.release(bslot)
        P.barrier()
        if stop == '4a':
            for kc in range(KC):
                P.dma('pool', 'dbg', lambda e, kc=kc: e.dma_start(out=out_d[kc * 128:(kc + 1) * 128, :], in_=M[:, kc * S:(kc + 1) * S]), rd=[], wr=[])
            P.barrier(dma_keys=('dbg',))
            stop_state['skip_out'] = True
        ck('4a')
        for kc in range(KC):
            P.dma("sp", "xld", lambda e, kc=kc: e.dma_start(out=XY[:, kc * S:(kc + 1) * S],
                                                           in_=xsp_d[kc * 128:(kc + 1) * 128, :]),
                  rd=[r_xsp], wr=r_x[kc])
        for hg in range(4):
            for c in range(8):
                WS.push(w_m1_d[l, hg * 8 + c])
            for oc in range(8):
                WS.push(w_m2_d[l, hg * 8 + oc])
        for oc in range(8):
            slot = WS.pop()
            w = wslot(slot)
            for tb in range(NB):
                bank = nbank()
                for kc in range(KC):
                    mm(PS[bank][:, :], w[:, kc, :], m_(kc, tb), kc == 0, kc == KC - 1,
                       rd=[r_w[slot], r_m[kc][tb]], wr=[r_ps[bank]])
                tt("dve", x_(oc, tb), x_(oc, tb), PS[bank][:, :], ALU.add, rd=[r_ps[bank], r_x[oc][tb]],
                   wr=[r_x[oc][tb]])
            WS.release(slot)
        ck('4b')
        norm(8)
        P.barrier()
        ck('norm2')
        for hg in range(4):
            for c in range(8):
                slot = WS.pop()
                for tb in range(NB):
                    bank = nbank()
                    proj_fm(slot, tb, bank)
                    j2 = (c * NB + tb) % 2
                    act(ft(j2), PS[bank][:, :], AF.Relu, rd=[r_ps[bank]], wr=[r_ft[j2]])
                    tt("pool", m_(c, tb), ft(j2), ft(j2), ALU.mult, rd=[r_ft[j2]], wr=[r_m[c][tb]])
                WS.release(slot)
            for oc in range(8):
                slot = WS.pop()
                w = wslot(slot)
                for tb in range(NB):
                    bank = nbank()
                    for c in range(8):
                        mm(PS[bank][:, :], w[:, c, :], m_(c, tb), c == 0, c == 7,
                           rd=[r_w[slot], r_m[c][tb]], wr=[r_ps[bank]])
                    tt("dve", x_(oc, tb), x_(oc, tb), PS[bank][:, :], ALU.add, rd=[r_ps[bank], r_x[oc][tb]],
                       wr=[r_x[oc][tb]])
                WS.release(slot)
        P.barrier()
        if dbg and l == 0:
            for kc in range(KC):
                P.dma("sp", "dbg", lambda e, kc=kc: e.dma_start(out=xdbg_d[kc * 128:(kc + 1) * 128, :],
                                                               in_=XY[:, kc * S:(kc + 1) * S]), rd=r_x[kc], wr=[])

    try:
        for l in range(nl):
            layer(l)
    except _Stop:
        pass

    for kc in range(KC if not stop_state.get('skip_out') else 0):
        P.dma("sp", "out", lambda e, kc=kc: e.dma_start(out=out_d[kc * 128:(kc + 1) * 128, :],
                                                       in_=XY[:, kc * S:(kc + 1) * S]), rd=r_x[kc], wr=[])
    final_waits = tuple((k, n) for k, n in P.dcnt.items() if k in ("out", "dbg", "xsp"))
    P.q["sp"].append((final_waits, None, None))

    keys = list(P.ENG) + sorted(P.dcnt.keys())
    sems = {k: es.enter_context(nc.semaphore("s_" + k)) for k in keys}
    stats = {e: len(P.q[e]) for e in P.ENG}

    def run(e, name):
        for waits, fn, inc in P.q[name]:
            for k, n in waits:
                e.wait_ge(sems[k], n)
            if fn is not None:
                fn(e).then_inc(sems[inc[0]], inc[1])

    with nc.Block() as block:
        @block.tensor
        def _(e):
            run(e, "pe")

        @block.scalar
        def _(e):
            run(e, "act")

        @block.vector
        def _(e):
            run(e, "dve")

        @block.gpsimd
        def _(e):
            run(e, "pool")

        @block.sync
        def _(e):
            run(e, "sp")
    es.close()
    return nc, stats


def _rope_tables():
    pos = np.arange(S, dtype=np.float32)

    def tab(rot, theta, hd):
        inv = (1.0 / (np.float32(theta) ** (np.arange(0, rot, 2, dtype=np.float32) / np.float32(rot)))).astype(np.float32)
        ang = (pos[:, None] * inv[None, :]).astype(np.float32)
        c, s = np.cos(ang).astype(np.float32), np.sin(ang).astype(np.float32)
        r2 = rot // 2
        cosF = np.ones((hd, S), np.float32)
        sinF = np.zeros((hd, S), np.float32)
        cosF[0:r2] = c.T
        cosF[r2:2 * r2] = c.T
        sinF[0:r2] = s.T
        sinF[r2:2 * r2] = s.T
        return np.tile(cosF, (128 // hd, 1)), np.tile(sinF, (128 // hd, 1))

    cab, sab = tab(16, 500000.0, 64)
    cc, sc = tab(64, 10000.0, 64)
    return np.ascontiguousarray(np.stack([cab, sab, cc, sc]).astype(np.float32))


def _rot_mat(r2, hd=64):
    R = np.zeros((128, 128), np.float32)
    for b in range(128 // hd):
        for d in range(r2):
            R[b * hd + d + r2, b * hd + d] = -1.0
            R[b * hd + d, b * hd + d + r2] = 1.0
    return R


def _constants():
    ones = np.ones((128, 128), np.float32)
    bones = np.zeros((128, 128), np.float32)
    bones[0:64, 0:64] = 1.0
    bones[64:128, 64:128] = 1.0
    ident = np.eye(128, dtype=np.float32)
    cmats = np.stack([ones, bones, _rot_mat(8), _rot_mat(32), ident]).astype(np.float32)
    k = np.arange(128)[:, None]
    q = np.arange(512)[None, :]
    masks = []
    for rel in range(-1, 5):
        masks.append((np.abs(q - (128 * rel + k)) <= 64).astype(np.float32))
    band = (np.abs(np.arange(128)[None, :] - k) <= 64).astype(np.float32)
    masks.append(np.tile(band, (1, 4)))
    cmasks = np.stack(masks).astype(np.float32)
    m = np.arange(128)[:, None].astype(np.float32)
    n = np.arange(128)[None, :].astype(np.float32)
    relf = np.maximum(n - m, 0.0)
    relb = np.maximum(m - n, 0.0)
    mkf8 = (n >= m).astype(np.float32) * 0.125
    mkb8 = (m >= n).astype(np.float32) * 0.125
    pos1 = np.broadcast_to(n + 1.0, (128, 128))
    posb = np.broadcast_to(128.0 - n, (128, 128))
    tok = np.arange(128, dtype=np.float32)[:, None]
    cf = np.concatenate([relf, relb, mkf8, mkb8, pos1, posb, ones, tok, 127.0 - tok], axis=1).astype(np.float32)
    return cmats, cmasks, np.ascontiguousarray(cf)


def _prep_layers(inp, layers):
    f = lambda a: np.asarray(a, dtype=np.float32)
    nl = len(layers)
    w_in = f(inp["w_in"])[layers].reshape(nl, 8, 128, 57, 128).transpose(0, 3, 2, 1, 4)
    wbr = np.concatenate([f(inp["w_br_a"])[layers], f(inp["w_br_b"])[layers], f(inp["w_br_c"])[layers]], axis=1)
    w_br = wbr.reshape(nl, 11, 128, 8, 128).transpose(0, 3, 2, 1, 4)
    w_o = f(inp["w_o"])[layers].reshape(nl, 8, 128, 8, 128).transpose(0, 3, 2, 1, 4)
    w_m1 = f(inp["w_mlp1"])[layers].reshape(nl, 8, 128, 32, 128).transpose(0, 3, 2, 1, 4)
    w_m2 = f(inp["w_mlp2"])[layers].reshape(nl, 4, 8, 128, 8, 128).transpose(0, 1, 4, 3, 2, 5).reshape(nl, 32, 128, 8, 128)
    pcol = np.zeros((nl, 128, NPC), np.float32)
    arow = np.zeros((nl, 4, 128, 128), np.float32)
    for i, l in enumerate(layers):
        pcol[i, :, 0:8] = f(inp["norm1_g"])[l].reshape(8, 128).T
        pcol[i, :, 8:16] = f(inp["norm2_g"])[l].reshape(8, 128).T
        pcol[i, :, 16] = np.tile(f(inp["a_q_norm_g"])[l], 2)
        pcol[i, :, 17] = np.tile(f(inp["a_k_norm_g"])[l], 2)
        pcol[i, :, 18] = np.tile(f(inp["b_q_norm_g"])[l], 2)
        pcol[i, :, 19] = np.tile(f(inp["b_k_norm_g"])[l], 2)
        pcol[i, :, 20] = f(inp["b_out_norm_g"])[l]
        pcol[i, :, 21] = f(inp["c_out_norm_g"])[l]
        pcol[i, :, 22] = np.tile(f(inp["b_lambda_q1"])[l], 2)
        pcol[i, :, 23] = np.tile(f(inp["b_lambda_k1"])[l], 2)
        pcol[i, :, 24] = np.tile(f(inp["b_lambda_q2"])[l], 2)
        pcol[i, :, 25] = np.tile(f(inp["b_lambda_k2"])[l], 2)
        af, ab = f(inp["c_decay_f"])[l], f(inp["c_decay_b"])[l]
        for p in range(2):
            pcol[i, :, 26 + p] = np.repeat(af[2 * p:2 * p + 2], 64)
            pcol[i, :, 28 + p] = np.repeat(ab[2 * p:2 * p + 2], 64)
            arow[i, p] = np.repeat(af[2 * p:2 * p + 2], 64)[None, :]
            arow[i, 2 + p] = np.repeat(ab[2 * p:2 * p + 2], 64)[None, :]
        pcol[i, :, 30:34] = af[None, :]
        pcol[i, :, 34:38] = ab[None, :]
    c = np.ascontiguousarray
    return dict(w_in=c(w_in), w_br=c(w_br), w_o=c(w_o), w_m1=c(w_m1), w_m2=c(w_m2), pcol=pcol, arow=arow)


_CACHE = {}


def _get_prog(nl, l0):
    if (nl, l0) not in _CACHE:
        _CACHE[(nl, l0)] = build(nl, l0=l0)[0]
    return _CACHE[(nl, l0)]


FUSED = True


def kernel(**inputs):
    x = np.asarray(inputs["x"], dtype=np.float32)
    B = x.shape[0]
    cmats, cmasks, cf = _constants()
    rope = _rope_tables()
    consts = dict(c_mats=cmats, c_masks=cmasks, c_f32=cf, rope=rope)
    xT = [np.ascontiguousarray(x[b].T) for b in range(B)]
    groups = [list(range(DEPTH))] if FUSED else [[l] for l in range(DEPTH)]
    for layers in groups:
        nc = _get_prog(len(layers), layers[0])
        shared = _prep_layers(inputs, layers)
        shared.update(consts)
        in_maps = []
        for b in range(B):
            d = dict(shared)
            d["xT"] = xT[b]
            in_maps.append(d)
        res = run_bass_kernel_spmd(nc, in_maps, core_ids=list(range(B)))
        xT = [np.ascontiguousarray(np.asarray(r["outT"], dtype=np.float32)) for r in res.results]
    out = np.stack([t.T for t in xT], axis=0)
    return np.ascontiguousarray(out.astype(np.float32))
```

```python
import math
from contextlib import ExitStack
import numpy as np
import ml_dtypes
import concourse.bass as bass
import concourse.mybir as mybir
from concourse.bass_utils import run_bass_kernel_spmd

F32, BF16 = mybir.dt.float32, mybir.dt.bfloat16
ALU, AF = mybir.AluOpType, mybir.ActivationFunctionType

S = 2048
D = 1024
KC = 8
NB = 4
TB = 512
DEPTH = 4
EPS = 1e-6
NPC = 38
C_AQ, C_AK, C_AV = 0, 3, 6
C_BQ, C_BK, C_BV = 9, 13, 17
C_CQ, C_CK, C_CV, C_CG = 21, 23, 25, 29
C_GATE = 33
NWSLOT = 5


class Res:
    __slots__ = ("w", "r", "x")

    def __init__(self, x=False):
        self.w = None
        self.r = {}
        self.x = x


class Prog:
    ENG = ("pe", "act", "dve", "pool", "sp")

    def __init__(self):
        self.q = {e: [] for e in self.ENG}
        self.cnt = {e: 0 for e in self.ENG}
        self.seen = {e: {} for e in self.ENG}
        self.dcnt = {}

    def _need(self, eng, dep, waits, raw):
        if dep is None:
            return
        k, n = dep
        if k == eng:
            if eng == "pe":
                return
        if k in self.dcnt:
            n = max(n, self.dcnt[k])
        if self.seen[eng].get(k, 0) >= n:
            return
        waits[k] = max(waits.get(k, 0), n)

    def _deps(self, eng, rd, wr):
        waits = {}
        for r in rd:
            self._need(eng, r.w, waits, True)
        for w in wr:
            self._need(eng, w.w, waits, False)
            for k, n in w.r.items():
                self._need(eng, (k, n), waits, False)
        for k, n in waits.items():
            self.seen[eng][k] = n
        return tuple(waits.items())

    def op(self, eng, fn, rd=(), wr=()):
        wr = list(wr) + [r for r in rd if r.x]
        rd = [r for r in rd if not r.x]
        waits = self._deps(eng, rd, wr)
        self.cnt[eng] += 1
        n = self.cnt[eng]
        self.q[eng].append((waits, fn, (eng, 1)))
        for r in rd:
            r.r[eng] = max(r.r.get(eng, 0), n)
        for w in wr:
            w.w = (eng, n)
            w.r = {}

    def dma(self, qeng, semkey, fn, rd=(), wr=()):
        waits = self._deps(qeng, rd, wr)
        self.dcnt[semkey] = self.dcnt.get(semkey, 0) + 16
        n = self.dcnt[semkey]
        self.q[qeng].append((waits, fn, (semkey, 16)))
        for r in rd:
            r.r[semkey] = max(r.r.get(semkey, 0), n)
        for w in wr:
            w.w = (semkey, n)
            w.r = {}

    def barrier(self, dma_keys=()):
        for e in self.ENG:
            waits = {}
            for f in self.ENG:
                if f != e and self.cnt[f] > self.seen[e].get(f, 0):
                    waits[f] = self.cnt[f]
            for k in dma_keys:
                if self.dcnt.get(k, 0) > self.seen[e].get(k, 0):
                    waits[k] = self.dcnt[k]
            for k, n in waits.items():
                self.seen[e][k] = n
            if waits:
                self.q[e].append((tuple(waits.items()), None, None))


class _Stop(Exception):
    pass


def build(nl, dbg=False, stop=None, l0=0):
    nc = bass.Bass("TRN2", target_bir_lowering=False)

    def din(name, shape, dt=F32):
        return nc.dram_tensor(name, list(shape), dt, kind="ExternalInput").ap()

    xT_d = din("xT", [D, S])
    w_in_d = din("w_in", [nl, 57, 128, 8, 128])
    w_br_d = din("w_br", [nl, 8, 128, 11, 128])
    w_o_d = din("w_o", [nl, 8, 128, 8, 128])
    w_m1_d = din("w_m1", [nl, 32, 128, 8, 128])
    w_m2_d = din("w_m2", [nl, 32, 128, 8, 128])
    pcol_d = din("pcol", [nl, 128, NPC])
    arow_d = din("arow", [nl, 4, 128, 128])
    cmat_d = din("c_mats", [5, 128, 128])
    cmask_d = din("c_masks", [7, 128, 512])
    cf32_d = din("c_f32", [128, 7 * 128 + 2])
    rope_d = din("rope", [4, 128, S])
    out_d = nc.dram_tensor("outT", [D, S], F32, kind="ExternalOutput").ap()
    xsp_d = nc.dram_tensor("xspill", [D, S], F32, kind="Internal").ap()
    if dbg:
        ydbg_d = nc.dram_tensor("ydbg", [11 * 128, S], BF16, kind="ExternalOutput").ap()
        xdbg_d = nc.dram_tensor("xdbg", [D, S], F32, kind="ExternalOutput").ap()

    P = Prog()
    es = ExitStack()

    def sb(name, shape, dt):
        return es.enter_context(nc.sbuf_tensor(name, list(shape), dt))

    XN = sb("XN", [128, KC * S], BF16)
    XY = sb("XY", [128, KC * S], F32)
    M = sb("M", [128, KC * S], BF16)
    WR = sb("WR", [128, NWSLOT * 11 * 128], BF16)
    CMB = sb("CMB", [128, 5 * 128], BF16)
    MASKS = sb("MASKS", [128, 7 * 512], BF16)
    CF = sb("CF", [128, 7 * 128 + 2], F32)
    PCOL = sb("PCOL", [128, NPC], F32)
    SMALL = sb("SMALL", [128, 64], F32)
    CTAB = sb("CTAB", [128, 13 * 128], F32)
    CDER = sb("CDER", [128, 6 * 2048], BF16)
    S32 = sb("S32", [128, 2 * 128], F32)
    SQB = sb("SQB", [128, 2 * 512], BF16)
    UB = sb("UB", [128, 2 * 512], BF16)
    FT = sb("FT", [128, 9 * 512], F32)

    Ybf = XY[:, 0:11 * 1024].bitcast(BF16)
    ROPE = XY[:, 11 * 1024:11 * 1024 + 4096]
    PB = XY[:, 15 * 1024:16 * 1024].bitcast(BF16)
    DTOT = CDER[:, 4 * 2048:6 * 2048].bitcast(F32)

    def x_(kc, tb):
        return XY[:, kc * S + tb * TB: kc * S + (tb + 1) * TB]

    def xn_(kc, tb):
        return XN[:, kc * S + tb * TB: kc * S + (tb + 1) * TB]

    def xnfull(kc):
        return XN[:, kc * S:(kc + 1) * S]

    def y_(ch):
        return Ybf[:, ch * S:(ch + 1) * S]

    def m_(kc, tb):
        return M[:, kc * S + tb * TB: kc * S + (tb + 1) * TB]

    def uq(s):
        return M[:, s * 8192: s * 8192 + 2048]

    def uk(s):
        return M[:, s * 8192 + 2048: s * 8192 + 4096]

    def uv(s):
        return M[:, s * 8192 + 4096: s * 8192 + 8192].rearrange("p (h t c) -> p h t c", h=2, c=128)

    def kz(s, m):
        return uk(s) if m == 0 else M[:, s * 8192 + 6144: s * 8192 + 8192]

    def wslot(s, nk=8):
        return WR[:, s * 1408: s * 1408 + nk * 128].rearrange("p (k n) -> p k n", n=128)

    ONES = CMB[:, 0:128]
    BONES = CMB[:, 128:256]
    RAB = CMB[:, 256:384]
    RC = CMB[:, 384:512]
    IDENT = CMB[:, 512:640]

    def mask_(i):
        return MASKS[:, i * 512:(i + 1) * 512]

    RELF, RELB = CF[:, 0:128], CF[:, 128:256]
    MKF8, MKB8 = CF[:, 256:384], CF[:, 384:512]
    POS1, POSB = CF[:, 512:640], CF[:, 640:768]
    ONESF = CF[:, 768:896]
    TOKC, TOKCR = CF[:, 896:897], CF[:, 897:898]

    def ctab(i):
        return CTAB[:, i * 128:(i + 1) * 128]

    def cder(i):
        return CDER[:, i * 2048:(i + 1) * 2048]

    KDF, KDB = cder(0).rearrange("p (t c) -> p t c", c=128), cder(1).rearrange("p (t c) -> p t c", c=128)
    QDF, QDB = cder(2), cder(3)
    STF, STB = cder(4).rearrange("p (t c) -> p t c", c=128), cder(5).rearrange("p (t c) -> p t c", c=128)

    def pb(i):
        return PB[:, i * 512:(i + 1) * 512]

    def sqb(i):
        return SQB[:, i * 512:(i + 1) * 512]

    def ub(i):
        return UB[:, i * 512:(i + 1) * 512]

    def ft(i):
        return FT[:, i * 512:(i + 1) * 512]

    PS = [es.enter_context(nc.psum_tensor("ps%d" % i, [128, 512], F32)) for i in range(8)]

    r_x = [[Res() for _ in range(NB)] for _ in range(KC)]
    r_xn = [Res() for _ in range(NB)]
    r_y = [[Res() for _ in range(NB)] for _ in range(11)]
    r_yall = [Res() for _ in range(11)]
    r_m = [[Res() for _ in range(NB)] for _ in range(KC)]
    r_uq = [Res(), Res()]
    r_uk = [Res(), Res()]
    r_uv = [Res(), Res()]
    r_w = [Res() for _ in range(NWSLOT)]
    r_ps = [Res(True) for _ in range(8)]
    r_pb = [Res() for _ in range(4)]
    r_sqb = [Res(), Res()]
    r_ub = [Res(), Res()]
    r_ft = [Res() for _ in range(9)]
    r_rope = Res()
    r_acc = Res()
    r_const = Res()
    r_pcol = Res()
    r_small = Res()
    r_ctab = Res()
    r_cder = [Res() for _ in range(6)]
    r_s32 = [Res(), Res()]
    r_xsp = Res()

    wstate = {"n": 0}

    def load_w(dram_ap, nk, s):
        dst = wslot(s, nk)
        P.dma("pool", "w%d" % s, lambda e, dst=dst, src=dram_ap: e.dma_start(out=dst, in_=src), wr=[r_w[s]])
        return s

    class WStream:
        def __init__(self):
            self.pending = []
            self.loaded = []
            self.free = list(range(NWSLOT))

        def push(self, ap, nk=8):
            self.pending.append((ap, nk))

        def prefetch(self):
            while self.pending and self.free:
                ap, nk = self.pending.pop(0)
                self.loaded.append(load_w(ap, nk, self.free.pop(0)))

        def pop(self):
            self.prefetch()
            return self.loaded.pop(0)

        def release(self, s):
            self.free.append(s)
            self.prefetch()

    WS = WStream()

    def mm(out, lhsT, rhs, start, stop, rd, wr):
        P.op("pe", lambda e: e.matmul(out, lhsT=lhsT, rhs=rhs, start=start, stop=stop), rd=rd, wr=wr)

    def act(out, in_, func, rd, wr, scale=None, bias=None):
        kw = {}
        if scale is not None:
            kw["scale"] = scale
        if bias is not None:
            kw["bias"] = bias
        P.op("act", lambda e: e.activation(out=out, in_=in_, func=func, **kw), rd=rd, wr=wr)

    def tt(eng, out, in0, in1, op, rd, wr):
        P.op(eng, lambda e: e.tensor_tensor(out=out, in0=in0, in1=in1, op=op), rd=rd, wr=wr)

    def stt(eng, out, in0, scalar, in1, op0, op1, rd, wr):
        P.op(eng, lambda e: e.scalar_tensor_tensor(out=out, in0=in0, scalar=scalar, in1=in1, op0=op0, op1=op1),
             rd=rd, wr=wr)

    def ts(eng, out, in0, s1, s2, op0, op1, rd, wr):
        if op1 is None:
            P.op(eng, lambda e: e.tensor_scalar(out=out, in0=in0, scalar1=s1, scalar2=None, op0=op0), rd=rd, wr=wr)
        else:
            P.op(eng, lambda e: e.tensor_scalar(out=out, in0=in0, scalar1=s1, scalar2=s2, op0=op0, op1=op1),
                 rd=rd, wr=wr)

    def cp(eng, out, in_, rd, wr):
        if eng == "act":
            act(out, in_, AF.Copy, rd, wr)
        else:
            P.op(eng, lambda e: e.tensor_copy(out=out, in_=in_), rd=rd, wr=wr)

    def recip(out, in_, rd, wr):
        P.op("dve", lambda e: e.reciprocal(out=out, in_=in_), rd=rd, wr=wr)

    def rstd_from(ps_ap, r_in, ftile, r_ftile, inv_n):
        act(ftile, ps_ap, AF.Ln, rd=[r_in], wr=[r_ftile], scale=inv_n, bias=EPS)
        act(ftile, ftile, AF.Exp, rd=[r_ftile], wr=[r_ftile], scale=-0.5)

    rot = {"ps": 0, "alt": 0}
    SBANK = (4, 5, 3)

    P.dma("pool", "cst", lambda e: e.dma_start(out=CMB[:, :].rearrange("p (c n) -> p c n", n=128),
                                                  in_=cmat_d.rearrange("c p n -> p c n")), wr=[r_const])
    P.dma("pool", "cst", lambda e: e.dma_start(out=MASKS[:, :].rearrange("p (c n) -> p c n", n=512),
                                                  in_=cmask_d.rearrange("c p n -> p c n")), wr=[r_const])
    P.dma("sp", "cstf", lambda e: e.dma_start(out=CF[:, :], in_=cf32_d), wr=[r_const])
    P.barrier(dma_keys=("cst", "cstf"))
    for kc in range(KC):
        P.dma("sp", "xld", lambda e, kc=kc: e.dma_start(out=XY[:, kc * S:(kc + 1) * S],
                                                       in_=xT_d[kc * 128:(kc + 1) * 128, :]),
              wr=r_x[kc])

    def norm(gcol0):
        for tb in range(NB):
            for kc in range(KC):
                i = kc % 2
                act(sqb(i), x_(kc, tb), AF.Square, rd=[r_x[kc][tb]], wr=[r_sqb[i]])
                mm(PS[2][:, :], ONES, sqb(i), kc == 0, kc == KC - 1, rd=[r_sqb[i], r_const], wr=[r_ps[2]])
            fr = tb % 2
            rstd_from(PS[2][:, :], r_ps[2], ft(fr), r_ft[fr], 1.0 / D)
            for kc in range(KC):
                stt("dve", xn_(kc, tb), x_(kc, tb), PCOL[:, gcol0 + kc: gcol0 + kc + 1], ft(fr), ALU.mult, ALU.mult,
                    rd=[r_x[kc][tb], r_ft[fr], r_pcol], wr=[r_xn[tb]])

    def proj_fm(slot, tb, bank):
        w = wslot(slot)
        for kc in range(KC):
            mm(PS[bank][:, :], w[:, kc, :], xn_(kc, tb), kc == 0, kc == KC - 1,
               rd=[r_w[slot], r_xn[tb]], wr=[r_ps[bank]])

    def perm_dst(buf, tb, delta):
        if delta == 1:
            return buf[:, tb * TB:(tb + 1) * TB], None
        n0 = tb * TB // delta
        return buf.rearrange("p (r n) -> p n r", r=delta)[:, n0:n0 + TB // delta, :], delta

    def srcv(ap, delta):
        if delta is None:
            return ap
        return ap.rearrange("p (n r) -> p n r", r=delta)

    def qk_post(bank, tb, gcol, rmat, normed, dsts, delta, feng="pool", aux=2):
        j = rot["alt"] % 2
        rot["alt"] += 1
        pj = PS[bank][:, :]
        cos = ROPE[:, tb * TB:(tb + 1) * TB]
        sin = ROPE[:, S + tb * TB: S + (tb + 1) * TB]
        if normed:
            act(sqb(j), pj, AF.Square, rd=[r_ps[bank]], wr=[r_sqb[j]])
            mm(PS[aux][:, :], BONES, sqb(j), True, True, rd=[r_sqb[j], r_const], wr=[r_ps[aux]])
            yield
            rstd_from(PS[aux][:, :], r_ps[aux], ft(j), r_ft[j], 1.0 / 64)
            stt("dve", ub(j), pj, PCOL[:, gcol:gcol + 1], ft(j), ALU.mult, ALU.mult,
                rd=[r_ps[bank], r_ft[j], r_pcol], wr=[r_ub[j]])
            yield
            mm(PS[aux][:, :], rmat, ub(j), True, True, rd=[r_ub[j], r_const], wr=[r_ps[aux]])
            tt(feng, ft(2 + j), ub(j), cos, ALU.mult, rd=[r_ub[j], r_rope], wr=[r_ft[2 + j]])
        else:
            cp("act", ub(j), pj, rd=[r_ps[bank]], wr=[r_ub[j]])
            tt("dve", ft(2 + j), pj, cos, ALU.mult, rd=[r_ps[bank], r_rope], wr=[r_ft[2 + j]])
            yield
            mm(PS[aux][:, :], rmat, ub(j), True, True, rd=[r_ub[j], r_const], wr=[r_ps[aux]])
        yield
        tt("dve", ft(4 + j), PS[aux][:, :], sin, ALU.mult, rd=[r_ps[aux], r_rope], wr=[r_ft[4 + j]])
        for buf, rows, res in dsts:
            dst, dl = perm_dst(buf, tb, delta)
            a0, a1 = srcv(ft(2 + j), dl), srcv(ft(4 + j), dl)
            if rows is not None:
                dst, a0, a1 = dst[rows[0]:rows[1]], a0[rows[0]:rows[1]], a1[rows[0]:rows[1]]
            tt(feng, dst, a0, a1, ALU.add, rd=[r_ft[2 + j], r_ft[4 + j]], wr=[res])

    def v_proj(slot, set_, coloff, delta):
        w = wslot(slot)
        V = uv(set_)
        for g4 in range(4):
            bank = 0
            for ti in range(4):
                t = g4 * 4 + ti
                for kc in range(KC):
                    if delta == 1:
                        lhs = xnfull(kc)[:, t * 128:(t + 1) * 128]
                    elif delta == 4:
                        lhs = xnfull(kc).rearrange("p (n r) -> p r n", r=4)[:, t // 4, (t % 4) * 128:(t % 4 + 1) * 128]
                    else:
                        lhs = xnfull(kc).rearrange("p (n r) -> p r n", r=16)[:, t, :]
                    mm(PS[bank][:, ti * 128:(ti + 1) * 128], lhs, w[:, kc, :], kc == 0, kc == KC - 1,
                       rd=[r_w[slot]] + r_xn, wr=[r_ps[bank]])
            cp("act", V[:, coloff // 128, g4 * 4:(g4 + 1) * 4, :],
               PS[bank][:, :].rearrange("p (t c) -> p t c", c=128), rd=[r_ps[bank]], wr=[r_uv[set_]])
            yield

    units = []
    for g in range(3):
        units.append(dict(kind="A", idx=g, q=C_AQ + g, k=C_AK + g, v=[C_AV + g], gq=16, gk=17, delta=(1, 4, 16)[g]))
    for h in range(4):
        units.append(dict(kind="B", idx=h, q=C_BQ + h, k=C_BK + h, v=[C_BV + h], gq=18, gk=19, delta=1))
    for p in range(2):
        units.append(dict(kind="C", idx=p, q=C_CQ + p, k=C_CK + p, v=[C_CV + 2 * p, C_CV + 2 * p + 1], gq=0, gk=0,
                          delta=1))

    def push_unit_weights(l, u):
        WS.push(w_in_d[l, u["q"]])
        WS.push(w_in_d[l, u["k"]])
        for c in u["v"]:
            WS.push(w_in_d[l, c])

    def rope_load(which):
        P.dma("sp", "rope", lambda e: e.dma_start(out=ROPE[:, 0:S], in_=rope_d[2 * which]), wr=[r_rope])
        P.dma("sp", "rope", lambda e: e.dma_start(out=ROPE[:, S:2 * S], in_=rope_d[2 * which + 1]), wr=[r_rope])

    def inproj(l, ui):
        u = units[ui]
        set_ = ui % 2
        normed = u["kind"] != "C"
        rmat = RC if u["kind"] == "C" else RAB
        if ui == 0:
            rope_load(0)
        if ui == 7:
            rope_load(1)
        feng = "dve" if ui == 0 else "pool"
        width = 2 if ui <= 3 else 1
        qslot = WS.pop()
        kslot = WS.pop()
        gens = []

        def cb_gen(slot, tb, gcol, dsts, k):
            bank = k % width
            proj_fm(slot, tb, bank)
            yield
            yield from qk_post(bank, tb, gcol, rmat, normed, dsts, u["delta"], feng, 2 + k % width)

        k = 0
        for which in ("q", "k"):
            if which == "q":
                dsts, gcol, slot = [(uq(set_), None, r_uq[set_])], u["gq"], qslot
            elif u["kind"] == "C":
                dsts, gcol, slot = [(uk(set_), None, r_uk[set_])], u["gk"], kslot
            else:
                k0, k1 = kz(set_, 0), kz(set_, 1)
                P.op("pool", lambda e, k0=k0: e.memset(k0[64:128, :], 0.0), wr=[r_uk[set_]])
                P.op("pool", lambda e, k1=k1: e.memset(k1[0:64, :], 0.0), wr=[r_uv[set_]])
                dsts, gcol, slot = [(k0, (0, 64), r_uk[set_]), (k1, (64, 128), r_uv[set_])], u["gk"], kslot
            for tb in range(NB):
                gens.append(cb_gen(slot, tb, gcol, dsts, k))
                k += 1
        active = []
        while gens or active:
            while gens and len(active) < width:
                active.append(gens.pop(0))
            for g_ in list(active):
                try:
                    next(g_)
                except StopIteration:
                    active.remove(g_)
            yield
        WS.release(qslot)
        WS.release(kslot)
        for vi, c in enumerate(u["v"]):
            slot = WS.pop()
            yield from v_proj(slot, set_, vi * 128, u["delta"])
            WS.release(slot)

    def attn_B(l, ui):
        u = units[ui]
        h = u["idx"]
        set_ = ui % 2
        qT, V = uq(set_), uv(set_)
        rdq = [r_uq[set_], r_uk[set_], r_uv[set_]]
        ch = 3 + h
        pending = []

        def make_epilogue(qb):
            def s1():
                act(sqb(0), ft(7), AF.Square, rd=[r_ft[7]], wr=[r_sqb[0]])

            def s2():
                mm(PS[1][:, :], ONES, sqb(0), True, True, rd=[r_sqb[0], r_const], wr=[r_ps[1]])

            def s3():
                rstd_from(PS[1][:, :], r_ps[1], ft(6), r_ft[6], 1.0 / 128)
                stt("dve", y_(ch)[:, qb * TB:(qb + 1) * TB], ft(7), SMALL[:, 1:2], ft(6), ALU.mult, ALU.mult,
                    rd=[r_ft[7], r_ft[6], r_small], wr=[r_y[ch][qb], r_yall[ch]])

            return [(9, s1), (11, s2), (13, s3)]

        for qb in range(NB):
            for m in range(2):
                def smm(kt):
                    bank = SBANK[kt % 3]
                    mm(PS[bank][:, :], kz(set_, m)[:, kt * 128:(kt + 1) * 128], qT[:, qb * TB:(qb + 1) * TB], True, True,
                       rd=rdq, wr=[r_ps[bank]])

                smm(0)
                smm(1)
                for kt in range(16):
                    if kt + 2 < 16:
                        smm(kt + 2)
                    bank = SBANK[kt % 3]
                    pi = kt % 4
                    act(pb(pi), PS[bank][:, :], AF.Exp, rd=[r_ps[bank]], wr=[r_pb[pi]], scale=0.125)
                    mm(PS[6][:, :], V[:, 0, kt, :], pb(pi), kt == 0, kt == 15, rd=[r_pb[pi], r_uv[set_]],
                       wr=[r_ps[6]])
                    mm(PS[7][:, :], ONES, pb(pi), kt == 0, kt == 15, rd=[r_pb[pi], r_const], wr=[r_ps[7]])
                    while pending and pending[0][0] <= kt:
                        pending.pop(0)[1]()
                    if kt % 4 == 3 and kt < 15:
                        yield
                cp("dve", ft(6), PS[7][:, :], rd=[r_ps[7]], wr=[r_ft[6]])
                if m == 0:
                    cp("dve", ft(7), PS[6][:, :], rd=[r_ps[6]], wr=[r_ft[7]])
                    recip(ft(6), ft(6), rd=[r_ft[6]], wr=[r_ft[6]])
                    tt("dve", ft(7), ft(7), ft(6), ALU.mult, rd=[r_ft[7], r_ft[6]], wr=[r_ft[7]])
                else:
                    cp("dve", ft(8), PS[6][:, :], rd=[r_ps[6]], wr=[r_ft[8]])
                    recip(ft(6), ft(6), rd=[r_ft[6]], wr=[r_ft[6]])
                    tt("dve", ft(8), ft(8), ft(6), ALU.mult, rd=[r_ft[8], r_ft[6]], wr=[r_ft[8]])
                    stt("dve", ft(7), ft(8), SMALL[:, 0:1], ft(7), ALU.mult, ALU.add,
                        rd=[r_ft[8], r_ft[7], r_small], wr=[r_ft[7]])
                    pending = make_epilogue(qb)
                yield
        for _, fn in pending:
            fn()
        yield

    def attn_A(l, ui):
        u = units[ui]
        g = u["idx"]
        delta = u["delta"]
        set_ = ui % 2
        qT, kT, V = uq(set_), uk(set_), uv(set_)
        rdq = [r_uq[set_], r_uk[set_], r_uv[set_]]
        for s2 in range(2):
            rows = slice(64 * s2, 64 * s2 + 64)
            for qb in range(NB):
                if g < 2:
                    rels = list(range(-1, 5)) if g == 0 else list(range(4))
                    kts = [(4 * qb + r, r) for r in rels if 0 <= 4 * qb + r < 16]
                    n = len(kts)

                    def win(i):
                        rel = kts[i][1]
                        if i == 0:
                            return 0, TB
                        return max(0, 128 * rel - 64), min(TB, 128 * rel + 192)

                    def smm(i):
                        kt, rel = kts[i]
                        bank = 4 + i % 2
                        lo, hi = win(i)
                        mm(PS[bank][:, lo:hi], kz(set_, s2)[:, kt * 128:(kt + 1) * 128],
                           qT[:, qb * TB + lo:qb * TB + hi], True, True, rd=rdq, wr=[r_ps[bank]])

                    smm(0)
                    for i in range(n):
                        if i + 1 < n:
                            smm(i + 1)
                        kt, rel = kts[i]
                        bank = 4 + i % 2
                        pi = i % 4
                        lo, hi = win(i)
                        act(pb(pi)[:, lo:hi], PS[bank][:, lo:hi], AF.Exp, rd=[r_ps[bank]], wr=[r_pb[pi]], scale=0.125)
                        tt("dve", pb(pi)[:, lo:hi], pb(pi)[:, lo:hi], mask_(rel + 1)[:, lo:hi], ALU.mult,
                           rd=[r_pb[pi], r_const], wr=[r_pb[pi]])
                        mm(PS[6][:, lo:hi], V[:, 0, kt, :], pb(pi)[:, lo:hi], i == 0, i == n - 1,
                           rd=[r_pb[pi], r_uv[set_]], wr=[r_ps[6]])
                        mm(PS[7][:, lo:hi], ONES, pb(pi)[:, lo:hi], i == 0, i == n - 1, rd=[r_pb[pi], r_const],
                           wr=[r_ps[7]])
                        if i < n - 1:
                            yield
                else:
                    bank = 4 + qb % 2
                    for rel in range(4):
                        kt = 4 * qb + rel
                        mm(PS[bank][:, rel * 128:(rel + 1) * 128], kz(set_, s2)[:, kt * 128:(kt + 1) * 128],
                           qT[:, kt * 128:(kt + 1) * 128], True, True, rd=rdq, wr=[r_ps[bank]])
                    pi = qb % 3
                    act(pb(pi), PS[bank][:, :], AF.Exp, rd=[r_ps[bank]], wr=[r_pb[pi]], scale=0.125)
                    tt("dve", pb(pi), pb(pi), mask_(6), ALU.mult, rd=[r_pb[pi], r_const], wr=[r_pb[pi]])
                    for rel in range(4):
                        kt = 4 * qb + rel
                        mm(PS[6][:, rel * 128:(rel + 1) * 128], V[:, 0, kt, :], pb(pi)[:, rel * 128:(rel + 1) * 128],
                           True, True, rd=[r_pb[pi], r_uv[set_]], wr=[r_ps[6]])
                        mm(PS[7][:, rel * 128:(rel + 1) * 128], ONES, pb(pi)[:, rel * 128:(rel + 1) * 128],
                           True, True, rd=[r_pb[pi], r_const], wr=[r_ps[7]])
                if delta == 1:
                    ydst = y_(g)[rows, qb * TB:(qb + 1) * TB]
                    ddst = DTOT[rows, qb * TB:(qb + 1) * TB]
                    nsrc, dsrc = PS[6][rows, :], PS[7][rows, :]
                elif delta == 4:
                    ydst = y_(g).rearrange("p (n r) -> p r n", r=4)[rows, qb, :]
                    ddst = DTOT[:, :].rearrange("p (n r) -> p r n", r=4)[rows, qb, :]
                    nsrc, dsrc = PS[6][rows, :], PS[7][rows, :]
                else:
                    ydst = y_(g).rearrange("p (n r) -> p r n", r=16)[rows, 4 * qb:4 * qb + 4, :]
                    ddst = DTOT[:, :].rearrange("p (n r) -> p r n", r=16)[rows, 4 * qb:4 * qb + 4, :]
                    nsrc = PS[6][rows, :].rearrange("p (a b) -> p a b", a=4)
                    dsrc = PS[7][rows, :].rearrange("p (a b) -> p a b", a=4)
                cp("act", ydst, nsrc, rd=[r_ps[6]], wr=[r_yall[g]] + r_y[g])
                if g == 0:
                    cp("dve", ddst, dsrc, rd=[r_ps[7]], wr=[r_cder[4], r_cder[5]])
                else:
                    tt("dve", ddst, ddst, dsrc, ALU.add, rd=[r_ps[7], r_cder[4], r_cder[5]], wr=[r_cder[4], r_cder[5]])
                yield
        if g == 2:
            for tb in range(NB):
                recip(ft(6), DTOT[:, tb * TB:(tb + 1) * TB], rd=[r_cder[4], r_cder[5]], wr=[r_ft[6]])
                for gg in range(3):
                    ysl = y_(gg)[:, tb * TB:(tb + 1) * TB]
                    tt("pool", ysl, ysl, ft(6), ALU.mult, rd=[r_ft[6], r_yall[gg]] + r_y[gg], wr=[r_y[gg][tb], r_yall[gg]])
                yield

    def attn_C(l, ui):
        u = units[ui]
        p = u["idx"]
        set_ = ui % 2
        qT, kT, V = uq(set_), uk(set_), uv(set_)
        DQF, DQB, DKF, DKB = ctab(4 + p), ctab(6 + p), ctab(8 + p), ctab(10 + p)
        g128f, g128b = SMALL[:, 8 + p:9 + p], SMALL[:, 10 + p:11 + p]
        for g4 in range(4):
            bank = 3
            tp = PS[bank][:, :].bitcast(BF16)[:, 0:512]
            for ti in range(4):
                t = g4 * 4 + ti
                P.op("pe", lambda e, t=t, ti=ti, tp=tp: e.transpose(tp[:, ti * 128:(ti + 1) * 128],
                                                                      kT[:, t * 128:(t + 1) * 128], IDENT),
                     rd=[r_uk[set_], r_const], wr=[r_ps[bank]])
            tp3 = tp.rearrange("p (t c) -> p t c", c=128)
            tt("dve", KDF[:, g4 * 4:(g4 + 1) * 4, :], tp3, DKF.unsqueeze(1).to_broadcast([128, 4, 128]), ALU.mult,
               rd=[r_ps[bank], r_ctab], wr=[r_cder[0]])
            tt("dve", KDB[:, g4 * 4:(g4 + 1) * 4, :], tp3, DKB.unsqueeze(1).to_broadcast([128, 4, 128]), ALU.mult,
               rd=[r_ps[bank], r_ctab], wr=[r_cder[1]])
            q3 = qT[:, g4 * TB:(g4 + 1) * TB].rearrange("p (t c) -> p t c", c=128)
            tt("pool", QDF[:, g4 * TB:(g4 + 1) * TB].rearrange("p (t c) -> p t c", c=128), q3,
               DQF.unsqueeze(1).to_broadcast([128, 4, 128]), ALU.mult, rd=[r_uq[set_], r_ctab], wr=[r_cder[2]])
            tt("pool", QDB[:, g4 * TB:(g4 + 1) * TB].rearrange("p (t c) -> p t c", c=128), q3,
               DQB.unsqueeze(1).to_broadcast([128, 4, 128]), ALU.mult, rd=[r_uq[set_], r_ctab], wr=[r_cder[3]])
            yield
        P.op("pool", lambda e: e.memset(S32[:, :], 0.0), wr=r_s32)
        P.op("pool", lambda e: e.memset(STF[:, 0, :], 0.0), wr=[r_cder[4]])
        P.op("pool", lambda e: e.memset(STB[:, 15, :], 0.0), wr=[r_cder[5]])
        for step in range(15):
            for d in range(2):
                c = step if d == 0 else 15 - step
                KD = KDF if d == 0 else KDB
                ST = STF if d == 0 else STB
                gcol = g128f if d == 0 else g128b
                s32 = S32[:, d * 128:(d + 1) * 128]
                bank = 4 + d
                for hh in range(2):
                    mm(PS[bank][:, hh * 128:(hh + 1) * 128], KD[:, c, :], V[:, hh, c, :], True, True,
                       rd=[r_cder[d], r_uv[set_]], wr=[r_ps[bank]])
                for hh in range(2):
                    rws = slice(64 * hh, 64 * hh + 64)
                    stt("dve", s32[rws, :], s32[rws, :], gcol[rws, :], PS[bank][rws, hh * 128:(hh + 1) * 128],
                        ALU.mult, ALU.add, rd=[r_ps[bank], r_s32[d], r_small], wr=[r_s32[d]])
                cn = c + 1 if d == 0 else c - 1
                cp("act", ST[:, cn, :], s32, rd=[r_s32[d]], wr=[r_cder[4 + d]])
            if step % 4 == 3:
                yield
        yield
        pend = [None]
        for hh in range(2):
            rws = slice(64 * hh, 64 * hh + 64)
            h = 2 * p + hh
            ch = 7 + h
            for cb in range(4):
                sbank = 4 + cb % 2
                for ci in range(4):
                    c = 4 * cb + ci
                    mm(PS[sbank][:, ci * 128:(ci + 1) * 128], kT[rws, c * 128:(c + 1) * 128],
                       qT[rws, c * 128:(c + 1) * 128], True, True, rd=[r_uq[set_], r_uk[set_]], wr=[r_ps[sbank]])
                pi = cb % 3
                tt("dve", pb(pi).rearrange("p (t c) -> p t c", c=128),
                   PS[sbank][:, :].rearrange("p (t c) -> p t c", c=128),
                   ctab(h).unsqueeze(1).to_broadcast([128, 4, 128]), ALU.mult, rd=[r_ps[sbank], r_ctab],
                   wr=[r_pb[pi]])
                for ci in range(4):
                    c = 4 * cb + ci
                    o = PS[6][:, ci * 128:(ci + 1) * 128]
                    mm(o, V[:, hh, c, :], pb(pi)[:, ci * 128:(ci + 1) * 128], True, False,
                       rd=[r_pb[pi], r_uv[set_]], wr=[r_ps[6]])
                    mm(o, STF[rws, c, :], QDF[rws, c * 128:(c + 1) * 128], False, False,
                       rd=[r_cder[4], r_cder[2]], wr=[r_ps[6]])
                    mm(o, STB[rws, c, :], QDB[rws, c * 128:(c + 1) * 128], False, True,
                       rd=[r_cder[5], r_cder[3]], wr=[r_ps[6]])
                if pend[0] is not None:
                    pend[0]()
                fo = 7 + (hh * 4 + cb) % 2
                cp("dve", ft(fo), PS[6][:, :], rd=[r_ps[6]], wr=[r_ft[fo]])

                def epi(fo=fo, ch=ch, cb=cb):
                    act(pb(3), ft(fo), AF.Square, rd=[r_ft[fo]], wr=[r_pb[3]])
                    mm(PS[1][:, :], ONES, pb(3), True, True, rd=[r_pb[3], r_const], wr=[r_ps[1]])
                    rstd_from(PS[1][:, :], r_ps[1], ft(6), r_ft[6], 1.0 / 128)
                    stt("dve", y_(ch)[:, cb * TB:(cb + 1) * TB], ft(fo), PCOL[:, 21:22], ft(6), ALU.mult, ALU.mult,
                        rd=[r_ft[fo], r_ft[6], r_pcol], wr=[r_y[ch][cb], r_yall[ch]])

                pend[0] = epi
                yield
        if pend[0] is not None:
            pend[0]()
        yield

    def layer_params(l):
        lam_init = 0.8 - 0.6 * math.exp(-0.3 * (l + l0))
        P.dma("sp", "par", lambda e: e.dma_start(out=PCOL[:, :], in_=pcol_d[l]), wr=[r_pcol])
        P.dma("sp", "par", lambda e: e.dma_start(out=CTAB[:, 8 * 128:12 * 128].rearrange("p (c n) -> p c n", n=128),
                                                   in_=arow_d[l].rearrange("c p n -> p c n")), wr=[r_ctab])
        rp = [r_pcol]
        tt("dve", SMALL[:, 2:3], PCOL[:, 22:23], PCOL[:, 23:24], ALU.mult, rd=rp, wr=[r_small])
        tt("dve", SMALL[:, 3:4], PCOL[:, 24:25], PCOL[:, 25:26], ALU.mult, rd=rp, wr=[r_small])
        mm(PS[2][:, 0:2], ONESF[0:64, :], SMALL[0:64, 2:4], True, True, rd=[r_small, r_const], wr=[r_ps[2]])
        act(SMALL[:, 4:6], PS[2][:, 0:2], AF.Exp, rd=[r_ps[2]], wr=[r_small])
        tt("dve", SMALL[:, 6:7], SMALL[:, 5:6], SMALL[:, 4:5], ALU.subtract, rd=[r_small], wr=[r_small])
        ts("dve", SMALL[:, 0:1], SMALL[:, 6:7], -lam_init, None, ALU.add, None, rd=[r_small], wr=[r_small])
        ts("dve", SMALL[:, 1:2], PCOL[:, 20:21], 1.0 - lam_init, None, ALU.mult, None, rd=rp + [r_small],
           wr=[r_small])
        act(SMALL[:, 12:16], PCOL[:, 26:30], AF.Exp, rd=rp + [r_small], wr=[r_small])
        ts("dve", SMALL[:, 12:16], SMALL[:, 12:16], -1.0, None, ALU.mult, None, rd=[r_small], wr=[r_small])
        act(SMALL[:, 8:12], SMALL[:, 12:16], AF.Exp, rd=[r_small], wr=[r_small], scale=128.0)
        act(SMALL[:, 16:24], PCOL[:, 30:38], AF.Exp, rd=rp + [r_small], wr=[r_small])
        ts("dve", SMALL[:, 16:24], SMALL[:, 16:24], -1.0, None, ALU.mult, None, rd=[r_small], wr=[r_small])
        rs = [r_small, r_const]
        for h in range(4):
            act(ctab(h), RELF, AF.Exp, rd=rs, wr=[r_ctab], scale=SMALL[:, 16 + h:17 + h])
            tt("dve", ctab(h), ctab(h), MKF8, ALU.mult, rd=[r_ctab, r_const], wr=[r_ctab])
            act(ctab(12), RELB, AF.Exp, rd=rs + [r_ctab], wr=[r_ctab], scale=SMALL[:, 20 + h:21 + h])
            tt("dve", ctab(12), ctab(12), MKB8, ALU.mult, rd=[r_ctab, r_const], wr=[r_ctab])
            tt("dve", ctab(h), ctab(h), ctab(12), ALU.add, rd=[r_ctab], wr=[r_ctab])
        for p in range(2):
            act(ctab(4 + p), POS1, AF.Exp, rd=rs + [r_ctab], wr=[r_ctab], scale=SMALL[:, 12 + p:13 + p])
            act(ctab(6 + p), POSB, AF.Exp, rd=rs + [r_ctab], wr=[r_ctab], scale=SMALL[:, 14 + p:15 + p])
            ts("dve", ctab(4 + p), ctab(4 + p), 0.125, None, ALU.mult, None, rd=[r_ctab], wr=[r_ctab])
            ts("dve", ctab(6 + p), ctab(6 + p), 0.125, None, ALU.mult, None, rd=[r_ctab], wr=[r_ctab])
            for d in range(2):
                dstt = ctab(8 + 2 * d + p)
                src = dstt
                act(dstt, src, AF.Exp, rd=rp + [r_ctab], wr=[r_ctab])
                ts("dve", dstt, dstt, -1.0, None, ALU.mult, None, rd=[r_ctab], wr=[r_ctab])
                act(dstt, dstt, AF.Exp, rd=[r_ctab, r_const], wr=[r_ctab], scale=(TOKCR if d == 0 else TOKC))

    def interleave(a, b):
        da = db = False
        while not (da and db):
            if not da:
                try:
                    next(a)
                except StopIteration:
                    da = True
            if b is None:
                db = True
            if not db:
                try:
                    next(b)
                except StopIteration:
                    db = True

    def drain(g):
        for _ in g:
            pass

    stop_state = {}
    attn_fn = {"A": attn_A, "B": attn_B, "C": attn_C}

    def ck(name):
        if stop == name:
            raise _Stop()

    def layer(l):
        layer_params(l)
        ck('params')
        for u in (units if stop not in ('xc', 'xc2') else units[7:]):
            push_unit_weights(l, u)
        norm(0)
        ck('norm1')
        for kc in range(KC):
            P.dma("sp", "xsp", lambda e, kc=kc: e.dma_start(out=xsp_d[kc * 128:(kc + 1) * 128, :],
                                                           in_=XY[:, kc * S:(kc + 1) * S]),
                  rd=r_x[kc], wr=[r_xsp])
        P.barrier(dma_keys=("xsp",))
        ck('spill')
        if stop in ('xc', 'xc2'):
            drain(inproj(l, 7))
            ck('xc')
            drain(attn_C(l, 7))
            ck('xc2')
        drain(inproj(l, 0))
        ck('inproj0')
        for ui in range(len(units)):
            a = attn_fn[units[ui]["kind"]](l, ui)
            b = inproj(l, ui + 1) if ui + 1 < len(units) else None
            interleave(a, b)
            ck('u%d' % ui)
        P.barrier()
        ck('mix')
        if dbg and l == 0:
            P.dma("sp", "dbg", lambda e: e.dma_start(out=ydbg_d.rearrange("(c p) s -> p c s", p=128),
                                                       in_=Ybf.rearrange("p (c s) -> p c s", s=S)), rd=[], wr=[])
            P.barrier(dma_keys=("dbg",))
        for h in range(4):
            WS.push(w_in_d[l, C_CG + h])
        for j in range(8):
            WS.push(w_br_d[l, j], 11)
            for i in range(3):
                WS.push(w_in_d[l, C_GATE + 8 * i + j])
        for oc in range(8):
            WS.push(w_o_d[l, oc])

        def nbank():
            b = rot["ps"] % 8
            rot["ps"] += 1
            return b

        for h in range(4):
            slot = WS.pop()
            ch = 7 + h
            for tb in range(NB):
                bank = nbank()
                proj_fm(slot, tb, bank)
                j2 = tb % 2
                act(ub(j2), PS[bank][:, :], AF.Silu, rd=[r_ps[bank]], wr=[r_ub[j2]])
                ysl = y_(ch)[:, tb * TB:(tb + 1) * TB]
                tt("pool", ysl, ysl, ub(j2), ALU.mult, rd=[r_ub[j2], r_y[ch][tb], r_yall[ch]], wr=[r_y[ch][tb]])
            WS.release(slot)
        ybr = ((0, 3), (3, 7), (7, 11))
        cnt4 = 0
        for j in range(8):
            bslot = WS.pop()
            wb = wslot(bslot, 11)
            for i in range(3):
                gslot = WS.pop()
                c0, c1 = ybr[i]
                for tb in range(NB):
                    gb = nbank()
                    proj_fm(gslot, tb, gb)
                    sg = 4 + cnt4 % 2
                    tmp = 6 + cnt4 % 2
                    cnt4 += 1
                    act(ft(sg), PS[gb][:, :], AF.Sigmoid, rd=[r_ps[gb]], wr=[r_ft[sg]])
                    bb = nbank()
                    for ch in range(c0, c1):
                        mm(PS[bb][:, :], wb[:, ch, :], y_(ch)[:, tb * TB:(tb + 1) * TB], ch == c0, ch == c1 - 1,
                           rd=[r_w[bslot], r_y[ch][tb], r_yall[ch]], wr=[r_ps[bb]])
                    if i == 0:
                        tt("dve", ft(tb), PS[bb][:, :], ft(sg), ALU.mult, rd=[r_ps[bb], r_ft[sg]], wr=[r_ft[tb]])
                    else:
                        tt("dve", ft(tmp), PS[bb][:, :], ft(sg), ALU.mult, rd=[r_ps[bb], r_ft[sg]], wr=[r_ft[tmp]])
                        if i == 1:
                            tt("pool", ft(tb), ft(tb), ft(tmp), ALU.add, rd=[r_ft[tb], r_ft[tmp]], wr=[r_ft[tb]])
                        else:
                            tt("pool", m_(j, tb), ft(tb), ft(tmp), ALU.add, rd=[r_ft[tb], r_ft[tmp]],
                               wr=[r_m[j][tb]])
                WS.release(gslot)
            WS.release(bslot)
        P.barrier()
        if stop == '4a':
            for kc in range(KC):
                P.dma('pool', 'dbg', lambda e, kc=kc: e.dma_start(out=out_d[kc * 128:(kc + 1) * 128, :], in_=M[:, kc * S:(kc + 1) * S]), rd=[], wr=[])
            P.barrier(dma_keys=('dbg',))
            stop_state['skip_out'] = True
        ck('4a')
        for kc in range(KC):
            P.dma("sp", "xld", lambda e, kc=kc: e.dma_start(out=XY[:, kc * S:(kc + 1) * S],
                                                           in_=xsp_d[kc * 128:(kc + 1) * 128, :]),
                  rd=[r_xsp], wr=r_x[kc])
        for hg in range(4):
            for c in range(8):
                WS.push(w_m1_d[l, hg * 8 + c])
            for oc in range(8):
                WS.push(w_m2_d[l, hg * 8 + oc])
        for oc in range(8):
            slot = WS.pop()
            w = wslot(slot)
            for tb in range(NB):
                bank = nbank()
                for kc in range(KC):
                    mm(PS[bank][:, :], w[:, kc, :], m_(kc, tb), kc == 0, kc == KC - 1,
                       rd=[r_w[slot], r_m[kc][tb]], wr=[r_ps[bank]])
                tt("dve", x_(oc, tb), x_(oc, tb), PS[bank][:, :], ALU.add, rd=[r_ps[bank], r_x[oc][tb]],
                   wr=[r_x[oc][tb]])
            WS.release(slot)
        ck('4b')
        norm(8)
        ck('norm2')
        for hg in range(4):
            for c in range(8):
                slot = WS.pop()
                for tb in range(NB):
                    bank = nbank()
                    proj_fm(slot, tb, bank)
                    j2 = (c * NB + tb) % 2
                    act(ft(j2), PS[bank][:, :], AF.Relu, rd=[r_ps[bank]], wr=[r_ft[j2]])
                    tt("pool", m_(c, tb), ft(j2), ft(j2), ALU.mult, rd=[r_ft[j2]], wr=[r_m[c][tb]])
                WS.release(slot)
            for oc in range(8):
                slot = WS.pop()
                w = wslot(slot)
                for tb in range(NB):
                    bank = nbank()
                    for c in range(8):
                        mm(PS[bank][:, :], w[:, c, :], m_(c, tb), c == 0, c == 7,
                           rd=[r_w[slot], r_m[c][tb]], wr=[r_ps[bank]])
                    tt("dve", x_(oc, tb), x_(oc, tb), PS[bank][:, :], ALU.add, rd=[r_ps[bank], r_x[oc][tb]],
                       wr=[r_x[oc][tb]])
                WS.release(slot)
        if dbg and l == 0:
            for kc in range(KC):
                P.dma("sp", "dbg", lambda e, kc=kc: e.dma_start(out=xdbg_d[kc * 128:(kc + 1) * 128, :],
                                                               in_=XY[:, kc * S:(kc + 1) * S]), rd=r_x[kc], wr=[])

    try:
        for l in range(nl):
            layer(l)
    except _Stop:
        pass

    for kc in range(KC if not stop_state.get('skip_out') else 0):
        P.dma("sp", "out", lambda e, kc=kc: e.dma_start(out=out_d[kc * 128:(kc + 1) * 128, :],
                                                       in_=XY[:, kc * S:(kc + 1) * S]), rd=r_x[kc], wr=[])
    final_waits = tuple((k, n) for k, n in P.dcnt.items() if k in ("out", "dbg", "xsp"))
    P.q["sp"].append((final_waits, None, None))

    keys = list(P.ENG) + sorted(P.dcnt.keys())
    sems = {k: es.enter_context(nc.semaphore("s_" + k)) for k in keys}
    stats = {e: len(P.q[e]) for e in P.ENG}

    def run(e, name):
        for waits, fn, inc in P.q[name]:
            for k, n in waits:
                e.wait_ge(sems[k], n)
            if fn is not None:
                fn(e).then_inc(sems[inc[0]], inc[1])

    with nc.Block() as block:
        @block.tensor
        def _(e):
            run(e, "pe")

        @block.scalar
        def _(e):
            run(e, "act")

        @block.vector
        def _(e):
            run(e, "dve")

        @block.gpsimd
        def _(e):
            run(e, "pool")

        @block.sync
        def _(e):
            run(e, "sp")
    es.close()
    return nc, stats


def _rope_tables():
    pos = np.arange(S, dtype=np.float32)

    def tab(rot, theta, hd):
        inv = (1.0 / (np.float32(theta) ** (np.arange(0, rot, 2, dtype=np.float32) / np.float32(rot)))).astype(np.float32)
        ang = (pos[:, None] * inv[None, :]).astype(np.float32)
        c, s = np.cos(ang).astype(np.float32), np.sin(ang).astype(np.float32)
        r2 = rot // 2
        cosF = np.ones((hd, S), np.float32)
        sinF = np.zeros((hd, S), np.float32)
        cosF[0:r2] = c.T
        cosF[r2:2 * r2] = c.T
        sinF[0:r2] = s.T
        sinF[r2:2 * r2] = s.T
        return np.tile(cosF, (128 // hd, 1)), np.tile(sinF, (128 // hd, 1))

    cab, sab = tab(16, 500000.0, 64)
    cc, sc = tab(64, 10000.0, 64)
    return np.ascontiguousarray(np.stack([cab, sab, cc, sc]).astype(np.float32))


def _rot_mat(r2, hd=64):
    R = np.zeros((128, 128), np.float32)
    for b in range(128 // hd):
        for d in range(r2):
            R[b * hd + d + r2, b * hd + d] = -1.0
            R[b * hd + d, b * hd + d + r2] = 1.0
    return R


def _constants():
    ones = np.ones((128, 128), np.float32)
    bones = np.zeros((128, 128), np.float32)
    bones[0:64, 0:64] = 1.0
    bones[64:128, 64:128] = 1.0
    ident = np.eye(128, dtype=np.float32)
    cmats = np.stack([ones, bones, _rot_mat(8), _rot_mat(32), ident]).astype(np.float32)
    k = np.arange(128)[:, None]
    q = np.arange(512)[None, :]
    masks = []
    for rel in range(-1, 5):
        masks.append((np.abs(q - (128 * rel + k)) <= 64).astype(np.float32))
    band = (np.abs(np.arange(128)[None, :] - k) <= 64).astype(np.float32)
    masks.append(np.tile(band, (1, 4)))
    cmasks = np.stack(masks).astype(np.float32)
    m = np.arange(128)[:, None].astype(np.float32)
    n = np.arange(128)[None, :].astype(np.float32)
    relf = np.maximum(n - m, 0.0)
    relb = np.maximum(m - n, 0.0)
    mkf8 = (n >= m).astype(np.float32) * 0.125
    mkb8 = (m >= n).astype(np.float32) * 0.125
    pos1 = np.broadcast_to(n + 1.0, (128, 128))
    posb = np.broadcast_to(128.0 - n, (128, 128))
    tok = np.arange(128, dtype=np.float32)[:, None]
    cf = np.concatenate([relf, relb, mkf8, mkb8, pos1, posb, ones, tok, 127.0 - tok], axis=1).astype(np.float32)
    return cmats, cmasks, np.ascontiguousarray(cf)


def _prep_layers(inp, layers):
    f = lambda a: np.asarray(a, dtype=np.float32)
    nl = len(layers)
    w_in = f(inp["w_in"])[layers].reshape(nl, 8, 128, 57, 128).transpose(0, 3, 2, 1, 4)
    wbr = np.concatenate([f(inp["w_br_a"])[layers], f(inp["w_br_b"])[layers], f(inp["w_br_c"])[layers]], axis=1)
    w_br = wbr.reshape(nl, 11, 128, 8, 128).transpose(0, 3, 2, 1, 4)
    w_o = f(inp["w_o"])[layers].reshape(nl, 8, 128, 8, 128).transpose(0, 3, 2, 1, 4)
    w_m1 = f(inp["w_mlp1"])[layers].reshape(nl, 8, 128, 32, 128).transpose(0, 3, 2, 1, 4)
    w_m2 = f(inp["w_mlp2"])[layers].reshape(nl, 4, 8, 128, 8, 128).transpose(0, 1, 4, 3, 2, 5).reshape(nl, 32, 128, 8, 128)
    pcol = np.zeros((nl, 128, NPC), np.float32)
    arow = np.zeros((nl, 4, 128, 128), np.float32)
    for i, l in enumerate(layers):
        pcol[i, :, 0:8] = f(inp["norm1_g"])[l].reshape(8, 128).T
        pcol[i, :, 8:16] = f(inp["norm2_g"])[l].reshape(8, 128).T
        pcol[i, :, 16] = np.tile(f(inp["a_q_norm_g"])[l], 2)
        pcol[i, :, 17] = np.tile(f(inp["a_k_norm_g"])[l], 2)
        pcol[i, :, 18] = np.tile(f(inp["b_q_norm_g"])[l], 2)
        pcol[i, :, 19] = np.tile(f(inp["b_k_norm_g"])[l], 2)
        pcol[i, :, 20] = f(inp["b_out_norm_g"])[l]
        pcol[i, :, 21] = f(inp["c_out_norm_g"])[l]
        pcol[i, :, 22] = np.tile(f(inp["b_lambda_q1"])[l], 2)
        pcol[i, :, 23] = np.tile(f(inp["b_lambda_k1"])[l], 2)
        pcol[i, :, 24] = np.tile(f(inp["b_lambda_q2"])[l], 2)
        pcol[i, :, 25] = np.tile(f(inp["b_lambda_k2"])[l], 2)
        af, ab = f(inp["c_decay_f"])[l], f(inp["c_decay_b"])[l]
        for p in range(2):
            pcol[i, :, 26 + p] = np.repeat(af[2 * p:2 * p + 2], 64)
            pcol[i, :, 28 + p] = np.repeat(ab[2 * p:2 * p + 2], 64)
            arow[i, p] = np.repeat(af[2 * p:2 * p + 2], 64)[None, :]
            arow[i, 2 + p] = np.repeat(ab[2 * p:2 * p + 2], 64)[None, :]
        pcol[i, :, 30:34] = af[None, :]
        pcol[i, :, 34:38] = ab[None, :]
    c = np.ascontiguousarray
    return dict(w_in=c(w_in), w_br=c(w_br), w_o=c(w_o), w_m1=c(w_m1), w_m2=c(w_m2), pcol=pcol, arow=arow)


_CACHE = {}


def _get_prog(nl, l0):
    if (nl, l0) not in _CACHE:
        _CACHE[(nl, l0)] = build(nl, l0=l0)[0]
    return _CACHE[(nl, l0)]


FUSED = True


def kernel(**inputs):
    x = np.asarray(inputs["x"], dtype=np.float32)
    B = x.shape[0]
    cmats, cmasks, cf = _constants()
    rope = _rope_tables()
    consts = dict(c_mats=cmats, c_masks=cmasks, c_f32=cf, rope=rope)
    xT = [np.ascontiguousarray(x[b].T) for b in range(B)]
    groups = [list(range(DEPTH))] if FUSED else [[l] for l in range(DEPTH)]
    for layers in groups:
        nc = _get_prog(len(layers), layers[0])
        shared = _prep_layers(inputs, layers)
        shared.update(consts)
        in_maps = []
        for b in range(B):
            d = dict(shared)
            d["xT"] = xT[b]
            in_maps.append(d)
        res = run_bass_kernel_spmd(nc, in_maps, core_ids=list(range(B)))
        xT = [np.ascontiguousarray(np.asarray(r["outT"], dtype=np.float32)) for r in res.results]
    out = np.stack([t.T for t in xT], axis=0)
    return np.ascontiguousarray(out.astype(np.float32))
```

```python
import math
from contextlib import ExitStack
import numpy as np
import ml_dtypes
import concourse.bass as bass
import concourse.mybir as mybir
from concourse.bass_utils import run_bass_kernel_spmd

F32, BF16 = mybir.dt.float32, mybir.dt.bfloat16
ALU, AF = mybir.AluOpType, mybir.ActivationFunctionType

S = 2048
D = 1024
KC = 8
NB = 4
TB = 512
DEPTH = 4
EPS = 1e-6
NPC = 38
C_AQ, C_AK, C_AV = 0, 3, 6
C_BQ, C_BK, C_BV = 9, 13, 17
C_CQ, C_CK, C_CV, C_CG = 21, 23, 25, 29
C_GATE = 33
NWSLOT = 5


class Res:
    __slots__ = ("w", "r", "x")

    def __init__(self, x=False):
        self.w = None
        self.r = {}
        self.x = x


class Prog:
    ENG = ("pe", "act", "dve", "pool", "sp")

    def __init__(self):
        self.q = {e: [] for e in self.ENG}
        self.cnt = {e: 0 for e in self.ENG}
        self.seen = {e: {} for e in self.ENG}
        self.dcnt = {}

    def _need(self, eng, dep, waits, raw):
        if dep is None:
            return
        k, n = dep
        if k == eng:
            if eng == "pe":
                return
        if k in self.dcnt:
            n = max(n, self.dcnt[k])
        if self.seen[eng].get(k, 0) >= n:
            return
        waits[k] = max(waits.get(k, 0), n)

    def _deps(self, eng, rd, wr):
        waits = {}
        for r in rd:
            self._need(eng, r.w, waits, True)
        for w in wr:
            self._need(eng, w.w, waits, False)
            for k, n in w.r.items():
                self._need(eng, (k, n), waits, False)
        for k, n in waits.items():
            self.seen[eng][k] = n
        return tuple(waits.items())

    def op(self, eng, fn, rd=(), wr=()):
        wr = list(wr) + [r for r in rd if r.x]
        rd = [r for r in rd if not r.x]
        waits = self._deps(eng, rd, wr)
        self.cnt[eng] += 1
        n = self.cnt[eng]
        self.q[eng].append((waits, fn, (eng, 1)))
        for r in rd:
            r.r[eng] = max(r.r.get(eng, 0), n)
        for w in wr:
            w.w = (eng, n)
            w.r = {}

    def dma(self, qeng, semkey, fn, rd=(), wr=()):
        waits = self._deps(qeng, rd, wr)
        self.dcnt[semkey] = self.dcnt.get(semkey, 0) + 16
        n = self.dcnt[semkey]
        self.q[qeng].append((waits, fn, (semkey, 16)))
        for r in rd:
            r.r[semkey] = max(r.r.get(semkey, 0), n)
        for w in wr:
            w.w = (semkey, n)
            w.r = {}

    def barrier(self, dma_keys=()):
        for e in self.ENG:
            waits = {}
            for f in self.ENG:
                if f != e and self.cnt[f] > self.seen[e].get(f, 0):
                    waits[f] = self.cnt[f]
            for k in dma_keys:
                if self.dcnt.get(k, 0) > self.seen[e].get(k, 0):
                    waits[k] = self.dcnt[k]
            for k, n in waits.items():
                self.seen[e][k] = n
            if waits:
                self.q[e].append((tuple(waits.items()), None, None))


class _Stop(Exception):
    pass


def build(nl, dbg=False, stop=None, l0=0):
    nc = bass.Bass("TRN2", target_bir_lowering=False)

    def din(name, shape, dt=F32):
        return nc.dram_tensor(name, list(shape), dt, kind="ExternalInput").ap()

    xT_d = din("xT", [D, S])
    w_in_d = din("w_in", [nl, 57, 128, 8, 128])
    w_br_d = din("w_br", [nl, 8, 128, 11, 128])
    w_o_d = din("w_o", [nl, 8, 128, 8, 128])
    w_m1_d = din("w_m1", [nl, 32, 128, 8, 128])
    w_m2_d = din("w_m2", [nl, 32, 128, 8, 128])
    pcol_d = din("pcol", [nl, 128, NPC])
    arow_d = din("arow", [nl, 4, 128, 128])
    cmat_d = din("c_mats", [5, 128, 128])
    cmask_d = din("c_masks", [7, 128, 512])
    cf32_d = din("c_f32", [128, 7 * 128 + 2])
    rope_d = din("rope", [4, 128, S])
    out_d = nc.dram_tensor("outT", [D, S], F32, kind="ExternalOutput").ap()
    xsp_d = nc.dram_tensor("xspill", [D, S], F32, kind="Internal").ap()
    if dbg:
        ydbg_d = nc.dram_tensor("ydbg", [11 * 128, S], BF16, kind="ExternalOutput").ap()
        xdbg_d = nc.dram_tensor("xdbg", [D, S], F32, kind="ExternalOutput").ap()

    P = Prog()
    es = ExitStack()

    def sb(name, shape, dt):
        return es.enter_context(nc.sbuf_tensor(name, list(shape), dt))

    XN = sb("XN", [128, KC * S], BF16)
    XY = sb("XY", [128, KC * S], F32)
    M = sb("M", [128, KC * S], BF16)
    WR = sb("WR", [128, NWSLOT * 11 * 128], BF16)
    CMB = sb("CMB", [128, 5 * 128], BF16)
    MASKS = sb("MASKS", [128, 7 * 512], BF16)
    CF = sb("CF", [128, 7 * 128 + 2], F32)
    PCOL = sb("PCOL", [128, NPC], F32)
    SMALL = sb("SMALL", [128, 64], F32)
    CTAB = sb("CTAB", [128, 13 * 128], F32)
    CDER = sb("CDER", [128, 6 * 2048], BF16)
    S32 = sb("S32", [128, 2 * 128], F32)
    SQB = sb("SQB", [128, 2 * 512], BF16)
    UB = sb("UB", [128, 2 * 512], BF16)
    FT = sb("FT", [128, 9 * 512], F32)

    Ybf = XY[:, 0:11 * 1024].bitcast(BF16)
    ROPE = XY[:, 11 * 1024:11 * 1024 + 4096]
    PB = XY[:, 15 * 1024:16 * 1024].bitcast(BF16)
    DTOT = CDER[:, 4 * 2048:6 * 2048].bitcast(F32)

    def x_(kc, tb):
        return XY[:, kc * S + tb * TB: kc * S + (tb + 1) * TB]

    def xn_(kc, tb):
        return XN[:, kc * S + tb * TB: kc * S + (tb + 1) * TB]

    def xnfull(kc):
        return XN[:, kc * S:(kc + 1) * S]

    def y_(ch):
        return Ybf[:, ch * S:(ch + 1) * S]

    def m_(kc, tb):
        return M[:, kc * S + tb * TB: kc * S + (tb + 1) * TB]

    def uq(s):
        return M[:, s * 8192: s * 8192 + 2048]

    def uk(s):
        return M[:, s * 8192 + 2048: s * 8192 + 4096]

    def uv(s):
        return M[:, s * 8192 + 4096: s * 8192 + 8192].rearrange("p (h t c) -> p h t c", h=2, c=128)

    def kz(s, m):
        return uk(s) if m == 0 else M[:, s * 8192 + 6144: s * 8192 + 8192]

    def wslot(s, nk=8):
        return WR[:, s * 1408: s * 1408 + nk * 128].rearrange("p (k n) -> p k n", n=128)

    ONES = CMB[:, 0:128]
    BONES = CMB[:, 128:256]
    RAB = CMB[:, 256:384]
    RC = CMB[:, 384:512]
    IDENT = CMB[:, 512:640]

    def mask_(i):
        return MASKS[:, i * 512:(i + 1) * 512]

    RELF, RELB = CF[:, 0:128], CF[:, 128:256]
    MKF8, MKB8 = CF[:, 256:384], CF[:, 384:512]
    POS1, POSB = CF[:, 512:640], CF[:, 640:768]
    ONESF = CF[:, 768:896]
    TOKC, TOKCR = CF[:, 896:897], CF[:, 897:898]

    def ctab(i):
        return CTAB[:, i * 128:(i + 1) * 128]

    def cder(i):
        return CDER[:, i * 2048:(i + 1) * 2048]

    KDF, KDB = cder(0).rearrange("p (t c) -> p t c", c=128), cder(1).rearrange("p (t c) -> p t c", c=128)
    QDF, QDB = cder(2), cder(3)
    STF, STB = cder(4).rearrange("p (t c) -> p t c", c=128), cder(5).rearrange("p (t c) -> p t c", c=128)

    def pb(i):
        return PB[:, i * 512:(i + 1) * 512]

    def sqb(i):
        return SQB[:, i * 512:(i + 1) * 512]

    def ub(i):
        return UB[:, i * 512:(i + 1) * 512]

    def ft(i):
        return FT[:, i * 512:(i + 1) * 512]

    PS = [es.enter_context(nc.psum_tensor("ps%d" % i, [128, 512], F32)) for i in range(8)]

    r_x = [[Res() for _ in range(NB)] for _ in range(KC)]
    r_xn = [Res() for _ in range(NB)]
    r_y = [[Res() for _ in range(NB)] for _ in range(11)]
    r_yall = [Res() for _ in range(11)]
    r_m = [[Res() for _ in range(NB)] for _ in range(KC)]
    r_uq = [Res(), Res()]
    r_uk = [Res(), Res()]
    r_uv = [Res(), Res()]
    r_w = [Res() for _ in range(NWSLOT)]
    r_ps = [Res(True) for _ in range(8)]
    r_pb = [Res() for _ in range(4)]
    r_sqb = [Res(), Res()]
    r_ub = [Res(), Res()]
    r_ft = [Res() for _ in range(9)]
    r_rope = Res()
    r_acc = Res()
    r_const = Res()
    r_pcol = Res()
    r_small = Res()
    r_ctab = Res()
    r_cder = [Res() for _ in range(6)]
    r_s32 = [Res(), Res()]
    r_xsp = Res()

    wstate = {"n": 0}

    def load_w(dram_ap, nk, s):
        dst = wslot(s, nk)
        P.dma("pool", "w%d" % s, lambda e, dst=dst, src=dram_ap: e.dma_start(out=dst, in_=src), wr=[r_w[s]])
        return s

    class WStream:
        def __init__(self):
            self.pending = []
            self.loaded = []
            self.free = list(range(NWSLOT))

        def push(self, ap, nk=8):
            self.pending.append((ap, nk))

        def prefetch(self):
            while self.pending and self.free:
                ap, nk = self.pending.pop(0)
                self.loaded.append(load_w(ap, nk, self.free.pop(0)))

        def pop(self):
            self.prefetch()
            return self.loaded.pop(0)

        def release(self, s):
            self.free.append(s)
            self.prefetch()

    WS = WStream()

    def mm(out, lhsT, rhs, start, stop, rd, wr):
        P.op("pe", lambda e: e.matmul(out, lhsT=lhsT, rhs=rhs, start=start, stop=stop), rd=rd, wr=wr)

    def act(out, in_, func, rd, wr, scale=None, bias=None):
        kw = {}
        if scale is not None:
            kw["scale"] = scale
        if bias is not None:
            kw["bias"] = bias
        P.op("act", lambda e: e.activation(out=out, in_=in_, func=func, **kw), rd=rd, wr=wr)

    def tt(eng, out, in0, in1, op, rd, wr):
        P.op(eng, lambda e: e.tensor_tensor(out=out, in0=in0, in1=in1, op=op), rd=rd, wr=wr)

    def stt(eng, out, in0, scalar, in1, op0, op1, rd, wr):
        P.op(eng, lambda e: e.scalar_tensor_tensor(out=out, in0=in0, scalar=scalar, in1=in1, op0=op0, op1=op1),
             rd=rd, wr=wr)

    def ts(eng, out, in0, s1, s2, op0, op1, rd, wr):
        if op1 is None:
            P.op(eng, lambda e: e.tensor_scalar(out=out, in0=in0, scalar1=s1, scalar2=None, op0=op0), rd=rd, wr=wr)
        else:
            P.op(eng, lambda e: e.tensor_scalar(out=out, in0=in0, scalar1=s1, scalar2=s2, op0=op0, op1=op1),
                 rd=rd, wr=wr)

    def cp(eng, out, in_, rd, wr):
        if eng == "act":
            act(out, in_, AF.Copy, rd, wr)
        else:
            P.op(eng, lambda e: e.tensor_copy(out=out, in_=in_), rd=rd, wr=wr)

    def recip(out, in_, rd, wr):
        P.op("dve", lambda e: e.reciprocal(out=out, in_=in_), rd=rd, wr=wr)

    def rstd_from(ps_ap, r_in, ftile, r_ftile, inv_n):
        act(ftile, ps_ap, AF.Ln, rd=[r_in], wr=[r_ftile], scale=inv_n, bias=EPS)
        act(ftile, ftile, AF.Exp, rd=[r_ftile], wr=[r_ftile], scale=-0.5)

    rot = {"ps": 0, "alt": 0}
    SBANK = (4, 5, 3)

    P.dma("pool", "cst", lambda e: e.dma_start(out=CMB[:, :].rearrange("p (c n) -> p c n", n=128),
                                                  in_=cmat_d.rearrange("c p n -> p c n")), wr=[r_const])
    P.dma("pool", "cst", lambda e: e.dma_start(out=MASKS[:, :].rearrange("p (c n) -> p c n", n=512),
                                                  in_=cmask_d.rearrange("c p n -> p c n")), wr=[r_const])
    P.dma("sp", "cstf", lambda e: e.dma_start(out=CF[:, :], in_=cf32_d), wr=[r_const])
    P.barrier(dma_keys=("cst", "cstf"))
    for tb in range(NB):
        P.dma("sp", "xld", lambda e, tb=tb: e.dma_start(
            out=XY[:, :].rearrange("p (k s) -> p k s", k=KC)[:, :, tb * TB:(tb + 1) * TB],
            in_=xT_d.rearrange("(k p) s -> p k s", p=128)[:, :, tb * TB:(tb + 1) * TB]),
            wr=[r_x[kc][tb] for kc in range(KC)])

    def norm(gcol0):
        for tb in range(NB):
            for kc in range(KC):
                i = kc % 2
                act(sqb(i), x_(kc, tb), AF.Square, rd=[r_x[kc][tb]], wr=[r_sqb[i]])
                mm(PS[2][:, :], ONES, sqb(i), kc == 0, kc == KC - 1, rd=[r_sqb[i], r_const], wr=[r_ps[2]])
            fr = tb % 2
            rstd_from(PS[2][:, :], r_ps[2], ft(fr), r_ft[fr], 1.0 / D)
            for kc in range(KC):
                stt("dve", xn_(kc, tb), x_(kc, tb), PCOL[:, gcol0 + kc: gcol0 + kc + 1], ft(fr), ALU.mult, ALU.mult,
                    rd=[r_x[kc][tb], r_ft[fr], r_pcol], wr=[r_xn[tb]])

    def proj_fm(slot, tb, bank):
        w = wslot(slot)
        for kc in range(KC):
            mm(PS[bank][:, :], w[:, kc, :], xn_(kc, tb), kc == 0, kc == KC - 1,
               rd=[r_w[slot], r_xn[tb]], wr=[r_ps[bank]])

    def perm_dst(buf, tb, delta):
        if delta == 1:
            return buf[:, tb * TB:(tb + 1) * TB], None
        n0 = tb * TB // delta
        return buf.rearrange("p (r n) -> p n r", r=delta)[:, n0:n0 + TB // delta, :], delta

    def srcv(ap, delta):
        if delta is None:
            return ap
        return ap.rearrange("p (n r) -> p n r", r=delta)

    def qk_post(bank, tb, gcol, rmat, normed, dsts, delta, feng="pool", aux=2):
        j = rot["alt"] % 2
        rot["alt"] += 1
        pj = PS[bank][:, :]
        cos = ROPE[:, tb * TB:(tb + 1) * TB]
        sin = ROPE[:, S + tb * TB: S + (tb + 1) * TB]
        if normed:
            act(sqb(j), pj, AF.Square, rd=[r_ps[bank]], wr=[r_sqb[j]])
            mm(PS[aux][:, :], BONES, sqb(j), True, True, rd=[r_sqb[j], r_const], wr=[r_ps[aux]])
            yield
            rstd_from(PS[aux][:, :], r_ps[aux], ft(j), r_ft[j], 1.0 / 64)
            stt("dve", ub(j), pj, PCOL[:, gcol:gcol + 1], ft(j), ALU.mult, ALU.mult,
                rd=[r_ps[bank], r_ft[j], r_pcol], wr=[r_ub[j]])
            yield
            mm(PS[aux][:, :], rmat, ub(j), True, True, rd=[r_ub[j], r_const], wr=[r_ps[aux]])
            tt(feng, ft(2 + j), ub(j), cos, ALU.mult, rd=[r_ub[j], r_rope], wr=[r_ft[2 + j]])
        else:
            cp("act", ub(j), pj, rd=[r_ps[bank]], wr=[r_ub[j]])
            tt("dve", ft(2 + j), pj, cos, ALU.mult, rd=[r_ps[bank], r_rope], wr=[r_ft[2 + j]])
            yield
            mm(PS[aux][:, :], rmat, ub(j), True, True, rd=[r_ub[j], r_const], wr=[r_ps[aux]])
        yield
        tt("dve", ft(4 + j), PS[aux][:, :], sin, ALU.mult, rd=[r_ps[aux], r_rope], wr=[r_ft[4 + j]])
        for buf, rows, res in dsts:
            dst, dl = perm_dst(buf, tb, delta)
            a0, a1 = srcv(ft(2 + j), dl), srcv(ft(4 + j), dl)
            if rows is not None:
                dst, a0, a1 = dst[rows[0]:rows[1]], a0[rows[0]:rows[1]], a1[rows[0]:rows[1]]
            tt(feng, dst, a0, a1, ALU.add, rd=[r_ft[2 + j], r_ft[4 + j]], wr=[res])

    def v_proj(slot, set_, coloff, delta):
        w = wslot(slot)
        V = uv(set_)
        for g4 in range(4):
            bank = 0
            for ti in range(4):
                t = g4 * 4 + ti
                for kc in range(KC):
                    if delta == 1:
                        lhs = xnfull(kc)[:, t * 128:(t + 1) * 128]
                    elif delta == 4:
                        lhs = xnfull(kc).rearrange("p (n r) -> p r n", r=4)[:, t // 4, (t % 4) * 128:(t % 4 + 1) * 128]
                    else:
                        lhs = xnfull(kc).rearrange("p (n r) -> p r n", r=16)[:, t, :]
                    mm(PS[bank][:, ti * 128:(ti + 1) * 128], lhs, w[:, kc, :], kc == 0, kc == KC - 1,
                       rd=[r_w[slot]] + r_xn, wr=[r_ps[bank]])
            cp("act", V[:, coloff // 128, g4 * 4:(g4 + 1) * 4, :],
               PS[bank][:, :].rearrange("p (t c) -> p t c", c=128), rd=[r_ps[bank]], wr=[r_uv[set_]])
            yield

    units = []
    for g in range(3):
        units.append(dict(kind="A", idx=g, q=C_AQ + g, k=C_AK + g, v=[C_AV + g], gq=16, gk=17, delta=(1, 4, 16)[g]))
    for h in range(4):
        units.append(dict(kind="B", idx=h, q=C_BQ + h, k=C_BK + h, v=[C_BV + h], gq=18, gk=19, delta=1))
    for p in range(2):
        units.append(dict(kind="C", idx=p, q=C_CQ + p, k=C_CK + p, v=[C_CV + 2 * p, C_CV + 2 * p + 1], gq=0, gk=0,
                          delta=1))

    def push_unit_weights(l, u):
        WS.push(w_in_d[l, u["q"]])
        WS.push(w_in_d[l, u["k"]])
        for c in u["v"]:
            WS.push(w_in_d[l, c])

    def rope_load(which):
        P.dma("sp", "rope", lambda e: e.dma_start(out=ROPE[:, 0:S], in_=rope_d[2 * which]), wr=[r_rope])
        P.dma("sp", "rope", lambda e: e.dma_start(out=ROPE[:, S:2 * S], in_=rope_d[2 * which + 1]), wr=[r_rope])

    def inproj(l, ui):
        u = units[ui]
        set_ = ui % 2
        normed = u["kind"] != "C"
        rmat = RC if u["kind"] == "C" else RAB
        if ui == 0:
            rope_load(0)
        if ui == 7:
            rope_load(1)
        feng = "dve" if ui == 0 else "pool"
        width = 2 if ui <= 3 else 1
        qslot = WS.pop()
        kslot = WS.pop()
        gens = []

        def cb_gen(slot, tb, gcol, dsts, k):
            bank = k % width
            proj_fm(slot, tb, bank)
            yield
            yield from qk_post(bank, tb, gcol, rmat, normed, dsts, u["delta"], feng, 2 + k % width)

        k = 0
        for which in ("q", "k"):
            if which == "q":
                dsts, gcol, slot = [(uq(set_), None, r_uq[set_])], u["gq"], qslot
            elif u["kind"] == "C":
                dsts, gcol, slot = [(uk(set_), None, r_uk[set_])], u["gk"], kslot
            else:
                k0, k1 = kz(set_, 0), kz(set_, 1)
                P.op("pool", lambda e, k0=k0: e.memset(k0[64:128, :], 0.0), wr=[r_uk[set_]])
                P.op("pool", lambda e, k1=k1: e.memset(k1[0:64, :], 0.0), wr=[r_uv[set_]])
                dsts, gcol, slot = [(k0, (0, 64), r_uk[set_]), (k1, (64, 128), r_uv[set_])], u["gk"], kslot
            for tb in range(NB):
                gens.append(cb_gen(slot, tb, gcol, dsts, k))
                k += 1
        active = []
        while gens or active:
            while gens and len(active) < width:
                active.append(gens.pop(0))
            for g_ in list(active):
                try:
                    next(g_)
                except StopIteration:
                    active.remove(g_)
            yield
        WS.release(qslot)
        WS.release(kslot)
        for vi, c in enumerate(u["v"]):
            slot = WS.pop()
            yield from v_proj(slot, set_, vi * 128, u["delta"])
            WS.release(slot)

    def attn_B(l, ui):
        u = units[ui]
        h = u["idx"]
        set_ = ui % 2
        qT, V = uq(set_), uv(set_)
        rdq = [r_uq[set_], r_uk[set_], r_uv[set_]]
        ch = 3 + h
        pending = []

        def make_epilogue(qb):
            def s1():
                act(sqb(0), ft(7), AF.Square, rd=[r_ft[7]], wr=[r_sqb[0]])

            def s2():
                mm(PS[1][:, :], ONES, sqb(0), True, True, rd=[r_sqb[0], r_const], wr=[r_ps[1]])

            def s3():
                rstd_from(PS[1][:, :], r_ps[1], ft(6), r_ft[6], 1.0 / 128)
                stt("dve", y_(ch)[:, qb * TB:(qb + 1) * TB], ft(7), SMALL[:, 1:2], ft(6), ALU.mult, ALU.mult,
                    rd=[r_ft[7], r_ft[6], r_small], wr=[r_y[ch][qb], r_yall[ch]])

            return [(9, s1), (11, s2), (13, s3)]

        for qb in range(NB):
            for m in range(2):
                def smm(kt):
                    bank = SBANK[kt % 3]
                    mm(PS[bank][:, :], kz(set_, m)[:, kt * 128:(kt + 1) * 128], qT[:, qb * TB:(qb + 1) * TB], True, True,
                       rd=rdq, wr=[r_ps[bank]])

                smm(0)
                smm(1)
                for kt in range(16):
                    if kt + 2 < 16:
                        smm(kt + 2)
                    bank = SBANK[kt % 3]
                    pi = kt % 4
                    act(pb(pi), PS[bank][:, :], AF.Exp, rd=[r_ps[bank]], wr=[r_pb[pi]], scale=0.125)
                    mm(PS[6][:, :], V[:, 0, kt, :], pb(pi), kt == 0, kt == 15, rd=[r_pb[pi], r_uv[set_]],
                       wr=[r_ps[6]])
                    mm(PS[7][:, :], ONES, pb(pi), kt == 0, kt == 15, rd=[r_pb[pi], r_const], wr=[r_ps[7]])
                    while pending and pending[0][0] <= kt:
                        pending.pop(0)[1]()
                    if kt % 4 == 3 and kt < 15:
                        yield
                cp("dve", ft(6), PS[7][:, :], rd=[r_ps[7]], wr=[r_ft[6]])
                if m == 0:
                    cp("dve", ft(7), PS[6][:, :], rd=[r_ps[6]], wr=[r_ft[7]])
                    recip(ft(6), ft(6), rd=[r_ft[6]], wr=[r_ft[6]])
                    tt("dve", ft(7), ft(7), ft(6), ALU.mult, rd=[r_ft[7], r_ft[6]], wr=[r_ft[7]])
                else:
                    cp("dve", ft(8), PS[6][:, :], rd=[r_ps[6]], wr=[r_ft[8]])
                    recip(ft(6), ft(6), rd=[r_ft[6]], wr=[r_ft[6]])
                    tt("dve", ft(8), ft(8), ft(6), ALU.mult, rd=[r_ft[8], r_ft[6]], wr=[r_ft[8]])
                    stt("dve", ft(7), ft(8), SMALL[:, 0:1], ft(7), ALU.mult, ALU.add,
                        rd=[r_ft[8], r_ft[7], r_small], wr=[r_ft[7]])
                    pending = make_epilogue(qb)
                yield
        for _, fn in pending:
            fn()
        yield

    def attn_A(l, ui):
        u = units[ui]
        g = u["idx"]
        delta = u["delta"]
        set_ = ui % 2
        qT, kT, V = uq(set_), uk(set_), uv(set_)
        rdq = [r_uq[set_], r_uk[set_], r_uv[set_]]
        for s2 in range(2):
            rows = slice(64 * s2, 64 * s2 + 64)
            for qb in range(NB):
                if g < 2:
                    rels = list(range(-1, 5)) if g == 0 else list(range(4))
                    kts = [(4 * qb + r, r) for r in rels if 0 <= 4 * qb + r < 16]
                    n = len(kts)

                    def win(i):
                        rel = kts[i][1]
                        if i == 0:
                            return 0, TB
                        return max(0, 128 * rel - 64), min(TB, 128 * rel + 192)

                    def smm(i):
                        kt, rel = kts[i]
                        bank = 4 + i % 2
                        lo, hi = win(i)
                        mm(PS[bank][:, lo:hi], kz(set_, s2)[:, kt * 128:(kt + 1) * 128],
                           qT[:, qb * TB + lo:qb * TB + hi], True, True, rd=rdq, wr=[r_ps[bank]])

                    smm(0)
                    for i in range(n):
                        if i + 1 < n:
                            smm(i + 1)
                        kt, rel = kts[i]
                        bank = 4 + i % 2
                        pi = i % 4
                        lo, hi = win(i)
                        act(pb(pi)[:, lo:hi], PS[bank][:, lo:hi], AF.Exp, rd=[r_ps[bank]], wr=[r_pb[pi]], scale=0.125)
                        tt("dve", pb(pi)[:, lo:hi], pb(pi)[:, lo:hi], mask_(rel + 1)[:, lo:hi], ALU.mult,
                           rd=[r_pb[pi], r_const], wr=[r_pb[pi]])
                        mm(PS[6][:, lo:hi], V[:, 0, kt, :], pb(pi)[:, lo:hi], i == 0, i == n - 1,
                           rd=[r_pb[pi], r_uv[set_]], wr=[r_ps[6]])
                        mm(PS[7][:, lo:hi], ONES, pb(pi)[:, lo:hi], i == 0, i == n - 1, rd=[r_pb[pi], r_const],
                           wr=[r_ps[7]])
                        if i < n - 1:
                            yield
                else:
                    bank = 4 + qb % 2
                    for rel in range(4):
                        kt = 4 * qb + rel
                        mm(PS[bank][:, rel * 128:(rel + 1) * 128], kz(set_, s2)[:, kt * 128:(kt + 1) * 128],
                           qT[:, kt * 128:(kt + 1) * 128], True, True, rd=rdq, wr=[r_ps[bank]])
                    pi = qb % 3
                    act(pb(pi), PS[bank][:, :], AF.Exp, rd=[r_ps[bank]], wr=[r_pb[pi]], scale=0.125)
                    tt("dve", pb(pi), pb(pi), mask_(6), ALU.mult, rd=[r_pb[pi], r_const], wr=[r_pb[pi]])
                    for rel in range(4):
                        kt = 4 * qb + rel
                        mm(PS[6][:, rel * 128:(rel + 1) * 128], V[:, 0, kt, :], pb(pi)[:, rel * 128:(rel + 1) * 128],
                           True, True, rd=[r_pb[pi], r_uv[set_]], wr=[r_ps[6]])
                        mm(PS[7][:, rel * 128:(rel + 1) * 128], ONES, pb(pi)[:, rel * 128:(rel + 1) * 128],
                           True, True, rd=[r_pb[pi], r_const], wr=[r_ps[7]])
                if delta == 1:
                    ydst = y_(g)[rows, qb * TB:(qb + 1) * TB]
                    ddst = DTOT[rows, qb * TB:(qb + 1) * TB]
                    nsrc, dsrc = PS[6][rows, :], PS[7][rows, :]
                elif delta == 4:
                    ydst = y_(g).rearrange("p (n r) -> p r n", r=4)[rows, qb, :]
                    ddst = DTOT[:, :].rearrange("p (n r) -> p r n", r=4)[rows, qb, :]
                    nsrc, dsrc = PS[6][rows, :], PS[7][rows, :]
                else:
                    ydst = y_(g).rearrange("p (n r) -> p r n", r=16)[rows, 4 * qb:4 * qb + 4, :]
                    ddst = DTOT[:, :].rearrange("p (n r) -> p r n", r=16)[rows, 4 * qb:4 * qb + 4, :]
                    nsrc = PS[6][rows, :].rearrange("p (a b) -> p a b", a=4)
                    dsrc = PS[7][rows, :].rearrange("p (a b) -> p a b", a=4)
                cp("act", ydst, nsrc, rd=[r_ps[6]], wr=[r_yall[g]] + r_y[g])
                if g == 0:
                    cp("dve", ddst, dsrc, rd=[r_ps[7]], wr=[r_cder[4], r_cder[5]])
                else:
                    tt("dve", ddst, ddst, dsrc, ALU.add, rd=[r_ps[7], r_cder[4], r_cder[5]], wr=[r_cder[4], r_cder[5]])
                yield
        if g == 2:
            for tb in range(NB):
                recip(ft(6), DTOT[:, tb * TB:(tb + 1) * TB], rd=[r_cder[4], r_cder[5]], wr=[r_ft[6]])
                for gg in range(3):
                    ysl = y_(gg)[:, tb * TB:(tb + 1) * TB]
                    tt("pool", ysl, ysl, ft(6), ALU.mult, rd=[r_ft[6], r_yall[gg]] + r_y[gg], wr=[r_y[gg][tb], r_yall[gg]])
                yield

    def attn_C(l, ui):
        u = units[ui]
        p = u["idx"]
        set_ = ui % 2
        qT, kT, V = uq(set_), uk(set_), uv(set_)
        DQF, DQB, DKF, DKB = ctab(4 + p), ctab(6 + p), ctab(8 + p), ctab(10 + p)
        g128f, g128b = SMALL[:, 8 + p:9 + p], SMALL[:, 10 + p:11 + p]
        for g4 in range(4):
            bank = 3
            tp = PS[bank][:, :].bitcast(BF16)[:, 0:512]
            for ti in range(4):
                t = g4 * 4 + ti
                P.op("pe", lambda e, t=t, ti=ti, tp=tp: e.transpose(tp[:, ti * 128:(ti + 1) * 128],
                                                                      kT[:, t * 128:(t + 1) * 128], IDENT),
                     rd=[r_uk[set_], r_const], wr=[r_ps[bank]])
            tp3 = tp.rearrange("p (t c) -> p t c", c=128)
            tt("dve", KDF[:, g4 * 4:(g4 + 1) * 4, :], tp3, DKF.unsqueeze(1).to_broadcast([128, 4, 128]), ALU.mult,
               rd=[r_ps[bank], r_ctab], wr=[r_cder[0]])
            tt("dve", KDB[:, g4 * 4:(g4 + 1) * 4, :], tp3, DKB.unsqueeze(1).to_broadcast([128, 4, 128]), ALU.mult,
               rd=[r_ps[bank], r_ctab], wr=[r_cder[1]])
            q3 = qT[:, g4 * TB:(g4 + 1) * TB].rearrange("p (t c) -> p t c", c=128)
            tt("pool", QDF[:, g4 * TB:(g4 + 1) * TB].rearrange("p (t c) -> p t c", c=128), q3,
               DQF.unsqueeze(1).to_broadcast([128, 4, 128]), ALU.mult, rd=[r_uq[set_], r_ctab], wr=[r_cder[2]])
            tt("pool", QDB[:, g4 * TB:(g4 + 1) * TB].rearrange("p (t c) -> p t c", c=128), q3,
               DQB.unsqueeze(1).to_broadcast([128, 4, 128]), ALU.mult, rd=[r_uq[set_], r_ctab], wr=[r_cder[3]])
            yield
        P.op("pool", lambda e: e.memset(S32[:, :], 0.0), wr=r_s32)
        P.op("pool", lambda e: e.memset(STF[:, 0, :], 0.0), wr=[r_cder[4]])
        P.op("pool", lambda e: e.memset(STB[:, 15, :], 0.0), wr=[r_cder[5]])
        for step in range(15):
            for d in range(2):
                c = step if d == 0 else 15 - step
                KD = KDF if d == 0 else KDB
                ST = STF if d == 0 else STB
                gcol = g128f if d == 0 else g128b
                s32 = S32[:, d * 128:(d + 1) * 128]
                bank = 4 + d
                for hh in range(2):
                    mm(PS[bank][:, hh * 128:(hh + 1) * 128], KD[:, c, :], V[:, hh, c, :], True, True,
                       rd=[r_cder[d], r_uv[set_]], wr=[r_ps[bank]])
                for hh in range(2):
                    rws = slice(64 * hh, 64 * hh + 64)
                    stt("dve", s32[rws, :], s32[rws, :], gcol[rws, :], PS[bank][rws, hh * 128:(hh + 1) * 128],
                        ALU.mult, ALU.add, rd=[r_ps[bank], r_s32[d], r_small], wr=[r_s32[d]])
                cn = c + 1 if d == 0 else c - 1
                cp("act", ST[:, cn, :], s32, rd=[r_s32[d]], wr=[r_cder[4 + d]])
            if step % 4 == 3:
                yield
        yield
        pend = [None]
        for hh in range(2):
            rws = slice(64 * hh, 64 * hh + 64)
            h = 2 * p + hh
            ch = 7 + h
            for cb in range(4):
                sbank = 4 + cb % 2
                for ci in range(4):
                    c = 4 * cb + ci
                    mm(PS[sbank][:, ci * 128:(ci + 1) * 128], kT[rws, c * 128:(c + 1) * 128],
                       qT[rws, c * 128:(c + 1) * 128], True, True, rd=[r_uq[set_], r_uk[set_]], wr=[r_ps[sbank]])
                pi = cb % 3
                tt("dve", pb(pi).rearrange("p (t c) -> p t c", c=128),
                   PS[sbank][:, :].rearrange("p (t c) -> p t c", c=128),
                   ctab(h).unsqueeze(1).to_broadcast([128, 4, 128]), ALU.mult, rd=[r_ps[sbank], r_ctab],
                   wr=[r_pb[pi]])
                for ci in range(4):
                    c = 4 * cb + ci
                    o = PS[6][:, ci * 128:(ci + 1) * 128]
                    mm(o, V[:, hh, c, :], pb(pi)[:, ci * 128:(ci + 1) * 128], True, False,
                       rd=[r_pb[pi], r_uv[set_]], wr=[r_ps[6]])
                    mm(o, STF[rws, c, :], QDF[rws, c * 128:(c + 1) * 128], False, False,
                       rd=[r_cder[4], r_cder[2]], wr=[r_ps[6]])
                    mm(o, STB[rws, c, :], QDB[rws, c * 128:(c + 1) * 128], False, True,
                       rd=[r_cder[5], r_cder[3]], wr=[r_ps[6]])
                if pend[0] is not None:
                    pend[0]()
                fo = 7 + (hh * 4 + cb) % 2
                cp("dve", ft(fo), PS[6][:, :], rd=[r_ps[6]], wr=[r_ft[fo]])

                def epi(fo=fo, ch=ch, cb=cb):
                    act(pb(3), ft(fo), AF.Square, rd=[r_ft[fo]], wr=[r_pb[3]])
                    mm(PS[1][:, :], ONES, pb(3), True, True, rd=[r_pb[3], r_const], wr=[r_ps[1]])
                    rstd_from(PS[1][:, :], r_ps[1], ft(6), r_ft[6], 1.0 / 128)
                    stt("dve", y_(ch)[:, cb * TB:(cb + 1) * TB], ft(fo), PCOL[:, 21:22], ft(6), ALU.mult, ALU.mult,
                        rd=[r_ft[fo], r_ft[6], r_pcol], wr=[r_y[ch][cb], r_yall[ch]])

                pend[0] = epi
                yield
        if pend[0] is not None:
            pend[0]()
        yield

    def layer_params(l):
        lam_init = 0.8 - 0.6 * math.exp(-0.3 * (l + l0))
        P.dma("sp", "par", lambda e: e.dma_start(out=PCOL[:, :], in_=pcol_d[l]), wr=[r_pcol])
        P.dma("sp", "par", lambda e: e.dma_start(out=CTAB[:, 8 * 128:12 * 128].rearrange("p (c n) -> p c n", n=128),
                                                   in_=arow_d[l].rearrange("c p n -> p c n")), wr=[r_ctab])
        rp = [r_pcol]
        tt("dve", SMALL[:, 2:3], PCOL[:, 22:23], PCOL[:, 23:24], ALU.mult, rd=rp, wr=[r_small])
        tt("dve", SMALL[:, 3:4], PCOL[:, 24:25], PCOL[:, 25:26], ALU.mult, rd=rp, wr=[r_small])
        mm(PS[2][:, 0:2], ONESF[0:64, :], SMALL[0:64, 2:4], True, True, rd=[r_small, r_const], wr=[r_ps[2]])
        act(SMALL[:, 4:6], PS[2][:, 0:2], AF.Exp, rd=[r_ps[2]], wr=[r_small])
        tt("dve", SMALL[:, 6:7], SMALL[:, 5:6], SMALL[:, 4:5], ALU.subtract, rd=[r_small], wr=[r_small])
        ts("dve", SMALL[:, 0:1], SMALL[:, 6:7], -lam_init, None, ALU.add, None, rd=[r_small], wr=[r_small])
        ts("dve", SMALL[:, 1:2], PCOL[:, 20:21], 1.0 - lam_init, None, ALU.mult, None, rd=rp + [r_small],
           wr=[r_small])
        act(SMALL[:, 12:16], PCOL[:, 26:30], AF.Exp, rd=rp + [r_small], wr=[r_small])
        ts("dve", SMALL[:, 12:16], SMALL[:, 12:16], -1.0, None, ALU.mult, None, rd=[r_small], wr=[r_small])
        act(SMALL[:, 8:12], SMALL[:, 12:16], AF.Exp, rd=[r_small], wr=[r_small], scale=128.0)
        act(SMALL[:, 16:24], PCOL[:, 30:38], AF.Exp, rd=rp + [r_small], wr=[r_small])
        ts("dve", SMALL[:, 16:24], SMALL[:, 16:24], -1.0, None, ALU.mult, None, rd=[r_small], wr=[r_small])
        rs = [r_small, r_const]
        for h in range(4):
            act(ctab(h), RELF, AF.Exp, rd=rs, wr=[r_ctab], scale=SMALL[:, 16 + h:17 + h])
            tt("dve", ctab(h), ctab(h), MKF8, ALU.mult, rd=[r_ctab, r_const], wr=[r_ctab])
            act(ctab(12), RELB, AF.Exp, rd=rs + [r_ctab], wr=[r_ctab], scale=SMALL[:, 20 + h:21 + h])
            tt("dve", ctab(12), ctab(12), MKB8, ALU.mult, rd=[r_ctab, r_const], wr=[r_ctab])
            tt("dve", ctab(h), ctab(h), ctab(12), ALU.add, rd=[r_ctab], wr=[r_ctab])
        for p in range(2):
            act(ctab(4 + p), POS1, AF.Exp, rd=rs + [r_ctab], wr=[r_ctab], scale=SMALL[:, 12 + p:13 + p])
            act(ctab(6 + p), POSB, AF.Exp, rd=rs + [r_ctab], wr=[r_ctab], scale=SMALL[:, 14 + p:15 + p])
            ts("dve", ctab(4 + p), ctab(4 + p), 0.125, None, ALU.mult, None, rd=[r_ctab], wr=[r_ctab])
            ts("dve", ctab(6 + p), ctab(6 + p), 0.125, None, ALU.mult, None, rd=[r_ctab], wr=[r_ctab])
            for d in range(2):
                dstt = ctab(8 + 2 * d + p)
                src = dstt
                act(dstt, src, AF.Exp, rd=rp + [r_ctab], wr=[r_ctab])
                ts("dve", dstt, dstt, -1.0, None, ALU.mult, None, rd=[r_ctab], wr=[r_ctab])
                act(dstt, dstt, AF.Exp, rd=[r_ctab, r_const], wr=[r_ctab], scale=(TOKCR if d == 0 else TOKC))

    def interleave(a, b):
        da = db = False
        while not (da and db):
            if not da:
                try:
                    next(a)
                except StopIteration:
                    da = True
            if b is None:
                db = True
            if not db:
                try:
                    next(b)
                except StopIteration:
                    db = True

    def drain(g):
        for _ in g:
            pass

    stop_state = {}
    attn_fn = {"A": attn_A, "B": attn_B, "C": attn_C}

    def ck(name):
        if stop == name:
            raise _Stop()

    def layer(l):
        layer_params(l)
        ck('params')
        for u in units:
            push_unit_weights(l, u)
        for h in range(4):
            WS.push(w_in_d[l, C_CG + h])
        for j in range(8):
            WS.push(w_br_d[l, j], 11)
            for i in range(3):
                WS.push(w_in_d[l, C_GATE + 8 * i + j])
        for oc in range(8):
            WS.push(w_o_d[l, oc])
        for hg in range(4):
            for c in range(8):
                WS.push(w_m1_d[l, hg * 8 + c])
            for oc in range(8):
                WS.push(w_m2_d[l, hg * 8 + oc])
        norm(0)
        ck('norm1')
        for kc in range(KC):
            P.dma("sp", "xsp", lambda e, kc=kc: e.dma_start(out=xsp_d[kc * 128:(kc + 1) * 128, :],
                                                           in_=XY[:, kc * S:(kc + 1) * S]),
                  rd=r_x[kc], wr=[r_xsp])
        P.barrier(dma_keys=("xsp",))
        ck('spill')
        if stop in ('xc', 'xc2'):
            drain(inproj(l, 7))
            ck('xc')
            drain(attn_C(l, 7))
            ck('xc2')
        drain(inproj(l, 0))
        ck('inproj0')
        for ui in range(len(units)):
            a = attn_fn[units[ui]["kind"]](l, ui)
            b = inproj(l, ui + 1) if ui + 1 < len(units) else None
            interleave(a, b)
            ck('u%d' % ui)
        P.barrier()
        ck('mix')
        if dbg and l == 0:
            P.dma("sp", "dbg", lambda e: e.dma_start(out=ydbg_d.rearrange("(c p) s -> p c s", p=128),
                                                       in_=Ybf.rearrange("p (c s) -> p c s", s=S)), rd=[], wr=[])
            P.barrier(dma_keys=("dbg",))

        def nbank():
            b = rot["ps"] % 8
            rot["ps"] += 1
            return b

        for h in range(4):
            slot = WS.pop()
            ch = 7 + h
            for tb in range(NB):
                bank = nbank()
                proj_fm(slot, tb, bank)
                j2 = tb % 2
                act(ub(j2), PS[bank][:, :], AF.Silu, rd=[r_ps[bank]], wr=[r_ub[j2]])
                ysl = y_(ch)[:, tb * TB:(tb + 1) * TB]
                tt("pool", ysl, ysl, ub(j2), ALU.mult, rd=[r_ub[j2], r_y[ch][tb], r_yall[ch]], wr=[r_y[ch][tb]])
            WS.release(slot)
        ybr = ((0, 3), (3, 7), (7, 11))
        cnt4 = 0
        for j in range(8):
            bslot = WS.pop()
            wb = wslot(bslot, 11)
            for i in range(3):
                gslot = WS.pop()
                c0, c1 = ybr[i]
                for tb in range(NB):
                    gb = nbank()
                    proj_fm(gslot, tb, gb)
                    sg = 4 + cnt4 % 2
                    tmp = 6 + cnt4 % 2
                    cnt4 += 1
                    act(ft(sg), PS[gb][:, :], AF.Sigmoid, rd=[r_ps[gb]], wr=[r_ft[sg]])
                    bb = nbank()
                    for ch in range(c0, c1):
                        mm(PS[bb][:, :], wb[:, ch, :], y_(ch)[:, tb * TB:(tb + 1) * TB], ch == c0, ch == c1 - 1,
                           rd=[r_w[bslot], r_y[ch][tb], r_yall[ch]], wr=[r_ps[bb]])
                    if i == 0:
                        tt("dve", ft(tb), PS[bb][:, :], ft(sg), ALU.mult, rd=[r_ps[bb], r_ft[sg]], wr=[r_ft[tb]])
                    else:
                        tt("dve", ft(tmp), PS[bb][:, :], ft(sg), ALU.mult, rd=[r_ps[bb], r_ft[sg]], wr=[r_ft[tmp]])
                        if i == 1:
                            tt("pool", ft(tb), ft(tb), ft(tmp), ALU.add, rd=[r_ft[tb], r_ft[tmp]], wr=[r_ft[tb]])
                        else:
                            tt("pool", m_(j, tb), ft(tb), ft(tmp), ALU.add, rd=[r_ft[tb], r_ft[tmp]],
                               wr=[r_m[j][tb]])
                WS.release(gslot)
            WS.release(bslot)
        P.barrier()
        if stop == '4a':
            for kc in range(KC):
                P.dma('pool', 'dbg', lambda e, kc=kc: e.dma_start(out=out_d[kc * 128:(kc + 1) * 128, :], in_=M[:, kc * S:(kc + 1) * S]), rd=[], wr=[])
            P.barrier(dma_keys=('dbg',))
            stop_state['skip_out'] = True
        ck('4a')
        for kc in range(KC):
            P.dma("sp", "xld", lambda e, kc=kc: e.dma_start(out=XY[:, kc * S:(kc + 1) * S],
                                                           in_=xsp_d[kc * 128:(kc + 1) * 128, :]),
                  rd=[r_xsp], wr=r_x[kc])
        for oc in range(8):
            slot = WS.pop()
            w = wslot(slot)
            for tb in range(NB):
                bank = nbank()
                for kc in range(KC):
                    mm(PS[bank][:, :], w[:, kc, :], m_(kc, tb), kc == 0, kc == KC - 1,
                       rd=[r_w[slot], r_m[kc][tb]], wr=[r_ps[bank]])
                tt("dve", x_(oc, tb), x_(oc, tb), PS[bank][:, :], ALU.add, rd=[r_ps[bank], r_x[oc][tb]],
                   wr=[r_x[oc][tb]])
            WS.release(slot)
        ck('4b')
        norm(8)
        ck('norm2')
        for hg in range(4):
            for c in range(8):
                slot = WS.pop()
                for tb in range(NB):
                    bank = nbank()
                    proj_fm(slot, tb, bank)
                    j2 = (c * NB + tb) % 2
                    act(ft(j2), PS[bank][:, :], AF.Relu, rd=[r_ps[bank]], wr=[r_ft[j2]])
                    tt("pool", m_(c, tb), ft(j2), ft(j2), ALU.mult, rd=[r_ft[j2]], wr=[r_m[c][tb]])
                WS.release(slot)
            for oc in range(8):
                slot = WS.pop()
                w = wslot(slot)
                for tb in range(NB):
                    bank = nbank()
                    for c in range(8):
                        mm(PS[bank][:, :], w[:, c, :], m_(c, tb), c == 0, c == 7,
                           rd=[r_w[slot], r_m[c][tb]], wr=[r_ps[bank]])
                    tt("dve", x_(oc, tb), x_(oc, tb), PS[bank][:, :], ALU.add, rd=[r_ps[bank], r_x[oc][tb]],
                       wr=[r_x[oc][tb]])
                WS.release(slot)
        if dbg and l == 0:
            for kc in range(KC):
                P.dma("sp", "dbg", lambda e, kc=kc: e.dma_start(out=xdbg_d[kc * 128:(kc + 1) * 128, :],
                                                               in_=XY[:, kc * S:(kc + 1) * S]), rd=r_x[kc], wr=[])

    try:
        for l in range(nl):
            layer(l)
    except _Stop:
        pass

    for kc in range(KC if not stop_state.get('skip_out') else 0):
        P.dma("sp", "out", lambda e, kc=kc: e.dma_start(out=out_d[kc * 128:(kc + 1) * 128, :],
                                                       in_=XY[:, kc * S:(kc + 1) * S]), rd=r_x[kc], wr=[])
    final_waits = tuple((k, n) for k, n in P.dcnt.items() if k in ("out", "dbg", "xsp"))
    P.q["sp"].append((final_waits, None, None))

    keys = list(P.ENG) + sorted(P.dcnt.keys())
    sems = {k: es.enter_context(nc.semaphore("s_" + k)) for k in keys}
    stats = {e: len(P.q[e]) for e in P.ENG}

    def run(e, name):
        for waits, fn, inc in P.q[name]:
            for k, n in waits:
                e.wait_ge(sems[k], n)
            if fn is not None:
                fn(e).then_inc(sems[inc[0]], inc[1])

    with nc.Block() as block:
        @block.tensor
        def _(e):
            run(e, "pe")

        @block.scalar
        def _(e):
            run(e, "act")

        @block.vector
        def _(e):
            run(e, "dve")

        @block.gpsimd
        def _(e):
            run(e, "pool")

        @block.sync
        def _(e):
            run(e, "sp")
    es.close()
    return nc, stats


def _rope_tables():
    pos = np.arange(S, dtype=np.float32)

    def tab(rot, theta, hd):
        inv = (1.0 / (np.float32(theta) ** (np.arange(0, rot, 2, dtype=np.float32) / np.float32(rot)))).astype(np.float32)
        ang = (pos[:, None] * inv[None, :]).astype(np.float32)
        c, s = np.cos(ang).astype(np.float32), np.sin(ang).astype(np.float32)
        r2 = rot // 2
        cosF = np.ones((hd, S), np.float32)
        sinF = np.zeros((hd, S), np.float32)
        cosF[0:r2] = c.T
        cosF[r2:2 * r2] = c.T
        sinF[0:r2] = s.T
        sinF[r2:2 * r2] = s.T
        return np.tile(cosF, (128 // hd, 1)), np.tile(sinF, (128 // hd, 1))

    cab, sab = tab(16, 500000.0, 64)
    cc, sc = tab(64, 10000.0, 64)
    return np.ascontiguousarray(np.stack([cab, sab, cc, sc]).astype(np.float32))


def _rot_mat(r2, hd=64):
    R = np.zeros((128, 128), np.float32)
    for b in range(128 // hd):
        for d in range(r2):
            R[b * hd + d + r2, b * hd + d] = -1.0
            R[b * hd + d, b * hd + d + r2] = 1.0
    return R


def _constants():
    ones = np.ones((128, 128), np.float32)
    bones = np.zeros((128, 128), np.float32)
    bones[0:64, 0:64] = 1.0
    bones[64:128, 64:128] = 1.0
    ident = np.eye(128, dtype=np.float32)
    cmats = np.stack([ones, bones, _rot_mat(8), _rot_mat(32), ident]).astype(np.float32)
    k = np.arange(128)[:, None]
    q = np.arange(512)[None, :]
    masks = []
    for rel in range(-1, 5):
        masks.append((np.abs(q - (128 * rel + k)) <= 64).astype(np.float32))
    band = (np.abs(np.arange(128)[None, :] - k) <= 64).astype(np.float32)
    masks.append(np.tile(band, (1, 4)))
    cmasks = np.stack(masks).astype(np.float32)
    m = np.arange(128)[:, None].astype(np.float32)
    n = np.arange(128)[None, :].astype(np.float32)
    relf = np.maximum(n - m, 0.0)
    relb = np.maximum(m - n, 0.0)
    mkf8 = (n >= m).astype(np.float32) * 0.125
    mkb8 = (m >= n).astype(np.float32) * 0.125
    pos1 = np.broadcast_to(n + 1.0, (128, 128))
    posb = np.broadcast_to(128.0 - n, (128, 128))
    tok = np.arange(128, dtype=np.float32)[:, None]
    cf = np.concatenate([relf, relb, mkf8, mkb8, pos1, posb, ones, tok, 127.0 - tok], axis=1).astype(np.float32)
    return cmats, cmasks, np.ascontiguousarray(cf)


def _prep_layers(inp, layers):
    f = lambda a: np.asarray(a, dtype=np.float32)
    nl = len(layers)
    w_in = f(inp["w_in"])[layers].reshape(nl, 8, 128, 57, 128).transpose(0, 3, 2, 1, 4)
    wbr = np.concatenate([f(inp["w_br_a"])[layers], f(inp["w_br_b"])[layers], f(inp["w_br_c"])[layers]], axis=1)
    w_br = wbr.reshape(nl, 11, 128, 8, 128).transpose(0, 3, 2, 1, 4)
    w_o = f(inp["w_o"])[layers].reshape(nl, 8, 128, 8, 128).transpose(0, 3, 2, 1, 4)
    w_m1 = f(inp["w_mlp1"])[layers].reshape(nl, 8, 128, 32, 128).transpose(0, 3, 2, 1, 4)
    w_m2 = f(inp["w_mlp2"])[layers].reshape(nl, 4, 8, 128, 8, 128).transpose(0, 1, 4, 3, 2, 5).reshape(nl, 32, 128, 8, 128)
    pcol = np.zeros((nl, 128, NPC), np.float32)
    arow = np.zeros((nl, 4, 128, 128), np.float32)
    for i, l in enumerate(layers):
        pcol[i, :, 0:8] = f(inp["norm1_g"])[l].reshape(8, 128).T
        pcol[i, :, 8:16] = f(inp["norm2_g"])[l].reshape(8, 128).T
        pcol[i, :, 16] = np.tile(f(inp["a_q_norm_g"])[l], 2)
        pcol[i, :, 17] = np.tile(f(inp["a_k_norm_g"])[l], 2)
        pcol[i, :, 18] = np.tile(f(inp["b_q_norm_g"])[l], 2)
        pcol[i, :, 19] = np.tile(f(inp["b_k_norm_g"])[l], 2)
        pcol[i, :, 20] = f(inp["b_out_norm_g"])[l]
        pcol[i, :, 21] = f(inp["c_out_norm_g"])[l]
        pcol[i, :, 22] = np.tile(f(inp["b_lambda_q1"])[l], 2)
        pcol[i, :, 23] = np.tile(f(inp["b_lambda_k1"])[l], 2)
        pcol[i, :, 24] = np.tile(f(inp["b_lambda_q2"])[l], 2)
        pcol[i, :, 25] = np.tile(f(inp["b_lambda_k2"])[l], 2)
        af, ab = f(inp["c_decay_f"])[l], f(inp["c_decay_b"])[l]
        for p in range(2):
            pcol[i, :, 26 + p] = np.repeat(af[2 * p:2 * p + 2], 64)
            pcol[i, :, 28 + p] = np.repeat(ab[2 * p:2 * p + 2], 64)
            arow[i, p] = np.repeat(af[2 * p:2 * p + 2], 64)[None, :]
            arow[i, 2 + p] = np.repeat(ab[2 * p:2 * p + 2], 64)[None, :]
        pcol[i, :, 30:34] = af[None, :]
        pcol[i, :, 34:38] = ab[None, :]
    c = np.ascontiguousarray
    return dict(w_in=c(w_in), w_br=c(w_br), w_o=c(w_o), w_m1=c(w_m1), w_m2=c(w_m2), pcol=pcol, arow=arow)


_CACHE = {}


def _get_prog(nl, l0):
    if (nl, l0) not in _CACHE:
        _CACHE[(nl, l0)] = build(nl, l0=l0)[0]
    return _CACHE[(nl, l0)]


FUSED = True


def kernel(**inputs):
    x = np.asarray(inputs["x"], dtype=np.float32)
    B = x.shape[0]
    cmats, cmasks, cf = _constants()
    rope = _rope_tables()
    consts = dict(c_mats=cmats, c_masks=cmasks, c_f32=cf, rope=rope)
    xT = [np.ascontiguousarray(x[b].T) for b in range(B)]
    groups = [list(range(DEPTH))] if FUSED else [[l] for l in range(DEPTH)]
    for layers in groups:
        nc = _get_prog(len(layers), layers[0])
        shared = _prep_layers(inputs, layers)
        shared.update(consts)
        in_maps = []
        for b in range(B):
            d = dict(shared)
            d["xT"] = xT[b]
            in_maps.append(d)
        res = run_bass_kernel_spmd(nc, in_maps, core_ids=list(range(B)))
        xT = [np.ascontiguousarray(np.asarray(r["outT"], dtype=np.float32)) for r in res.results]
    out = np.stack([t.T for t in xT], axis=0)
    return np.ascontiguousarray(out.astype(np.float32))
```
